# Optimizing a Trainium2 kernel written in Bass

```python
import math
import jax, jax.numpy as jnp
from jax import lax
import numpy as np

D_MODEL = 1024
BATCH = 32
SEQ = 256
DEPTH = 2
DEC_BATCH = 8
DEC_SEQ = 4096
PAST_LEN = 512

GRID_W = 64
N_HEADS = 8
N_KV_HEADS = 2
HEAD_DIM = 64
GQ = N_HEADS // N_KV_HEADS
ATTN_W = N_HEADS * HEAD_DIM
KV_W = N_KV_HEADS * HEAD_DIM
WINDOW = 128
BLOCK = 128
ROPE_BASE = 10000.0
ROPE_PAIRS = HEAD_DIM // 4
LRU_W = 512
LRU_BLOCKS = 8
LRU_BW = LRU_W // LRU_BLOCKS
LRU_C = 8.0
CONV_W = 4
CONV_LEFT = 2
S5_W = 512
S5_GROUP_CH = 16
S5_GROUPS = S5_W // S5_GROUP_CH
S5_N = 64
D_FF = 2816
N_MOD = 9
RMS_EPS = 1e-6
NEG_INF = -1e30
IN_SPLITS = (ATTN_W, ATTN_W + KV_W, ATTN_W + 2 * KV_W, ATTN_W + 2 * KV_W + LRU_W,
             ATTN_W + 2 * KV_W + 2 * LRU_W, ATTN_W + 2 * KV_W + 2 * LRU_W + S5_W)
IN_COLS = IN_SPLITS[-1] + 3 * D_MODEL

kernel_name = "hybrid_lru_s5_swa_diffusion_step"

F32 = jnp.float32


def _rms(x, g):
    xf = x.astype(F32)
    y = xf * lax.rsqrt(jnp.mean(xf * xf, axis=-1, keepdims=True) + RMS_EPS)
    return (y * g.astype(F32)).astype(x.dtype)


def _swiglu(h, wg, wu, wd):
    return (jax.nn.silu(h @ wg) * (h @ wu)) @ wd


def _axial_rope(x):
    T = x.shape[1]
    n_rows = T // GRID_W
    row = jnp.repeat(jnp.arange(n_rows), GRID_W).astype(F32)
    col = jnp.tile(jnp.arange(GRID_W), n_rows).astype(F32)
    inv = 1.0 / (ROPE_BASE ** (jnp.arange(ROPE_PAIRS, dtype=F32) / ROPE_PAIRS))

    def rot(xa, pos):
        ang = pos[:, None] * inv
        cos = jnp.cos(ang)[None, :, None, :]
        sin = jnp.sin(ang)[None, :, None, :]
        x1, x2 = jnp.split(xa, 2, axis=-1)
        return jnp.concatenate([x1 * cos - x2 * sin, x2 * cos + x1 * sin], axis=-1)

    xr, xc = jnp.split(x.astype(F32), 2, axis=-1)
    return jnp.concatenate([rot(xr, row), rot(xc, col)], axis=-1).astype(x.dtype)


def _scores(q, k):
    return jnp.einsum("bqhgd,bkhd->bhgqk", q.astype(F32), k.astype(F32)) * (HEAD_DIM ** -0.5)


def _sink_attend(scores, values, sink):
    sink_l = sink.astype(F32).reshape(1, N_KV_HEADS, GQ, 1)
    m = sink_l
    for s in scores:
        m = jnp.maximum(m, s.max(axis=-1))
    probs = [jnp.exp(s - m[..., None]) for s in scores]
    denom = jnp.exp(sink_l - m) + sum(p.sum(axis=-1) for p in probs)
    return sum(jnp.einsum("bhgqk,bkhd->bqhgd", p / denom[..., None], v.astype(F32))
               for p, v in zip(probs, values))


def _context_attention(q, k, v, sink):
    B, T = q.shape[:2]
    nb = T // BLOCK
    qb = jnp.moveaxis(q.reshape(B, nb, BLOCK, N_KV_HEADS, GQ, HEAD_DIM), 1, 0)
    o = lax.map(lambda qn: _sink_attend([_scores(qn, k)], [v], sink), qb)
    return jnp.moveaxis(o, 0, 1).reshape(B, T, ATTN_W)


def _latent_attention(q, k, v, ck, cv, sink):
    B, T = q.shape[:2]
    nb = T // BLOCK
    qb = jnp.moveaxis(q.reshape(B, nb, BLOCK, N_KV_HEADS, GQ, HEAD_DIM), 1, 0)

    def band(t):
        tp = jnp.pad(t, ((0, 0), (BLOCK, BLOCK), (0, 0), (0, 0)))
        tp = tp.reshape(B, nb + 2, BLOCK, N_KV_HEADS, HEAD_DIM)
        tb = jnp.concatenate([tp[:, :nb], tp[:, 1:nb + 1], tp[:, 2:]], axis=2)
        return jnp.moveaxis(tb, 1, 0)

    kb, vb = band(k), band(v)
    blk = jnp.arange(nb)[:, None, None]
    q_pos = blk * BLOCK + jnp.arange(BLOCK)[None, :, None]
    k_pos = (blk - 1) * BLOCK + jnp.arange(3 * BLOCK)[None, None, :]
    valid = (jnp.abs(k_pos - q_pos) <= WINDOW) & (k_pos >= 0) & (k_pos < T)

    def one_block(args):
        qn, kn, vn, mn = args
        s_loc = jnp.where(mn, _scores(qn, kn), NEG_INF)
        s_ctx = _scores(qn, ck)
        return _sink_attend([s_loc, s_ctx], [vn, cv], sink)

    o = lax.map(one_block, (qb, kb, vb, valid))
    return jnp.moveaxis(o, 0, 1).reshape(B, T, ATTN_W)


def _conv_centred(x, w, b):
    T = x.shape[1]
    xp = jnp.pad(x, ((0, 0), (CONV_LEFT, CONV_W - 1 - CONV_LEFT), (0, 0)))
    return sum(xp[:, j:j + T] * w[j] for j in range(CONV_W)) + b


def _blockdiag(x, w):
    B, T, _ = x.shape
    y = jnp.einsum("btnc,ncd->btnd", x.reshape(B, T, LRU_BLOCKS, LRU_BW), w.astype(F32))
    return y.reshape(B, T, LRU_W)


def _linear_scan(a, b, reverse):
    def combine(lo, hi):
        a_l, b_l = lo
        a_h, b_h = hi
        return a_l * a_h, a_h * b_l + b_h
    return lax.associative_scan(combine, (a, b), reverse=reverse, axis=1)[1]


def _complex_scan(a_re, a_im, b_re, b_im, reverse):
    def combine(lo, hi):
        ar_l, ai_l, br_l, bi_l = lo
        ar_h, ai_h, br_h, bi_h = hi
        return (ar_h * ar_l - ai_h * ai_l, ar_h * ai_l + ai_h * ar_l,
                ar_h * br_l - ai_h * bi_l + br_h, ar_h * bi_l + ai_h * br_l + bi_h)
    _, _, h_re, h_im = lax.associative_scan(combine, (a_re, a_im, b_re, b_im), reverse=reverse, axis=1)
    return h_re, h_im


def _rglru_branch(xl, yl, lp, init):
    dt = xl.dtype
    xc = _conv_centred(xl, lp["w_conv"], lp["b_conv"]).astype(F32)
    y = jnp.zeros_like(xc)
    finals = []
    for d, rev in enumerate((False, True)):
        r = jax.nn.sigmoid(_blockdiag(xc, lp["w_lru_a"][d]) + lp["b_lru_a"][d].astype(F32))
        i = jax.nn.sigmoid(_blockdiag(xc, lp["w_lru_x"][d]) + lp["b_lru_x"][d].astype(F32))
        log_a = -LRU_C * r * jax.nn.softplus(-lp["lru_lambda"][d].astype(F32))
        a = jnp.exp(log_a)
        b = jnp.sqrt(-jnp.expm1(2.0 * log_a)) * (i * xc)
        if init is not None:
            e0 = -1 if rev else 0
            b = b.at[:, e0].add(a[:, e0] * init[:, d].astype(F32))
        hs = _linear_scan(a, b, rev)
        y = y + hs
        if init is None:
            finals.append(hs[:, 0] if rev else hs[:, -1])
    out = y.astype(dt) * jax.nn.gelu(yl)
    fin = jnp.stack(finals, axis=1).astype(dt) if init is None else None
    return out, fin


def _s5_discretise(lam_re, lam_im, log_step, b_re, b_im):
    lam_re = jnp.minimum(lam_re.astype(F32), -1e-4)
    lam_im = lam_im.astype(F32)
    step = jnp.exp(log_step.astype(F32))[:, None]
    mag = jnp.exp(lam_re * step)
    ang = lam_im * step
    ab_re, ab_im = mag * jnp.cos(ang), mag * jnp.sin(ang)
    den = lam_re * lam_re + lam_im * lam_im
    num_re = ab_re - 1.0
    coef_re = (num_re * lam_re + ab_im * lam_im) / den
    coef_im = (ab_im * lam_re - num_re * lam_im) / den
    b_re, b_im = b_re.astype(F32), b_im.astype(F32)
    bb_re = coef_re[..., None] * b_re - coef_im[..., None] * b_im
    bb_im = coef_re[..., None] * b_im + coef_im[..., None] * b_re
    return ab_re, ab_im, bb_re, bb_im


def _s5_branch(u, lp, init):
    B, T, _ = u.shape
    dt = u.dtype
    uf = u.astype(F32).reshape(B, T, S5_GROUPS, S5_GROUP_CH)
    y = jnp.zeros_like(uf)
    finals = []
    for d, rev in enumerate((False, True)):
        ab_re, ab_im, bb_re, bb_im = _s5_discretise(lp["s5_lambda_re"][d], lp["s5_lambda_im"][d],
                                                    lp["s5_log_step"][d], lp["s5_b_re"][d], lp["s5_b_im"][d])
        b_re = jnp.einsum("btgc,gnc->btgn", uf, bb_re)
        b_im = jnp.einsum("btgc,gnc->btgn", uf, bb_im)
        if init is not None:
            e0 = -1 if rev else 0
            h0r = init[:, d, 0].astype(F32)
            h0i = init[:, d, 1].astype(F32)
            b_re = b_re.at[:, e0].add(ab_re * h0r - ab_im * h0i)
            b_im = b_im.at[:, e0].add(ab_re * h0i + ab_im * h0r)
        a_re = jnp.broadcast_to(ab_re, (1, T, S5_GROUPS, S5_N))
        a_im = jnp.broadcast_to(ab_im, (1, T, S5_GROUPS, S5_N))
        h_re, h_im = _complex_scan(a_re, a_im, b_re, b_im, rev)
        y = (y + jnp.einsum("btgn,gcn->btgc", h_re, lp["s5_c_re"][d].astype(F32))
               - jnp.einsum("btgn,gcn->btgc", h_im, lp["s5_c_im"][d].astype(F32)))
        if init is None:
            fe = 0 if rev else -1
            finals.append(jnp.stack([h_re[:, fe], h_im[:, fe]], axis=1))
    y = y.reshape(B, T, S5_W) + lp["s5_d"].astype(F32) * u.astype(F32)
    z = jax.nn.gelu(y).astype(dt) @ lp["w_glu"]
    za, zb = jnp.split(z, 2, axis=-1)
    fin = jnp.stack(finals, axis=1).astype(dt) if init is None else None
    return za * jax.nn.sigmoid(zb), fin


def _mixer(h, lp, ctx):
    B, T, _ = h.shape
    dt = h.dtype
    proj = h @ lp["w_in"]
    q, k, v, xl, yl, u, gates = jnp.split(proj, IN_SPLITS, axis=-1)
    q = q.reshape(B, T, N_HEADS, HEAD_DIM)
    k = k.reshape(B, T, N_KV_HEADS, HEAD_DIM)
    v = v.reshape(B, T, N_KV_HEADS, HEAD_DIM)
    if ctx is None:
        att = _context_attention(q, k, v, lp["attn_sink"])
        lru_init, ssm_init = None, None
    else:
        ck, cv, lru_init, ssm_init = ctx
        att = _latent_attention(_axial_rope(q), _axial_rope(k), v, ck, cv, lp["attn_sink"])
    lru_out, lru_fin = _rglru_branch(xl, yl, lp, lru_init)
    s5_out, ssm_fin = _s5_branch(u, lp, ssm_init)
    g_a, g_b, g_c = jnp.split(jax.nn.sigmoid(gates), 3, axis=-1)
    merged = (g_a * (lru_out @ lp["w_o_lru"]) + g_b * s5_out
              + g_c * (att.astype(dt) @ lp["w_o_attn"]))
    out = merged @ lp["w_out"]
    new_ctx = (k, v, lru_fin, ssm_fin) if ctx is None else None
    return out, new_ctx


def _layer(x, mod, lp, ctx):
    s0, c0, g0, s1, c1, g1, s2, c2, g2 = jnp.split(mod, N_MOD, axis=-1)
    h = _rms(x, lp["g_pre"][0]) * (1 + c0) + s0
    f = _swiglu(h, lp["w_ffn_gate"][0], lp["w_ffn_up"][0], lp["w_ffn_down"][0])
    x = x + 0.5 * g0 * _rms(f, lp["g_post"][0])
    h = _rms(x, lp["g_pre"][1]) * (1 + c1) + s1
    mix, new_ctx = _mixer(h, lp, ctx)
    x = x + g1 * _rms(mix, lp["g_post"][1])
    h = _rms(x, lp["g_pre"][2]) * (1 + c2) + s2
    f = _swiglu(h, lp["w_ffn_gate"][1], lp["w_ffn_up"][1], lp["w_ffn_down"][1])
    x = x + 0.5 * g2 * _rms(f, lp["g_post"][2])
    return x, new_ctx


def setup_inputs(seed: int = 0) -> dict:
    key = jax.random.key(seed)
    keys = iter(jax.random.split(key, 48))

    def nrm(shape, scale):
        return jax.random.normal(next(keys), shape, F32) * scale

    def unif(shape, lo, hi):
        return jax.random.uniform(next(keys), shape, F32, lo, hi)

    L = DEPTH
    x_prompt = nrm((BATCH, SEQ, D_MODEL), 1.0)
    x_sample = nrm((DEC_BATCH, DEC_SEQ, D_MODEL), 1.0)
    c = nrm((DEC_BATCH, D_MODEL), 1.0)
    cache_k = nrm((DEC_BATCH, L, PAST_LEN, N_KV_HEADS, HEAD_DIM), 1.0)
    cache_v = nrm((DEC_BATCH, L, PAST_LEN, N_KV_HEADS, HEAD_DIM), 1.0)
    state_lru = nrm((DEC_BATCH, L, 2, LRU_W), 0.5)
    state_ssm = nrm((DEC_BATCH, L, 2, 2, S5_GROUPS, S5_N), 0.1)
    c_ctx = nrm((D_MODEL,), 1.0)
    w_mod = nrm((L, D_MODEL, N_MOD * D_MODEL), 0.01)
    b_mod = nrm((L, N_MOD * D_MODEL), 0.02)
    g_pre = 1.0 + nrm((L, 3, D_MODEL), 0.02)
    g_post = 1.0 + nrm((L, 3, D_MODEL), 0.02)
    w_ffn_gate = nrm((L, 2, D_MODEL, D_FF), D_MODEL ** -0.5)
    w_ffn_up = nrm((L, 2, D_MODEL, D_FF), D_MODEL ** -0.5)
    w_ffn_down = nrm((L, 2, D_FF, D_MODEL), D_FF ** -0.5)
    w_in = nrm((L, D_MODEL, IN_COLS), D_MODEL ** -0.5)
    w_conv = nrm((L, CONV_W, LRU_W), CONV_W ** -0.5)
    b_conv = nrm((L, LRU_W), 0.02)
    w_lru_a = nrm((L, 2, LRU_BLOCKS, LRU_BW, LRU_BW), LRU_BW ** -0.5)
    b_lru_a = nrm((L, 2, LRU_W), 0.02)
    w_lru_x = nrm((L, 2, LRU_BLOCKS, LRU_BW, LRU_BW), LRU_BW ** -0.5)
    b_lru_x = nrm((L, 2, LRU_W), 0.02)
    a0 = unif((L, 2, LRU_W), 0.9, 0.999)
    lru_lambda = jnp.log(a0) - jnp.log1p(-a0)
    s5_lambda_re = -0.5 + nrm((L, 2, S5_GROUPS, S5_N), 0.01)
    s5_lambda_im = math.pi * jnp.arange(S5_N, dtype=F32) + nrm((L, 2, S5_GROUPS, S5_N), 0.01)
    s5_log_step = unif((L, 2, S5_GROUPS), math.log(1e-3), math.log(1e-1))
    s5_b_re = nrm((L, 2, S5_GROUPS, S5_N, S5_GROUP_CH), (2 * S5_GROUP_CH) ** -0.5)
    s5_b_im = nrm((L, 2, S5_GROUPS, S5_N, S5_GROUP_CH), (2 * S5_GROUP_CH) ** -0.5)
    s5_c_re = nrm((L, 2, S5_GROUPS, S5_GROUP_CH, S5_N), 0.5)
    s5_c_im = nrm((L, 2, S5_GROUPS, S5_GROUP_CH, S5_N), 0.5)
    s5_d = nrm((L, S5_W), 1.0)
    w_glu = nrm((L, S5_W, 2 * D_MODEL), S5_W ** -0.5)
    attn_sink = nrm((L, N_HEADS), 0.5)
    w_o_lru = nrm((L, LRU_W, D_MODEL), LRU_W ** -0.5)
    w_o_attn = nrm((L, ATTN_W, D_MODEL), ATTN_W ** -0.5)
    w_out = nrm((L, D_MODEL, D_MODEL), D_MODEL ** -0.5)
    return {"x_prompt": x_prompt, "x_sample": x_sample, "c": c,
            "cache_k": cache_k, "cache_v": cache_v, "state_lru": state_lru, "state_ssm": state_ssm,
            "c_ctx": c_ctx, "w_mod": w_mod, "b_mod": b_mod, "g_pre": g_pre, "g_post": g_post,
            "w_ffn_gate": w_ffn_gate, "w_ffn_up": w_ffn_up, "w_ffn_down": w_ffn_down,
            "w_in": w_in, "w_conv": w_conv, "b_conv": b_conv,
            "w_lru_a": w_lru_a, "b_lru_a": b_lru_a, "w_lru_x": w_lru_x, "b_lru_x": b_lru_x,
            "lru_lambda": lru_lambda, "s5_lambda_re": s5_lambda_re, "s5_lambda_im": s5_lambda_im,
            "s5_log_step": s5_log_step, "s5_b_re": s5_b_re, "s5_b_im": s5_b_im,
            "s5_c_re": s5_c_re, "s5_c_im": s5_c_im, "s5_d": s5_d, "w_glu": w_glu,
            "attn_sink": attn_sink, "w_o_lru": w_o_lru, "w_o_attn": w_o_attn, "w_out": w_out}


def reference(x_prompt, x_sample, c, cache_k, cache_v, state_lru, state_ssm, c_ctx, w_mod, b_mod,
              g_pre, g_post, w_ffn_gate, w_ffn_up, w_ffn_down, w_in, w_conv, b_conv,
              w_lru_a, b_lru_a, w_lru_x, b_lru_x, lru_lambda, s5_lambda_re, s5_lambda_im,
              s5_log_step, s5_b_re, s5_b_im, s5_c_re, s5_c_im, s5_d, w_glu, attn_sink,
              w_o_lru, w_o_attn, w_out):
    yp = x_prompt
    ys = x_sample
    new_k, new_v, new_lru, new_ssm = [], [], [], []
    for l in range(DEPTH):
        lp = {"g_pre": g_pre[l], "g_post": g_post[l], "w_ffn_gate": w_ffn_gate[l],
              "w_ffn_up": w_ffn_up[l], "w_ffn_down": w_ffn_down[l], "w_in": w_in[l],
              "w_conv": w_conv[l], "b_conv": b_conv[l], "w_lru_a": w_lru_a[l], "b_lru_a": b_lru_a[l],
              "w_lru_x": w_lru_x[l], "b_lru_x": b_lru_x[l], "lru_lambda": lru_lambda[l],
              "s5_lambda_re": s5_lambda_re[l], "s5_lambda_im": s5_lambda_im[l],
              "s5_log_step": s5_log_step[l], "s5_b_re": s5_b_re[l], "s5_b_im": s5_b_im[l],
              "s5_c_re": s5_c_re[l], "s5_c_im": s5_c_im[l], "s5_d": s5_d[l], "w_glu": w_glu[l],
              "attn_sink": attn_sink[l], "w_o_lru": w_o_lru[l], "w_o_attn": w_o_attn[l],
              "w_out": w_out[l]}
        mod_ctx = jax.nn.silu(c_ctx) @ w_mod[l] + b_mod[l]
        yp, (k_l, v_l, lru_l, ssm_l) = _layer(yp, mod_ctx, lp, None)
        new_k.append(k_l)
        new_v.append(v_l)
        new_lru.append(lru_l)
        new_ssm.append(ssm_l)
        mod_lat = (jax.nn.silu(c) @ w_mod[l] + b_mod[l])[:, None, :]
        ys, _ = _layer(ys, mod_lat, lp, (cache_k[:, l], cache_v[:, l], state_lru[:, l], state_ssm[:, l]))
    new_cache_k = jnp.stack(new_k, axis=1)
    new_cache_v = jnp.stack(new_v, axis=1)
    new_state_lru = jnp.stack(new_lru, axis=1)
    new_state_ssm = jnp.stack(new_ssm, axis=1)
    return (yp, ys, new_cache_k, new_cache_v, new_state_lru, new_state_ssm)
```

```python
import math
import contextlib
import numpy as np
import concourse.bass as bass
import concourse.mybir as mybir
from concourse.bass_utils import run_bass_kernel_spmd

F32 = mybir.dt.float32
BF16 = mybir.dt.bfloat16
AF = mybir.ActivationFunctionType
ALU = mybir.AluOpType
AX = mybir.AxisListType

NDMA_SEM = 8
L = 2
D = 1024
TS = 4096
TP = 256
NPB = 4
NTOK = TS + NPB * TP
TT = 512
NT = NTOK // TT
DFF = 2816
NFC = DFF // 128
WINP = 6016
C_Q, C_K, C_QS, C_KS, C_V, C_XL, C_YL, C_U, C_G = 0, 512, 640, 1152, 1280, 1408, 1920, 2432, 2944
ARENA_F = 52352
EPS = 1e-6


class Buf:
    __slots__ = ("w", "r")

    def __init__(self):
        self.w = {}
        self.r = {}


class Op:
    __slots__ = ("eng", "fn", "waits", "idx", "needed", "semval", "is_dma", "slot")

    def __init__(self, eng, fn):
        self.eng = eng
        self.fn = fn
        self.waits = {}
        self.needed = False
        self.semval = 0
        self.is_dma = False
        self.slot = 0


class Prog:
    ENGS = ("pe", "act", "dve", "pool", "sp")

    def __init__(self, nc):
        self.nc = nc
        self.ops = {e: [] for e in self.ENGS}
        self.ndma = {e: 0 for e in self.ENGS}
        self.pending = {e: {} for e in self.ENGS}
        self.mute = False

    def _add(self, eng, fn, reads, writes, is_dma):
        if self.mute:
            return None
        op = Op(eng, fn)
        op.is_dma = is_dma
        lst = self.ops[eng]
        op.idx = len(lst)
        lst.append(op)
        deps = op.waits
        if self.pending[eng]:
            deps.update(self.pending[eng])
            self.pending[eng] = {}
        if is_dma:
            didx = self.ndma[eng]
            self.ndma[eng] += 1
            op.slot = didx
            pkey = ("d", eng, didx % NDMA_SEM)
            pidx = didx
            if didx >= NDMA_SEM and deps.get(pkey, -1) < didx - NDMA_SEM:
                deps[pkey] = didx - NDMA_SEM
        else:
            pkey = ("c", eng)
            pidx = op.idx
        for b in reads:
            for k, v in b.w.items():
                if deps.get(k, -1) < v:
                    deps[k] = v
        for b in writes:
            for k, v in b.w.items():
                if deps.get(k, -1) < v:
                    deps[k] = v
            for k, v in b.r.items():
                if deps.get(k, -1) < v:
                    deps[k] = v
        for b in reads:
            if b.r.get(pkey, -1) < pidx:
                b.r[pkey] = pidx
        for b in writes:
            b.w = {pkey: pidx}
            b.r = {}
        if eng == "pe" and not is_dma:
            deps.pop(("c", "pe"), None)
        return op

    def op(self, eng, fn, reads=(), writes=()):
        return self._add(eng, fn, reads, writes, False)

    def dma(self, eng, fn, reads=(), writes=()):
        return self._add(eng, fn, reads, writes, True)

    def barrier(self):
        deps = {}
        for e in self.ENGS:
            last = None
            for op in reversed(self.ops[e]):
                if not op.is_dma:
                    last = op.idx
                    break
            if last is not None:
                deps[("c", e)] = last
            n = self.ndma[e]
            for s in range(NDMA_SEM):
                if n > s:
                    li = ((n - 1 - s) // NDMA_SEM) * NDMA_SEM + s
                    deps[("d", e, s)] = li
        for e in self.ENGS:
            p = self.pending[e]
            for k, v in deps.items():
                if p.get(k, -1) < v:
                    p[k] = v

    def emit(self):
        nc = self.nc
        for e in self.ENGS:
            seen = {}
            for op in self.ops[e]:
                new = {}
                for k, v in op.waits.items():
                    if seen.get(k, -1) >= v:
                        continue
                    seen[k] = v
                    new[k] = v
                op.waits = new
        for e in self.ENGS:
            for op in self.ops[e]:
                for k, v in op.waits.items():
                    if k[0] == "c":
                        self.ops[k[1]][v].needed = True
        for e in self.ENGS:
            c = 0
            for op in self.ops[e]:
                if op.is_dma:
                    continue
                if op.needed:
                    c += 1
                op.semval = c
        handles = {"pe": "tensor", "act": "scalar", "dve": "vector", "pool": "gpsimd", "sp": "sync"}
        with contextlib.ExitStack() as st:
            csem = {e: st.enter_context(nc.semaphore("c_" + e)) for e in self.ENGS}
            dsem = {e: [st.enter_context(nc.semaphore("d_%s_%d" % (e, i))) for i in range(NDMA_SEM)]
                    for e in self.ENGS if self.ndma[e]}
            block = st.enter_context(nc.Block())
            prog = self

            def run(e, eng):
                for op in prog.ops[e]:
                    for k, v in op.waits.items():
                        if k[0] == "c":
                            eng.wait_ge(csem[k[1]], prog.ops[k[1]][v].semval)
                        else:
                            eng.wait_ge(dsem[k[1]][k[2]], 16 * (v // NDMA_SEM + 1))
                    ins = op.fn(eng)
                    if op.is_dma:
                        ins.then_inc(dsem[e][op.slot % NDMA_SEM], 16)
                    elif op.needed:
                        ins.then_inc(csem[e], 1)

            for e in self.ENGS:
                if not self.ops[e]:
                    continue
                getattr(block, handles[e])(lambda eng, e=e: run(e, eng))


def mkap(t, offset, pairs):
    return bass.AP(t, offset, [list(p) for p in pairs])


def view(ap2, shape):
    if len(shape) == 1:
        return ap2
    names = " ".join("a%d" % i for i in range(len(shape)))
    kw = {"a%d" % i: s for i, s in enumerate(shape)}
    return ap2.rearrange("p (%s) -> p %s" % (names, names), **kw)


class Tl:
    __slots__ = ("ap", "b", "bs")

    def __init__(self, ap, nb=0):
        self.ap = ap
        self.b = Buf()
        self.bs = [Buf() for _ in range(nb)]


class Arena:
    def __init__(self, t, size):
        self.t = t
        self.size = size
        self.top = 0

    def reset(self):
        self.top = 0

    def f32(self, *shape, nb=0):
        n = int(np.prod(shape))
        assert self.top + n <= self.size, ("arena overflow", self.top, n)
        ap = self.t[:, self.top:self.top + n]
        self.top += n
        return Tl(view(ap, shape), nb)

    def bf16(self, *shape, nb=0):
        n = int(np.prod(shape))
        nf = (n + 1) // 2
        assert self.top + nf <= self.size, ("arena overflow", self.top, nf)
        ap = self.t[:, self.top:self.top + nf].bitcast(BF16)[:, 0:n]
        self.top += nf
        return Tl(view(ap, shape), nb)


def bcast_free(ap, n):
    return mkap(ap.tensor, ap.offset, [list(ap.ap[0]), [0, n]])


class Kern:
    def __init__(self, dbg=False):
        self.dbg = dbg
        nc = self.nc = bass.Bass("TRN2", target_bir_lowering=False)
        self.P = Prog(nc)
        self.st = contextlib.ExitStack()
        I = self.I = {}
        O = self.O = {}

        def inp(name, shape, dt=F32):
            I[name] = nc.dram_tensor(name, list(shape), dt, kind="ExternalInput").ap()

        def outp(name, shape, dt=F32):
            O[name] = nc.dram_tensor(name, list(shape), dt, kind="ExternalOutput").ap()

        inp("xin", [NTOK, D]); inp("cc", [2, D]); inp("cache_k", [L, 512, 128]); inp("cache_v", [L, 512, 128])
        inp("state_lru", [L, 2, 512]); inp("state_ssm", [L, 2, 2, 32, 64])
        inp("w_mod", [L, D, 9 * D]); inp("b_mod", [L, 9 * D]); inp("g_pre", [L, 3, D]); inp("g_post", [L, 3, D])
        inp("w_ffn_gate", [L, 2, D, DFF]); inp("w_ffn_up", [L, 2, D, DFF]); inp("w_ffn_down", [L, 2, DFF, D])
        inp("w_in_p", [L, D, WINP]); inp("w_conv", [L, 4, 512]); inp("b_conv", [L, 512])
        inp("w_lru_a", [L, 2, 8, 64, 64]); inp("b_lru_a", [L, 2, 512]); inp("w_lru_x", [L, 2, 8, 64, 64])
        inp("b_lru_x", [L, 2, 512]); inp("lru_lambda", [L, 2, 512])
        inp("s5_lambda_re", [L, 2, 32, 64]); inp("s5_lambda_im", [L, 2, 32, 64]); inp("s5_log_step", [L, 2, 32])
        inp("s5_b_re", [L, 2, 32, 64, 16]); inp("s5_b_im", [L, 2, 32, 64, 16])
        inp("s5_c_re", [L, 2, 32, 16, 64]); inp("s5_c_im", [L, 2, 32, 16, 64]); inp("s5_d", [L, 512])
        inp("w_glu", [L, 512, 2048]); inp("attn_sink", [L, 8]); inp("w_o_lru", [L, 512, D])
        inp("w_o_attn_p", [L, 512, D]); inp("w_out", [L, D, D])
        inp("cst", [128, 5 * 128]); inp("rope", [2, 128, TS])
        outp("y", [NTOK, D]); outp("nk", [NPB, L, TP, 128]); outp("nv", [NPB, L, TP, 128])
        outp("nlru", [NPB, L, 2, 512]); outp("nssm", [NPB, L, 2, 2, 32, 64])
        self.obufs = {k: Buf() for k in O}
        if dbg:
            outp("d_sc", [128, L * 144])

        def scr(name, shape, dt):
            kind = "ExternalOutput" if dbg else "Internal"
            t = nc.dram_tensor(name, list(shape), dt, kind=kind).ap()
            return Tl(t)

        self.XT = scr("s_xt", [8, 128, NTOK], F32)
        self.Qs = scr("s_q", [4, 128, NTOK], BF16)
        self.Ks = scr("s_k", [128, NTOK], BF16)
        self.Vs = scr("s_v", [NTOK, 130], BF16)
        self.XLs = scr("s_xl", [4, 128, NTOK], F32)
        self.YLs = scr("s_yl", [4, 128, NTOK], BF16)
        self.UFs = scr("s_uf", [NT, 128, 32 * 64], BF16)
        self.Gs = scr("s_g", [24, 128, NTOK], BF16)
        self.ATs = scr("s_att", [4, 128, NTOK], BF16)
        self.LRs = scr("s_lru", [4, 128, NTOK], BF16)
        self.SYs = scr("s_s5y", [4, 128, NTOK], BF16)
        self.PFd = [scr("s_pf%d" % l, [128, 2 * 16 * 2 * 128], BF16) for l in range(L)]
        self.Qd = [scr("s_qd%d" % l, [128, 2 * 16 * 2 * 128], BF16) for l in range(L)]
        self.MLd = [scr("s_ml%d" % l, [128, 32 * 128], BF16) for l in range(L)]
        self.A12 = [scr("s_a12%d" % l, [128, 128], F32) for l in range(L)]

        self.sb_t = self.st.enter_context(nc.sbuf_tensor("arena", [128, ARENA_F], F32))
        self.pc_t = self.st.enter_context(nc.sbuf_tensor("persist", [128, 832], F32))
        self.ps_t = self.st.enter_context(nc.psum_tensor("psum", [128, 4096], F32))
        self.A = Arena(self.sb_t, ARENA_F)
        self.PA = Arena(self.pc_t, 832)
        self.pbufs = [Buf() for _ in range(8)]

    def bank(self, i):
        return self.ps_t[:, i * 512:(i + 1) * 512]

    def mm(self, out, lhsT, rhs, start, stop, reads, writes):
        self.P.op("pe", lambda e: e.matmul(out, lhsT=lhsT, rhs=rhs, start=start, stop=stop), reads, writes)

    def tr(self, out, in_, ident, reads, writes):
        self.P.op("pe", lambda e: e.transpose(out, in_, ident), reads, writes)

    def act(self, out, in_, func, reads, writes, scale=1.0, bias=0.0):
        self.P.op("act", lambda e: e.activation(out=out, in_=in_, func=func, scale=scale, bias=bias), reads, writes)

    def tt(self, eng, out, in0, in1, op, reads, writes):
        self.P.op(eng, lambda e: e.tensor_tensor(out=out, in0=in0, in1=in1, op=op), reads, writes)

    def ts(self, eng, out, in0, s1, s2, op0, op1, reads, writes):
        if s2 is None:
            self.P.op(eng, lambda e: e.tensor_scalar(out=out, in0=in0, scalar1=s1, scalar2=None, op0=op0), reads, writes)
        else:
            self.P.op(eng, lambda e: e.tensor_scalar(out=out, in0=in0, scalar1=s1, scalar2=s2, op0=op0, op1=op1), reads, writes)

    def stt(self, out, in0, scalar, in1, op0, op1, reads, writes):
        self.P.op("dve", lambda e: e.scalar_tensor_tensor(out=out, in0=in0, scalar=scalar, in1=in1, op0=op0, op1=op1), reads, writes)

    def cp(self, eng, out, in_, reads, writes):
        if eng == "act":
            self.P.op("act", lambda e: e.copy(out=out, in_=in_), reads, writes)
        else:
            self.P.op(eng, lambda e: e.tensor_copy(out=out, in_=in_), reads, writes)

    def ms(self, eng, out, val, writes):
        self.P.op(eng, lambda e: e.memset(out, val), (), writes)

    def dma(self, q, out, in_, reads, writes, slow=False):
        assert not slow
        shp = tuple(out.shape)
        if len(shp) >= 3 and shp[0] * shp[1] > 256 and shp[1] > 1 and tuple(in_.shape)[:2] == shp[:2]:
            step = max(1, 256 // shp[0])
            for a in range(0, shp[1], step):
                e_ = min(a + step, shp[1])
                o_ = out[:, a:e_]
                i_ = in_[:, a:e_]
                self.P.dma(q, lambda e, o_=o_, i_=i_: e.dma_start(out=o_, in_=i_), reads, writes)
            return
        self.P.dma(q, lambda e: e.dma_start(out=out, in_=in_), reads, writes)

    def vecT(self, dst, src_rows, n, writes):
        stg = self.A.f32(128)
        self.P.dma("sp", lambda e: e.dma_start(out=stg.ap[0:n, :], in_=src_rows), [], [stg.b])
        self.tr(self.bank(7)[:, 0:n], stg.ap[0:n, :], self.identf.ap[0:n, 0:n], [stg.b, self.identf.b], [self.pbufs[7]])
        return self.bank(7)[:, 0:n], self.pbufs[7]

    def recip(self, out, in_, reads, writes):
        self.P.op("dve", lambda e: e.reciprocal(out=out, in_=in_), reads, writes)

    def phase(self):
        self.P.barrier()
        self.A.reset()
        self.pbufs = [Buf() for _ in range(8)]

    def prologue(self):
        A, PA, I = self.A, self.PA, self.I
        cst = A.f32(5, 128)
        self.dma("sp", cst.ap, I["cst"].rearrange("p (a b) -> p a b", a=5), [], [cst.b])
        self.identf = PA.f32(128)
        self.identb = PA.bf16(128)
        self.onesb = PA.bf16(128)
        self.onesf = PA.f32(128)
        self.mprev = PA.bf16(128)
        self.mnext = PA.bf16(128)
        self.cp("dve", self.identf.ap, cst.ap[:, 0, :], [cst.b], [self.identf.b])
        self.cp("dve", self.identb.ap, cst.ap[:, 0, :], [cst.b], [self.identb.b])
        self.cp("dve", self.mprev.ap, cst.ap[:, 1, :], [cst.b], [self.mprev.b])
        self.cp("dve", self.mnext.ap, cst.ap[:, 2, :], [cst.b], [self.mnext.b])
        self.ms("pool", self.onesb.ap, 1.0, [self.onesb.b])
        self.ms("pool", self.onesf.ap, 1.0, [self.onesf.b])
        self.SC = PA.f32(L, 3, 3, 8, 2)
        ccT = A.f32(8, 2)
        src, sb_ = self.vecT(None, I["cc"].rearrange("w (kc p) -> (w kc) p", p=128), 16, None)
        self.cp("dve", ccT.ap.rearrange("p kc w -> p w kc"), view(src, (2, 8)), [sb_], [ccT.b])
        sg = A.f32(8, 2)
        self.act(sg.ap, ccT.ap, AF.Sigmoid, [ccT.b], [sg.b])
        self.tt("dve", ccT.ap, ccT.ap, sg.ap, ALU.mult, [ccT.b, sg.b], [ccT.b])
        wm = [A.f32(8, 1152), A.f32(8, 1152)]
        modt = A.f32(L, 72, 2)
        bm = A.f32(L, 72)
        gp = A.f32(L, 3, 8)
        gq = A.f32(L, 3, 8)
        for l in range(L):
            src, sb_ = self.vecT(None, I["b_mod"][l].rearrange("(c p) -> c p", p=128), 72, None)
            self.cp("dve", bm.ap[:, l, :], src, [sb_], [bm.b])
            src, sb_ = self.vecT(None, I["g_pre"][l].rearrange("i (c p) -> (i c) p", p=128), 24, None)
            self.cp("dve", gp.ap[:, l].rearrange("p i c -> p (i c)"), src, [sb_], [gp.b])
            src, sb_ = self.vecT(None, I["g_post"][l].rearrange("i (c p) -> (i c) p", p=128), 24, None)
            self.cp("dve", gq.ap[:, l].rearrange("p i c -> p (i c)"), src, [sb_], [gq.b])
        pm = self.bank(0)
        pmb = self.pbufs[0]
        k = 0
        for l in range(L):
            for piece in range(8):
                w_ = wm[k % 2]
                q = "sp" if k % 2 == 0 else "act"
                k += 1
                src = I["w_mod"][l][:, piece * 1152:(piece + 1) * 1152].rearrange("(kc p) c -> p kc c", p=128)
                self.dma(q, w_.ap, src, [], [w_.b])
                for cch in range(9):
                    col = (l * 72 + piece * 9 + cch) * 2
                    for kc in range(8):
                        self.mm(pm[:, col:col + 2], w_.ap[:, kc, cch * 128:(cch + 1) * 128], ccT.ap[:, kc, :],
                                kc == 0, kc == 7, [w_.b, ccT.b], [pmb])
        self.cp("dve", modt.ap, view(pm[:, 0:L * 144], (L, 72, 2)), [pmb], [modt.b])
        for w in range(2):
            self.tt("dve", modt.ap[:, :, :, w], modt.ap[:, :, :, w], bm.ap, ALU.add, [modt.b, bm.b], [modt.b])
        for l in range(L):
            m5 = modt.ap[:, l].rearrange("p (i k c) w -> p i k c w", i=3, k=3)
            for w in range(2):
                self.stt(self.SC.ap[:, l, 0, :, :, w], m5[:, :, 1, :, w], 1.0, gp.ap[:, l], ALU.add, ALU.mult,
                         [modt.b, gp.b], [self.SC.b])
                self.cp("dve", self.SC.ap[:, l, 1, :, :, w], m5[:, :, 0, :, w], [modt.b], [self.SC.b])
                self.tt("dve", self.SC.ap[:, l, 2, :, :, w], m5[:, :, 2, :, w], gq.ap[:, l], ALU.mult,
                        [modt.b, gq.b], [self.SC.b])
            for i in (0, 2):
                self.ts("dve", self.SC.ap[:, l, 2, i], self.SC.ap[:, l, 2, i], 0.5, None, ALU.mult, None,
                        [self.SC.b], [self.SC.b])
        if self.dbg:
            self.dma("sp", self.O["d_sc"], self.SC.ap.rearrange("p l k i c w -> p (l k i c w)"), [self.SC.b], [])
        for l in range(L):
            self.s5_prep(l, cst)

    def scal(self, l, kind, i, c, w):
        return self.SC.ap[:, l, kind, i, c, w:w + 1]

    def s5_prep(self, l, cst_unused=None):
        self.phase()
        A, I = self.A, self.I
        cst = A.f32(5, 128)
        self.dma("sp", cst.ap, I["cst"].rearrange("p (a b) -> p a b", a=5), [], [cst.b])
        idf = self.identf
        pb = self.pbufs
        lre_r = A.f32(128); lim_r = A.f32(128); lst_r = A.f32(2); lst_x = A.f32(128)
        self.dma("sp", lre_r.ap[0:32, :], I["s5_lambda_re"][l].rearrange("d (gp g2) n -> (d gp) (g2 n)", g2=2), [], [lre_r.b])
        self.dma("sp", lim_r.ap[0:32, :], I["s5_lambda_im"][l].rearrange("d (gp g2) n -> (d gp) (g2 n)", g2=2), [], [lim_r.b])
        self.dma("sp", lst_r.ap[0:32, :], I["s5_log_step"][l].rearrange("d (gp g2) -> (d gp) g2", g2=2), [], [lst_r.b])
        a_ = lst_r.ap[0:32, :]
        self.cp("dve", view(lst_x.ap[0:32, :], (2, 64)), mkap(a_.tensor, a_.offset, [list(a_.ap[0]), [1, 2], [0, 64]]),
                [lst_r.b], [lst_x.b])
        RR = A.f32(8192)
        braw = [Tl(view(RR.ap[:, c * 2048:(c + 1) * 2048], (128, 16))) for c in range(2)]
        craw = [Tl(view(RR.ap[:, (2 + c) * 2048:(3 + c) * 2048], (16, 2, 64))) for c in range(2)]
        for c, nm in enumerate(("s5_b_re", "s5_b_im")):
            self.dma("act", braw[c].ap[0:32], I[nm][l].rearrange("d (gp g2) n ci -> (d gp) (g2 n) ci", g2=2), [], [braw[c].b])
        for c, nm in enumerate(("s5_c_re", "s5_c_im")):
            srcv = I[nm][l].rearrange("d (gp g2) co n -> (d gp) g2 co n", g2=2)
            for g2 in range(2):
                self.dma("act", craw[c].ap[0:32, :, g2, :], srcv[:, g2], [], [craw[c].b])
        draw = A.f32(16); dx = A.f32(8, 16)
        self.dma("sp", draw.ap[0:32, :], I["s5_d"][l].rearrange("(g ci) -> g ci", ci=16), [], [draw.b])
        a_ = draw.ap[0:32, :]
        self.cp("dve", dx.ap[0:32], mkap(a_.tensor, a_.offset, [list(a_.ap[0]), [0, 8], [1, 16]]), [draw.b], [dx.b])
        sc = A.f32(40, 32)
        names = {}

        def S(nm):
            if nm not in names:
                names[nm] = len(names)
                assert len(names) <= 40
            return sc.ap[:, names[nm], :]

        scb = sc.b
        pt = self.bank(0)
        self.tr(pt[:, 0:32], lre_r.ap[0:32, :], idf.ap[0:32, 0:32], [lre_r.b, idf.b], [pb[0]])
        self.tr(pt[:, 32:64], lim_r.ap[0:32, :], idf.ap[0:32, 0:32], [lim_r.b, idf.b], [pb[0]])
        self.tr(pt[:, 64:96], lst_x.ap[0:32, :], idf.ap[0:32, 0:32], [lst_x.b, idf.b], [pb[0]])
        self.tr(pt[:, 96:128], dx.ap[0:32].rearrange("p a b -> p (a b)"), idf.ap[0:32, 0:32], [dx.b, idf.b], [pb[0]])
        self.cp("dve", S("lr"), pt[:, 0:32], [pb[0]], [scb])
        self.cp("dve", S("li"), pt[:, 32:64], [pb[0]], [scb])
        self.cp("dve", S("ls"), pt[:, 64:96], [pb[0]], [scb])
        dcol = A.f32(32)
        self.cp("dve", dcol.ap, pt[:, 96:128], [pb[0]], [dcol.b])
        BC = []
        for idx, raw in enumerate(braw + craw):
            bk = self.bank(1 + idx % 2)
            bb = pb[1 + idx % 2]
            for j in range(16):
                if idx < 2:
                    src = raw.ap[0:32, :, j]
                else:
                    src = raw.ap[0:32, j].rearrange("p a b -> p (a b)")
                self.tr(bk[:, j * 32:(j + 1) * 32], src, idf.ap[0:32, 0:32], [raw.b, idf.b], [bb])
            t = A.f32(16, 32)
            self.cp("act", t.ap, view(bk, (16, 32)), [bb], [t.b])
            BC.append(t)
        Bre, Bim, Cre, Cim = BC

        def dv(out, a, b, op):
            self.tt("dve", out, a, b, op, [scb], [scb])

        self.ts("dve", S("lr"), S("lr"), -1e-4, None, ALU.min, None, [scb], [scb])
        self.act(S("step"), S("ls"), AF.Exp, [scb], [scb])
        dv(S("xre"), S("lr"), S("step"), ALU.mult)
        dv(S("ang"), S("li"), S("step"), ALU.mult)
        self.act(S("mag"), S("xre"), AF.Exp, [scb], [scb])
        hp = A.f32(1)
        self.ms("dve", hp.ap, math.pi / 2, [hp.b])
        self.act(S("s"), S("ang"), AF.Sin, [scb], [scb], scale=1.0 / 16)
        self.act(S("c"), S("ang"), AF.Sin, [scb, hp.b], [scb], scale=-1.0 / 16, bias=hp.ap[:, 0:1])
        for _ in range(4):
            dv(S("t1"), S("c"), S("c"), ALU.mult)
            dv(S("t2"), S("s"), S("s"), ALU.mult)
            dv(S("t3"), S("c"), S("s"), ALU.mult)
            dv(S("c"), S("t1"), S("t2"), ALU.subtract)
            self.ts("dve", S("s"), S("t3"), 2.0, None, ALU.mult, None, [scb], [scb])
        dv(S("are"), S("mag"), S("c"), ALU.mult)
        dv(S("aim"), S("mag"), S("s"), ALU.mult)
        dv(S("t1"), S("lr"), S("lr"), ALU.mult)
        dv(S("t2"), S("li"), S("li"), ALU.mult)
        dv(S("den"), S("t1"), S("t2"), ALU.add)
        self.recip(S("rden"), S("den"), [scb], [scb])
        self.ts("dve", S("nre"), S("are"), -1.0, None, ALU.add, None, [scb], [scb])
        dv(S("t1"), S("nre"), S("lr"), ALU.mult)
        dv(S("t2"), S("aim"), S("li"), ALU.mult)
        dv(S("t1"), S("t1"), S("t2"), ALU.add)
        dv(S("cre"), S("t1"), S("rden"), ALU.mult)
        dv(S("t1"), S("aim"), S("lr"), ALU.mult)
        dv(S("t2"), S("nre"), S("li"), ALU.mult)
        dv(S("t1"), S("t1"), S("t2"), ALU.subtract)
        dv(S("cim"), S("t1"), S("rden"), ALU.mult)
        dv(S("t1"), S("mag"), S("mag"), ALU.mult)
        self.recip(S("t2"), S("t1"), [scb], [scb])
        dv(S("iare"), S("are"), S("t2"), ALU.mult)
        dv(S("t3"), S("aim"), S("t2"), ALU.mult)
        self.ts("dve", S("iaim"), S("t3"), -1.0, None, ALU.mult, None, [scb], [scb])
        pw = A.f32(9, 2, 32)
        self.ms("dve", pw.ap[:, 0, 0, :], 1.0, [pw.b])
        self.ms("dve", pw.ap[:, 0, 1, :], 0.0, [pw.b])
        for j in range(1, 9):
            for (o, x1, y1, x2, y2, op) in ((pw.ap[:, j, 0, :], pw.ap[:, j - 1, 0, :], S("are"), pw.ap[:, j - 1, 1, :], S("aim"), ALU.subtract),
                                            (pw.ap[:, j, 1, :], pw.ap[:, j - 1, 0, :], S("aim"), pw.ap[:, j - 1, 1, :], S("are"), ALU.add)):
                self.tt("dve", S("t1"), x1, y1, ALU.mult, [pw.b, scb], [scb])
                self.tt("dve", S("t2"), x2, y2, ALU.mult, [pw.b, scb], [scb])
                self.tt("dve", o, S("t1"), S("t2"), op, [scb], [pw.b])
        a12 = A.f32(2, 2, 16, 2)
        for d in range(2):
            for c in range(2):
                self.cp("dve", a12.ap[:, d, 0, :, c], pw.ap[:, 8, 0, d * 16:(d + 1) * 16], [pw.b], [a12.b])
            self.ts("dve", a12.ap[:, d, 1, :, 0], pw.ap[:, 8, 1, d * 16:(d + 1) * 16], -1.0, None, ALU.mult, None, [pw.b], [a12.b])
            self.cp("dve", a12.ap[:, d, 1, :, 1], pw.ap[:, 8, 1, d * 16:(d + 1) * 16], [pw.b], [a12.b])
        self.dma("sp", self.A12[l].ap, a12.ap.rearrange("p d k g c -> p (d k g c)"), [a12.b], [])
        self.cp("dve", S("pr"), S("iare"), [scb], [scb])
        self.cp("dve", S("pi"), S("iaim"), [scb], [scb])
        for _ in range(3):
            dv(S("t1"), S("pr"), S("pr"), ALU.mult)
            dv(S("t2"), S("pi"), S("pi"), ALU.mult)
            dv(S("t3"), S("pr"), S("pi"), ALU.mult)
            dv(S("pr"), S("t1"), S("t2"), ALU.subtract)
            self.ts("dve", S("pi"), S("t3"), 2.0, None, ALU.mult, None, [scb], [scb])
        Bbr = A.f32(16, 32); Bbi = A.f32(16, 32); T1 = A.f32(16, 32); T2 = A.f32(16, 32)

        def bc16(ap):
            return mkap(ap.tensor, ap.offset, [list(ap.ap[0]), [0, 16], [1, 32]])

        for (o, x1, x2, op) in ((Bbr, Bre, Bim, ALU.subtract), (Bbi, Bim, Bre, ALU.add)):
            self.tt("dve", T1.ap, x1.ap, bc16(S("cre")), ALU.mult, [x1.b, scb], [T1.b])
            self.tt("dve", T2.ap, x2.ap, bc16(S("cim")), ALU.mult, [x2.b, scb], [T2.b])
            self.tt("dve", o.ap, T1.ap, T2.ap, op, [T1.b, T2.b], [o.b])
        XS = A.f32(2, 16, 2, 8, 16)
        Q = A.f32(2, 16, 2, 8, 16)
        tmp = [[A.f32(16, 16), A.f32(16, 16)] for _ in range(2)]

        def pwv(j, c, d):
            a = pw.ap[:, j, c, d * 16:(d + 1) * 16]
            return mkap(a.tensor, a.offset, [list(a.ap[0]), [1, 16], [0, 16]])

        def mat(tl, d):
            return tl.ap[:, :, d * 16:(d + 1) * 16].rearrange("p x g -> p g x")

        k = 0
        for d in range(2):
            for s in range(8):
                for (dst, Mr, Mi, j, neg_im) in ((XS, Bbr, Bbi, (7 - s) if d == 0 else s, False),
                                                 (Q, Cre, Cim, (s + 1) if d == 0 else (8 - s), True)):
                    eng = "dve" if k % 2 == 0 else "pool"
                    t1, t2 = tmp[k % 2]
                    k += 1
                    rd = [Mr.b, Mi.b, pw.b]
                    self.tt(eng, t1.ap, mat(Mr, d), pwv(j, 0, d), ALU.mult, rd, [t1.b])
                    self.tt(eng, t2.ap, mat(Mi, d), pwv(j, 1, d), ALU.mult, rd, [t2.b])
                    self.tt(eng, dst.ap[:, d, :, 0, s, :], t1.ap, t2.ap, ALU.subtract, [t1.b, t2.b], [dst.b])
                    self.tt(eng, t1.ap, mat(Mr, d), pwv(j, 1, d), ALU.mult, rd, [t1.b])
                    self.tt(eng, t2.ap, mat(Mi, d), pwv(j, 0, d), ALU.mult, rd, [t2.b])
                    self.tt(eng, dst.ap[:, d, :, 1, s, :], t1.ap, t2.ap, ALU.add, [t1.b, t2.b], [dst.b])
        qim = Q.ap[:, :, :, 1].rearrange("p d g s c -> p (d g) (s c)")
        self.ts("pool", qim, qim, -1.0, None, ALU.mult, None, [Q.b], [Q.b])
        XM = A.f32(2, 16, 2, 128)
        X4 = XS.ap.rearrange("p d g c s i -> p d g c (s i)")
        big = [A.f32(16, 128), A.f32(16, 128)]

        def pv(nm, d):
            a = S(nm)[:, d * 16:(d + 1) * 16]
            return mkap(a.tensor, a.offset, [list(a.ap[0]), [1, 16], [0, 128]])

        for d in range(2):
            eng = "dve" if d == 0 else "pool"
            t1, t2 = big
            rd = [XS.b, scb]
            self.tt(eng, t1.ap, X4[:, d, :, 0, :], pv("pr", d), ALU.mult, rd, [t1.b])
            self.tt(eng, t2.ap, X4[:, d, :, 1, :], pv("pi", d), ALU.mult, rd, [t2.b])
            self.tt(eng, XM.ap[:, d, :, 0, :], t1.ap, t2.ap, ALU.subtract, [t1.b, t2.b], [XM.b])
            self.tt(eng, t1.ap, X4[:, d, :, 1, :], pv("pr", d), ALU.mult, rd, [t1.b])
            self.tt(eng, t2.ap, X4[:, d, :, 0, :], pv("pi", d), ALU.mult, rd, [t2.b])
            self.tt(eng, XM.ap[:, d, :, 1, :], t1.ap, t2.ap, ALU.add, [t1.b, t2.b], [XM.b])
        self.P.barrier()
        PFs = Tl(view(RR.ap[:, 0:4096].bitcast(BF16), (64, 128)))
        Q4 = Q.ap.rearrange("p d g c s i -> p (d g c) (s i)")
        XS3 = XS.ap.rearrange("p d g c s i -> p (d g c) (s i)")
        for q4 in range(16):
            bk = self.bank(3 + q4 % 2); bb = pb[3 + q4 % 2]
            for jj in range(4):
                self.tr(bk[:, jj * 128:(jj + 1) * 128], XS3[:, q4 * 4 + jj, :], idf.ap, [XS.b, idf.b], [bb])
            self.cp("act", PFs.ap[:, q4 * 4:(q4 + 1) * 4, :], view(bk, (4, 128)), [bb], [PFs.b])
        self.dma("sp", self.PFd[l].ap, PFs.ap.rearrange("p a b -> p (a b)"), [PFs.b], [])
        Qb = Tl(view(RR.ap[:, 4096:8192].bitcast(BF16), (64, 128)))
        self.cp("pool", Qb.ap, Q4, [Q.b], [Qb.b])
        self.dma("sp", self.Qd[l].ap, Qb.ap.rearrange("p a b -> p (a b)"), [Qb.b], [])
        MLs = A.bf16(32, 128)
        mt = [A.f32(128), A.f32(128)]
        mu = [A.f32(128), A.f32(128)]
        XM4 = XM.ap
        Q5 = Q.ap.rearrange("p d g c s i -> p d g c (s i)")
        for g in range(32):
            gp_, g2 = g // 2, g % 2
            bk = self.bank(5 + g % 2); bb = pb[5 + g % 2]
            sl = slice(g2 * 64, (g2 + 1) * 64)
            for d in range(2):
                for c in range(2):
                    self.mm(bk[:, d * 128:(d + 1) * 128], XM4[sl, d, gp_, c, :], Q5[sl, d, gp_, c, :], c == 0, c == 1,
                            [XM.b, Q.b], [bb])
            t = mt[g % 2]
            u = mu[g % 2]
            self.tt("dve", t.ap, bk[:, 0:128], cst.ap[:, 3, :], ALU.mult, [bb, cst.b], [t.b])
            self.tt("dve", u.ap, bk[:, 128:256], cst.ap[:, 4, :], ALU.mult, [bb, cst.b], [u.b])
            self.tt("pool", t.ap, t.ap, u.ap, ALU.add, [t.b, u.b], [t.b])
            self.stt(MLs.ap[:, g, :], idf.ap, dcol.ap[:, g:g + 1], t.ap, ALU.mult, ALU.add, [idf.b, dcol.b, t.b], [MLs.b])
        self.dma("sp", self.MLd[l].ap, MLs.ap.rearrange("p a b -> p (a b)"), [MLs.b], [])

    def alloc_norm(self):
        A = self.A
        self.n_rstd = A.f32(TT)
        self.n_tmp = [A.f32(TT), A.f32(TT)]
        self.n_k = 0

    def rstd_from(self, bankidx):
        r = self.n_rstd
        pbk = self.pbufs[bankidx]
        self.ts("dve", r.ap, self.bank(bankidx), 1.0 / D, EPS, ALU.mult, ALU.add, [pbk], [r.b])
        self.act(r.ap, r.ap, AF.Sqrt, [r.b], [r.b])
        self.recip(r.ap, r.ap, [r.b], [r.b])
        return r

    def load_x(self, xt, ti, first=False, alias=None):
        tok0 = ti * TT
        if not first:
            self.dma("sp", xt.ap, self.XT.ap[:, :, tok0:tok0 + TT].rearrange("c p t -> p c t"), [], xt.bs)
            return
        xtm, extra = alias
        self.dma("sp", xtm.ap, self.I["xin"][tok0:tok0 + TT].rearrange("(b p) d -> p b d", p=128), [], [xtm.b] + extra)
        for c in range(8):
            bi = c % 2
            for blk in range(4):
                self.tr(self.bank(bi)[:, blk * 128:(blk + 1) * 128], xtm.ap[:, blk, c * 128:(c + 1) * 128], self.identf.ap,
                        [xtm.b, self.identf.b] + extra, [self.pbufs[bi]])
            self.cp("act" if c % 2 else "dve", xt.ap[:, c, :], self.bank(bi), [self.pbufs[bi]], [xt.bs[c]])

    def store_x(self, xt, ti, last=False, alias=None):
        tok0 = ti * TT
        if not last:
            self.dma("sp", self.XT.ap[:, :, tok0:tok0 + TT].rearrange("c p t -> p c t"), xt.ap, xt.bs, [])
            return
        yst, extra = alias
        for blk in range(4):
            for half in range(2):
                bi = (blk * 2 + half) % 2
                for cc in range(4):
                    c = half * 4 + cc
                    self.tr(self.bank(bi)[:, cc * 128:(cc + 1) * 128], xt.ap[:, c, blk * 128:(blk + 1) * 128], self.identf.ap,
                            [xt.bs[c], self.identf.b], [self.pbufs[bi]])
                self.cp("act" if half else "dve", yst.ap[:, blk, half * 512:(half + 1) * 512], self.bank(bi),
                        [self.pbufs[bi]], [yst.b] + extra)
        self.dma("sp", self.O["y"][tok0:tok0 + TT].rearrange("(b p) d -> p b d", p=128), yst.ap, [yst.b] + extra, [])

    def prenorm(self, l, sub, which, xt, hT, sq, bankidx=7):
        pbk = self.pbufs[bankidx]
        for c in range(8):
            self.tt("pool", sq.ap[:, c, :], xt.ap[:, c, :], xt.ap[:, c, :], ALU.mult, [xt.bs[c]], [sq.b])
        for c in range(8):
            self.mm(self.bank(bankidx), self.onesb.ap, sq.ap[:, c, :], c == 0, c == 7, [self.onesb.b, sq.b], [pbk])
        r = self.rstd_from(bankidx)
        for c in range(8):
            t = self.n_tmp[self.n_k % 2]
            self.n_k += 1
            self.tt("dve", t.ap, xt.ap[:, c, :], r.ap, ALU.mult, [xt.bs[c], r.b], [t.b])
            self.ts("pool", hT.ap[:, c, :], t.ap, self.scal(l, 0, sub, c, which), self.scal(l, 1, sub, c, which),
                    ALU.mult, ALU.add, [t.b, self.SC.b], [hT.b])

    def post_chunk(self, m, pbank, fT, sqr, ssbank):
        pbk = self.pbufs[pbank]
        s = sqr[m % 2]
        self.cp("dve", fT.ap[:, m, :], self.bank(pbank), [pbk], [fT.b])
        self.tt("pool", s.ap, fT.ap[:, m, :], fT.ap[:, m, :], ALU.mult, [fT.b], [s.b])
        if m > 0:
            p_ = sqr[(m - 1) % 2]
            self.mm(self.bank(ssbank), self.onesb.ap, p_.ap, m == 1, False, [self.onesb.b, p_.b], [self.pbufs[ssbank]])
        if m == 7:
            self._last_sq = s

    def post_update(self, l, sub, which, xt, fT, ssbank):
        p_ = self._last_sq
        self.mm(self.bank(ssbank), self.onesb.ap, p_.ap, False, True, [self.onesb.b, p_.b], [self.pbufs[ssbank]])
        r = self.rstd_from(ssbank)
        for m in range(8):
            t = self.n_tmp[self.n_k % 2]
            self.n_k += 1
            self.stt(t.ap, fT.ap[:, m, :], self.scal(l, 2, sub, m, which), r.ap, ALU.mult, ALU.mult,
                     [fT.b, self.SC.b, r.b], [t.b])
            self.tt("pool", xt.ap[:, m, :], xt.ap[:, m, :], t.ap, ALU.add, [xt.bs[m], t.b], [xt.bs[m]])

    def ffn_pass(self, l, i, first=False, last=False):
        self.phase()
        A, I = self.A, self.I
        sub = 0 if i == 0 else 2
        wg = A.bf16(8, DFF, nb=2); wu = A.bf16(8, DFF, nb=2); wd = A.bf16(NFC, D, nb=2)
        for h in range(2):
            cs = slice(h * 1408, (h + 1) * 1408)
            self.dma("pool", wg.ap[:, :, cs], I["w_ffn_gate"][l, i][:, cs].rearrange("(kc p) f -> p kc f", p=128), [], [wg.bs[h]])
            self.dma("pool", wu.ap[:, :, cs], I["w_ffn_up"][l, i][:, cs].rearrange("(kc p) f -> p kc f", p=128), [], [wu.bs[h]])
        wdv = I["w_ffn_down"][l, i].rearrange("(fc p) d -> p fc d", p=128)
        for h in range(2):
            fs = slice(h * 11, (h + 1) * 11)
            self.dma("pool", wd.ap[:, fs, :], wdv[:, fs, :], [], [wd.bs[h]])
        xt = A.f32(8, TT, nb=8)
        r1 = A.f32(8, TT)
        hT = Tl(view(r1.ap.rearrange("p a b -> p (a b)")[:, 0:2048].bitcast(BF16), (8, TT)))
        sq = Tl(view(r1.ap.rearrange("p a b -> p (a b)")[:, 2048:4096].bitcast(BF16), (8, TT)))
        fT = r1
        hT.b = sq.b = fT.b
        actT = A.bf16(NFC, TT, nb=NFC)
        al = Tl(view(actT.ap.rearrange("p a b -> p (a b)")[:, 0:8192].bitcast(F32), (4, D)))
        sqr = [A.bf16(TT), A.bf16(TT)]
        sgr = [A.f32(TT), A.f32(TT)]
        self.alloc_norm()
        pb = self.pbufs
        import os
        nt_ = int(os.environ.get("DBG_NT", NT))
        lvl = int(os.environ.get("DBG_LVL", 9))
        for ti in range(nt_):
            which = 0 if ti < 8 else 1
            self.load_x(xt, ti, first, (al, actT.bs))
            if lvl < 1:
                self.store_x(xt, ti, last, (al, actT.bs))
                continue
            self.prenorm(l, sub, which, xt, hT, sq)
            if lvl < 2:
                self.store_x(xt, ti, last, (al, actT.bs))
                continue
            for j in range(NFC):
                bg, bu = j % 2, 2 + j % 2
                cs = slice(j * 128, (j + 1) * 128)
                for kc in range(8):
                    self.mm(self.bank(bg), wg.ap[:, kc, cs], hT.ap[:, kc, :], kc == 0, kc == 7, [wg.bs[j // 11], hT.b], [pb[bg]])
                for kc in range(8):
                    self.mm(self.bank(bu), wu.ap[:, kc, cs], hT.ap[:, kc, :], kc == 0, kc == 7, [wu.bs[j // 11], hT.b], [pb[bu]])
                s = sgr[j % 2]
                self.act(s.ap, self.bank(bg), AF.Silu, [pb[bg]], [s.b])
                self.tt("dve", actT.ap[:, j, :], s.ap, self.bank(bu), ALU.mult, [s.b, pb[bu]], [actT.bs[j]])
            if lvl < 3:
                self.store_x(xt, ti, last, (al, actT.bs))
                continue
            for m in range(8):
                bf = 4 + m % 2
                for j in range(NFC):
                    self.mm(self.bank(bf), wd.ap[:, j, m * 128:(m + 1) * 128], actT.ap[:, j, :], j == 0, j == NFC - 1,
                            [wd.bs[j // 11], actT.bs[j]], [pb[bf]])
                if lvl >= 4:
                    self.post_chunk(m, bf, fT, sqr, 6)
            if lvl >= 5:
                self.post_update(l, sub, which, xt, fT, 6)
            self.store_x(xt, ti, last, (al, actT.bs))


    def ffn_pass2(self, l, i, first=False, last=False):
        self.phase()
        A, I = self.A, self.I
        TF = 256
        NTF = NTOK // TF
        sub = 0 if i == 0 else 2
        wg = A.bf16(8, DFF, nb=2); wu = A.bf16(8, DFF, nb=2); wd = A.bf16(NFC, D, nb=2)
        for h in range(2):
            cs = slice(h * 1408, (h + 1) * 1408)
            self.dma("pool", wg.ap[:, :, cs], I["w_ffn_gate"][l, i][:, cs].rearrange("(kc p) f -> p kc f", p=128), [], [wg.bs[h]])
            self.dma("pool", wu.ap[:, :, cs], I["w_ffn_up"][l, i][:, cs].rearrange("(kc p) f -> p kc f", p=128), [], [wu.bs[h]])
        wdv = I["w_ffn_down"][l, i].rearrange("(fc p) d -> p fc d", p=128)
        for h in range(2):
            fs = slice(h * 11, (h + 1) * 11)
            self.dma("pool", wd.ap[:, fs, :], wdv[:, fs, :], [], [wd.bs[h]])
        xts = [A.f32(8, TF, nb=8), A.f32(8, TF, nb=8)]
        hTs = [A.bf16(8, TF), A.bf16(8, TF)]
        sq = A.bf16(8, TF)
        fT = A.f32(8, TF)
        actT = A.bf16(NFC, TF, nb=NFC)
        sqr = [A.bf16(TF), A.bf16(TF)]
        sgr = [A.f32(TF), A.f32(TF)]
        stg = A.f32(2, D) if (first or last) else None
        rs_pre = A.f32(TF); rs_post = A.f32(TF)
        tmp_pre = [A.f32(TF), A.f32(TF)]; tmp_post = [A.f32(TF), A.f32(TF)]
        pb = self.pbufs
        bk = lambda b_: self.bank(b_)[:, 0:TF]

        def rstd(r, bankidx):
            self.ts("dve", r.ap, bk(bankidx), 1.0 / D, EPS, ALU.mult, ALU.add, [pb[bankidx]], [r.b])
            self.act(r.ap, r.ap, AF.Sqrt, [r.b], [r.b])
            self.recip(r.ap, r.ap, [r.b], [r.b])

        def load(ti):
            xt = xts[ti % 2]
            tok0 = ti * TF
            if not first:
                self.dma("sp", xt.ap, self.XT.ap[:, :, tok0:tok0 + TF].rearrange("c p t -> p c t"), [], xt.bs)
                return
            self.dma("sp", stg.ap, I["xin"][tok0:tok0 + TF].rearrange("(b p) d -> p b d", p=128), [], [stg.b])
            for c in range(8):
                bi = c % 2
                for blk in range(2):
                    self.tr(self.bank(bi)[:, blk * 128:(blk + 1) * 128], stg.ap[:, blk, c * 128:(c + 1) * 128], self.identf.ap,
                            [stg.b, self.identf.b], [pb[bi]])
                self.cp("act", xt.ap[:, c, :], bk(bi), [pb[bi]], [xt.bs[c]])

        def store(ti):
            xt = xts[ti % 2]
            tok0 = ti * TF
            if not last:
                self.dma("sp", self.XT.ap[:, :, tok0:tok0 + TF].rearrange("c p t -> p c t"), xt.ap, xt.bs, [])
                return
            for blk in range(2):
                for half in range(2):
                    bi = half
                    for cc in range(4):
                        c = half * 4 + cc
                        self.tr(self.bank(bi)[:, cc * 128:(cc + 1) * 128], xt.ap[:, c, blk * 128:(blk + 1) * 128], self.identf.ap,
                                [xt.bs[c], self.identf.b], [pb[bi]])
                    self.cp("act", stg.ap[:, blk, half * 512:(half + 1) * 512], self.bank(bi), [pb[bi]], [stg.b])
            self.dma("sp", self.O["y"][tok0:tok0 + TF].rearrange("(b p) d -> p b d", p=128), stg.ap, [stg.b], [])

        def prenorm(ti):
            xt = xts[ti % 2]; hT = hTs[ti % 2]
            which = 0 if ti * TF < TS else 1
            for c in range(8):
                self.tt("pool", sq.ap[:, c, :], xt.ap[:, c, :], xt.ap[:, c, :], ALU.mult, [xt.bs[c]], [sq.b])
            for c in range(8):
                self.mm(bk(7), self.onesb.ap, sq.ap[:, c, :], c == 0, c == 7, [self.onesb.b, sq.b], [pb[7]])
            rstd(rs_pre, 7)
            for c in range(8):
                t = tmp_pre[c % 2]
                self.tt("dve", t.ap, xt.ap[:, c, :], rs_pre.ap, ALU.mult, [xt.bs[c], rs_pre.b], [t.b])
                self.ts("pool", hT.ap[:, c, :], t.ap, self.scal(l, 0, sub, c, which), self.scal(l, 1, sub, c, which),
                        ALU.mult, ALU.add, [t.b, self.SC.b], [hT.b])

        load(0)
        prenorm(0)
        for ti in range(NTF):
            xt = xts[ti % 2]; hT = hTs[ti % 2]
            which = 0 if ti * TF < TS else 1
            for j in range(NFC):
                bg, bu = j % 2, 2 + j % 2
                cs = slice(j * 128, (j + 1) * 128)
                for kc in range(8):
                    self.mm(bk(bg), wg.ap[:, kc, cs], hT.ap[:, kc, :], kc == 0, kc == 7, [wg.bs[j // 11], hT.b], [pb[bg]])
                for kc in range(8):
                    self.mm(bk(bu), wu.ap[:, kc, cs], hT.ap[:, kc, :], kc == 0, kc == 7, [wu.bs[j // 11], hT.b], [pb[bu]])
                s = sgr[j % 2]
                self.act(s.ap, bk(bg), AF.Silu, [pb[bg]], [s.b])
                self.tt("dve", actT.ap[:, j, :], s.ap, bk(bu), ALU.mult, [s.b, pb[bu]], [actT.bs[j]])
            if ti >= 1:
                store(ti - 1)
            if ti + 1 < NTF:
                load(ti + 1)
                prenorm(ti + 1)
            for m in range(8):
                bf = 4 + m % 2
                for j in range(NFC):
                    self.mm(bk(bf), wd.ap[:, j, m * 128:(m + 1) * 128], actT.ap[:, j, :], j == 0, j == NFC - 1,
                            [wd.bs[j // 11], actT.bs[j]], [pb[bf]])
                s = sqr[m % 2]
                self.cp("dve", fT.ap[:, m, :], bk(bf), [pb[bf]], [fT.b])
                self.tt("pool", s.ap, fT.ap[:, m, :], fT.ap[:, m, :], ALU.mult, [fT.b], [s.b])
                if m > 0:
                    p_ = sqr[(m - 1) % 2]
                    self.mm(bk(6), self.onesb.ap, p_.ap, m == 1, False, [self.onesb.b, p_.b], [pb[6]])
            p_ = sqr[1]
            self.mm(bk(6), self.onesb.ap, p_.ap, False, True, [self.onesb.b, p_.b], [pb[6]])
            rstd(rs_post, 6)
            for m in range(8):
                t = tmp_post[m % 2]
                self.stt(t.ap, fT.ap[:, m, :], self.scal(l, 2, sub, m, which), rs_post.ap, ALU.mult, ALU.mult,
                         [fT.b, self.SC.b, rs_post.b], [t.b])
                self.tt("pool", xt.ap[:, m, :], xt.ap[:, m, :], t.ap, ALU.add, [xt.bs[m], t.b], [xt.bs[m]])
        store(NTF - 1)

    def p2_pass(self, l):
        self.phase()
        A, I, O = self.A, self.I, self.O
        W = A.bf16(8, WINP, nb=4)
        wv = I["w_in_p"][l].rearrange("(kc p) c -> p kc c", p=128)
        bounds = [0, C_V, C_U, C_G + 1536, WINP]
        for h in range(4):
            self.dma("pool", W.ap[:, :, bounds[h]:bounds[h + 1]], wv[:, :, bounds[h]:bounds[h + 1]], [], [W.bs[h]])

        def wb(col):
            for h in range(4):
                if col < bounds[h + 1]:
                    return W.bs[h]

        xt = A.f32(8, TT, nb=8)
        hTs = [A.bf16(8, TT), A.bf16(8, TT)]; sq = A.bf16(8, TT)
        rt = A.f32(2, TT)
        qst = [A.bf16(TT) for _ in range(3)]
        rtmp = [A.f32(TT) for _ in range(4)]
        xlst = A.f32(4, TT); ylst = A.bf16(4, TT)
        vst = A.bf16(4, 2, 65); vf = A.f32(4, 128); kf = A.f32(4, 128)
        utok = A.bf16(32, 8, 16); ufst = A.bf16(32, 64)
        gst = [A.bf16(8, TT), A.bf16(8, TT)]
        self.alloc_norm()
        pb = self.pbufs
        self.ms("pool", vst.ap[:, :, :, 64:65], 1.0, [vst.b])
        nfm = 0
        import os
        sec = os.environ.get("DBG_P2", "ABCDE")
        for ti in range(int(os.environ.get("DBG_NT", NT))):
            which = 0 if ti < 8 else 1
            sample = ti < 8
            tok0 = ti * TT
            hT = hTs[ti % 2]
            if ti == 0:
                self.load_x(xt, 0)
                self.prenorm(l, 1, 0, xt, hTs[0], sq)
            if sample:
                self.dma("act", rt.ap, I["rope"][:, :, tok0:tok0 + TT].rearrange("a p t -> p a t"), [], [rt.b])
            if ti + 1 < NT:
                t1_ = (ti + 1) * TT
                self.dma("act", xt.ap, self.XT.ap[:, :, t1_:t1_ + TT].rearrange("c p t -> p c t"), [], xt.bs)

            def fm(col, bi):
                for kc in range(8):
                    self.mm(self.bank(bi), W.ap[:, kc, col:col + 128], hT.ap[:, kc, :], kc == 0, kc == 7, [wb(col), hT.b], [pb[bi]])

            self.P.mute = "A" not in sec
            for ci in range(5):
                col = C_Q + ci * 128 if ci < 4 else C_K
                cols = C_QS + ci * 128 if ci < 4 else C_KS
                b0 = ci % 2
                fm(col, b0)
                q_ = qst[ci % 3]
                if sample:
                    fm(cols, 2 + b0)
                    t1 = rtmp[(ci % 2) * 2]; t2 = rtmp[(ci % 2) * 2 + 1]
                    self.tt("dve", t1.ap, self.bank(b0), rt.ap[:, 0, :], ALU.mult, [pb[b0], rt.b], [t1.b])
                    self.tt("dve", t2.ap, self.bank(2 + b0), rt.ap[:, 1, :], ALU.mult, [pb[2 + b0], rt.b], [t2.b])
                    self.tt("pool", q_.ap, t1.ap, t2.ap, ALU.add, [t1.b, t2.b], [q_.b])
                else:
                    self.cp("act", q_.ap, self.bank(b0), [pb[b0]], [q_.b])
                dst = self.Qs.ap[ci][:, tok0:tok0 + TT] if ci < 4 else self.Ks.ap[:, tok0:tok0 + TT]
                self.dma("sp", dst, q_.ap, [q_.b], [])
            self.P.mute = False
            if ti + 1 < NT:
                self.prenorm(l, 1, 0 if ti + 1 < 8 else 1, xt, hTs[(ti + 1) % 2], sq)
            self.P.mute = "B" not in sec
            for blk in range(4):
                for kc in range(8):
                    self.mm(self.bank(4)[:, blk * 128:(blk + 1) * 128], hT.ap[:, kc, blk * 128:(blk + 1) * 128],
                            W.ap[:, kc, C_V:C_V + 128], kc == 0, kc == 7, [hT.b, wb(C_V)], [pb[4]])
            self.cp("act", vst.ap[:, :, :, 0:64], view(self.bank(4), (4, 2, 64)), [pb[4]], [vst.b])
            if not sample:
                self.cp("act", vf.ap, view(self.bank(4), (4, 128)), [pb[4]], [vf.b])
            self.dma("sp", self.Vs.ap[tok0:tok0 + TT].rearrange("(b p) c -> p b c", p=128),
                     vst.ap.rearrange("p b h c -> p b (h c)"), [vst.b], [])
            if not sample:
                for pp in range(2):
                    pj = 2 * (ti - 8) + pp
                    self.dma("sp", O["nv"][pj, l].rearrange("(b p) c -> p b c", p=128), vf.ap[:, 2 * pp:2 * pp + 2, :],
                             [vf.b], [])
                for blk in range(4):
                    for kc in range(8):
                        self.mm(self.bank(4)[:, blk * 128:(blk + 1) * 128], hT.ap[:, kc, blk * 128:(blk + 1) * 128],
                                W.ap[:, kc, C_K:C_K + 128], kc == 0, kc == 7, [hT.b, wb(C_K)], [pb[4]])
                self.cp("dve", kf.ap, view(self.bank(4), (4, 128)), [pb[4]], [kf.b])
                for pp in range(2):
                    pj = 2 * (ti - 8) + pp
                    self.dma("sp", O["nk"][pj, l].rearrange("(b p) c -> p b c", p=128), kf.ap[:, 2 * pp:2 * pp + 2, :],
                             [kf.b], [])
            self.P.mute = "C" not in sec
            for c in range(4):
                b0 = c % 2
                fm(C_XL + c * 128, b0)
                self.cp("act", xlst.ap[:, c, :], self.bank(b0), [pb[b0]], [xlst.b])
            self.dma("sp", self.XLs.ap[:, :, tok0:tok0 + TT].rearrange("c p t -> p c t"), xlst.ap, [xlst.b], [])
            for c in range(4):
                b0 = c % 2
                fm(C_YL + c * 128, b0)
                self.cp("dve", ylst.ap[:, c, :], self.bank(b0), [pb[b0]], [ylst.b])
            self.dma("sp", self.YLs.ap[:, :, tok0:tok0 + TT].rearrange("c p t -> p c t"), ylst.ap, [ylst.b], [])
            self.P.mute = "D" not in sec
            hs = hT.ap.rearrange("p c (k s) -> p c s k", s=8)
            for s in range(8):
                bi = 5 + s % 2
                for kc in range(8):
                    self.mm(self.bank(bi)[0:64, :], hs[:, kc, s, :], W.ap[:, kc, C_U:C_U + 512], kc == 0, kc == 7,
                            [hT.b, wb(C_U)], [pb[bi]])
                self.cp("act" if s % 2 else "dve", utok.ap[0:64, :, s, :], view(self.bank(bi)[0:64, :], (32, 16)), [pb[bi]], [utok.b])
            for half in range(2):
                bi = 2 + half
                pbf = self.bank(bi).bitcast(BF16)
                for gg in range(16):
                    g = half * 16 + gg
                    self.tr(pbf[:, gg * 64:(gg + 1) * 64], utok.ap[0:64, g].rearrange("p s c -> p (s c)"),
                            self.identb.ap[0:64, 0:64], [utok.b, self.identb.b], [pb[bi]])
                self.cp("act" if half else "dve", ufst.ap[:, half * 16:(half + 1) * 16, :], view(pbf[:, 0:1024], (16, 64)),
                        [pb[bi]], [ufst.b])
            self.dma("sp", self.UFs.ap[ti], ufst.ap.rearrange("p g k -> p (g k)"), [ufst.b], [])
            self.P.mute = "E" not in sec
            for cc in range(24):
                b0 = cc % 2
                fm(C_G + cc * 128, b0)
                g_ = gst[(cc // 8) % 2]
                self.act(g_.ap[:, cc % 8, :], self.bank(b0), AF.Sigmoid, [pb[b0]], [g_.b])
                if cc % 8 == 7:
                    grp = cc // 8
                    self.dma("sp", self.Gs.ap[grp * 8:(grp + 1) * 8][:, :, tok0:tok0 + TT].rearrange("c p t -> p c t"), g_.ap,
                             [g_.b], [])
            self.P.mute = False

    def p4_pass(self, l):
        self.phase()
        A, I = self.A, self.I
        wol = A.bf16(4, D); woa = A.bf16(4, D); wgl = A.bf16(4, 2048); wo = A.bf16(8, D)
        self.dma("pool", wgl.ap, I["w_glu"][l].rearrange("(kc p) c -> p kc c", p=128), [], [wgl.b])
        self.dma("pool", wol.ap, I["w_o_lru"][l].rearrange("(kc p) c -> p kc c", p=128), [], [wol.b])
        self.dma("pool", woa.ap, I["w_o_attn_p"][l].rearrange("(kc p) c -> p kc c", p=128), [], [woa.b])
        self.dma("pool", wo.ap, I["w_out"][l].rearrange("(kc p) c -> p kc c", p=128), [], [wo.b])
        xt = A.f32(8, TT, nb=8)
        ins = [(A.bf16(4, TT), A.bf16(4, TT), A.bf16(4, TT), A.bf16(24, TT)) for _ in range(2)]
        mg = A.bf16(8, TT)
        fT = A.f32(8, TT)
        sqr = [A.bf16(TT), A.bf16(TT)]
        tmp = [A.f32(TT) for _ in range(6)]
        self.alloc_norm()
        pb = self.pbufs

        def loads(ti):
            tok0 = ti * TT
            lr, at, sy, G = ins[ti % 2]
            for (dst, src) in ((sy, self.SYs), (lr, self.LRs), (at, self.ATs), (G, self.Gs)):
                self.dma("sp", dst.ap, src.ap[:, :, tok0:tok0 + TT].rearrange("c p t -> p c t"), [], [dst.b])

        loads(0)
        for ti in range(NT):
            which = 0 if ti < 8 else 1
            if ti + 1 < NT:
                loads(ti + 1)
            self.load_x(xt, ti)
            lr, at, sy, G = ins[ti % 2]
            for m in range(8):
                ba, bz = m % 2, 2 + m % 2
                for kc in range(4):
                    self.mm(self.bank(ba), wgl.ap[:, kc, m * 128:(m + 1) * 128], sy.ap[:, kc, :], kc == 0, kc == 3, [wgl.b, sy.b], [pb[ba]])
                for kc in range(4):
                    self.mm(self.bank(bz), wgl.ap[:, kc, 1024 + m * 128:1024 + (m + 1) * 128], sy.ap[:, kc, :], kc == 0, kc == 3,
                            [wgl.b, sy.b], [pb[bz]])
                s = tmp[m % 2]; t = tmp[2 + m % 2]
                self.act(s.ap, self.bank(bz), AF.Sigmoid, [pb[bz]], [s.b])
                self.tt("dve", t.ap, self.bank(ba), s.ap, ALU.mult, [pb[ba], s.b], [t.b])
                self.tt("pool", fT.ap[:, m, :], t.ap, G.ap[:, 8 + m, :], ALU.mult, [t.b, G.b], [fT.b])
            for m in range(8):
                ba, bc = 4 + m % 2, 6 + m % 2
                for kc in range(4):
                    self.mm(self.bank(ba), wol.ap[:, kc, m * 128:(m + 1) * 128], lr.ap[:, kc, :], kc == 0, kc == 3, [wol.b, lr.b], [pb[ba]])
                for kc in range(4):
                    self.mm(self.bank(bc), woa.ap[:, kc, m * 128:(m + 1) * 128], at.ap[:, kc, :], kc == 0, kc == 3, [woa.b, at.b], [pb[bc]])
                u1 = tmp[m % 2]; u3 = tmp[2 + m % 2]; u4 = tmp[4 + m % 2]
                self.tt("dve", u1.ap, self.bank(ba), G.ap[:, m, :], ALU.mult, [pb[ba], G.b], [u1.b])
                self.tt("dve", u3.ap, self.bank(bc), G.ap[:, 16 + m, :], ALU.mult, [pb[bc], G.b], [u3.b])
                self.tt("pool", u4.ap, fT.ap[:, m, :], u1.ap, ALU.add, [fT.b, u1.b], [u4.b])
                self.tt("pool", mg.ap[:, m, :], u4.ap, u3.ap, ALU.add, [u4.b, u3.b], [mg.b])
            for m in range(8):
                bo = m % 2
                for kc in range(8):
                    self.mm(self.bank(bo), wo.ap[:, kc, m * 128:(m + 1) * 128], mg.ap[:, kc, :], kc == 0, kc == 7, [wo.b, mg.b], [pb[bo]])
                self.post_chunk(m, bo, fT, sqr, 2)
            self.post_update(l, 1, which, xt, fT, 2)
            self.store_x(xt, ti)

    def p3a_attention(self, l):
        self.phase()
        A, I = self.A, self.I
        kT = A.bf16(2, NTOK); Qt = A.bf16(4, NTOK); Vt = A.bf16(NTOK // 128, 130)
        self.ms("pool", kT.ap[64:128, 0, :], 0.0, [kT.b])
        self.ms("pool", kT.ap[0:64, 1, :], 0.0, [kT.b])
        for h in range(2):
            sl = slice(h * 2560, (h + 1) * 2560)
            self.dma("sp", Qt.ap[:, :, sl], self.Qs.ap[:, :, sl].rearrange("c p t -> p c t"), [], [Qt.b])
        self.dma("act", kT.ap[0:64, 0, :], self.Ks.ap[0:64, :], [], [kT.b])
        self.dma("act", kT.ap[64:128, 1, :], self.Ks.ap[64:128, :], [], [kT.b])
        self.dma("act", Vt.ap, self.Vs.ap.rearrange("(b p) c -> p b c", p=128), [], [Vt.b])
        ckr = A.f32(4, 128); cvr = A.f32(4, 128)
        self.dma("sp", ckr.ap, I["cache_k"][l].rearrange("(b p) c -> p b c", p=128), [], [ckr.b])
        self.dma("sp", cvr.ap, I["cache_v"][l].rearrange("(b p) c -> p b c", p=128), [], [cvr.b])
        ckT = A.bf16(2, 512); cv = A.bf16(4, 2, 65)
        pb = self.pbufs
        for blk in range(4):
            self.tr(self.bank(0)[:, blk * 128:(blk + 1) * 128], ckr.ap[:, blk, :], self.identf.ap, [ckr.b, self.identf.b], [pb[0]])
        self.ms("pool", ckT.ap, 0.0, [ckT.b])
        self.cp("dve", ckT.ap[0:64, 0, :], self.bank(0)[0:64, :], [pb[0]], [ckT.b])
        self.cp("dve", ckT.ap[64:128, 1, :], self.bank(0)[64:128, :], [pb[0]], [ckT.b])
        self.ms("pool", cv.ap[:, :, :, 64:65], 1.0, [cv.b])
        self.cp("dve", cv.ap[:, :, :, 0:64], cvr.ap.rearrange("p b (h c) -> p b h c", h=2), [cvr.b], [cv.b])
        cvf = cv.ap.rearrange("p b h c -> p b (h c)")
        sk = A.f32(8)
        self.dma("sp", sk.ap[64:65, :], I["attn_sink"][l:l + 1, :], [], [sk.b])
        self.act(sk.ap[64:65, :], sk.ap[64:65, :], AF.Exp, [sk.b], [sk.b])
        pT = [A.bf16(TT) for _ in range(4)]
        osb = [A.f32(TT) for _ in range(3)]
        rrow = [A.f32(TT) for _ in range(3)]
        ast = [A.bf16(4, TT), A.bf16(4, TT)]
        segs = [(0, TS, True)] + [(TS + j * TP, TP, False) for j in range(NPB)]
        its = []
        nst = 0
        for (tok0, T, samp) in segs:
            nqb = T // 128
            for qb in range(nqb):
                for kvh in range(2):
                    its.append((tok0, T, samp, nqb, qb, kvh, nst))
                if qb % 4 == 3 or qb == nqb - 1:
                    nst += 1
        npt = [0]

        def front(n):
            tok0, T, samp, nqb, qb, kvh, st_ = its[n]
            q0 = tok0 + qb * 128
            blocks = []
            if samp:
                for nb in (qb - 1, qb, qb + 1):
                    if 0 <= nb < nqb:
                        msk = self.mprev if nb == qb - 1 else (self.mnext if nb == qb + 1 else None)
                        blocks.append((kT.ap[:, kvh, tok0 + nb * 128:tok0 + (nb + 1) * 128], kT.b,
                                       Vt.ap[:, (tok0 // 128) + nb, kvh * 65:(kvh + 1) * 65], Vt.b, msk))
                for cb_ in range(4):
                    blocks.append((ckT.ap[:, kvh, cb_ * 128:(cb_ + 1) * 128], ckT.b, cvf[:, cb_, kvh * 65:(kvh + 1) * 65], cv.b, None))
            else:
                for nb in range(nqb):
                    blocks.append((kT.ap[:, kvh, tok0 + nb * 128:tok0 + (nb + 1) * 128], kT.b,
                                   Vt.ap[:, (tok0 // 128) + nb, kvh * 65:(kvh + 1) * 65], Vt.b, None))
            po = 3 + n % 3
            rhs_q = Qt.ap[:, :, q0:q0 + 128]
            pts = []
            for bi_, (kap, kb, vap, vb, msk) in enumerate(blocks):
                sb_ = npt[0] % 3
                p_ = pT[npt[0] % 4]
                npt[0] += 1
                self.mm(view(self.bank(sb_), (4, 128)), kap, rhs_q, True, True, [kb, Qt.b], [pb[sb_]])
                self.act(p_.ap, self.bank(sb_), AF.Exp, [pb[sb_]], [p_.b], scale=0.125)
                if msk is not None:
                    m_ = msk.ap
                    mb = mkap(m_.tensor, m_.offset, [list(m_.ap[0]), [0, 4], [1, 128]])
                    self.tt("pool", view(p_.ap, (4, 128)), view(p_.ap, (4, 128)), mb, ALU.mult, [p_.b, msk.b], [p_.b])
                pts.append((p_, vap, vb))
                if bi_ >= 1:
                    pp_, vap_, vb_ = pts[bi_ - 1]
                    self.mm(self.bank(po)[0:65, :], vap_, pp_.ap, bi_ == 1, False, [vb_, pp_.b], [pb[po]])
            pp_, vap_, vb_ = pts[-1]
            self.mm(self.bank(po)[0:65, :], vap_, pp_.ap, len(pts) == 1, True, [vb_, pp_.b], [pb[po]])

        def back1(n):
            tok0, T, samp, nqb, qb, kvh, st_ = its[n]
            po = 3 + n % 3
            o_ = osb[n % 3]; r_ = rrow[n % 3]
            self.cp("act", o_.ap[0:65, :], self.bank(po)[0:65, :], [pb[po]], [o_.b])
            s_ = sk.ap[64:65, kvh * 4:(kvh + 1) * 4]
            sbc = mkap(s_.tensor, s_.offset, [list(s_.ap[0]), [1, 4], [0, 128]])
            self.tt("dve", view(r_.ap[64:65, :], (4, 128)), view(o_.ap[64:65, :], (4, 128)), sbc, ALU.add, [o_.b, sk.b], [r_.b])
            self.recip(r_.ap[64:65, :], r_.ap[64:65, :], [r_.b], [r_.b])

        def back2(n):
            tok0, T, samp, nqb, qb, kvh, st_ = its[n]
            q0 = tok0 + qb * 128
            hs = slice(kvh * 64, (kvh + 1) * 64)
            a_t = ast[st_ % 2]
            qcol = (qb % 4) * 128
            o_ = osb[n % 3]; r_ = rrow[n % 3]
            bb = 6 + n % 2
            self.mm(self.bank(bb)[0:64, :], self.onesf.ap[64:65, 0:64], r_.ap[64:65, :], True, True, [self.onesf.b, r_.b], [pb[bb]])
            self.tt("dve", a_t.ap[hs, :, qcol:qcol + 128], view(o_.ap[0:64, :], (4, 128)), view(self.bank(bb)[0:64, :], (4, 128)),
                    ALU.mult, [o_.b, pb[bb]], [a_t.b])
            if kvh == 1 and (qb % 4 == 3 or qb == nqb - 1):
                nn = (qb % 4 + 1) * 128
                t0 = q0 + 128 - nn
                self.dma("sp", self.ATs.ap[:, :, t0:t0 + nn].rearrange("c p t -> p c t"), a_t.ap[:, :, 0:nn], [a_t.b], [])

        N = len(its)
        for n in range(N + 2):
            if n < N:
                front(n)
            if 0 <= n - 1 < N:
                back1(n - 1)
            if 0 <= n - 2 < N:
                back2(n - 2)

    def p3b_lru(self, l):
        self.phase()
        A, I, O = self.A, self.I, self.O
        pb = self.pbufs
        cw = A.f32(4, 4); cb = A.f32(4); onec = A.f32(1)
        src, sb_ = self.vecT(None, I["w_conv"][l].rearrange("j (c p) -> (j c) p", p=128), 16, None)
        self.cp("dve", cw.ap.rearrange("p c j -> p j c"), view(src, (4, 4)), [sb_], [cw.b])
        src, sb_ = self.vecT(None, I["b_conv"][l].rearrange("(c p) -> c p", p=128), 4, None)
        self.cp("dve", cb.ap, src, [sb_], [cb.b])
        self.ms("dve", onec.ap, 1.0, [onec.b])
        ba = A.f32(2, 4); bx = A.f32(2, 4); cl = A.f32(2, 4); st0 = A.f32(2, 4)
        for (t_, nm) in ((ba, "b_lru_a"), (bx, "b_lru_x"), (cl, "lru_lambda"), (st0, "state_lru")):
            src, sb_ = self.vecT(None, I[nm][l].rearrange("d (c p) -> (d c) p", p=128), 8, None)
            self.cp("dve", t_.ap.rearrange("p d c -> p (d c)"), src, [sb_], [t_.b])
        self.act(cl.ap, cl.ap, AF.Exp, [cl.b], [cl.b], scale=-1.0)
        self.act(cl.ap, cl.ap, AF.Ln, [cl.b, onec.b], [cl.b], bias=onec.ap[:, 0:1])
        self.ts("dve", cl.ap, cl.ap, -8.0, None, ALU.mult, None, [cl.b], [cl.b])
        cl2 = A.f32(2, 4)
        self.ts("dve", cl2.ap, cl.ap, 2.0, None, ALU.mult, None, [cl.b], [cl2.b])
        BD = {}
        for d in range(2):
            for nm in ("w_lru_a", "w_lru_x"):
                t_ = A.bf16(4, 128)
                self.ms("pool", t_.ap, 0.0, [t_.b])
                for c in range(4):
                    self.dma("pool", t_.ap[0:64, c, 0:64], I[nm][l, d, 2 * c], [], [t_.b])
                    self.dma("pool", t_.ap[64:128, c, 64:128], I[nm][l, d, 2 * c + 1], [], [t_.b])
                BD[(d, nm)] = t_
        hf = A.f32(4, TS); xc = A.f32(4, TS)
        xlt = A.f32(4, TT + 3); xcb = A.bf16(4, TT)
        R = A.f32(4, TT)
        IIs = [A.f32(4, TT), A.f32(4, TT)]
        AAs = [A.f32(4, TT), A.f32(4, TT)]
        hbt = A.f32(4, TT); carry = A.f32(4)
        ylts = [A.bf16(4, TT), A.bf16(4, TT)]
        fin = A.f32(NPB, 2, 4)
        segs = [(0, TS, True)] + [(TS + j * TP, TP, False) for j in range(NPB)]
        gk = [0]
        xcbufs = [Buf() for _ in range(8)]

        def rev(ap, n):
            return mkap(ap.tensor, ap.offset + n - 1, [list(ap.ap[0]), [-1, n]])

        def gates(d, t0l, n):
            AA = AAs[gk[0] % 2]; II = IIs[gk[0] % 2]
            gk[0] += 1
            SQ = R
            for c in range(4):
                self.cp("pool", xcb.ap[:, c, 0:n], xc.ap[:, c, t0l:t0l + n], [xcbufs[t0l // n]], [xcb.b])
            for c in range(4):
                self.mm(self.bank(c)[:, 0:n], BD[(d, "w_lru_a")].ap[:, c, :], xcb.ap[:, c, 0:n], True, True, [BD[(d, "w_lru_a")].b, xcb.b], [pb[c]])
                self.mm(self.bank(4 + c)[:, 0:n], BD[(d, "w_lru_x")].ap[:, c, :], xcb.ap[:, c, 0:n], True, True, [BD[(d, "w_lru_x")].b, xcb.b], [pb[4 + c]])
            for c in range(4):
                self.act(R.ap[:, c, 0:n], self.bank(c)[:, 0:n], AF.Sigmoid, [pb[c], ba.b], [R.b], bias=ba.ap[:, d, c:c + 1])
            for c in range(4):
                self.act(II.ap[:, c, 0:n], self.bank(4 + c)[:, 0:n], AF.Sigmoid, [pb[4 + c], bx.b], [II.b], bias=bx.ap[:, d, c:c + 1])
            for c in range(4):
                self.tt("pool", II.ap[:, c, 0:n], II.ap[:, c, 0:n], xc.ap[:, c, t0l:t0l + n], ALU.mult, [II.b, xcbufs[t0l // n]], [II.b])
            for c in range(4):
                self.act(AA.ap[:, c, 0:n], R.ap[:, c, 0:n], AF.Exp, [R.b, cl.b], [AA.b], scale=cl.ap[:, d, c:c + 1])
            for c in range(4):
                self.act(SQ.ap[:, c, 0:n], R.ap[:, c, 0:n], AF.Exp, [R.b, cl2.b], [SQ.b], scale=cl2.ap[:, d, c:c + 1])
            for c in range(4):
                self.act(SQ.ap[:, c, 0:n], SQ.ap[:, c, 0:n], AF.Sqrt, [SQ.b, onec.b], [SQ.b], scale=-1.0, bias=onec.ap[:, 0:1])
            for c in range(4):
                self.tt("dve", II.ap[:, c, 0:n], II.ap[:, c, 0:n], SQ.ap[:, c, 0:n], ALU.mult, [II.b, SQ.b], [II.b])
            return AA, II

        for si, (tok0, T, samp) in enumerate(segs):
            n = min(TT, T)
            ntl = T // n

            def load_conv(tl):
                t0l = tl * n
                lo = max(t0l - 2, 0); hi = min(t0l + n + 1, T)
                if lo > t0l - 2:
                    self.ms("dve", xlt.ap[:, :, 0:2], 0.0, [xlt.b])
                if hi < t0l + n + 1:
                    self.ms("dve", xlt.ap[:, :, n + 2:n + 3], 0.0, [xlt.b])
                self.dma("sp", xlt.ap[:, :, lo - (t0l - 2):hi - (t0l - 2)], self.XLs.ap[:, :, tok0 + lo:tok0 + hi].rearrange("c p t -> p c t"), [], [xlt.b])
                for c in range(4):
                    o_ = xc.ap[:, c, t0l:t0l + n]
                    self.act(o_, xlt.ap[:, c, 0:n], AF.Identity, [xlt.b, cw.b, cb.b], [xcbufs[tl]], scale=cw.ap[:, c, 0:1], bias=cb.ap[:, c:c + 1])
                    for j in range(1, 4):
                        self.stt(o_, xlt.ap[:, c, j:j + n], cw.ap[:, c, j:j + 1], o_, ALU.mult, ALU.add, [xlt.b, cw.b, xcbufs[tl]], [xcbufs[tl]])

            load_conv(0)
            for tl in range(ntl):
                t0l = tl * n
                if tl + 1 < ntl:
                    load_conv(tl + 1)
                AA, II = gates(0, t0l, n)
                for c in range(4):
                    if tl == 0:
                        init = st0.ap[:, 0, c:c + 1] if samp else 0.0
                    else:
                        init = hf.ap[:, c, t0l - 1:t0l]
                    self.P.op("dve", lambda e, o=hf.ap[:, c, t0l:t0l + n], a=AA.ap[:, c, 0:n], b=II.ap[:, c, 0:n], i0=init:
                              e.tensor_tensor_scan(out=o, data0=a, data1=b, initial=i0, op0=ALU.mult, op1=ALU.add),
                              [AA.b, II.b, hf.b, st0.b], [hf.b])
            if not samp:
                self.cp("dve", fin.ap[:, si - 1, 0, :], hf.ap[:, :, T - 1], [hf.b], [fin.b])
            def load_yl(tl):
                y_ = ylts[tl % 2]
                t0l_ = tl * n
                self.dma("sp", y_.ap[:, :, 0:n], self.YLs.ap[:, :, tok0 + t0l_:tok0 + t0l_ + n].rearrange("c p t -> p c t"), [], [y_.b])
                for c in range(4):
                    self.act(y_.ap[:, c, 0:n], y_.ap[:, c, 0:n], AF.Gelu_apprx_tanh, [y_.b], [y_.b])

            load_yl(ntl - 1)
            for tl in reversed(range(ntl)):
                t0l = tl * n
                cur = hbt
                ylt = ylts[tl % 2]
                AA, II = gates(1, t0l, n)
                if tl - 1 >= 0:
                    load_yl(tl - 1)
                for c in range(4):
                    if tl == ntl - 1:
                        init = st0.ap[:, 1, c:c + 1] if samp else 0.0
                    else:
                        init = carry.ap[:, c:c + 1]
                    self.P.op("dve", lambda e, o=rev(cur.ap[:, c, 0:n], n), a=rev(AA.ap[:, c, 0:n], n), b=rev(II.ap[:, c, 0:n], n), i0=init:
                              e.tensor_tensor_scan(out=o, data0=a, data1=b, initial=i0, op0=ALU.mult, op1=ALU.add),
                              [AA.b, II.b, carry.b, st0.b], [cur.b])
                self.cp("dve", carry.ap, cur.ap[:, :, 0], [cur.b], [carry.b])
                if not samp and tl == 0:
                    self.cp("dve", fin.ap[:, si - 1, 1, :], cur.ap[:, :, 0], [cur.b], [fin.b])
                for c in range(4):
                    self.tt("dve", cur.ap[:, c, 0:n], cur.ap[:, c, 0:n], hf.ap[:, c, t0l:t0l + n], ALU.add, [cur.b, hf.b], [cur.b])
                    self.tt("dve", ylt.ap[:, c, 0:n], cur.ap[:, c, 0:n], ylt.ap[:, c, 0:n], ALU.mult, [cur.b, ylt.b], [ylt.b])
                self.dma("sp", self.LRs.ap[:, :, tok0 + t0l:tok0 + t0l + n].rearrange("c p t -> p c t"), ylt.ap[:, :, 0:n], [ylt.b], [])
        self.tr(self.bank(0)[0:32, 0:128], fin.ap.rearrange("p s d c -> p (s d c)"), self.identf.ap, [fin.b, self.identf.b], [pb[0]])
        fo = A.f32(128)
        self.cp("dve", fo.ap[0:32, :], self.bank(0)[0:32, 0:128], [pb[0]], [fo.b])
        for s_ in range(NPB):
            self.dma("sp", O["nlru"][s_, l].rearrange("d (c p) -> (d c) p", p=128), fo.ap[s_ * 8:(s_ + 1) * 8, :], [fo.b], [])

    def p3c_s5(self, l):
        self.phase()
        A, I, O = self.A, self.I, self.O
        pb = self.pbufs
        Qw = A.bf16(2, 16, 2, 128); ML = A.bf16(32, 128)
        self.dma("sp", Qw.ap.rearrange("p d g c x -> p (d g c x)"), self.Qd[l].ap, [], [Qw.b])
        self.dma("sp", ML.ap.rearrange("p g x -> p (g x)"), self.MLd[l].ap, [], [ML.b])
        a12 = A.f32(2, 2, 16, 2)
        self.dma("sp", a12.ap.rearrange("p d k g c -> p (d k g c)"), self.A12[l].ap, [], [a12.b])
        h0r = A.f32(2, 128); h0s = A.f32(2, 16, 2)
        for d in range(2):
            self.dma("sp", h0r.ap[0:32, d, :], I["state_ssm"][l, d].rearrange("c (gp g2) n -> (c gp) (g2 n)", g2=2), [], [h0r.b])
            self.tr(self.bank(0)[:, d * 32:(d + 1) * 32], h0r.ap[0:32, d, :], self.identf.ap[0:32, 0:32], [h0r.b, self.identf.b], [pb[0]])
            self.cp("dve", h0s.ap[:, d].rearrange("p g c -> p c g"), view(self.bank(0)[:, d * 32:(d + 1) * 32], (2, 16)), [pb[0]], [h0s.b])
        zero = A.f32(16, 2, 4)
        self.ms("dve", zero.ap, 0.0, [zero.b])
        fin = A.f32(NPB, 2, 2, 16)
        mark0 = A.top
        for (tile0, ntile, nseq, Kseq, KB, tokbase) in ((0, 8, 1, 512, 256, 0), (8, 2, 4, 32, 128, TS)):
            A.top = mark0
            self.P.barrier()
            K = nseq * Kseq
            nblk = K // KB
            Uf = A.bf16(ntile, 32, 64)
            self.dma("sp", Uf.ap.rearrange("p t g k -> p t (g k)"), self.UFs.ap[tile0:tile0 + ntile].rearrange("t p x -> p t x"), [], [Uf.b])
            Hbf = [A.bf16(16, 2, nseq, Kseq + 1), A.bf16(16, 2, nseq, Kseq + 1)]
            mark1 = A.top
            PF = A.bf16(2, 16, 2, 128)
            self.dma("sp", PF.ap.rearrange("p d g c x -> p (d g c x)"), self.PFd[l].ap, [], [PF.b])
            Sd = [A.f32(16, 2, KB), A.f32(16, 2, KB)]
            tA = [A.f32(16, 2, nseq), A.f32(16, 2, nseq)]
            tB = [A.f32(16, 2, nseq), A.f32(16, 2, nseq)]
            hinit = [A.f32(16, 2, nseq), A.f32(16, 2, nseq)]
            tpb = KB // 64
            spb = KB // Kseq if nseq > 1 else 1
            for d in range(2):
                eng = "dve" if d == 0 else "pool"
                S = Sd[d]
                if nseq == 1:
                    self.cp(eng, hinit[d].ap[:, :, :, 0], h0s.ap[:, d], [h0s.b], [hinit[d].b])
                    self.cp("act", Hbf[d].ap[:, :, :, 0, 0 if d == 0 else Kseq], h0s.ap[:, d], [h0s.b], [Hbf[d].b])
                else:
                    self.ms(eng, hinit[d].ap, 0.0, [hinit[d].b])
                    self.ms(eng, Hbf[d].ap[:, :, :, :, 0 if d == 0 else Kseq], 0.0, [Hbf[d].b])
            for bidx in range(nblk):
                for d in range(2):
                    eng = "dve" if d == 0 else "pool"
                    S = Sd[d]
                    blk = bidx if d == 0 else nblk - 1 - bidx
                    for gp_ in range(16):
                        for c in range(2):
                            bi = (0 if d == 0 else 4) + (gp_ * 2 + c) % 4
                            for g2 in range(2):
                                g = 2 * gp_ + g2
                                self.mm(view(self.bank(bi)[g2 * 64:(g2 + 1) * 64, 0:KB], (tpb, 64)),
                                        PF.ap[:, d, gp_, c, g2 * 64:(g2 + 1) * 64],
                                        Uf.ap[:, blk * tpb:(blk + 1) * tpb, g, :], True, True, [PF.b, Uf.b], [pb[bi]])
                            self.cp("act", S.ap[:, gp_, c, :], self.bank(bi)[:, 0:KB], [pb[bi]], [S.b])
                for d in range(2):
                    eng = "dve" if d == 0 else "pool"
                    S = Sd[d]
                    blk = bidx if d == 0 else nblk - 1 - bidx
                    first_blk = bidx == 0
                    s_ = S.ap
                    base = s_.offset
                    pp = list(s_.ap[0])
                    nk = Kseq if nseq > 1 else KB

                    def col(kk, swap=False):
                        if swap:
                            return mkap(s_.tensor, base + KB + kk, [pp, [2 * KB, 16], [-KB, 2], [Kseq, spb]])
                        return mkap(s_.tensor, base + kk, [pp, [2 * KB, 16], [KB, 2], [Kseq, spb]])

                    def hv(t, swap=False):
                        a = t.ap
                        if swap:
                            return mkap(a.tensor, a.offset + nseq, [list(a.ap[0]), [2 * nseq, 16], [-nseq, 2], [1, spb]])
                        return mkap(a.tensor, a.offset, [list(a.ap[0]), [2 * nseq, 16], [nseq, 2], [1, spb]])

                    a1 = a12.ap[:, d, 0]
                    a2 = a12.ap[:, d, 1]
                    A1 = mkap(a1.tensor, a1.offset, [list(a1.ap[0]), [2, 16], [1, 2], [0, spb]])
                    A2 = mkap(a2.tensor, a2.offset, [list(a2.ap[0]), [2, 16], [1, 2], [0, spb]])
                    order = range(nk) if d == 0 else reversed(range(nk))
                    prev = None
                    for kk in order:
                        if prev is None:
                            if first_blk or nseq > 1:
                                pv_, psw = hv(hinit[d]), hv(hinit[d], True)
                                rdx = [hinit[d].b]
                            else:
                                pv_, psw = hv(hinit[d]), hv(hinit[d], True)
                                rdx = [hinit[d].b]
                        else:
                            pv_, psw = col(prev), col(prev, True)
                            rdx = []
                        ta, tb = tA[d], tB[d]
                        self.tt(eng, hv(ta), A1, pv_, ALU.mult, [a12.b, S.b] + rdx, [ta.b])
                        self.tt(eng, hv(tb), A2, psw, ALU.mult, [a12.b, S.b] + rdx, [tb.b])
                        self.tt(eng, hv(ta), hv(ta), hv(tb), ALU.add, [ta.b, tb.b], [ta.b])
                        self.tt(eng, col(kk), col(kk), hv(ta), ALU.add, [S.b, ta.b], [S.b])
                        prev = kk
                    if nseq == 1:
                        self.cp(eng, hv(hinit[d]), col(prev), [S.b], [hinit[d].b])
                    else:
                        for sq_ in range(spb):
                            seq = blk * spb + sq_
                            kcol = sq_ * Kseq + (Kseq - 1 if d == 0 else 0)
                            self.cp(eng, fin.ap[:, seq, d, :, :], S.ap[:, :, :, kcol].rearrange("p g c -> p c g"), [S.b], [fin.b])
                    for c in range(2):
                        if nseq == 1:
                            o0 = blk * KB + (1 if d == 0 else 0)
                            self.cp("act", Hbf[d].ap[:, :, c, 0, o0:o0 + KB], S.ap[:, :, c, :], [S.b], [Hbf[d].b])
                        else:
                            o0 = 1 if d == 0 else 0
                            self.cp("act", Hbf[d].ap[:, :, c, blk * spb:(blk + 1) * spb, o0:o0 + Kseq],
                                    S.ap[:, :, c, :].rearrange("p g (s k) -> p g s k", k=Kseq), [S.b], [Hbf[d].b])
            self.P.barrier()
            A.top = mark1
            Yf = A.bf16(32, K)
            Ytok = A.bf16(8, 512)
            YT = A.bf16(4, 1024)
            for g in range(32):
                gp_, g2 = g // 2, g % 2
                bi = g % 4
                hs = slice(g2 * 64, (g2 + 1) * 64)
                out = view(self.bank(bi)[:, 0:K], (ntile, 64))
                self.mm(out, ML.ap[:, g, :], Uf.ap[:, :, g, :], True, False, [ML.b, Uf.b], [pb[bi]])
                for d in range(2):
                    for c in range(2):
                        o0 = 0 if d == 0 else 1
                        self.mm(view(self.bank(bi)[:, 0:K], (nseq, Kseq)), Qw.ap[hs, d, gp_, c, :], Hbf[d].ap[hs, gp_, c, :, o0:o0 + Kseq],
                                False, d == 1 and c == 1, [Qw.b, Hbf[d].b], [pb[bi]])
                self.act(Yf.ap[:, g, :], self.bank(bi)[:, 0:K], AF.Gelu_apprx_tanh, [pb[bi]], [Yf.b])
            for kb in range(K // 128):
                for q4 in range(4):
                    bi = 4 + q4 % 2
                    pbf = self.bank(bi).bitcast(BF16)
                    for gg in range(8):
                        g = q4 * 8 + gg
                        self.tr(pbf[:, gg * 128:(gg + 1) * 128], Yf.ap[:, g, kb * 128:(kb + 1) * 128], self.identb.ap, [Yf.b, self.identb.b], [pb[bi]])
                    self.cp("dve" if q4 % 2 else "act", Ytok.ap[:, :, q4 * 128:(q4 + 1) * 128].rearrange("p t (g c) -> p g t c", c=16),
                            pbf[:, 0:1024].rearrange("p (g t c) -> p g t c", g=8, t=8), [pb[bi]], [Ytok.b])
                for t in range(8):
                    bi = 6 + t % 2
                    pbf = self.bank(bi).bitcast(BF16)
                    for cc in range(4):
                        self.tr(pbf[:, cc * 128:(cc + 1) * 128], Ytok.ap[:, t, cc * 128:(cc + 1) * 128], self.identb.ap, [Ytok.b, self.identb.b], [pb[bi]])
                    self.cp("dve" if t % 2 else "act", YT.ap.rearrange("p c (k s) -> p c s k", s=8)[:, :, t, :],
                            view(pbf[:, 0:512], (4, 128)), [pb[bi]], [YT.b])
                t0 = tokbase + kb * 1024
                self.dma("sp", self.SYs.ap[:, :, t0:t0 + 1024].rearrange("c p t -> p c t"), YT.ap, [YT.b], [])
        fo = A.f32(2, 128)
        ff = fin.ap.rearrange("p s d c g -> p (s d c g)")
        for h in range(2):
            self.tr(self.bank(h)[:, 0:128], ff[:, h * 128:(h + 1) * 128], self.identf.ap, [fin.b, self.identf.b], [pb[h]])
            self.cp("dve", fo.ap[:, h, :], self.bank(h)[:, 0:128], [pb[h]], [fo.b])
        for seq in range(NPB):
            h, r0 = seq // 2, (seq % 2) * 64
            self.dma("sp", O["nssm"][seq, l].rearrange("d c (gp g2) n -> (d c gp) (g2 n)", g2=2), fo.ap[r0:r0 + 64, h, :], [fo.b], [])


def build(dbg=False, stages=None):
    K = Kern(dbg)
    on = lambda nm: stages is None or nm in stages
    with K.st:
        if on("pro"):
            K.prologue()
        for l in range(L):
            if on("ffa%d" % l):
                K.ffn_pass2(l, 0, first=(l == 0))
            if on("p2%d" % l):
                K.p2_pass(l)
            if on("p3a%d" % l):
                K.p3a_attention(l)
            if on("p3b%d" % l):
                K.p3b_lru(l)
            if on("p3c%d" % l):
                K.p3c_s5(l)
            if on("p4%d" % l):
                K.p4_pass(l)
            if on("ffb%d" % l):
                K.ffn_pass2(l, 1, last=(l == L - 1))
        K.P.barrier()
        K.P.op("sp", lambda e: e.nop(), [], [])
        K.P.emit()
    return K


def _perm_q():
    idx = []
    for c in range(4):
        for h in (c, 4 + c):
            idx.extend(range(h * 64, (h + 1) * 64))
    return np.array(idx)


def _partner():
    p = np.zeros(64, np.int64)
    for d in range(64):
        p[d] = d + 16 if (d % 32) < 16 else d - 16
    return p


def _consts():
    cst = np.zeros((128, 5, 128), np.float32)
    j = np.arange(128)[:, None]
    i = np.arange(128)[None, :]
    cst[:, 0] = (j == i)
    cst[:, 1] = (j >= i)
    cst[:, 2] = (j <= i)
    cst[:, 3] = ((j // 16) <= (i // 16))
    cst[:, 4] = ((j // 16) >= (i // 16))
    t = np.arange(TS)
    row = (t // 64).astype(np.float64)
    colp = (t % 64).astype(np.float64)
    inv = 1.0 / (10000.0 ** (np.arange(16, dtype=np.float64) / 16))
    cos = np.zeros((64, TS)); sin = np.zeros((64, TS))
    for d in range(64):
        pos = row if d < 32 else colp
        ang = (pos.astype(np.float32) * np.float32(inv[d % 16]).astype(np.float32)).astype(np.float32)
        cos[d] = np.cos(ang)
        sgn = -1.0 if (d % 32) < 16 else 1.0
        sin[d] = sgn * np.sin(ang)
    rope = np.zeros((2, 128, TS), np.float32)
    rope[0, :64] = cos; rope[0, 64:] = cos
    rope[1, :64] = sin; rope[1, 64:] = sin
    return cst.reshape(128, 640), rope


_CACHE = {}


def kernel(**inp):
    f = lambda a: np.ascontiguousarray(np.asarray(a, dtype=np.float32))
    if "K" not in _CACHE:
        _CACHE["K"] = build()
    K = _CACHE["K"]
    pq = _perm_q()
    part = _partner()
    w_in = f(inp["w_in"])
    q_cols = pq
    qs_cols = np.array([(c // 64) * 64 + part[c % 64] for c in pq])
    k_cols = 512 + np.arange(128)
    ks_cols = 512 + np.array([(c // 64) * 64 + part[c % 64] for c in range(128)])
    rest = np.arange(640, 5376)
    cols = np.concatenate([q_cols, k_cols, qs_cols, ks_cols, rest])
    w_in_p = np.ascontiguousarray(w_in[:, :, cols])
    w_o_attn_p = np.ascontiguousarray(f(inp["w_o_attn"])[:, pq, :])
    cst, rope = _consts()
    shared = {k: f(inp[k]) for k in ("w_mod", "b_mod", "g_pre", "g_post", "w_ffn_gate", "w_ffn_up", "w_ffn_down", "w_conv",
                                     "b_conv", "w_lru_a", "b_lru_a", "w_lru_x", "b_lru_x", "lru_lambda", "s5_lambda_re",
                                     "s5_lambda_im", "s5_log_step", "s5_b_re", "s5_b_im", "s5_c_re", "s5_c_im", "s5_d",
                                     "w_glu", "attn_sink", "w_o_lru", "w_out")}
    shared["w_in_p"] = w_in_p
    shared["w_o_attn_p"] = w_o_attn_p
    shared["cst"] = cst
    shared["rope"] = rope
    xs = f(inp["x_sample"]); xp = f(inp["x_prompt"]); c = f(inp["c"]); cctx = f(inp["c_ctx"])
    ck = f(inp["cache_k"]); cv = f(inp["cache_v"]); sl = f(inp["state_lru"]); ss = f(inp["state_ssm"])
    in_maps = []
    for b in range(8):
        m = dict(shared)
        m["xin"] = np.ascontiguousarray(np.concatenate([xs[b], xp[4 * b:4 * b + 4].reshape(NPB * TP, D)], axis=0))
        m["cc"] = np.ascontiguousarray(np.stack([c[b], cctx], axis=0))
        m["cache_k"] = np.ascontiguousarray(ck[b].reshape(L, 512, 128))
        m["cache_v"] = np.ascontiguousarray(cv[b].reshape(L, 512, 128))
        m["state_lru"] = np.ascontiguousarray(sl[b])
        m["state_ssm"] = np.ascontiguousarray(ss[b])
        in_maps.append(m)
    res = run_bass_kernel_spmd(K.nc, in_maps, core_ids=list(range(8)))
    _CACHE["res"] = res
    R = res.results
    y_s = np.stack([R[b]["y"][:TS] for b in range(8)], axis=0)
    y_p = np.concatenate([R[b]["y"][TS:].reshape(NPB, TP, D) for b in range(8)], axis=0)
    nk = np.concatenate([R[b]["nk"].reshape(NPB, L, TP, 2, 64) for b in range(8)], axis=0)
    nv = np.concatenate([R[b]["nv"].reshape(NPB, L, TP, 2, 64) for b in range(8)], axis=0)
    nl = np.concatenate([R[b]["nlru"] for b in range(8)], axis=0)
    ns = np.concatenate([R[b]["nssm"] for b in range(8)], axis=0)
    return (y_p.astype(np.float32), y_s.astype(np.float32), nk.astype(np.float32), nv.astype(np.float32),
            nl.astype(np.float32), ns.astype(np.float32))
```

```python
import math
import contextlib
import numpy as np
import concourse.bass as bass
import concourse.mybir as mybir
from concourse.bass_utils import run_bass_kernel_spmd

F32 = mybir.dt.float32
BF16 = mybir.dt.bfloat16
AF = mybir.ActivationFunctionType
ALU = mybir.AluOpType
AX = mybir.AxisListType

NDMA_SEM = 8
L = 2
D = 1024
TS = 4096
TP = 256
NPB = 4
NTOK = TS + NPB * TP
TT = 512
NT = NTOK // TT
DFF = 2816
NFC = DFF // 128
WINP = 6016
C_Q, C_K, C_QS, C_KS, C_V, C_XL, C_YL, C_U, C_G = 0, 512, 640, 1152, 1280, 1408, 1920, 2432, 2944
ARENA_F = 52352
EPS = 1e-6


class Buf:
    __slots__ = ("w", "r")

    def __init__(self):
        self.w = {}
        self.r = {}


class Op:
    __slots__ = ("eng", "fn", "waits", "idx", "needed", "semval", "is_dma", "slot")

    def __init__(self, eng, fn):
        self.eng = eng
        self.fn = fn
        self.waits = {}
        self.needed = False
        self.semval = 0
        self.is_dma = False
        self.slot = 0


class Prog:
    ENGS = ("pe", "act", "dve", "pool", "sp")

    def __init__(self, nc):
        self.nc = nc
        self.ops = {e: [] for e in self.ENGS}
        self.ndma = {e: 0 for e in self.ENGS}
        self.pending = {e: {} for e in self.ENGS}
        self.mute = False

    def _add(self, eng, fn, reads, writes, is_dma):
        if self.mute:
            return None
        op = Op(eng, fn)
        op.is_dma = is_dma
        lst = self.ops[eng]
        op.idx = len(lst)
        lst.append(op)
        deps = op.waits
        if self.pending[eng]:
            deps.update(self.pending[eng])
            self.pending[eng] = {}
        if is_dma:
            didx = self.ndma[eng]
            self.ndma[eng] += 1
            op.slot = didx
            pkey = ("d", eng, didx % NDMA_SEM)
            pidx = didx
            if didx >= NDMA_SEM and deps.get(pkey, -1) < didx - NDMA_SEM:
                deps[pkey] = didx - NDMA_SEM
        else:
            pkey = ("c", eng)
            pidx = op.idx
        for b in reads:
            for k, v in b.w.items():
                if deps.get(k, -1) < v:
                    deps[k] = v
        for b in writes:
            for k, v in b.w.items():
                if deps.get(k, -1) < v:
                    deps[k] = v
            for k, v in b.r.items():
                if deps.get(k, -1) < v:
                    deps[k] = v
        for b in reads:
            if b.r.get(pkey, -1) < pidx:
                b.r[pkey] = pidx
        for b in writes:
            b.w = {pkey: pidx}
            b.r = {}
        if eng == "pe" and not is_dma:
            deps.pop(("c", "pe"), None)
        return op

    def op(self, eng, fn, reads=(), writes=()):
        return self._add(eng, fn, reads, writes, False)

    def dma(self, eng, fn, reads=(), writes=()):
        return self._add(eng, fn, reads, writes, True)

    def barrier(self):
        deps = {}
        for e in self.ENGS:
            last = None
            for op in reversed(self.ops[e]):
                if not op.is_dma:
                    last = op.idx
                    break
            if last is not None:
                deps[("c", e)] = last
            n = self.ndma[e]
            for s in range(NDMA_SEM):
                if n > s:
                    li = ((n - 1 - s) // NDMA_SEM) * NDMA_SEM + s
                    deps[("d", e, s)] = li
        for e in self.ENGS:
            p = self.pending[e]
            for k, v in deps.items():
                if p.get(k, -1) < v:
                    p[k] = v

    def emit(self):
        nc = self.nc
        for e in self.ENGS:
            seen = {}
            for op in self.ops[e]:
                new = {}
                for k, v in op.waits.items():
                    if seen.get(k, -1) >= v:
                        continue
                    seen[k] = v
                    new[k] = v
                op.waits = new
        for e in self.ENGS:
            for op in self.ops[e]:
                for k, v in op.waits.items():
                    if k[0] == "c":
                        self.ops[k[1]][v].needed = True
        for e in self.ENGS:
            c = 0
            for op in self.ops[e]:
                if op.is_dma:
                    continue
                if op.needed:
                    c += 1
                op.semval = c
        handles = {"pe": "tensor", "act": "scalar", "dve": "vector", "pool": "gpsimd", "sp": "sync"}
        with contextlib.ExitStack() as st:
            csem = {e: st.enter_context(nc.semaphore("c_" + e)) for e in self.ENGS}
            dsem = {e: [st.enter_context(nc.semaphore("d_%s_%d" % (e, i))) for i in range(NDMA_SEM)]
                    for e in self.ENGS if self.ndma[e]}
            block = st.enter_context(nc.Block())
            prog = self

            def run(e, eng):
                for op in prog.ops[e]:
                    for k, v in op.waits.items():
                        if k[0] == "c":
                            eng.wait_ge(csem[k[1]], prog.ops[k[1]][v].semval)
                        else:
                            eng.wait_ge(dsem[k[1]][k[2]], 16 * (v // NDMA_SEM + 1))
                    ins = op.fn(eng)
                    if op.is_dma:
                        ins.then_inc(dsem[e][op.slot % NDMA_SEM], 16)
                    elif op.needed:
                        ins.then_inc(csem[e], 1)

            for e in self.ENGS:
                if not self.ops[e]:
                    continue
                getattr(block, handles[e])(lambda eng, e=e: run(e, eng))


def mkap(t, offset, pairs):
    return bass.AP(t, offset, [list(p) for p in pairs])


def view(ap2, shape):
    if len(shape) == 1:
        return ap2
    names = " ".join("a%d" % i for i in range(len(shape)))
    kw = {"a%d" % i: s for i, s in enumerate(shape)}
    return ap2.rearrange("p (%s) -> p %s" % (names, names), **kw)


class Tl:
    __slots__ = ("ap", "b", "bs")

    def __init__(self, ap, nb=0):
        self.ap = ap
        self.b = Buf()
        self.bs = [Buf() for _ in range(nb)]


class Arena:
    def __init__(self, t, size):
        self.t = t
        self.size = size
        self.top = 0

    def reset(self):
        self.top = 0

    def f32(self, *shape, nb=0):
        n = int(np.prod(shape))
        assert self.top + n <= self.size, ("arena overflow", self.top, n)
        ap = self.t[:, self.top:self.top + n]
        self.top += n
        return Tl(view(ap, shape), nb)

    def bf16(self, *shape, nb=0):
        n = int(np.prod(shape))
        nf = (n + 1) // 2
        assert self.top + nf <= self.size, ("arena overflow", self.top, nf)
        ap = self.t[:, self.top:self.top + nf].bitcast(BF16)[:, 0:n]
        self.top += nf
        return Tl(view(ap, shape), nb)


def bcast_free(ap, n):
    return mkap(ap.tensor, ap.offset, [list(ap.ap[0]), [0, n]])


class Kern:
    def __init__(self, dbg=False):
        self.dbg = dbg
        nc = self.nc = bass.Bass("TRN2", target_bir_lowering=False)
        self.P = Prog(nc)
        self.st = contextlib.ExitStack()
        I = self.I = {}
        O = self.O = {}

        def inp(name, shape, dt=F32):
            I[name] = nc.dram_tensor(name, list(shape), dt, kind="ExternalInput").ap()

        def outp(name, shape, dt=F32):
            O[name] = nc.dram_tensor(name, list(shape), dt, kind="ExternalOutput").ap()

        inp("xin", [NTOK, D]); inp("cc", [2, D]); inp("cache_k", [L, 512, 128]); inp("cache_v", [L, 512, 128])
        inp("state_lru", [L, 2, 512]); inp("state_ssm", [L, 2, 2, 32, 64])
        inp("w_mod", [L, D, 9 * D]); inp("b_mod", [L, 9 * D]); inp("g_pre", [L, 3, D]); inp("g_post", [L, 3, D])
        inp("w_ffn_gate", [L, 2, D, DFF]); inp("w_ffn_up", [L, 2, D, DFF]); inp("w_ffn_down", [L, 2, DFF, D])
        inp("w_in_p", [L, D, WINP]); inp("w_conv", [L, 4, 512]); inp("b_conv", [L, 512])
        inp("w_lru_a", [L, 2, 8, 64, 64]); inp("b_lru_a", [L, 2, 512]); inp("w_lru_x", [L, 2, 8, 64, 64])
        inp("b_lru_x", [L, 2, 512]); inp("lru_lambda", [L, 2, 512])
        inp("s5_lambda_re", [L, 2, 32, 64]); inp("s5_lambda_im", [L, 2, 32, 64]); inp("s5_log_step", [L, 2, 32])
        inp("s5_b_re", [L, 2, 32, 64, 16]); inp("s5_b_im", [L, 2, 32, 64, 16])
        inp("s5_c_re", [L, 2, 32, 16, 64]); inp("s5_c_im", [L, 2, 32, 16, 64]); inp("s5_d", [L, 512])
        inp("w_glu", [L, 512, 2048]); inp("attn_sink", [L, 8]); inp("w_o_lru", [L, 512, D])
        inp("w_o_attn_p", [L, 512, D]); inp("w_out", [L, D, D])
        inp("cst", [128, 5 * 128]); inp("rope", [2, 128, TS])
        outp("y", [NTOK, D]); outp("nk", [NPB, L, TP, 128]); outp("nv", [NPB, L, TP, 128])
        outp("nlru", [NPB, L, 2, 512]); outp("nssm", [NPB, L, 2, 2, 32, 64])
        self.obufs = {k: Buf() for k in O}
        if dbg:
            outp("d_sc", [128, L * 144])

        def scr(name, shape, dt):
            kind = "ExternalOutput" if dbg else "Internal"
            t = nc.dram_tensor(name, list(shape), dt, kind=kind).ap()
            return Tl(t)

        self.XT = scr("s_xt", [8, 128, NTOK], F32)
        self.Qs = scr("s_q", [4, 128, NTOK], BF16)
        self.Ks = scr("s_k", [128, NTOK], BF16)
        self.Vs = scr("s_v", [NTOK, 130], BF16)
        self.XLs = scr("s_xl", [4, 128, NTOK], F32)
        self.YLs = scr("s_yl", [4, 128, NTOK], BF16)
        self.UFs = scr("s_uf", [NT, 128, 32 * 64], BF16)
        self.Gs = scr("s_g", [24, 128, NTOK], BF16)
        self.ATs = scr("s_att", [4, 128, NTOK], BF16)
        self.LRs = scr("s_lru", [4, 128, NTOK], BF16)
        self.SYs = scr("s_s5y", [4, 128, NTOK], BF16)
        self.PFd = [scr("s_pf%d" % l, [128, 2 * 16 * 2 * 128], BF16) for l in range(L)]
        self.Qd = [scr("s_qd%d" % l, [128, 2 * 16 * 2 * 128], BF16) for l in range(L)]
        self.MLd = [scr("s_ml%d" % l, [128, 32 * 128], BF16) for l in range(L)]
        self.A12 = [scr("s_a12%d" % l, [128, 128], F32) for l in range(L)]

        self.sb_t = self.st.enter_context(nc.sbuf_tensor("arena", [128, ARENA_F], F32))
        self.pc_t = self.st.enter_context(nc.sbuf_tensor("persist", [128, 832], F32))
        self.ps_t = self.st.enter_context(nc.psum_tensor("psum", [128, 4096], F32))
        self.A = Arena(self.sb_t, ARENA_F)
        self.PA = Arena(self.pc_t, 832)
        self.pbufs = [Buf() for _ in range(8)]

    def bank(self, i):
        return self.ps_t[:, i * 512:(i + 1) * 512]

    def mm(self, out, lhsT, rhs, start, stop, reads, writes):
        self.P.op("pe", lambda e: e.matmul(out, lhsT=lhsT, rhs=rhs, start=start, stop=stop), reads, writes)

    def tr(self, out, in_, ident, reads, writes):
        self.P.op("pe", lambda e: e.transpose(out, in_, ident), reads, writes)

    def act(self, out, in_, func, reads, writes, scale=1.0, bias=0.0):
        self.P.op("act", lambda e: e.activation(out=out, in_=in_, func=func, scale=scale, bias=bias), reads, writes)

    def tt(self, eng, out, in0, in1, op, reads, writes):
        self.P.op(eng, lambda e: e.tensor_tensor(out=out, in0=in0, in1=in1, op=op), reads, writes)

    def ts(self, eng, out, in0, s1, s2, op0, op1, reads, writes):
        if s2 is None:
            self.P.op(eng, lambda e: e.tensor_scalar(out=out, in0=in0, scalar1=s1, scalar2=None, op0=op0), reads, writes)
        else:
            self.P.op(eng, lambda e: e.tensor_scalar(out=out, in0=in0, scalar1=s1, scalar2=s2, op0=op0, op1=op1), reads, writes)

    def stt(self, out, in0, scalar, in1, op0, op1, reads, writes):
        self.P.op("dve", lambda e: e.scalar_tensor_tensor(out=out, in0=in0, scalar=scalar, in1=in1, op0=op0, op1=op1), reads, writes)

    def cp(self, eng, out, in_, reads, writes):
        if eng == "act":
            self.P.op("act", lambda e: e.copy(out=out, in_=in_), reads, writes)
        else:
            self.P.op(eng, lambda e: e.tensor_copy(out=out, in_=in_), reads, writes)

    def ms(self, eng, out, val, writes):
        self.P.op(eng, lambda e: e.memset(out, val), (), writes)

    def dma(self, q, out, in_, reads, writes, slow=False):
        assert not slow
        shp = tuple(out.shape)
        if len(shp) >= 3 and shp[0] * shp[1] > 256 and shp[1] > 1 and tuple(in_.shape)[:2] == shp[:2]:
            step = max(1, 256 // shp[0])
            for a in range(0, shp[1], step):
                e_ = min(a + step, shp[1])
                o_ = out[:, a:e_]
                i_ = in_[:, a:e_]
                self.P.dma(q, lambda e, o_=o_, i_=i_: e.dma_start(out=o_, in_=i_), reads, writes)
            return
        self.P.dma(q, lambda e: e.dma_start(out=out, in_=in_), reads, writes)

    def vecT(self, dst, src_rows, n, writes):
        stg = self.A.f32(128)
        self.P.dma("sp", lambda e: e.dma_start(out=stg.ap[0:n, :], in_=src_rows), [], [stg.b])
        self.tr(self.bank(7)[:, 0:n], stg.ap[0:n, :], self.identf.ap[0:n, 0:n], [stg.b, self.identf.b], [self.pbufs[7]])
        return self.bank(7)[:, 0:n], self.pbufs[7]

    def recip(self, out, in_, reads, writes):
        self.P.op("dve", lambda e: e.reciprocal(out=out, in_=in_), reads, writes)

    def phase(self):
        self.P.barrier()
        self.A.reset()
        self.pbufs = [Buf() for _ in range(8)]

    def prologue(self):
        A, PA, I = self.A, self.PA, self.I
        cst = A.f32(5, 128)
        self.dma("sp", cst.ap, I["cst"].rearrange("p (a b) -> p a b", a=5), [], [cst.b])
        self.identf = PA.f32(128)
        self.identb = PA.bf16(128)
        self.onesb = PA.bf16(128)
        self.onesf = PA.f32(128)
        self.mprev = PA.bf16(128)
        self.mnext = PA.bf16(128)
        self.cp("dve", self.identf.ap, cst.ap[:, 0, :], [cst.b], [self.identf.b])
        self.cp("dve", self.identb.ap, cst.ap[:, 0, :], [cst.b], [self.identb.b])
        self.cp("dve", self.mprev.ap, cst.ap[:, 1, :], [cst.b], [self.mprev.b])
        self.cp("dve", self.mnext.ap, cst.ap[:, 2, :], [cst.b], [self.mnext.b])
        self.ms("pool", self.onesb.ap, 1.0, [self.onesb.b])
        self.ms("pool", self.onesf.ap, 1.0, [self.onesf.b])
        self.SC = PA.f32(L, 3, 3, 8, 2)
        ccT = A.f32(8, 2)
        src, sb_ = self.vecT(None, I["cc"].rearrange("w (kc p) -> (w kc) p", p=128), 16, None)
        self.cp("dve", ccT.ap.rearrange("p kc w -> p w kc"), view(src, (2, 8)), [sb_], [ccT.b])
        sg = A.f32(8, 2)
        self.act(sg.ap, ccT.ap, AF.Sigmoid, [ccT.b], [sg.b])
        self.tt("dve", ccT.ap, ccT.ap, sg.ap, ALU.mult, [ccT.b, sg.b], [ccT.b])
        wm = [A.f32(8, 1152), A.f32(8, 1152)]
        modt = A.f32(L, 72, 2)
        bm = A.f32(L, 72)
        gp = A.f32(L, 3, 8)
        gq = A.f32(L, 3, 8)
        for l in range(L):
            src, sb_ = self.vecT(None, I["b_mod"][l].rearrange("(c p) -> c p", p=128), 72, None)
            self.cp("dve", bm.ap[:, l, :], src, [sb_], [bm.b])
            src, sb_ = self.vecT(None, I["g_pre"][l].rearrange("i (c p) -> (i c) p", p=128), 24, None)
            self.cp("dve", gp.ap[:, l].rearrange("p i c -> p (i c)"), src, [sb_], [gp.b])
            src, sb_ = self.vecT(None, I["g_post"][l].rearrange("i (c p) -> (i c) p", p=128), 24, None)
            self.cp("dve", gq.ap[:, l].rearrange("p i c -> p (i c)"), src, [sb_], [gq.b])
        pm = self.bank(0)
        pmb = self.pbufs[0]
        k = 0
        for l in range(L):
            for piece in range(8):
                w_ = wm[k % 2]
                q = "sp" if k % 2 == 0 else "act"
                k += 1
                src = I["w_mod"][l][:, piece * 1152:(piece + 1) * 1152].rearrange("(kc p) c -> p kc c", p=128)
                self.dma(q, w_.ap, src, [], [w_.b])
                for cch in range(9):
                    col = (l * 72 + piece * 9 + cch) * 2
                    for kc in range(8):
                        self.mm(pm[:, col:col + 2], w_.ap[:, kc, cch * 128:(cch + 1) * 128], ccT.ap[:, kc, :],
                                kc == 0, kc == 7, [w_.b, ccT.b], [pmb])
        self.cp("dve", modt.ap, view(pm[:, 0:L * 144], (L, 72, 2)), [pmb], [modt.b])
        for w in range(2):
            self.tt("dve", modt.ap[:, :, :, w], modt.ap[:, :, :, w], bm.ap, ALU.add, [modt.b, bm.b], [modt.b])
        for l in range(L):
            m5 = modt.ap[:, l].rearrange("p (i k c) w -> p i k c w", i=3, k=3)
            for w in range(2):
                self.stt(self.SC.ap[:, l, 0, :, :, w], m5[:, :, 1, :, w], 1.0, gp.ap[:, l], ALU.add, ALU.mult,
                         [modt.b, gp.b], [self.SC.b])
                self.cp("dve", self.SC.ap[:, l, 1, :, :, w], m5[:, :, 0, :, w], [modt.b], [self.SC.b])
                self.tt("dve", self.SC.ap[:, l, 2, :, :, w], m5[:, :, 2, :, w], gq.ap[:, l], ALU.mult,
                        [modt.b, gq.b], [self.SC.b])
            for i in (0, 2):
                self.ts("dve", self.SC.ap[:, l, 2, i], self.SC.ap[:, l, 2, i], 0.5, None, ALU.mult, None,
                        [self.SC.b], [self.SC.b])
        if self.dbg:
            self.dma("sp", self.O["d_sc"], self.SC.ap.rearrange("p l k i c w -> p (l k i c w)"), [self.SC.b], [])
        for l in range(L):
            self.s5_prep(l, cst)

    def scal(self, l, kind, i, c, w):
        return self.SC.ap[:, l, kind, i, c, w:w + 1]

    def s5_prep(self, l, cst_unused=None):
        self.phase()
        A, I = self.A, self.I
        cst = A.f32(5, 128)
        self.dma("sp", cst.ap, I["cst"].rearrange("p (a b) -> p a b", a=5), [], [cst.b])
        idf = self.identf
        pb = self.pbufs
        lre_r = A.f32(128); lim_r = A.f32(128); lst_r = A.f32(2); lst_x = A.f32(128)
        self.dma("sp", lre_r.ap[0:32, :], I["s5_lambda_re"][l].rearrange("d (gp g2) n -> (d gp) (g2 n)", g2=2), [], [lre_r.b])
        self.dma("sp", lim_r.ap[0:32, :], I["s5_lambda_im"][l].rearrange("d (gp g2) n -> (d gp) (g2 n)", g2=2), [], [lim_r.b])
        self.dma("sp", lst_r.ap[0:32, :], I["s5_log_step"][l].rearrange("d (gp g2) -> (d gp) g2", g2=2), [], [lst_r.b])
        a_ = lst_r.ap[0:32, :]
        self.cp("dve", view(lst_x.ap[0:32, :], (2, 64)), mkap(a_.tensor, a_.offset, [list(a_.ap[0]), [1, 2], [0, 64]]),
                [lst_r.b], [lst_x.b])
        RR = A.f32(8192)
        braw = [Tl(view(RR.ap[:, c * 2048:(c + 1) * 2048], (128, 16))) for c in range(2)]
        craw = [Tl(view(RR.ap[:, (2 + c) * 2048:(3 + c) * 2048], (16, 2, 64))) for c in range(2)]
        for c, nm in enumerate(("s5_b_re", "s5_b_im")):
            self.dma("act", braw[c].ap[0:32], I[nm][l].rearrange("d (gp g2) n ci -> (d gp) (g2 n) ci", g2=2), [], [braw[c].b])
        for c, nm in enumerate(("s5_c_re", "s5_c_im")):
            srcv = I[nm][l].rearrange("d (gp g2) co n -> (d gp) g2 co n", g2=2)
            for g2 in range(2):
                self.dma("act", craw[c].ap[0:32, :, g2, :], srcv[:, g2], [], [craw[c].b])
        draw = A.f32(16); dx = A.f32(8, 16)
        self.dma("sp", draw.ap[0:32, :], I["s5_d"][l].rearrange("(g ci) -> g ci", ci=16), [], [draw.b])
        a_ = draw.ap[0:32, :]
        self.cp("dve", dx.ap[0:32], mkap(a_.tensor, a_.offset, [list(a_.ap[0]), [0, 8], [1, 16]]), [draw.b], [dx.b])
        sc = A.f32(40, 32)
        names = {}

        def S(nm):
            if nm not in names:
                names[nm] = len(names)
                assert len(names) <= 40
            return sc.ap[:, names[nm], :]

        scb = sc.b
        pt = self.bank(0)
        self.tr(pt[:, 0:32], lre_r.ap[0:32, :], idf.ap[0:32, 0:32], [lre_r.b, idf.b], [pb[0]])
        self.tr(pt[:, 32:64], lim_r.ap[0:32, :], idf.ap[0:32, 0:32], [lim_r.b, idf.b], [pb[0]])
        self.tr(pt[:, 64:96], lst_x.ap[0:32, :], idf.ap[0:32, 0:32], [lst_x.b, idf.b], [pb[0]])
        self.tr(pt[:, 96:128], dx.ap[0:32].rearrange("p a b -> p (a b)"), idf.ap[0:32, 0:32], [dx.b, idf.b], [pb[0]])
        self.cp("dve", S("lr"), pt[:, 0:32], [pb[0]], [scb])
        self.cp("dve", S("li"), pt[:, 32:64], [pb[0]], [scb])
        self.cp("dve", S("ls"), pt[:, 64:96], [pb[0]], [scb])
        dcol = A.f32(32)
        self.cp("dve", dcol.ap, pt[:, 96:128], [pb[0]], [dcol.b])
        BC = []
        for idx, raw in enumerate(braw + craw):
            bk = self.bank(1 + idx % 2)
            bb = pb[1 + idx % 2]
            for j in range(16):
                if idx < 2:
                    src = raw.ap[0:32, :, j]
                else:
                    src = raw.ap[0:32, j].rearrange("p a b -> p (a b)")
                self.tr(bk[:, j * 32:(j + 1) * 32], src, idf.ap[0:32, 0:32], [raw.b, idf.b], [bb])
            t = A.f32(16, 32)
            self.cp("act", t.ap, view(bk, (16, 32)), [bb], [t.b])
            BC.append(t)
        Bre, Bim, Cre, Cim = BC

        def dv(out, a, b, op):
            self.tt("dve", out, a, b, op, [scb], [scb])

        self.ts("dve", S("lr"), S("lr"), -1e-4, None, ALU.min, None, [scb], [scb])
        self.act(S("step"), S("ls"), AF.Exp, [scb], [scb])
        dv(S("xre"), S("lr"), S("step"), ALU.mult)
        dv(S("ang"), S("li"), S("step"), ALU.mult)
        self.act(S("mag"), S("xre"), AF.Exp, [scb], [scb])
        hp = A.f32(1)
        self.ms("dve", hp.ap, math.pi / 2, [hp.b])
        self.act(S("s"), S("ang"), AF.Sin, [scb], [scb], scale=1.0 / 16)
        self.act(S("c"), S("ang"), AF.Sin, [scb, hp.b], [scb], scale=-1.0 / 16, bias=hp.ap[:, 0:1])
        for _ in range(4):
            dv(S("t1"), S("c"), S("c"), ALU.mult)
            dv(S("t2"), S("s"), S("s"), ALU.mult)
            dv(S("t3"), S("c"), S("s"), ALU.mult)
            dv(S("c"), S("t1"), S("t2"), ALU.subtract)
            self.ts("dve", S("s"), S("t3"), 2.0, None, ALU.mult, None, [scb], [scb])
        dv(S("are"), S("mag"), S("c"), ALU.mult)
        dv(S("aim"), S("mag"), S("s"), ALU.mult)
        dv(S("t1"), S("lr"), S("lr"), ALU.mult)
        dv(S("t2"), S("li"), S("li"), ALU.mult)
        dv(S("den"), S("t1"), S("t2"), ALU.add)
        self.recip(S("rden"), S("den"), [scb], [scb])
        self.ts("dve", S("nre"), S("are"), -1.0, None, ALU.add, None, [scb], [scb])
        dv(S("t1"), S("nre"), S("lr"), ALU.mult)
        dv(S("t2"), S("aim"), S("li"), ALU.mult)
        dv(S("t1"), S("t1"), S("t2"), ALU.add)
        dv(S("cre"), S("t1"), S("rden"), ALU.mult)
        dv(S("t1"), S("aim"), S("lr"), ALU.mult)
        dv(S("t2"), S("nre"), S("li"), ALU.mult)
        dv(S("t1"), S("t1"), S("t2"), ALU.subtract)
        dv(S("cim"), S("t1"), S("rden"), ALU.mult)
        dv(S("t1"), S("mag"), S("mag"), ALU.mult)
        self.recip(S("t2"), S("t1"), [scb], [scb])
        dv(S("iare"), S("are"), S("t2"), ALU.mult)
        dv(S("t3"), S("aim"), S("t2"), ALU.mult)
        self.ts("dve", S("iaim"), S("t3"), -1.0, None, ALU.mult, None, [scb], [scb])
        pw = A.f32(9, 2, 32)
        self.ms("dve", pw.ap[:, 0, 0, :], 1.0, [pw.b])
        self.ms("dve", pw.ap[:, 0, 1, :], 0.0, [pw.b])
        for j in range(1, 9):
            for (o, x1, y1, x2, y2, op) in ((pw.ap[:, j, 0, :], pw.ap[:, j - 1, 0, :], S("are"), pw.ap[:, j - 1, 1, :], S("aim"), ALU.subtract),
                                            (pw.ap[:, j, 1, :], pw.ap[:, j - 1, 0, :], S("aim"), pw.ap[:, j - 1, 1, :], S("are"), ALU.add)):
                self.tt("dve", S("t1"), x1, y1, ALU.mult, [pw.b, scb], [scb])
                self.tt("dve", S("t2"), x2, y2, ALU.mult, [pw.b, scb], [scb])
                self.tt("dve", o, S("t1"), S("t2"), op, [scb], [pw.b])
        a12 = A.f32(2, 2, 16, 2)
        for d in range(2):
            for c in range(2):
                self.cp("dve", a12.ap[:, d, 0, :, c], pw.ap[:, 8, 0, d * 16:(d + 1) * 16], [pw.b], [a12.b])
            self.ts("dve", a12.ap[:, d, 1, :, 0], pw.ap[:, 8, 1, d * 16:(d + 1) * 16], -1.0, None, ALU.mult, None, [pw.b], [a12.b])
            self.cp("dve", a12.ap[:, d, 1, :, 1], pw.ap[:, 8, 1, d * 16:(d + 1) * 16], [pw.b], [a12.b])
        self.dma("sp", self.A12[l].ap, a12.ap.rearrange("p d k g c -> p (d k g c)"), [a12.b], [])
        self.cp("dve", S("pr"), S("iare"), [scb], [scb])
        self.cp("dve", S("pi"), S("iaim"), [scb], [scb])
        for _ in range(3):
            dv(S("t1"), S("pr"), S("pr"), ALU.mult)
            dv(S("t2"), S("pi"), S("pi"), ALU.mult)
            dv(S("t3"), S("pr"), S("pi"), ALU.mult)
            dv(S("pr"), S("t1"), S("t2"), ALU.subtract)
            self.ts("dve", S("pi"), S("t3"), 2.0, None, ALU.mult, None, [scb], [scb])
        Bbr = A.f32(16, 32); Bbi = A.f32(16, 32); T1 = A.f32(16, 32); T2 = A.f32(16, 32)

        def bc16(ap):
            return mkap(ap.tensor, ap.offset, [list(ap.ap[0]), [0, 16], [1, 32]])

        for (o, x1, x2, op) in ((Bbr, Bre, Bim, ALU.subtract), (Bbi, Bim, Bre, ALU.add)):
            self.tt("dve", T1.ap, x1.ap, bc16(S("cre")), ALU.mult, [x1.b, scb], [T1.b])
            self.tt("dve", T2.ap, x2.ap, bc16(S("cim")), ALU.mult, [x2.b, scb], [T2.b])
            self.tt("dve", o.ap, T1.ap, T2.ap, op, [T1.b, T2.b], [o.b])
        XS = A.f32(2, 16, 2, 8, 16)
        Q = A.f32(2, 16, 2, 8, 16)
        tmp = [[A.f32(16, 16), A.f32(16, 16)] for _ in range(2)]

        def pwv(j, c, d):
            a = pw.ap[:, j, c, d * 16:(d + 1) * 16]
            return mkap(a.tensor, a.offset, [list(a.ap[0]), [1, 16], [0, 16]])

        def mat(tl, d):
            return tl.ap[:, :, d * 16:(d + 1) * 16].rearrange("p x g -> p g x")

        k = 0
        for d in range(2):
            for s in range(8):
                for (dst, Mr, Mi, j, neg_im) in ((XS, Bbr, Bbi, (7 - s) if d == 0 else s, False),
                                                 (Q, Cre, Cim, (s + 1) if d == 0 else (8 - s), True)):
                    eng = "dve" if k % 2 == 0 else "pool"
                    t1, t2 = tmp[k % 2]
                    k += 1
                    rd = [Mr.b, Mi.b, pw.b]
                    self.tt(eng, t1.ap, mat(Mr, d), pwv(j, 0, d), ALU.mult, rd, [t1.b])
                    self.tt(eng, t2.ap, mat(Mi, d), pwv(j, 1, d), ALU.mult, rd, [t2.b])
                    self.tt(eng, dst.ap[:, d, :, 0, s, :], t1.ap, t2.ap, ALU.subtract, [t1.b, t2.b], [dst.b])
                    self.tt(eng, t1.ap, mat(Mr, d), pwv(j, 1, d), ALU.mult, rd, [t1.b])
                    self.tt(eng, t2.ap, mat(Mi, d), pwv(j, 0, d), ALU.mult, rd, [t2.b])
                    self.tt(eng, dst.ap[:, d, :, 1, s, :], t1.ap, t2.ap, ALU.add, [t1.b, t2.b], [dst.b])
        qim = Q.ap[:, :, :, 1].rearrange("p d g s c -> p (d g) (s c)")
        self.ts("pool", qim, qim, -1.0, None, ALU.mult, None, [Q.b], [Q.b])
        XM = A.f32(2, 16, 2, 128)
        X4 = XS.ap.rearrange("p d g c s i -> p d g c (s i)")
        big = [A.f32(16, 128), A.f32(16, 128)]

        def pv(nm, d):
            a = S(nm)[:, d * 16:(d + 1) * 16]
            return mkap(a.tensor, a.offset, [list(a.ap[0]), [1, 16], [0, 128]])

        for d in range(2):
            eng = "dve" if d == 0 else "pool"
            t1, t2 = big
            rd = [XS.b, scb]
            self.tt(eng, t1.ap, X4[:, d, :, 0, :], pv("pr", d), ALU.mult, rd, [t1.b])
            self.tt(eng, t2.ap, X4[:, d, :, 1, :], pv("pi", d), ALU.mult, rd, [t2.b])
            self.tt(eng, XM.ap[:, d, :, 0, :], t1.ap, t2.ap, ALU.subtract, [t1.b, t2.b], [XM.b])
            self.tt(eng, t1.ap, X4[:, d, :, 1, :], pv("pr", d), ALU.mult, rd, [t1.b])
            self.tt(eng, t2.ap, X4[:, d, :, 0, :], pv("pi", d), ALU.mult, rd, [t2.b])
            self.tt(eng, XM.ap[:, d, :, 1, :], t1.ap, t2.ap, ALU.add, [t1.b, t2.b], [XM.b])
        self.P.barrier()
        PFs = Tl(view(RR.ap[:, 0:4096].bitcast(BF16), (64, 128)))
        Q4 = Q.ap.rearrange("p d g c s i -> p (d g c) (s i)")
        XS3 = XS.ap.rearrange("p d g c s i -> p (d g c) (s i)")
        for q4 in range(16):
            bk = self.bank(3 + q4 % 2); bb = pb[3 + q4 % 2]
            for jj in range(4):
                self.tr(bk[:, jj * 128:(jj + 1) * 128], XS3[:, q4 * 4 + jj, :], idf.ap, [XS.b, idf.b], [bb])
            self.cp("act", PFs.ap[:, q4 * 4:(q4 + 1) * 4, :], view(bk, (4, 128)), [bb], [PFs.b])
        self.dma("sp", self.PFd[l].ap, PFs.ap.rearrange("p a b -> p (a b)"), [PFs.b], [])
        Qb = Tl(view(RR.ap[:, 4096:8192].bitcast(BF16), (64, 128)))
        self.cp("pool", Qb.ap, Q4, [Q.b], [Qb.b])
        self.dma("sp", self.Qd[l].ap, Qb.ap.rearrange("p a b -> p (a b)"), [Qb.b], [])
        MLs = A.bf16(32, 128)
        mt = [A.f32(128), A.f32(128)]
        mu = [A.f32(128), A.f32(128)]
        XM4 = XM.ap
        Q5 = Q.ap.rearrange("p d g c s i -> p d g c (s i)")
        for g in range(32):
            gp_, g2 = g // 2, g % 2
            bk = self.bank(5 + g % 2); bb = pb[5 + g % 2]
            sl = slice(g2 * 64, (g2 + 1) * 64)
            for d in range(2):
                for c in range(2):
                    self.mm(bk[:, d * 128:(d + 1) * 128], XM4[sl, d, gp_, c, :], Q5[sl, d, gp_, c, :], c == 0, c == 1,
                            [XM.b, Q.b], [bb])
            t = mt[g % 2]
            u = mu[g % 2]
            self.tt("dve", t.ap, bk[:, 0:128], cst.ap[:, 3, :], ALU.mult, [bb, cst.b], [t.b])
            self.tt("dve", u.ap, bk[:, 128:256], cst.ap[:, 4, :], ALU.mult, [bb, cst.b], [u.b])
            self.tt("pool", t.ap, t.ap, u.ap, ALU.add, [t.b, u.b], [t.b])
            self.stt(MLs.ap[:, g, :], idf.ap, dcol.ap[:, g:g + 1], t.ap, ALU.mult, ALU.add, [idf.b, dcol.b, t.b], [MLs.b])
        self.dma("sp", self.MLd[l].ap, MLs.ap.rearrange("p a b -> p (a b)"), [MLs.b], [])

    def alloc_norm(self):
        A = self.A
        self.n_rstd = A.f32(TT)
        self.n_tmp = [A.f32(TT), A.f32(TT)]
        self.n_k = 0

    def rstd_from(self, bankidx):
        r = self.n_rstd
        pbk = self.pbufs[bankidx]
        self.ts("dve", r.ap, self.bank(bankidx), 1.0 / D, EPS, ALU.mult, ALU.add, [pbk], [r.b])
        self.act(r.ap, r.ap, AF.Sqrt, [r.b], [r.b])
        self.recip(r.ap, r.ap, [r.b], [r.b])
        return r

    def load_x(self, xt, ti, first=False, alias=None):
        tok0 = ti * TT
        if not first:
            self.dma("sp", xt.ap, self.XT.ap[:, :, tok0:tok0 + TT].rearrange("c p t -> p c t"), [], xt.bs)
            return
        xtm, extra = alias
        self.dma("sp", xtm.ap, self.I["xin"][tok0:tok0 + TT].rearrange("(b p) d -> p b d", p=128), [], [xtm.b] + extra)
        for c in range(8):
            bi = c % 2
            for blk in range(4):
                self.tr(self.bank(bi)[:, blk * 128:(blk + 1) * 128], xtm.ap[:, blk, c * 128:(c + 1) * 128], self.identf.ap,
                        [xtm.b, self.identf.b] + extra, [self.pbufs[bi]])
            self.cp("act" if c % 2 else "dve", xt.ap[:, c, :], self.bank(bi), [self.pbufs[bi]], [xt.bs[c]])

    def store_x(self, xt, ti, last=False, alias=None):
        tok0 = ti * TT
        if not last:
            self.dma("sp", self.XT.ap[:, :, tok0:tok0 + TT].rearrange("c p t -> p c t"), xt.ap, xt.bs, [])
            return
        yst, extra = alias
        for blk in range(4):
            for half in range(2):
                bi = (blk * 2 + half) % 2
                for cc in range(4):
                    c = half * 4 + cc
                    self.tr(self.bank(bi)[:, cc * 128:(cc + 1) * 128], xt.ap[:, c, blk * 128:(blk + 1) * 128], self.identf.ap,
                            [xt.bs[c], self.identf.b], [self.pbufs[bi]])
                self.cp("act" if half else "dve", yst.ap[:, blk, half * 512:(half + 1) * 512], self.bank(bi),
                        [self.pbufs[bi]], [yst.b] + extra)
        self.dma("sp", self.O["y"][tok0:tok0 + TT].rearrange("(b p) d -> p b d", p=128), yst.ap, [yst.b] + extra, [])

    def prenorm(self, l, sub, which, xt, hT, sq, bankidx=7):
        pbk = self.pbufs[bankidx]
        for c in range(8):
            self.tt("pool", sq.ap[:, c, :], xt.ap[:, c, :], xt.ap[:, c, :], ALU.mult, [xt.bs[c]], [sq.b])
        for c in range(8):
            self.mm(self.bank(bankidx), self.onesb.ap, sq.ap[:, c, :], c == 0, c == 7, [self.onesb.b, sq.b], [pbk])
        r = self.rstd_from(bankidx)
        for c in range(8):
            t = self.n_tmp[self.n_k % 2]
            self.n_k += 1
            self.tt("dve", t.ap, xt.ap[:, c, :], r.ap, ALU.mult, [xt.bs[c], r.b], [t.b])
            self.ts("pool", hT.ap[:, c, :], t.ap, self.scal(l, 0, sub, c, which), self.scal(l, 1, sub, c, which),
                    ALU.mult, ALU.add, [t.b, self.SC.b], [hT.b])

    def post_chunk(self, m, pbank, fT, sqr, ssbank):
        pbk = self.pbufs[pbank]
        s = sqr[m % 2]
        self.cp("dve", fT.ap[:, m, :], self.bank(pbank), [pbk], [fT.b])
        self.tt("pool", s.ap, fT.ap[:, m, :], fT.ap[:, m, :], ALU.mult, [fT.b], [s.b])
        if m > 0:
            p_ = sqr[(m - 1) % 2]
            self.mm(self.bank(ssbank), self.onesb.ap, p_.ap, m == 1, False, [self.onesb.b, p_.b], [self.pbufs[ssbank]])
        if m == 7:
            self._last_sq = s

    def post_update(self, l, sub, which, xt, fT, ssbank):
        p_ = self._last_sq
        self.mm(self.bank(ssbank), self.onesb.ap, p_.ap, False, True, [self.onesb.b, p_.b], [self.pbufs[ssbank]])
        r = self.rstd_from(ssbank)
        for m in range(8):
            t = self.n_tmp[self.n_k % 2]
            self.n_k += 1
            self.stt(t.ap, fT.ap[:, m, :], self.scal(l, 2, sub, m, which), r.ap, ALU.mult, ALU.mult,
                     [fT.b, self.SC.b, r.b], [t.b])
            self.tt("pool", xt.ap[:, m, :], xt.ap[:, m, :], t.ap, ALU.add, [xt.bs[m], t.b], [xt.bs[m]])

    def ffn_pass(self, l, i, first=False, last=False):
        self.phase()
        A, I = self.A, self.I
        sub = 0 if i == 0 else 2
        wg = A.bf16(8, DFF, nb=2); wu = A.bf16(8, DFF, nb=2); wd = A.bf16(NFC, D, nb=2)
        for h in range(2):
            cs = slice(h * 1408, (h + 1) * 1408)
            self.dma("pool", wg.ap[:, :, cs], I["w_ffn_gate"][l, i][:, cs].rearrange("(kc p) f -> p kc f", p=128), [], [wg.bs[h]])
            self.dma("pool", wu.ap[:, :, cs], I["w_ffn_up"][l, i][:, cs].rearrange("(kc p) f -> p kc f", p=128), [], [wu.bs[h]])
        wdv = I["w_ffn_down"][l, i].rearrange("(fc p) d -> p fc d", p=128)
        for h in range(2):
            fs = slice(h * 11, (h + 1) * 11)
            self.dma("pool", wd.ap[:, fs, :], wdv[:, fs, :], [], [wd.bs[h]])
        xt = A.f32(8, TT, nb=8)
        r1 = A.f32(8, TT)
        hT = Tl(view(r1.ap.rearrange("p a b -> p (a b)")[:, 0:2048].bitcast(BF16), (8, TT)))
        sq = Tl(view(r1.ap.rearrange("p a b -> p (a b)")[:, 2048:4096].bitcast(BF16), (8, TT)))
        fT = r1
        hT.b = sq.b = fT.b
        actT = A.bf16(NFC, TT, nb=NFC)
        al = Tl(view(actT.ap.rearrange("p a b -> p (a b)")[:, 0:8192].bitcast(F32), (4, D)))
        sqr = [A.bf16(TT), A.bf16(TT)]
        sgr = [A.f32(TT), A.f32(TT)]
        self.alloc_norm()
        pb = self.pbufs
        import os
        nt_ = int(os.environ.get("DBG_NT", NT))
        lvl = int(os.environ.get("DBG_LVL", 9))
        for ti in range(nt_):
            which = 0 if ti < 8 else 1
            self.load_x(xt, ti, first, (al, actT.bs))
            if lvl < 1:
                self.store_x(xt, ti, last, (al, actT.bs))
                continue
            self.prenorm(l, sub, which, xt, hT, sq)
            if lvl < 2:
                self.store_x(xt, ti, last, (al, actT.bs))
                continue
            for j in range(NFC):
                bg, bu = j % 2, 2 + j % 2
                cs = slice(j * 128, (j + 1) * 128)
                for kc in range(8):
                    self.mm(self.bank(bg), wg.ap[:, kc, cs], hT.ap[:, kc, :], kc == 0, kc == 7, [wg.bs[j // 11], hT.b], [pb[bg]])
                for kc in range(8):
                    self.mm(self.bank(bu), wu.ap[:, kc, cs], hT.ap[:, kc, :], kc == 0, kc == 7, [wu.bs[j // 11], hT.b], [pb[bu]])
                s = sgr[j % 2]
                self.act(s.ap, self.bank(bg), AF.Silu, [pb[bg]], [s.b])
                self.tt("dve", actT.ap[:, j, :], s.ap, self.bank(bu), ALU.mult, [s.b, pb[bu]], [actT.bs[j]])
            if lvl < 3:
                self.store_x(xt, ti, last, (al, actT.bs))
                continue
            for m in range(8):
                bf = 4 + m % 2
                for j in range(NFC):
                    self.mm(self.bank(bf), wd.ap[:, j, m * 128:(m + 1) * 128], actT.ap[:, j, :], j == 0, j == NFC - 1,
                            [wd.bs[j // 11], actT.bs[j]], [pb[bf]])
                if lvl >= 4:
                    self.post_chunk(m, bf, fT, sqr, 6)
            if lvl >= 5:
                self.post_update(l, sub, which, xt, fT, 6)
            self.store_x(xt, ti, last, (al, actT.bs))


    def ffn_pass2(self, l, i, first=False, last=False):
        self.phase()
        A, I = self.A, self.I
        TF = 256
        NTF = NTOK // TF
        sub = 0 if i == 0 else 2
        wg = A.bf16(8, DFF, nb=2); wu = A.bf16(8, DFF, nb=2); wd = A.bf16(NFC, D, nb=2)
        for h in range(2):
            cs = slice(h * 1408, (h + 1) * 1408)
            self.dma("pool", wg.ap[:, :, cs], I["w_ffn_gate"][l, i][:, cs].rearrange("(kc p) f -> p kc f", p=128), [], [wg.bs[h]])
            self.dma("pool", wu.ap[:, :, cs], I["w_ffn_up"][l, i][:, cs].rearrange("(kc p) f -> p kc f", p=128), [], [wu.bs[h]])
        wdv = I["w_ffn_down"][l, i].rearrange("(fc p) d -> p fc d", p=128)
        for h in range(2):
            fs = slice(h * 11, (h + 1) * 11)
            self.dma("pool", wd.ap[:, fs, :], wdv[:, fs, :], [], [wd.bs[h]])
        xts = [A.f32(8, TF, nb=8), A.f32(8, TF, nb=8)]
        hTs = [A.bf16(8, TF), A.bf16(8, TF)]
        sq = A.bf16(8, TF)
        fT = A.f32(8, TF)
        actT = A.bf16(NFC, TF, nb=NFC)
        sqr = [A.bf16(TF), A.bf16(TF)]
        sgr = [A.f32(TF), A.f32(TF)]
        stg = A.f32(2, D) if (first or last) else None
        rs_pre = A.f32(TF); rs_post = A.f32(TF)
        tmp_pre = [A.f32(TF), A.f32(TF)]; tmp_post = [A.f32(TF), A.f32(TF)]
        pb = self.pbufs
        bk = lambda b_: self.bank(b_)[:, 0:TF]

        def rstd(r, bankidx):
            self.ts("dve", r.ap, bk(bankidx), 1.0 / D, EPS, ALU.mult, ALU.add, [pb[bankidx]], [r.b])
            self.act(r.ap, r.ap, AF.Sqrt, [r.b], [r.b])
            self.recip(r.ap, r.ap, [r.b], [r.b])

        def load(ti):
            xt = xts[ti % 2]
            tok0 = ti * TF
            if not first:
                self.dma("sp", xt.ap, self.XT.ap[:, :, tok0:tok0 + TF].rearrange("c p t -> p c t"), [], xt.bs)
                return
            self.dma("sp", stg.ap, I["xin"][tok0:tok0 + TF].rearrange("(b p) d -> p b d", p=128), [], [stg.b])
            for c in range(8):
                bi = c % 2
                for blk in range(2):
                    self.tr(self.bank(bi)[:, blk * 128:(blk + 1) * 128], stg.ap[:, blk, c * 128:(c + 1) * 128], self.identf.ap,
                            [stg.b, self.identf.b], [pb[bi]])
                self.cp("act", xt.ap[:, c, :], bk(bi), [pb[bi]], [xt.bs[c]])

        def store(ti):
            xt = xts[ti % 2]
            tok0 = ti * TF
            if not last:
                self.dma("sp", self.XT.ap[:, :, tok0:tok0 + TF].rearrange("c p t -> p c t"), xt.ap, xt.bs, [])
                return
            for blk in range(2):
                for half in range(2):
                    bi = half
                    for cc in range(4):
                        c = half * 4 + cc
                        self.tr(self.bank(bi)[:, cc * 128:(cc + 1) * 128], xt.ap[:, c, blk * 128:(blk + 1) * 128], self.identf.ap,
                                [xt.bs[c], self.identf.b], [pb[bi]])
                    self.cp("act", stg.ap[:, blk, half * 512:(half + 1) * 512], self.bank(bi), [pb[bi]], [stg.b])
            self.dma("sp", self.O["y"][tok0:tok0 + TF].rearrange("(b p) d -> p b d", p=128), stg.ap, [stg.b], [])

        def prenorm(ti):
            xt = xts[ti % 2]; hT = hTs[ti % 2]
            which = 0 if ti * TF < TS else 1
            for c in range(8):
                self.tt("pool", sq.ap[:, c, :], xt.ap[:, c, :], xt.ap[:, c, :], ALU.mult, [xt.bs[c]], [sq.b])
            for c in range(8):
                self.mm(bk(7), self.onesb.ap, sq.ap[:, c, :], c == 0, c == 7, [self.onesb.b, sq.b], [pb[7]])
            rstd(rs_pre, 7)
            for c in range(8):
                t = tmp_pre[c % 2]
                self.tt("dve", t.ap, xt.ap[:, c, :], rs_pre.ap, ALU.mult, [xt.bs[c], rs_pre.b], [t.b])
                self.ts("pool", hT.ap[:, c, :], t.ap, self.scal(l, 0, sub, c, which), self.scal(l, 1, sub, c, which),
                        ALU.mult, ALU.add, [t.b, self.SC.b], [hT.b])

        load(0)
        prenorm(0)

        def upd(tj, m):
            xt_ = xts[tj % 2]
            wh_ = 0 if tj * TF < TS else 1
            t = tmp_post[m % 2]
            self.stt(t.ap, fT.ap[:, m, :], self.scal(l, 2, sub, m, wh_), rs_post.ap, ALU.mult, ALU.mult,
                     [fT.b, self.SC.b, rs_post.b], [t.b])
            self.tt("pool", xt_.ap[:, m, :], xt_.ap[:, m, :], t.ap, ALU.add, [xt_.bs[m], t.b], [xt_.bs[m]])

        for ti in range(NTF):
            xt = xts[ti % 2]; hT = hTs[ti % 2]
            for j in range(NFC):
                bg, bu = j % 2, 2 + j % 2
                cs = slice(j * 128, (j + 1) * 128)
                for kc in range(8):
                    self.mm(bk(bg), wg.ap[:, kc, cs], hT.ap[:, kc, :], kc == 0, kc == 7, [wg.bs[j // 11], hT.b], [pb[bg]])
                for kc in range(8):
                    self.mm(bk(bu), wu.ap[:, kc, cs], hT.ap[:, kc, :], kc == 0, kc == 7, [wu.bs[j // 11], hT.b], [pb[bu]])
                s = sgr[j % 2]
                self.act(s.ap, bk(bg), AF.Silu, [pb[bg]], [s.b])
                self.tt("dve", actT.ap[:, j, :], s.ap, bk(bu), ALU.mult, [s.b, pb[bu]], [actT.bs[j]])
                if ti >= 1 and 2 <= j < 10:
                    upd(ti - 1, j - 2)
            if ti >= 1:
                store(ti - 1)
            if ti + 1 < NTF:
                load(ti + 1)
                prenorm(ti + 1)
            for m in range(8):
                bf = 4 + m % 2
                for j in range(NFC):
                    self.mm(bk(bf), wd.ap[:, j, m * 128:(m + 1) * 128], actT.ap[:, j, :], j == 0, j == NFC - 1,
                            [wd.bs[j // 11], actT.bs[j]], [pb[bf]])
                s = sqr[m % 2]
                self.cp("act", fT.ap[:, m, :], bk(bf), [pb[bf]], [fT.b])
                self.tt("pool", s.ap, fT.ap[:, m, :], fT.ap[:, m, :], ALU.mult, [fT.b], [s.b])
                if m > 0:
                    p_ = sqr[(m - 1) % 2]
                    self.mm(bk(6), self.onesb.ap, p_.ap, m == 1, False, [self.onesb.b, p_.b], [pb[6]])
            p_ = sqr[1]
            self.mm(bk(6), self.onesb.ap, p_.ap, False, True, [self.onesb.b, p_.b], [pb[6]])
            rstd(rs_post, 6)
        for m in range(8):
            upd(NTF - 1, m)
        store(NTF - 1)

    def p2_pass(self, l):
        self.phase()
        A, I, O = self.A, self.I, self.O
        W = A.bf16(8, WINP, nb=4)
        wv = I["w_in_p"][l].rearrange("(kc p) c -> p kc c", p=128)
        bounds = [0, C_V, C_U, C_G + 1536, WINP]
        for h in range(4):
            self.dma("pool", W.ap[:, :, bounds[h]:bounds[h + 1]], wv[:, :, bounds[h]:bounds[h + 1]], [], [W.bs[h]])

        def wb(col):
            for h in range(4):
                if col < bounds[h + 1]:
                    return W.bs[h]

        xt = A.f32(8, TT, nb=8)
        hTs = [A.bf16(8, TT), A.bf16(8, TT)]; sq = A.bf16(8, TT)
        rt = A.f32(2, TT)
        qst = [A.bf16(TT) for _ in range(3)]
        rtmp = [A.f32(TT) for _ in range(4)]
        xlst = A.f32(4, TT); ylst = A.bf16(4, TT)
        vst = A.bf16(4, 2, 65); vf = A.f32(4, 128); kf = A.f32(4, 128)
        utok = A.bf16(32, 8, 16); ufst = A.bf16(32, 64)
        gst = [A.bf16(8, TT), A.bf16(8, TT)]
        self.alloc_norm()
        pb = self.pbufs
        self.ms("pool", vst.ap[:, :, :, 64:65], 1.0, [vst.b])
        nfm = 0
        import os
        sec = os.environ.get("DBG_P2", "ABCDE")
        for ti in range(int(os.environ.get("DBG_NT", NT))):
            which = 0 if ti < 8 else 1
            sample = ti < 8
            tok0 = ti * TT
            hT = hTs[ti % 2]
            if ti == 0:
                self.load_x(xt, 0)
                self.prenorm(l, 1, 0, xt, hTs[0], sq)
            if sample:
                self.dma("act", rt.ap, I["rope"][:, :, tok0:tok0 + TT].rearrange("a p t -> p a t"), [], [rt.b])
            if ti + 1 < NT:
                t1_ = (ti + 1) * TT
                self.dma("act", xt.ap, self.XT.ap[:, :, t1_:t1_ + TT].rearrange("c p t -> p c t"), [], xt.bs)

            def fm(col, bi):
                for kc in range(8):
                    self.mm(self.bank(bi), W.ap[:, kc, col:col + 128], hT.ap[:, kc, :], kc == 0, kc == 7, [wb(col), hT.b], [pb[bi]])

            self.P.mute = "A" not in sec
            for ci in range(5):
                col = C_Q + ci * 128 if ci < 4 else C_K
                cols = C_QS + ci * 128 if ci < 4 else C_KS
                b0 = ci % 2
                fm(col, b0)
                q_ = qst[ci % 3]
                if sample:
                    fm(cols, 2 + b0)
                    t1 = rtmp[(ci % 2) * 2]; t2 = rtmp[(ci % 2) * 2 + 1]
                    self.tt("dve", t1.ap, self.bank(b0), rt.ap[:, 0, :], ALU.mult, [pb[b0], rt.b], [t1.b])
                    self.tt("dve", t2.ap, self.bank(2 + b0), rt.ap[:, 1, :], ALU.mult, [pb[2 + b0], rt.b], [t2.b])
                    self.tt("pool", q_.ap, t1.ap, t2.ap, ALU.add, [t1.b, t2.b], [q_.b])
                else:
                    self.cp("act", q_.ap, self.bank(b0), [pb[b0]], [q_.b])
                dst = self.Qs.ap[ci][:, tok0:tok0 + TT] if ci < 4 else self.Ks.ap[:, tok0:tok0 + TT]
                self.dma("sp", dst, q_.ap, [q_.b], [])
            self.P.mute = False
            if ti + 1 < NT:
                self.prenorm(l, 1, 0 if ti + 1 < 8 else 1, xt, hTs[(ti + 1) % 2], sq)
            self.P.mute = "B" not in sec
            for blk in range(4):
                for kc in range(8):
                    self.mm(self.bank(4)[:, blk * 128:(blk + 1) * 128], hT.ap[:, kc, blk * 128:(blk + 1) * 128],
                            W.ap[:, kc, C_V:C_V + 128], kc == 0, kc == 7, [hT.b, wb(C_V)], [pb[4]])
            self.cp("act", vst.ap[:, :, :, 0:64], view(self.bank(4), (4, 2, 64)), [pb[4]], [vst.b])
            if not sample:
                self.cp("act", vf.ap, view(self.bank(4), (4, 128)), [pb[4]], [vf.b])
            self.dma("sp", self.Vs.ap[tok0:tok0 + TT].rearrange("(b p) c -> p b c", p=128),
                     vst.ap.rearrange("p b h c -> p b (h c)"), [vst.b], [])
            if not sample:
                for pp in range(2):
                    pj = 2 * (ti - 8) + pp
                    self.dma("sp", O["nv"][pj, l].rearrange("(b p) c -> p b c", p=128), vf.ap[:, 2 * pp:2 * pp + 2, :],
                             [vf.b], [])
                for blk in range(4):
                    for kc in range(8):
                        self.mm(self.bank(4)[:, blk * 128:(blk + 1) * 128], hT.ap[:, kc, blk * 128:(blk + 1) * 128],
                                W.ap[:, kc, C_K:C_K + 128], kc == 0, kc == 7, [hT.b, wb(C_K)], [pb[4]])
                self.cp("dve", kf.ap, view(self.bank(4), (4, 128)), [pb[4]], [kf.b])
                for pp in range(2):
                    pj = 2 * (ti - 8) + pp
                    self.dma("sp", O["nk"][pj, l].rearrange("(b p) c -> p b c", p=128), kf.ap[:, 2 * pp:2 * pp + 2, :],
                             [kf.b], [])
            self.P.mute = "C" not in sec
            for c in range(4):
                b0 = c % 2
                fm(C_XL + c * 128, b0)
                self.cp("act", xlst.ap[:, c, :], self.bank(b0), [pb[b0]], [xlst.b])
            self.dma("sp", self.XLs.ap[:, :, tok0:tok0 + TT].rearrange("c p t -> p c t"), xlst.ap, [xlst.b], [])
            for c in range(4):
                b0 = c % 2
                fm(C_YL + c * 128, b0)
                self.cp("dve", ylst.ap[:, c, :], self.bank(b0), [pb[b0]], [ylst.b])
            self.dma("sp", self.YLs.ap[:, :, tok0:tok0 + TT].rearrange("c p t -> p c t"), ylst.ap, [ylst.b], [])
            self.P.mute = "D" not in sec
            hs = hT.ap.rearrange("p c (k s) -> p c s k", s=8)
            for s in range(8):
                bi = 5 + s % 2
                for kc in range(8):
                    self.mm(self.bank(bi)[0:64, :], hs[:, kc, s, :], W.ap[:, kc, C_U:C_U + 512], kc == 0, kc == 7,
                            [hT.b, wb(C_U)], [pb[bi]])
                self.cp("act" if s % 2 else "dve", utok.ap[0:64, :, s, :], view(self.bank(bi)[0:64, :], (32, 16)), [pb[bi]], [utok.b])
            for half in range(2):
                bi = 2 + half
                pbf = self.bank(bi).bitcast(BF16)
                for gg in range(16):
                    g = half * 16 + gg
                    self.tr(pbf[:, gg * 64:(gg + 1) * 64], utok.ap[0:64, g].rearrange("p s c -> p (s c)"),
                            self.identb.ap[0:64, 0:64], [utok.b, self.identb.b], [pb[bi]])
                self.cp("act" if half else "dve", ufst.ap[:, half * 16:(half + 1) * 16, :], view(pbf[:, 0:1024], (16, 64)),
                        [pb[bi]], [ufst.b])
            self.dma("sp", self.UFs.ap[ti], ufst.ap.rearrange("p g k -> p (g k)"), [ufst.b], [])
            self.P.mute = "E" not in sec
            for cc in range(24):
                b0 = cc % 2
                fm(C_G + cc * 128, b0)
                g_ = gst[(cc // 8) % 2]
                self.act(g_.ap[:, cc % 8, :], self.bank(b0), AF.Sigmoid, [pb[b0]], [g_.b])
                if cc % 8 == 7:
                    grp = cc // 8
                    self.dma("sp", self.Gs.ap[grp * 8:(grp + 1) * 8][:, :, tok0:tok0 + TT].rearrange("c p t -> p c t"), g_.ap,
                             [g_.b], [])
            self.P.mute = False

    def p4_pass(self, l):
        self.phase()
        A, I = self.A, self.I
        wol = A.bf16(4, D); woa = A.bf16(4, D); wgl = A.bf16(4, 2048); wo = A.bf16(8, D)
        self.dma("pool", wgl.ap, I["w_glu"][l].rearrange("(kc p) c -> p kc c", p=128), [], [wgl.b])
        self.dma("pool", wol.ap, I["w_o_lru"][l].rearrange("(kc p) c -> p kc c", p=128), [], [wol.b])
        self.dma("pool", woa.ap, I["w_o_attn_p"][l].rearrange("(kc p) c -> p kc c", p=128), [], [woa.b])
        self.dma("pool", wo.ap, I["w_out"][l].rearrange("(kc p) c -> p kc c", p=128), [], [wo.b])
        xt = A.f32(8, TT, nb=8)
        ins = [(A.bf16(4, TT), A.bf16(4, TT), A.bf16(4, TT), A.bf16(24, TT)) for _ in range(2)]
        mg = A.bf16(8, TT)
        fT = A.f32(8, TT)
        sqr = [A.bf16(TT), A.bf16(TT)]
        tmp = [A.f32(TT) for _ in range(6)]
        self.alloc_norm()
        pb = self.pbufs

        def loads(ti):
            tok0 = ti * TT
            lr, at, sy, G = ins[ti % 2]
            for (dst, src) in ((sy, self.SYs), (lr, self.LRs), (at, self.ATs), (G, self.Gs)):
                self.dma("sp", dst.ap, src.ap[:, :, tok0:tok0 + TT].rearrange("c p t -> p c t"), [], [dst.b])

        loads(0)
        for ti in range(NT):
            which = 0 if ti < 8 else 1
            if ti + 1 < NT:
                loads(ti + 1)
            self.load_x(xt, ti)
            lr, at, sy, G = ins[ti % 2]
            for m in range(8):
                ba, bz = m % 2, 2 + m % 2
                for kc in range(4):
                    self.mm(self.bank(ba), wgl.ap[:, kc, m * 128:(m + 1) * 128], sy.ap[:, kc, :], kc == 0, kc == 3, [wgl.b, sy.b], [pb[ba]])
                for kc in range(4):
                    self.mm(self.bank(bz), wgl.ap[:, kc, 1024 + m * 128:1024 + (m + 1) * 128], sy.ap[:, kc, :], kc == 0, kc == 3,
                            [wgl.b, sy.b], [pb[bz]])
                s = tmp[m % 2]; t = tmp[2 + m % 2]
                self.act(s.ap, self.bank(bz), AF.Sigmoid, [pb[bz]], [s.b])
                self.tt("dve", t.ap, self.bank(ba), s.ap, ALU.mult, [pb[ba], s.b], [t.b])
                self.tt("pool", fT.ap[:, m, :], t.ap, G.ap[:, 8 + m, :], ALU.mult, [t.b, G.b], [fT.b])
            for m in range(8):
                ba, bc = 4 + m % 2, 6 + m % 2
                for kc in range(4):
                    self.mm(self.bank(ba), wol.ap[:, kc, m * 128:(m + 1) * 128], lr.ap[:, kc, :], kc == 0, kc == 3, [wol.b, lr.b], [pb[ba]])
                for kc in range(4):
                    self.mm(self.bank(bc), woa.ap[:, kc, m * 128:(m + 1) * 128], at.ap[:, kc, :], kc == 0, kc == 3, [woa.b, at.b], [pb[bc]])
                u1 = tmp[m % 2]; u3 = tmp[2 + m % 2]; u4 = tmp[4 + m % 2]
                self.tt("dve", u1.ap, self.bank(ba), G.ap[:, m, :], ALU.mult, [pb[ba], G.b], [u1.b])
                self.tt("dve", u3.ap, self.bank(bc), G.ap[:, 16 + m, :], ALU.mult, [pb[bc], G.b], [u3.b])
                self.tt("pool", u4.ap, fT.ap[:, m, :], u1.ap, ALU.add, [fT.b, u1.b], [u4.b])
                self.tt("pool", mg.ap[:, m, :], u4.ap, u3.ap, ALU.add, [u4.b, u3.b], [mg.b])
            for m in range(8):
                bo = m % 2
                for kc in range(8):
                    self.mm(self.bank(bo), wo.ap[:, kc, m * 128:(m + 1) * 128], mg.ap[:, kc, :], kc == 0, kc == 7, [wo.b, mg.b], [pb[bo]])
                self.post_chunk(m, bo, fT, sqr, 2)
            self.post_update(l, 1, which, xt, fT, 2)
            self.store_x(xt, ti)

    def p3a_attention(self, l):
        self.phase()
        A, I = self.A, self.I
        kT = A.bf16(2, NTOK); Qt = A.bf16(4, NTOK); Vt = A.bf16(NTOK // 128, 130)
        self.ms("pool", kT.ap[64:128, 0, :], 0.0, [kT.b])
        self.ms("pool", kT.ap[0:64, 1, :], 0.0, [kT.b])
        for h in range(2):
            sl = slice(h * 2560, (h + 1) * 2560)
            self.dma("sp", Qt.ap[:, :, sl], self.Qs.ap[:, :, sl].rearrange("c p t -> p c t"), [], [Qt.b])
        self.dma("act", kT.ap[0:64, 0, :], self.Ks.ap[0:64, :], [], [kT.b])
        self.dma("act", kT.ap[64:128, 1, :], self.Ks.ap[64:128, :], [], [kT.b])
        self.dma("act", Vt.ap, self.Vs.ap.rearrange("(b p) c -> p b c", p=128), [], [Vt.b])
        ckr = A.f32(4, 128); cvr = A.f32(4, 128)
        self.dma("sp", ckr.ap, I["cache_k"][l].rearrange("(b p) c -> p b c", p=128), [], [ckr.b])
        self.dma("sp", cvr.ap, I["cache_v"][l].rearrange("(b p) c -> p b c", p=128), [], [cvr.b])
        ckT = A.bf16(2, 512); cv = A.bf16(4, 2, 65)
        pb = self.pbufs
        for blk in range(4):
            self.tr(self.bank(0)[:, blk * 128:(blk + 1) * 128], ckr.ap[:, blk, :], self.identf.ap, [ckr.b, self.identf.b], [pb[0]])
        self.ms("pool", ckT.ap, 0.0, [ckT.b])
        self.cp("dve", ckT.ap[0:64, 0, :], self.bank(0)[0:64, :], [pb[0]], [ckT.b])
        self.cp("dve", ckT.ap[64:128, 1, :], self.bank(0)[64:128, :], [pb[0]], [ckT.b])
        self.ms("pool", cv.ap[:, :, :, 64:65], 1.0, [cv.b])
        self.cp("dve", cv.ap[:, :, :, 0:64], cvr.ap.rearrange("p b (h c) -> p b h c", h=2), [cvr.b], [cv.b])
        cvf = cv.ap.rearrange("p b h c -> p b (h c)")
        sk = A.f32(8)
        self.dma("sp", sk.ap[64:65, :], I["attn_sink"][l:l + 1, :], [], [sk.b])
        self.act(sk.ap[64:65, :], sk.ap[64:65, :], AF.Exp, [sk.b], [sk.b])
        pT = [A.bf16(TT) for _ in range(4)]
        osb = [A.f32(TT) for _ in range(3)]
        rrow = [A.f32(TT) for _ in range(3)]
        ast = [A.bf16(4, TT), A.bf16(4, TT)]
        segs = [(0, TS, True)] + [(TS + j * TP, TP, False) for j in range(NPB)]
        its = []
        nst = 0
        for (tok0, T, samp) in segs:
            nqb = T // 128
            for qb in range(nqb):
                for kvh in range(2):
                    its.append((tok0, T, samp, nqb, qb, kvh, nst))
                if qb % 4 == 3 or qb == nqb - 1:
                    nst += 1
        npt = [0]

        def front(n):
            tok0, T, samp, nqb, qb, kvh, st_ = its[n]
            q0 = tok0 + qb * 128
            blocks = []
            if samp:
                for nb in (qb - 1, qb, qb + 1):
                    if 0 <= nb < nqb:
                        msk = self.mprev if nb == qb - 1 else (self.mnext if nb == qb + 1 else None)
                        blocks.append((kT.ap[:, kvh, tok0 + nb * 128:tok0 + (nb + 1) * 128], kT.b,
                                       Vt.ap[:, (tok0 // 128) + nb, kvh * 65:(kvh + 1) * 65], Vt.b, msk))
                for cb_ in range(4):
                    blocks.append((ckT.ap[:, kvh, cb_ * 128:(cb_ + 1) * 128], ckT.b, cvf[:, cb_, kvh * 65:(kvh + 1) * 65], cv.b, None))
            else:
                for nb in range(nqb):
                    blocks.append((kT.ap[:, kvh, tok0 + nb * 128:tok0 + (nb + 1) * 128], kT.b,
                                   Vt.ap[:, (tok0 // 128) + nb, kvh * 65:(kvh + 1) * 65], Vt.b, None))
            po = 3 + n % 3
            rhs_q = Qt.ap[:, :, q0:q0 + 128]
            pts = []
            for bi_, (kap, kb, vap, vb, msk) in enumerate(blocks):
                sb_ = npt[0] % 3
                p_ = pT[npt[0] % 4]
                npt[0] += 1
                self.mm(view(self.bank(sb_), (4, 128)), kap, rhs_q, True, True, [kb, Qt.b], [pb[sb_]])
                self.act(p_.ap, self.bank(sb_), AF.Exp, [pb[sb_]], [p_.b], scale=0.125)
                if msk is not None:
                    m_ = msk.ap
                    mb = mkap(m_.tensor, m_.offset, [list(m_.ap[0]), [0, 4], [1, 128]])
                    self.tt("pool", view(p_.ap, (4, 128)), view(p_.ap, (4, 128)), mb, ALU.mult, [p_.b, msk.b], [p_.b])
                pts.append((p_, vap, vb))
                if bi_ >= 1:
                    pp_, vap_, vb_ = pts[bi_ - 1]
                    self.mm(self.bank(po)[0:65, :], vap_, pp_.ap, bi_ == 1, False, [vb_, pp_.b], [pb[po]])
            pp_, vap_, vb_ = pts[-1]
            self.mm(self.bank(po)[0:65, :], vap_, pp_.ap, len(pts) == 1, True, [vb_, pp_.b], [pb[po]])

        def back1(n):
            tok0, T, samp, nqb, qb, kvh, st_ = its[n]
            po = 3 + n % 3
            o_ = osb[n % 3]; r_ = rrow[n % 3]
            self.cp("act", o_.ap[0:65, :], self.bank(po)[0:65, :], [pb[po]], [o_.b])
            s_ = sk.ap[64:65, kvh * 4:(kvh + 1) * 4]
            sbc = mkap(s_.tensor, s_.offset, [list(s_.ap[0]), [1, 4], [0, 128]])
            self.tt("dve", view(r_.ap[64:65, :], (4, 128)), view(o_.ap[64:65, :], (4, 128)), sbc, ALU.add, [o_.b, sk.b], [r_.b])
            self.recip(r_.ap[64:65, :], r_.ap[64:65, :], [r_.b], [r_.b])

        def back2(n):
            tok0, T, samp, nqb, qb, kvh, st_ = its[n]
            q0 = tok0 + qb * 128
            hs = slice(kvh * 64, (kvh + 1) * 64)
            a_t = ast[st_ % 2]
            qcol = (qb % 4) * 128
            o_ = osb[n % 3]; r_ = rrow[n % 3]
            bb = 6 + n % 2
            self.mm(self.bank(bb)[0:64, :], self.onesf.ap[64:65, 0:64], r_.ap[64:65, :], True, True, [self.onesf.b, r_.b], [pb[bb]])
            self.tt("dve", a_t.ap[hs, :, qcol:qcol + 128], view(o_.ap[0:64, :], (4, 128)), view(self.bank(bb)[0:64, :], (4, 128)),
                    ALU.mult, [o_.b, pb[bb]], [a_t.b])
            if kvh == 1 and (qb % 4 == 3 or qb == nqb - 1):
                nn = (qb % 4 + 1) * 128
                t0 = q0 + 128 - nn
                self.dma("sp", self.ATs.ap[:, :, t0:t0 + nn].rearrange("c p t -> p c t"), a_t.ap[:, :, 0:nn], [a_t.b], [])

        N = len(its)
        for n in range(N + 2):
            if n < N:
                front(n)
            if 0 <= n - 1 < N:
                back1(n - 1)
            if 0 <= n - 2 < N:
                back2(n - 2)

    def p3b_lru(self, l):
        self.phase()
        A, I, O = self.A, self.I, self.O
        pb = self.pbufs
        cw = A.f32(4, 4); cb = A.f32(4); onec = A.f32(1)
        src, sb_ = self.vecT(None, I["w_conv"][l].rearrange("j (c p) -> (j c) p", p=128), 16, None)
        self.cp("dve", cw.ap.rearrange("p c j -> p j c"), view(src, (4, 4)), [sb_], [cw.b])
        src, sb_ = self.vecT(None, I["b_conv"][l].rearrange("(c p) -> c p", p=128), 4, None)
        self.cp("dve", cb.ap, src, [sb_], [cb.b])
        self.ms("dve", onec.ap, 1.0, [onec.b])
        ba = A.f32(2, 4); bx = A.f32(2, 4); cl = A.f32(2, 4); st0 = A.f32(2, 4)
        for (t_, nm) in ((ba, "b_lru_a"), (bx, "b_lru_x"), (cl, "lru_lambda"), (st0, "state_lru")):
            src, sb_ = self.vecT(None, I[nm][l].rearrange("d (c p) -> (d c) p", p=128), 8, None)
            self.cp("dve", t_.ap.rearrange("p d c -> p (d c)"), src, [sb_], [t_.b])
        self.act(cl.ap, cl.ap, AF.Exp, [cl.b], [cl.b], scale=-1.0)
        self.act(cl.ap, cl.ap, AF.Ln, [cl.b, onec.b], [cl.b], bias=onec.ap[:, 0:1])
        self.ts("dve", cl.ap, cl.ap, -8.0, None, ALU.mult, None, [cl.b], [cl.b])
        cl2 = A.f32(2, 4)
        self.ts("dve", cl2.ap, cl.ap, 2.0, None, ALU.mult, None, [cl.b], [cl2.b])
        BD = {}
        for d in range(2):
            for nm in ("w_lru_a", "w_lru_x"):
                t_ = A.bf16(4, 128)
                self.ms("pool", t_.ap, 0.0, [t_.b])
                for c in range(4):
                    self.dma("pool", t_.ap[0:64, c, 0:64], I[nm][l, d, 2 * c], [], [t_.b])
                    self.dma("pool", t_.ap[64:128, c, 64:128], I[nm][l, d, 2 * c + 1], [], [t_.b])
                BD[(d, nm)] = t_
        hf = A.f32(4, TS); xc = A.f32(4, TS)
        xlt = A.f32(4, TT + 3); xcb = A.bf16(4, TT)
        R = A.f32(4, TT)
        IIs = [A.f32(4, TT), A.f32(4, TT)]
        AAs = [A.f32(4, TT), A.f32(4, TT)]
        hbt = A.f32(4, TT); carry = A.f32(4)
        ylts = [A.bf16(4, TT), A.bf16(4, TT)]
        fin = A.f32(NPB, 2, 4)
        segs = [(0, TS, True)] + [(TS + j * TP, TP, False) for j in range(NPB)]
        gk = [0]
        xcbufs = [Buf() for _ in range(8)]

        def rev(ap, n):
            return mkap(ap.tensor, ap.offset + n - 1, [list(ap.ap[0]), [-1, n]])

        def gates(d, t0l, n):
            AA = AAs[gk[0] % 2]; II = IIs[gk[0] % 2]
            gk[0] += 1
            SQ = R
            for c in range(4):
                self.cp("pool", xcb.ap[:, c, 0:n], xc.ap[:, c, t0l:t0l + n], [xcbufs[t0l // n]], [xcb.b])
            for c in range(4):
                self.mm(self.bank(c)[:, 0:n], BD[(d, "w_lru_a")].ap[:, c, :], xcb.ap[:, c, 0:n], True, True, [BD[(d, "w_lru_a")].b, xcb.b], [pb[c]])
                self.mm(self.bank(4 + c)[:, 0:n], BD[(d, "w_lru_x")].ap[:, c, :], xcb.ap[:, c, 0:n], True, True, [BD[(d, "w_lru_x")].b, xcb.b], [pb[4 + c]])
            for c in range(4):
                self.act(R.ap[:, c, 0:n], self.bank(c)[:, 0:n], AF.Sigmoid, [pb[c], ba.b], [R.b], bias=ba.ap[:, d, c:c + 1])
            for c in range(4):
                self.act(II.ap[:, c, 0:n], self.bank(4 + c)[:, 0:n], AF.Sigmoid, [pb[4 + c], bx.b], [II.b], bias=bx.ap[:, d, c:c + 1])
            for c in range(4):
                self.tt("pool", II.ap[:, c, 0:n], II.ap[:, c, 0:n], xc.ap[:, c, t0l:t0l + n], ALU.mult, [II.b, xcbufs[t0l // n]], [II.b])
            for c in range(4):
                self.act(AA.ap[:, c, 0:n], R.ap[:, c, 0:n], AF.Exp, [R.b, cl.b], [AA.b], scale=cl.ap[:, d, c:c + 1])
            for c in range(4):
                self.act(SQ.ap[:, c, 0:n], R.ap[:, c, 0:n], AF.Exp, [R.b, cl2.b], [SQ.b], scale=cl2.ap[:, d, c:c + 1])
            for c in range(4):
                self.act(SQ.ap[:, c, 0:n], SQ.ap[:, c, 0:n], AF.Sqrt, [SQ.b, onec.b], [SQ.b], scale=-1.0, bias=onec.ap[:, 0:1])
            for c in range(4):
                self.tt("dve", II.ap[:, c, 0:n], II.ap[:, c, 0:n], SQ.ap[:, c, 0:n], ALU.mult, [II.b, SQ.b], [II.b])
            return AA, II

        for si, (tok0, T, samp) in enumerate(segs):
            n = min(TT, T)
            ntl = T // n

            def load_conv(tl):
                t0l = tl * n
                lo = max(t0l - 2, 0); hi = min(t0l + n + 1, T)
                if lo > t0l - 2:
                    self.ms("dve", xlt.ap[:, :, 0:2], 0.0, [xlt.b])
                if hi < t0l + n + 1:
                    self.ms("dve", xlt.ap[:, :, n + 2:n + 3], 0.0, [xlt.b])
                self.dma("sp", xlt.ap[:, :, lo - (t0l - 2):hi - (t0l - 2)], self.XLs.ap[:, :, tok0 + lo:tok0 + hi].rearrange("c p t -> p c t"), [], [xlt.b])
                for c in range(4):
                    o_ = xc.ap[:, c, t0l:t0l + n]
                    self.act(o_, xlt.ap[:, c, 0:n], AF.Identity, [xlt.b, cw.b, cb.b], [xcbufs[tl]], scale=cw.ap[:, c, 0:1], bias=cb.ap[:, c:c + 1])
                    for j in range(1, 4):
                        self.stt(o_, xlt.ap[:, c, j:j + n], cw.ap[:, c, j:j + 1], o_, ALU.mult, ALU.add, [xlt.b, cw.b, xcbufs[tl]], [xcbufs[tl]])

            load_conv(0)
            for tl in range(ntl):
                t0l = tl * n
                if tl + 1 < ntl:
                    load_conv(tl + 1)
                AA, II = gates(0, t0l, n)
                for c in range(4):
                    if tl == 0:
                        init = st0.ap[:, 0, c:c + 1] if samp else 0.0
                    else:
                        init = hf.ap[:, c, t0l - 1:t0l]
                    self.P.op("dve", lambda e, o=hf.ap[:, c, t0l:t0l + n], a=AA.ap[:, c, 0:n], b=II.ap[:, c, 0:n], i0=init:
                              e.tensor_tensor_scan(out=o, data0=a, data1=b, initial=i0, op0=ALU.mult, op1=ALU.add),
                              [AA.b, II.b, hf.b, st0.b], [hf.b])
            if not samp:
                self.cp("dve", fin.ap[:, si - 1, 0, :], hf.ap[:, :, T - 1], [hf.b], [fin.b])
            def load_yl(tl):
                y_ = ylts[tl % 2]
                t0l_ = tl * n
                self.dma("sp", y_.ap[:, :, 0:n], self.YLs.ap[:, :, tok0 + t0l_:tok0 + t0l_ + n].rearrange("c p t -> p c t"), [], [y_.b])
                for c in range(4):
                    self.act(y_.ap[:, c, 0:n], y_.ap[:, c, 0:n], AF.Gelu_apprx_tanh, [y_.b], [y_.b])

            load_yl(ntl - 1)
            for tl in reversed(range(ntl)):
                t0l = tl * n
                cur = hbt
                ylt = ylts[tl % 2]
                AA, II = gates(1, t0l, n)
                if tl - 1 >= 0:
                    load_yl(tl - 1)
                for c in range(4):
                    if tl == ntl - 1:
                        init = st0.ap[:, 1, c:c + 1] if samp else 0.0
                    else:
                        init = carry.ap[:, c:c + 1]
                    self.P.op("dve", lambda e, o=rev(cur.ap[:, c, 0:n], n), a=rev(AA.ap[:, c, 0:n], n), b=rev(II.ap[:, c, 0:n], n), i0=init:
                              e.tensor_tensor_scan(out=o, data0=a, data1=b, initial=i0, op0=ALU.mult, op1=ALU.add),
                              [AA.b, II.b, carry.b, st0.b], [cur.b])
                self.cp("dve", carry.ap, cur.ap[:, :, 0], [cur.b], [carry.b])
                if not samp and tl == 0:
                    self.cp("dve", fin.ap[:, si - 1, 1, :], cur.ap[:, :, 0], [cur.b], [fin.b])
                for c in range(4):
                    self.tt("dve", cur.ap[:, c, 0:n], cur.ap[:, c, 0:n], hf.ap[:, c, t0l:t0l + n], ALU.add, [cur.b, hf.b], [cur.b])
                    self.tt("dve", ylt.ap[:, c, 0:n], cur.ap[:, c, 0:n], ylt.ap[:, c, 0:n], ALU.mult, [cur.b, ylt.b], [ylt.b])
                self.dma("sp", self.LRs.ap[:, :, tok0 + t0l:tok0 + t0l + n].rearrange("c p t -> p c t"), ylt.ap[:, :, 0:n], [ylt.b], [])
        self.tr(self.bank(0)[0:32, 0:128], fin.ap.rearrange("p s d c -> p (s d c)"), self.identf.ap, [fin.b, self.identf.b], [pb[0]])
        fo = A.f32(128)
        self.cp("dve", fo.ap[0:32, :], self.bank(0)[0:32, 0:128], [pb[0]], [fo.b])
        for s_ in range(NPB):
            self.dma("sp", O["nlru"][s_, l].rearrange("d (c p) -> (d c) p", p=128), fo.ap[s_ * 8:(s_ + 1) * 8, :], [fo.b], [])

    def p3c_s5(self, l):
        self.phase()
        A, I, O = self.A, self.I, self.O
        pb = self.pbufs
        Qw = A.bf16(2, 16, 2, 128); ML = A.bf16(32, 128)
        self.dma("sp", Qw.ap.rearrange("p d g c x -> p (d g c x)"), self.Qd[l].ap, [], [Qw.b])
        self.dma("sp", ML.ap.rearrange("p g x -> p (g x)"), self.MLd[l].ap, [], [ML.b])
        a12 = A.f32(2, 2, 16, 2)
        self.dma("sp", a12.ap.rearrange("p d k g c -> p (d k g c)"), self.A12[l].ap, [], [a12.b])
        h0r = A.f32(2, 128); h0s = A.f32(2, 16, 2)
        for d in range(2):
            self.dma("sp", h0r.ap[0:32, d, :], I["state_ssm"][l, d].rearrange("c (gp g2) n -> (c gp) (g2 n)", g2=2), [], [h0r.b])
            self.tr(self.bank(0)[:, d * 32:(d + 1) * 32], h0r.ap[0:32, d, :], self.identf.ap[0:32, 0:32], [h0r.b, self.identf.b], [pb[0]])
            self.cp("dve", h0s.ap[:, d].rearrange("p g c -> p c g"), view(self.bank(0)[:, d * 32:(d + 1) * 32], (2, 16)), [pb[0]], [h0s.b])
        zero = A.f32(16, 2, 4)
        self.ms("dve", zero.ap, 0.0, [zero.b])
        fin = A.f32(NPB, 2, 2, 16)
        mark0 = A.top
        for (tile0, ntile, nseq, Kseq, KB, tokbase) in ((0, 8, 1, 512, 256, 0), (8, 2, 4, 32, 128, TS)):
            A.top = mark0
            self.P.barrier()
            K = nseq * Kseq
            nblk = K // KB
            Uf = A.bf16(ntile, 32, 64)
            self.dma("sp", Uf.ap.rearrange("p t g k -> p t (g k)"), self.UFs.ap[tile0:tile0 + ntile].rearrange("t p x -> p t x"), [], [Uf.b])
            Hbf = [A.bf16(16, 2, nseq, Kseq + 1), A.bf16(16, 2, nseq, Kseq + 1)]
            mark1 = A.top
            PF = A.bf16(2, 16, 2, 128)
            self.dma("sp", PF.ap.rearrange("p d g c x -> p (d g c x)"), self.PFd[l].ap, [], [PF.b])
            Sd = [A.f32(16, 2, KB), A.f32(16, 2, KB)]
            tA = [A.f32(16, 2, nseq), A.f32(16, 2, nseq)]
            tB = [A.f32(16, 2, nseq), A.f32(16, 2, nseq)]
            hinit = [A.f32(16, 2, nseq), A.f32(16, 2, nseq)]
            tpb = KB // 64
            spb = KB // Kseq if nseq > 1 else 1
            for d in range(2):
                eng = "dve" if d == 0 else "pool"
                S = Sd[d]
                if nseq == 1:
                    self.cp(eng, hinit[d].ap[:, :, :, 0], h0s.ap[:, d], [h0s.b], [hinit[d].b])
                    self.cp("act", Hbf[d].ap[:, :, :, 0, 0 if d == 0 else Kseq], h0s.ap[:, d], [h0s.b], [Hbf[d].b])
                else:
                    self.ms(eng, hinit[d].ap, 0.0, [hinit[d].b])
                    self.ms(eng, Hbf[d].ap[:, :, :, :, 0 if d == 0 else Kseq], 0.0, [Hbf[d].b])
            for bidx in range(nblk):
                for d in range(2):
                    eng = "dve" if d == 0 else "pool"
                    S = Sd[d]
                    blk = bidx if d == 0 else nblk - 1 - bidx
                    for gp_ in range(16):
                        for c in range(2):
                            bi = (0 if d == 0 else 4) + (gp_ * 2 + c) % 4
                            for g2 in range(2):
                                g = 2 * gp_ + g2
                                self.mm(view(self.bank(bi)[g2 * 64:(g2 + 1) * 64, 0:KB], (tpb, 64)),
                                        PF.ap[:, d, gp_, c, g2 * 64:(g2 + 1) * 64],
                                        Uf.ap[:, blk * tpb:(blk + 1) * tpb, g, :], True, True, [PF.b, Uf.b], [pb[bi]])
                            self.cp("act", S.ap[:, gp_, c, :], self.bank(bi)[:, 0:KB], [pb[bi]], [S.b])
                for d in range(2):
                    eng = "dve" if d == 0 else "pool"
                    S = Sd[d]
                    blk = bidx if d == 0 else nblk - 1 - bidx
                    first_blk = bidx == 0
                    s_ = S.ap
                    base = s_.offset
                    pp = list(s_.ap[0])
                    nk = Kseq if nseq > 1 else KB

                    def col(kk, swap=False):
                        if swap:
                            return mkap(s_.tensor, base + KB + kk, [pp, [2 * KB, 16], [-KB, 2], [Kseq, spb]])
                        return mkap(s_.tensor, base + kk, [pp, [2 * KB, 16], [KB, 2], [Kseq, spb]])

                    def hv(t, swap=False):
                        a = t.ap
                        if swap:
                            return mkap(a.tensor, a.offset + nseq, [list(a.ap[0]), [2 * nseq, 16], [-nseq, 2], [1, spb]])
                        return mkap(a.tensor, a.offset, [list(a.ap[0]), [2 * nseq, 16], [nseq, 2], [1, spb]])

                    a1 = a12.ap[:, d, 0]
                    a2 = a12.ap[:, d, 1]
                    A1 = mkap(a1.tensor, a1.offset, [list(a1.ap[0]), [2, 16], [1, 2], [0, spb]])
                    A2 = mkap(a2.tensor, a2.offset, [list(a2.ap[0]), [2, 16], [1, 2], [0, spb]])
                    order = range(nk) if d == 0 else reversed(range(nk))
                    prev = None
                    for kk in order:
                        if prev is None:
                            if first_blk or nseq > 1:
                                pv_, psw = hv(hinit[d]), hv(hinit[d], True)
                                rdx = [hinit[d].b]
                            else:
                                pv_, psw = hv(hinit[d]), hv(hinit[d], True)
                                rdx = [hinit[d].b]
                        else:
                            pv_, psw = col(prev), col(prev, True)
                            rdx = []
                        ta, tb = tA[d], tB[d]
                        self.tt(eng, hv(ta), A1, pv_, ALU.mult, [a12.b, S.b] + rdx, [ta.b])
                        self.tt(eng, hv(tb), A2, psw, ALU.mult, [a12.b, S.b] + rdx, [tb.b])
                        self.tt(eng, hv(ta), hv(ta), hv(tb), ALU.add, [ta.b, tb.b], [ta.b])
                        self.tt(eng, col(kk), col(kk), hv(ta), ALU.add, [S.b, ta.b], [S.b])
                        prev = kk
                    if nseq == 1:
                        self.cp(eng, hv(hinit[d]), col(prev), [S.b], [hinit[d].b])
                    else:
                        for sq_ in range(spb):
                            seq = blk * spb + sq_
                            kcol = sq_ * Kseq + (Kseq - 1 if d == 0 else 0)
                            self.cp(eng, fin.ap[:, seq, d, :, :], S.ap[:, :, :, kcol].rearrange("p g c -> p c g"), [S.b], [fin.b])
                    for c in range(2):
                        if nseq == 1:
                            o0 = blk * KB + (1 if d == 0 else 0)
                            self.cp("act", Hbf[d].ap[:, :, c, 0, o0:o0 + KB], S.ap[:, :, c, :], [S.b], [Hbf[d].b])
                        else:
                            o0 = 1 if d == 0 else 0
                            self.cp("act", Hbf[d].ap[:, :, c, blk * spb:(blk + 1) * spb, o0:o0 + Kseq],
                                    S.ap[:, :, c, :].rearrange("p g (s k) -> p g s k", k=Kseq), [S.b], [Hbf[d].b])
            self.P.barrier()
            A.top = mark1
            Yf = A.bf16(32, K)
            Ytok = A.bf16(8, 512)
            YT = A.bf16(4, 1024)
            for g in range(32):
                gp_, g2 = g // 2, g % 2
                bi = g % 4
                hs = slice(g2 * 64, (g2 + 1) * 64)
                out = view(self.bank(bi)[:, 0:K], (ntile, 64))
                self.mm(out, ML.ap[:, g, :], Uf.ap[:, :, g, :], True, False, [ML.b, Uf.b], [pb[bi]])
                for d in range(2):
                    for c in range(2):
                        o0 = 0 if d == 0 else 1
                        self.mm(view(self.bank(bi)[:, 0:K], (nseq, Kseq)), Qw.ap[hs, d, gp_, c, :], Hbf[d].ap[hs, gp_, c, :, o0:o0 + Kseq],
                                False, d == 1 and c == 1, [Qw.b, Hbf[d].b], [pb[bi]])
                self.act(Yf.ap[:, g, :], self.bank(bi)[:, 0:K], AF.Gelu_apprx_tanh, [pb[bi]], [Yf.b])
            for kb in range(K // 128):
                for q4 in range(4):
                    bi = 4 + q4 % 2
                    pbf = self.bank(bi).bitcast(BF16)
                    for gg in range(8):
                        g = q4 * 8 + gg
                        self.tr(pbf[:, gg * 128:(gg + 1) * 128], Yf.ap[:, g, kb * 128:(kb + 1) * 128], self.identb.ap, [Yf.b, self.identb.b], [pb[bi]])
                    self.cp("dve" if q4 % 2 else "act", Ytok.ap[:, :, q4 * 128:(q4 + 1) * 128].rearrange("p t (g c) -> p g t c", c=16),
                            pbf[:, 0:1024].rearrange("p (g t c) -> p g t c", g=8, t=8), [pb[bi]], [Ytok.b])
                for t in range(8):
                    bi = 6 + t % 2
                    pbf = self.bank(bi).bitcast(BF16)
                    for cc in range(4):
                        self.tr(pbf[:, cc * 128:(cc + 1) * 128], Ytok.ap[:, t, cc * 128:(cc + 1) * 128], self.identb.ap, [Ytok.b, self.identb.b], [pb[bi]])
                    self.cp("dve" if t % 2 else "act", YT.ap.rearrange("p c (k s) -> p c s k", s=8)[:, :, t, :],
                            view(pbf[:, 0:512], (4, 128)), [pb[bi]], [YT.b])
                t0 = tokbase + kb * 1024
                self.dma("sp", self.SYs.ap[:, :, t0:t0 + 1024].rearrange("c p t -> p c t"), YT.ap, [YT.b], [])
        fo = A.f32(2, 128)
        ff = fin.ap.rearrange("p s d c g -> p (s d c g)")
        for h in range(2):
            self.tr(self.bank(h)[:, 0:128], ff[:, h * 128:(h + 1) * 128], self.identf.ap, [fin.b, self.identf.b], [pb[h]])
            self.cp("dve", fo.ap[:, h, :], self.bank(h)[:, 0:128], [pb[h]], [fo.b])
        for seq in range(NPB):
            h, r0 = seq // 2, (seq % 2) * 64
            self.dma("sp", O["nssm"][seq, l].rearrange("d c (gp g2) n -> (d c gp) (g2 n)", g2=2), fo.ap[r0:r0 + 64, h, :], [fo.b], [])


def build(dbg=False, stages=None):
    K = Kern(dbg)
    on = lambda nm: stages is None or nm in stages
    with K.st:
        if on("pro"):
            K.prologue()
        for l in range(L):
            if on("ffa%d" % l):
                K.ffn_pass2(l, 0, first=(l == 0))
            if on("p2%d" % l):
                K.p2_pass(l)
            if on("p3a%d" % l):
                K.p3a_attention(l)
            if on("p3b%d" % l):
                K.p3b_lru(l)
            if on("p3c%d" % l):
                K.p3c_s5(l)
            if on("p4%d" % l):
                K.p4_pass(l)
            if on("ffb%d" % l):
                K.ffn_pass2(l, 1, last=(l == L - 1))
        K.P.barrier()
        K.P.op("sp", lambda e: e.nop(), [], [])
        K.P.emit()
    return K


def _perm_q():
    idx = []
    for c in range(4):
        for h in (c, 4 + c):
            idx.extend(range(h * 64, (h + 1) * 64))
    return np.array(idx)


def _partner():
    p = np.zeros(64, np.int64)
    for d in range(64):
        p[d] = d + 16 if (d % 32) < 16 else d - 16
    return p


def _consts():
    cst = np.zeros((128, 5, 128), np.float32)
    j = np.arange(128)[:, None]
    i = np.arange(128)[None, :]
    cst[:, 0] = (j == i)
    cst[:, 1] = (j >= i)
    cst[:, 2] = (j <= i)
    cst[:, 3] = ((j // 16) <= (i // 16))
    cst[:, 4] = ((j // 16) >= (i // 16))
    t = np.arange(TS)
    row = (t // 64).astype(np.float64)
    colp = (t % 64).astype(np.float64)
    inv = 1.0 / (10000.0 ** (np.arange(16, dtype=np.float64) / 16))
    cos = np.zeros((64, TS)); sin = np.zeros((64, TS))
    for d in range(64):
        pos = row if d < 32 else colp
        ang = (pos.astype(np.float32) * np.float32(inv[d % 16]).astype(np.float32)).astype(np.float32)
        cos[d] = np.cos(ang)
        sgn = -1.0 if (d % 32) < 16 else 1.0
        sin[d] = sgn * np.sin(ang)
    rope = np.zeros((2, 128, TS), np.float32)
    rope[0, :64] = cos; rope[0, 64:] = cos
    rope[1, :64] = sin; rope[1, 64:] = sin
    return cst.reshape(128, 640), rope


_CACHE = {}


def kernel(**inp):
    f = lambda a: np.ascontiguousarray(np.asarray(a, dtype=np.float32))
    if "K" not in _CACHE:
        _CACHE["K"] = build()
    K = _CACHE["K"]
    pq = _perm_q()
    part = _partner()
    w_in = f(inp["w_in"])
    q_cols = pq
    qs_cols = np.array([(c // 64) * 64 + part[c % 64] for c in pq])
    k_cols = 512 + np.arange(128)
    ks_cols = 512 + np.array([(c // 64) * 64 + part[c % 64] for c in range(128)])
    rest = np.arange(640, 5376)
    cols = np.concatenate([q_cols, k_cols, qs_cols, ks_cols, rest])
    w_in_p = np.ascontiguousarray(w_in[:, :, cols])
    w_o_attn_p = np.ascontiguousarray(f(inp["w_o_attn"])[:, pq, :])
    cst, rope = _consts()
    shared = {k: f(inp[k]) for k in ("w_mod", "b_mod", "g_pre", "g_post", "w_ffn_gate", "w_ffn_up", "w_ffn_down", "w_conv",
                                     "b_conv", "w_lru_a", "b_lru_a", "w_lru_x", "b_lru_x", "lru_lambda", "s5_lambda_re",
                                     "s5_lambda_im", "s5_log_step", "s5_b_re", "s5_b_im", "s5_c_re", "s5_c_im", "s5_d",
                                     "w_glu", "attn_sink", "w_o_lru", "w_out")}
    shared["w_in_p"] = w_in_p
    shared["w_o_attn_p"] = w_o_attn_p
    shared["cst"] = cst
    shared["rope"] = rope
    xs = f(inp["x_sample"]); xp = f(inp["x_prompt"]); c = f(inp["c"]); cctx = f(inp["c_ctx"])
    ck = f(inp["cache_k"]); cv = f(inp["cache_v"]); sl = f(inp["state_lru"]); ss = f(inp["state_ssm"])
    in_maps = []
    for b in range(8):
        m = dict(shared)
        m["xin"] = np.ascontiguousarray(np.concatenate([xs[b], xp[4 * b:4 * b + 4].reshape(NPB * TP, D)], axis=0))
        m["cc"] = np.ascontiguousarray(np.stack([c[b], cctx], axis=0))
        m["cache_k"] = np.ascontiguousarray(ck[b].reshape(L, 512, 128))
        m["cache_v"] = np.ascontiguousarray(cv[b].reshape(L, 512, 128))
        m["state_lru"] = np.ascontiguousarray(sl[b])
        m["state_ssm"] = np.ascontiguousarray(ss[b])
        in_maps.append(m)
    res = run_bass_kernel_spmd(K.nc, in_maps, core_ids=list(range(8)))
    _CACHE["res"] = res
    R = res.results
    y_s = np.stack([R[b]["y"][:TS] for b in range(8)], axis=0)
    y_p = np.concatenate([R[b]["y"][TS:].reshape(NPB, TP, D) for b in range(8)], axis=0)
    nk = np.concatenate([R[b]["nk"].reshape(NPB, L, TP, 2, 64) for b in range(8)], axis=0)
    nv = np.concatenate([R[b]["nv"].reshape(NPB, L, TP, 2, 64) for b in range(8)], axis=0)
    nl = np.concatenate([R[b]["nlru"] for b in range(8)], axis=0)
    ns = np.concatenate([R[b]["nssm"] for b in range(8)], axis=0)
    return (y_p.astype(np.float32), y_s.astype(np.float32), nk.astype(np.float32), nv.astype(np.float32),
            nl.astype(np.float32), ns.astype(np.float32))
```

```python
import math
import contextlib
import numpy as np
import concourse.bass as bass
import concourse.mybir as mybir
from concourse.bass_utils import run_bass_kernel_spmd

F32 = mybir.dt.float32
BF16 = mybir.dt.bfloat16
AF = mybir.ActivationFunctionType
ALU = mybir.AluOpType
AX = mybir.AxisListType

NDMA_SEM = 8
L = 2
D = 1024
TS = 4096
TP = 256
NPB = 4
NTOK = TS + NPB * TP
TT = 512
NT = NTOK // TT
DFF = 2816
NFC = DFF // 128
WINP = 6016
C_Q, C_K, C_QS, C_KS, C_V, C_XL, C_YL, C_U, C_G = 0, 512, 640, 1152, 1280, 1408, 1920, 2432, 2944
ARENA_F = 52352
EPS = 1e-6


class Buf:
    __slots__ = ("w", "r")

    def __init__(self):
        self.w = {}
        self.r = {}


class Op:
    __slots__ = ("eng", "fn", "waits", "idx", "needed", "semval", "is_dma", "slot")

    def __init__(self, eng, fn):
        self.eng = eng
        self.fn = fn
        self.waits = {}
        self.needed = False
        self.semval = 0
        self.is_dma = False
        self.slot = 0


class Prog:
    ENGS = ("pe", "act", "dve", "pool", "sp")

    def __init__(self, nc):
        self.nc = nc
        self.ops = {e: [] for e in self.ENGS}
        self.ndma = {e: 0 for e in self.ENGS}
        self.pending = {e: {} for e in self.ENGS}
        self.mute = False

    def _add(self, eng, fn, reads, writes, is_dma):
        if self.mute:
            return None
        op = Op(eng, fn)
        op.is_dma = is_dma
        lst = self.ops[eng]
        op.idx = len(lst)
        lst.append(op)
        deps = op.waits
        if self.pending[eng]:
            deps.update(self.pending[eng])
            self.pending[eng] = {}
        if is_dma:
            didx = self.ndma[eng]
            self.ndma[eng] += 1
            op.slot = didx
            pkey = ("d", eng, didx % NDMA_SEM)
            pidx = didx
            if didx >= NDMA_SEM and deps.get(pkey, -1) < didx - NDMA_SEM:
                deps[pkey] = didx - NDMA_SEM
        else:
            pkey = ("c", eng)
            pidx = op.idx
        for b in reads:
            for k, v in b.w.items():
                if deps.get(k, -1) < v:
                    deps[k] = v
        for b in writes:
            for k, v in b.w.items():
                if deps.get(k, -1) < v:
                    deps[k] = v
            for k, v in b.r.items():
                if deps.get(k, -1) < v:
                    deps[k] = v
        for b in reads:
            if b.r.get(pkey, -1) < pidx:
                b.r[pkey] = pidx
        for b in writes:
            b.w = {pkey: pidx}
            b.r = {}
        if eng == "pe" and not is_dma:
            deps.pop(("c", "pe"), None)
        return op

    def op(self, eng, fn, reads=(), writes=()):
        return self._add(eng, fn, reads, writes, False)

    def dma(self, eng, fn, reads=(), writes=()):
        return self._add(eng, fn, reads, writes, True)

    def dma_group(self, eng, fns, reads=(), writes=()):
        if self.mute:
            return
        writes = list(writes)
        snap = [(dict(b.w), dict(b.r)) for b in writes]
        acc = [dict() for _ in writes]
        for fn in fns:
            for b, (w, r) in zip(writes, snap):
                b.w = dict(w)
                b.r = dict(r)
            self._add(eng, fn, reads, writes, True)
            for b, nw in zip(writes, acc):
                nw.update(b.w)
        for b, nw in zip(writes, acc):
            b.w = nw
            b.r = {}

    def barrier(self):
        deps = {}
        for e in self.ENGS:
            last = None
            for op in reversed(self.ops[e]):
                if not op.is_dma:
                    last = op.idx
                    break
            if last is not None:
                deps[("c", e)] = last
            n = self.ndma[e]
            for s in range(NDMA_SEM):
                if n > s:
                    li = ((n - 1 - s) // NDMA_SEM) * NDMA_SEM + s
                    deps[("d", e, s)] = li
        for e in self.ENGS:
            p = self.pending[e]
            for k, v in deps.items():
                if p.get(k, -1) < v:
                    p[k] = v

    def emit(self):
        nc = self.nc
        for e in self.ENGS:
            seen = {}
            for op in self.ops[e]:
                new = {}
                for k, v in op.waits.items():
                    if seen.get(k, -1) >= v:
                        continue
                    seen[k] = v
                    new[k] = v
                op.waits = new
        for e in self.ENGS:
            for op in self.ops[e]:
                for k, v in op.waits.items():
                    if k[0] == "c":
                        self.ops[k[1]][v].needed = True
        for e in self.ENGS:
            c = 0
            for op in self.ops[e]:
                if op.is_dma:
                    continue
                if op.needed:
                    c += 1
                op.semval = c
        handles = {"pe": "tensor", "act": "scalar", "dve": "vector", "pool": "gpsimd", "sp": "sync"}
        with contextlib.ExitStack() as st:
            csem = {e: st.enter_context(nc.semaphore("c_" + e)) for e in self.ENGS}
            dsem = {e: [st.enter_context(nc.semaphore("d_%s_%d" % (e, i))) for i in range(NDMA_SEM)]
                    for e in self.ENGS if self.ndma[e]}
            block = st.enter_context(nc.Block())
            prog = self

            def run(e, eng):
                for op in prog.ops[e]:
                    for k, v in op.waits.items():
                        if k[0] == "c":
                            eng.wait_ge(csem[k[1]], prog.ops[k[1]][v].semval)
                        else:
                            eng.wait_ge(dsem[k[1]][k[2]], 16 * (v // NDMA_SEM + 1))
                    ins = op.fn(eng)
                    if op.is_dma:
                        ins.then_inc(dsem[e][op.slot % NDMA_SEM], 16)
                    elif op.needed:
                        ins.then_inc(csem[e], 1)

            for e in self.ENGS:
                if not self.ops[e]:
                    continue
                getattr(block, handles[e])(lambda eng, e=e: run(e, eng))


def mkap(t, offset, pairs):
    return bass.AP(t, offset, [list(p) for p in pairs])


def view(ap2, shape):
    if len(shape) == 1:
        return ap2
    names = " ".join("a%d" % i for i in range(len(shape)))
    kw = {"a%d" % i: s for i, s in enumerate(shape)}
    return ap2.rearrange("p (%s) -> p %s" % (names, names), **kw)


class Tl:
    __slots__ = ("ap", "b", "bs")

    def __init__(self, ap, nb=0):
        self.ap = ap
        self.b = Buf()
        self.bs = [Buf() for _ in range(nb)]


class Arena:
    def __init__(self, t, size):
        self.t = t
        self.size = size
        self.top = 0

    def reset(self):
        self.top = 0

    def f32(self, *shape, nb=0):
        n = int(np.prod(shape))
        assert self.top + n <= self.size, ("arena overflow", self.top, n)
        ap = self.t[:, self.top:self.top + n]
        self.top += n
        return Tl(view(ap, shape), nb)

    def bf16(self, *shape, nb=0):
        n = int(np.prod(shape))
        nf = (n + 1) // 2
        assert self.top + nf <= self.size, ("arena overflow", self.top, nf)
        ap = self.t[:, self.top:self.top + nf].bitcast(BF16)[:, 0:n]
        self.top += nf
        return Tl(view(ap, shape), nb)


def bcast_free(ap, n):
    return mkap(ap.tensor, ap.offset, [list(ap.ap[0]), [0, n]])


class Kern:
    def __init__(self, dbg=False):
        self.dbg = dbg
        nc = self.nc = bass.Bass("TRN2", target_bir_lowering=False)
        self.P = Prog(nc)
        self.st = contextlib.ExitStack()
        I = self.I = {}
        O = self.O = {}

        def inp(name, shape, dt=F32):
            I[name] = nc.dram_tensor(name, list(shape), dt, kind="ExternalInput").ap()

        def outp(name, shape, dt=F32):
            O[name] = nc.dram_tensor(name, list(shape), dt, kind="ExternalOutput").ap()

        inp("xin", [NTOK, D]); inp("cc", [2, D]); inp("cache_k", [L, 512, 128]); inp("cache_v", [L, 512, 128])
        inp("state_lru", [L, 2, 512]); inp("state_ssm", [L, 2, 2, 32, 64])
        inp("w_mod", [L, D, 9 * D]); inp("b_mod", [L, 9 * D]); inp("g_pre", [L, 3, D]); inp("g_post", [L, 3, D])
        inp("w_ffn_gate", [L, 2, D, DFF]); inp("w_ffn_up", [L, 2, D, DFF]); inp("w_ffn_down", [L, 2, DFF, D])
        inp("w_in_p", [L, D, WINP]); inp("w_conv", [L, 4, 512]); inp("b_conv", [L, 512])
        inp("w_lru_a", [L, 2, 8, 64, 64]); inp("b_lru_a", [L, 2, 512]); inp("w_lru_x", [L, 2, 8, 64, 64])
        inp("b_lru_x", [L, 2, 512]); inp("lru_lambda", [L, 2, 512])
        inp("s5_lambda_re", [L, 2, 32, 64]); inp("s5_lambda_im", [L, 2, 32, 64]); inp("s5_log_step", [L, 2, 32])
        inp("s5_b_re", [L, 2, 32, 64, 16]); inp("s5_b_im", [L, 2, 32, 64, 16])
        inp("s5_c_re", [L, 2, 32, 16, 64]); inp("s5_c_im", [L, 2, 32, 16, 64]); inp("s5_d", [L, 512])
        inp("w_glu", [L, 512, 2048]); inp("attn_sink", [L, 8]); inp("w_o_lru", [L, 512, D])
        inp("w_o_attn_p", [L, 512, D]); inp("w_out", [L, D, D])
        inp("cst", [128, 5 * 128]); inp("rope", [2, 128, TS])
        outp("y", [NTOK, D]); outp("nk", [NPB, L, TP, 128]); outp("nv", [NPB, L, TP, 128])
        outp("nlru", [NPB, L, 2, 512]); outp("nssm", [NPB, L, 2, 2, 32, 64])
        self.obufs = {k: Buf() for k in O}
        if dbg:
            outp("d_sc", [128, L * 144])

        def scr(name, shape, dt):
            kind = "ExternalOutput" if dbg else "Internal"
            t = nc.dram_tensor(name, list(shape), dt, kind=kind).ap()
            return Tl(t)

        self.XT = scr("s_xt", [8, 128, NTOK], F32)
        self.Qs = scr("s_q", [4, 128, NTOK], BF16)
        self.Ks = scr("s_k", [128, NTOK], BF16)
        self.Vs = scr("s_v", [NTOK, 130], BF16)
        self.XLs = scr("s_xl", [4, 128, NTOK], F32)
        self.YLs = scr("s_yl", [4, 128, NTOK], BF16)
        self.UFs = scr("s_uf", [NT, 128, 32 * 64], BF16)
        self.Gs = scr("s_g", [24, 128, NTOK], BF16)
        self.ATs = scr("s_att", [4, 128, NTOK], BF16)
        self.LRs = scr("s_lru", [4, 128, NTOK], BF16)
        self.SYs = scr("s_s5y", [4, 128, NTOK], BF16)
        self.PFd = [scr("s_pf%d" % l, [128, 2 * 16 * 2 * 128], BF16) for l in range(L)]
        self.Qd = [scr("s_qd%d" % l, [128, 2 * 16 * 2 * 128], BF16) for l in range(L)]
        self.MLd = [scr("s_ml%d" % l, [128, 32 * 128], BF16) for l in range(L)]
        self.A12 = [scr("s_a12%d" % l, [128, 128], F32) for l in range(L)]

        self.sb_t = self.st.enter_context(nc.sbuf_tensor("arena", [128, ARENA_F], F32))
        self.pc_t = self.st.enter_context(nc.sbuf_tensor("persist", [128, 832], F32))
        self.ps_t = self.st.enter_context(nc.psum_tensor("psum", [128, 4096], F32))
        self.A = Arena(self.sb_t, ARENA_F)
        self.PA = Arena(self.pc_t, 832)
        self.pbufs = [Buf() for _ in range(8)]

    def bank(self, i):
        return self.ps_t[:, i * 512:(i + 1) * 512]

    def mm(self, out, lhsT, rhs, start, stop, reads, writes):
        self.P.op("pe", lambda e: e.matmul(out, lhsT=lhsT, rhs=rhs, start=start, stop=stop), reads, writes)

    def tr(self, out, in_, ident, reads, writes):
        self.P.op("pe", lambda e: e.transpose(out, in_, ident), reads, writes)

    def act(self, out, in_, func, reads, writes, scale=1.0, bias=0.0):
        self.P.op("act", lambda e: e.activation(out=out, in_=in_, func=func, scale=scale, bias=bias), reads, writes)

    def tt(self, eng, out, in0, in1, op, reads, writes):
        self.P.op(eng, lambda e: e.tensor_tensor(out=out, in0=in0, in1=in1, op=op), reads, writes)

    def ts(self, eng, out, in0, s1, s2, op0, op1, reads, writes):
        if s2 is None:
            self.P.op(eng, lambda e: e.tensor_scalar(out=out, in0=in0, scalar1=s1, scalar2=None, op0=op0), reads, writes)
        else:
            self.P.op(eng, lambda e: e.tensor_scalar(out=out, in0=in0, scalar1=s1, scalar2=s2, op0=op0, op1=op1), reads, writes)

    def stt(self, out, in0, scalar, in1, op0, op1, reads, writes):
        self.P.op("dve", lambda e: e.scalar_tensor_tensor(out=out, in0=in0, scalar=scalar, in1=in1, op0=op0, op1=op1), reads, writes)

    def cp(self, eng, out, in_, reads, writes):
        if eng == "act":
            self.P.op("act", lambda e: e.copy(out=out, in_=in_), reads, writes)
        else:
            self.P.op(eng, lambda e: e.tensor_copy(out=out, in_=in_), reads, writes)

    def ms(self, eng, out, val, writes):
        self.P.op(eng, lambda e: e.memset(out, val), (), writes)

    def dma(self, q, out, in_, reads, writes, slow=False):
        assert not slow
        shp = tuple(out.shape)
        if len(shp) >= 3 and shp[0] * shp[1] > 256 and shp[1] > 1 and tuple(in_.shape)[:2] == shp[:2]:
            step = max(1, 256 // shp[0])
            fns = []
            for a in range(0, shp[1], step):
                e_ = min(a + step, shp[1])
                o_ = out[:, a:e_]
                i_ = in_[:, a:e_]
                fns.append(lambda e, o_=o_, i_=i_: e.dma_start(out=o_, in_=i_))
            self.P.dma_group(q, fns, reads, writes)
            return
        self.P.dma(q, lambda e: e.dma_start(out=out, in_=in_), reads, writes)

    def vecT(self, dst, src_rows, n, writes):
        stg = self.A.f32(128)
        self.P.dma("sp", lambda e: e.dma_start(out=stg.ap[0:n, :], in_=src_rows), [], [stg.b])
        self.tr(self.bank(7)[:, 0:n], stg.ap[0:n, :], self.identf.ap[0:n, 0:n], [stg.b, self.identf.b], [self.pbufs[7]])
        return self.bank(7)[:, 0:n], self.pbufs[7]

    def recip(self, out, in_, reads, writes):
        self.P.op("dve", lambda e: e.reciprocal(out=out, in_=in_), reads, writes)

    def phase(self):
        self.P.barrier()
        self.A.reset()
        self.pbufs = [Buf() for _ in range(8)]

    def prologue(self):
        A, PA, I = self.A, self.PA, self.I
        cst = A.f32(5, 128)
        self.dma("sp", cst.ap, I["cst"].rearrange("p (a b) -> p a b", a=5), [], [cst.b])
        self.identf = PA.f32(128)
        self.identb = PA.bf16(128)
        self.onesb = PA.bf16(128)
        self.onesf = PA.f32(128)
        self.mprev = PA.bf16(128)
        self.mnext = PA.bf16(128)
        self.cp("dve", self.identf.ap, cst.ap[:, 0, :], [cst.b], [self.identf.b])
        self.cp("dve", self.identb.ap, cst.ap[:, 0, :], [cst.b], [self.identb.b])
        self.cp("dve", self.mprev.ap, cst.ap[:, 1, :], [cst.b], [self.mprev.b])
        self.cp("dve", self.mnext.ap, cst.ap[:, 2, :], [cst.b], [self.mnext.b])
        self.ms("pool", self.onesb.ap, 1.0, [self.onesb.b])
        self.ms("pool", self.onesf.ap, 1.0, [self.onesf.b])
        self.SC = PA.f32(L, 3, 3, 8, 2)
        ccT = A.f32(8, 2)
        src, sb_ = self.vecT(None, I["cc"].rearrange("w (kc p) -> (w kc) p", p=128), 16, None)
        self.cp("dve", ccT.ap.rearrange("p kc w -> p w kc"), view(src, (2, 8)), [sb_], [ccT.b])
        sg = A.f32(8, 2)
        self.act(sg.ap, ccT.ap, AF.Sigmoid, [ccT.b], [sg.b])
        self.tt("dve", ccT.ap, ccT.ap, sg.ap, ALU.mult, [ccT.b, sg.b], [ccT.b])
        wm = [A.f32(8, 1152), A.f32(8, 1152)]
        modt = A.f32(L, 72, 2)
        bm = A.f32(L, 72)
        gp = A.f32(L, 3, 8)
        gq = A.f32(L, 3, 8)
        for l in range(L):
            src, sb_ = self.vecT(None, I["b_mod"][l].rearrange("(c p) -> c p", p=128), 72, None)
            self.cp("dve", bm.ap[:, l, :], src, [sb_], [bm.b])
            src, sb_ = self.vecT(None, I["g_pre"][l].rearrange("i (c p) -> (i c) p", p=128), 24, None)
            self.cp("dve", gp.ap[:, l].rearrange("p i c -> p (i c)"), src, [sb_], [gp.b])
            src, sb_ = self.vecT(None, I["g_post"][l].rearrange("i (c p) -> (i c) p", p=128), 24, None)
            self.cp("dve", gq.ap[:, l].rearrange("p i c -> p (i c)"), src, [sb_], [gq.b])
        pm = self.bank(0)
        pmb = self.pbufs[0]
        k = 0
        for l in range(L):
            for piece in range(8):
                w_ = wm[k % 2]
                q = "sp" if k % 2 == 0 else "act"
                k += 1
                src = I["w_mod"][l][:, piece * 1152:(piece + 1) * 1152].rearrange("(kc p) c -> p kc c", p=128)
                self.dma(q, w_.ap, src, [], [w_.b])
                for cch in range(9):
                    col = (l * 72 + piece * 9 + cch) * 2
                    for kc in range(8):
                        self.mm(pm[:, col:col + 2], w_.ap[:, kc, cch * 128:(cch + 1) * 128], ccT.ap[:, kc, :],
                                kc == 0, kc == 7, [w_.b, ccT.b], [pmb])
        self.cp("dve", modt.ap, view(pm[:, 0:L * 144], (L, 72, 2)), [pmb], [modt.b])
        for w in range(2):
            self.tt("dve", modt.ap[:, :, :, w], modt.ap[:, :, :, w], bm.ap, ALU.add, [modt.b, bm.b], [modt.b])
        for l in range(L):
            m5 = modt.ap[:, l].rearrange("p (i k c) w -> p i k c w", i=3, k=3)
            for w in range(2):
                self.stt(self.SC.ap[:, l, 0, :, :, w], m5[:, :, 1, :, w], 1.0, gp.ap[:, l], ALU.add, ALU.mult,
                         [modt.b, gp.b], [self.SC.b])
                self.cp("dve", self.SC.ap[:, l, 1, :, :, w], m5[:, :, 0, :, w], [modt.b], [self.SC.b])
                self.tt("dve", self.SC.ap[:, l, 2, :, :, w], m5[:, :, 2, :, w], gq.ap[:, l], ALU.mult,
                        [modt.b, gq.b], [self.SC.b])
            for i in (0, 2):
                self.ts("dve", self.SC.ap[:, l, 2, i], self.SC.ap[:, l, 2, i], 0.5, None, ALU.mult, None,
                        [self.SC.b], [self.SC.b])
        if self.dbg:
            self.dma("sp", self.O["d_sc"], self.SC.ap.rearrange("p l k i c w -> p (l k i c w)"), [self.SC.b], [])
        for l in range(L):
            self.s5_prep(l, cst)

    def scal(self, l, kind, i, c, w):
        return self.SC.ap[:, l, kind, i, c, w:w + 1]

    def s5_prep(self, l, cst_unused=None):
        self.phase()
        A, I = self.A, self.I
        cst = A.f32(5, 128)
        self.dma("sp", cst.ap, I["cst"].rearrange("p (a b) -> p a b", a=5), [], [cst.b])
        idf = self.identf
        pb = self.pbufs
        lre_r = A.f32(128); lim_r = A.f32(128); lst_r = A.f32(2); lst_x = A.f32(128)
        self.dma("sp", lre_r.ap[0:32, :], I["s5_lambda_re"][l].rearrange("d (gp g2) n -> (d gp) (g2 n)", g2=2), [], [lre_r.b])
        self.dma("sp", lim_r.ap[0:32, :], I["s5_lambda_im"][l].rearrange("d (gp g2) n -> (d gp) (g2 n)", g2=2), [], [lim_r.b])
        self.dma("sp", lst_r.ap[0:32, :], I["s5_log_step"][l].rearrange("d (gp g2) -> (d gp) g2", g2=2), [], [lst_r.b])
        a_ = lst_r.ap[0:32, :]
        self.cp("dve", view(lst_x.ap[0:32, :], (2, 64)), mkap(a_.tensor, a_.offset, [list(a_.ap[0]), [1, 2], [0, 64]]),
                [lst_r.b], [lst_x.b])
        RR = A.f32(8192)
        braw = [Tl(view(RR.ap[:, c * 2048:(c + 1) * 2048], (128, 16))) for c in range(2)]
        craw = [Tl(view(RR.ap[:, (2 + c) * 2048:(3 + c) * 2048], (16, 2, 64))) for c in range(2)]
        for c, nm in enumerate(("s5_b_re", "s5_b_im")):
            self.dma("act", braw[c].ap[0:32], I[nm][l].rearrange("d (gp g2) n ci -> (d gp) (g2 n) ci", g2=2), [], [braw[c].b])
        for c, nm in enumerate(("s5_c_re", "s5_c_im")):
            srcv = I[nm][l].rearrange("d (gp g2) co n -> (d gp) g2 co n", g2=2)
            for g2 in range(2):
                self.dma("act", craw[c].ap[0:32, :, g2, :], srcv[:, g2], [], [craw[c].b])
        draw = A.f32(16); dx = A.f32(8, 16)
        self.dma("sp", draw.ap[0:32, :], I["s5_d"][l].rearrange("(g ci) -> g ci", ci=16), [], [draw.b])
        a_ = draw.ap[0:32, :]
        self.cp("dve", dx.ap[0:32], mkap(a_.tensor, a_.offset, [list(a_.ap[0]), [0, 8], [1, 16]]), [draw.b], [dx.b])
        sc = A.f32(40, 32)
        names = {}

        def S(nm):
            if nm not in names:
                names[nm] = len(names)
                assert len(names) <= 40
            return sc.ap[:, names[nm], :]

        scb = sc.b
        pt = self.bank(0)
        self.tr(pt[:, 0:32], lre_r.ap[0:32, :], idf.ap[0:32, 0:32], [lre_r.b, idf.b], [pb[0]])
        self.tr(pt[:, 32:64], lim_r.ap[0:32, :], idf.ap[0:32, 0:32], [lim_r.b, idf.b], [pb[0]])
        self.tr(pt[:, 64:96], lst_x.ap[0:32, :], idf.ap[0:32, 0:32], [lst_x.b, idf.b], [pb[0]])
        self.tr(pt[:, 96:128], dx.ap[0:32].rearrange("p a b -> p (a b)"), idf.ap[0:32, 0:32], [dx.b, idf.b], [pb[0]])
        self.cp("dve", S("lr"), pt[:, 0:32], [pb[0]], [scb])
        self.cp("dve", S("li"), pt[:, 32:64], [pb[0]], [scb])
        self.cp("dve", S("ls"), pt[:, 64:96], [pb[0]], [scb])
        dcol = A.f32(32)
        self.cp("dve", dcol.ap, pt[:, 96:128], [pb[0]], [dcol.b])
        BC = []
        for idx, raw in enumerate(braw + craw):
            bk = self.bank(1 + idx % 2)
            bb = pb[1 + idx % 2]
            for j in range(16):
                if idx < 2:
                    src = raw.ap[0:32, :, j]
                else:
                    src = raw.ap[0:32, j].rearrange("p a b -> p (a b)")
                self.tr(bk[:, j * 32:(j + 1) * 32], src, idf.ap[0:32, 0:32], [raw.b, idf.b], [bb])
            t = A.f32(16, 32)
            self.cp("act", t.ap, view(bk, (16, 32)), [bb], [t.b])
            BC.append(t)
        Bre, Bim, Cre, Cim = BC

        def dv(out, a, b, op):
            self.tt("dve", out, a, b, op, [scb], [scb])

        self.ts("dve", S("lr"), S("lr"), -1e-4, None, ALU.min, None, [scb], [scb])
        self.act(S("step"), S("ls"), AF.Exp, [scb], [scb])
        dv(S("xre"), S("lr"), S("step"), ALU.mult)
        dv(S("ang"), S("li"), S("step"), ALU.mult)
        self.act(S("mag"), S("xre"), AF.Exp, [scb], [scb])
        hp = A.f32(1)
        self.ms("dve", hp.ap, math.pi / 2, [hp.b])
        self.act(S("s"), S("ang"), AF.Sin, [scb], [scb], scale=1.0 / 16)
        self.act(S("c"), S("ang"), AF.Sin, [scb, hp.b], [scb], scale=-1.0 / 16, bias=hp.ap[:, 0:1])
        for _ in range(4):
            dv(S("t1"), S("c"), S("c"), ALU.mult)
            dv(S("t2"), S("s"), S("s"), ALU.mult)
            dv(S("t3"), S("c"), S("s"), ALU.mult)
            dv(S("c"), S("t1"), S("t2"), ALU.subtract)
            self.ts("dve", S("s"), S("t3"), 2.0, None, ALU.mult, None, [scb], [scb])
        dv(S("are"), S("mag"), S("c"), ALU.mult)
        dv(S("aim"), S("mag"), S("s"), ALU.mult)
        dv(S("t1"), S("lr"), S("lr"), ALU.mult)
        dv(S("t2"), S("li"), S("li"), ALU.mult)
        dv(S("den"), S("t1"), S("t2"), ALU.add)
        self.recip(S("rden"), S("den"), [scb], [scb])
        self.ts("dve", S("nre"), S("are"), -1.0, None, ALU.add, None, [scb], [scb])
        dv(S("t1"), S("nre"), S("lr"), ALU.mult)
        dv(S("t2"), S("aim"), S("li"), ALU.mult)
        dv(S("t1"), S("t1"), S("t2"), ALU.add)
        dv(S("cre"), S("t1"), S("rden"), ALU.mult)
        dv(S("t1"), S("aim"), S("lr"), ALU.mult)
        dv(S("t2"), S("nre"), S("li"), ALU.mult)
        dv(S("t1"), S("t1"), S("t2"), ALU.subtract)
        dv(S("cim"), S("t1"), S("rden"), ALU.mult)
        dv(S("t1"), S("mag"), S("mag"), ALU.mult)
        self.recip(S("t2"), S("t1"), [scb], [scb])
        dv(S("iare"), S("are"), S("t2"), ALU.mult)
        dv(S("t3"), S("aim"), S("t2"), ALU.mult)
        self.ts("dve", S("iaim"), S("t3"), -1.0, None, ALU.mult, None, [scb], [scb])
        pw = A.f32(9, 2, 32)
        self.ms("dve", pw.ap[:, 0, 0, :], 1.0, [pw.b])
        self.ms("dve", pw.ap[:, 0, 1, :], 0.0, [pw.b])
        for j in range(1, 9):
            for (o, x1, y1, x2, y2, op) in ((pw.ap[:, j, 0, :], pw.ap[:, j - 1, 0, :], S("are"), pw.ap[:, j - 1, 1, :], S("aim"), ALU.subtract),
                                            (pw.ap[:, j, 1, :], pw.ap[:, j - 1, 0, :], S("aim"), pw.ap[:, j - 1, 1, :], S("are"), ALU.add)):
                self.tt("dve", S("t1"), x1, y1, ALU.mult, [pw.b, scb], [scb])
                self.tt("dve", S("t2"), x2, y2, ALU.mult, [pw.b, scb], [scb])
                self.tt("dve", o, S("t1"), S("t2"), op, [scb], [pw.b])
        a12 = A.f32(2, 2, 16, 2)
        for d in range(2):
            for c in range(2):
                self.cp("dve", a12.ap[:, d, 0, :, c], pw.ap[:, 8, 0, d * 16:(d + 1) * 16], [pw.b], [a12.b])
            self.ts("dve", a12.ap[:, d, 1, :, 0], pw.ap[:, 8, 1, d * 16:(d + 1) * 16], -1.0, None, ALU.mult, None, [pw.b], [a12.b])
            self.cp("dve", a12.ap[:, d, 1, :, 1], pw.ap[:, 8, 1, d * 16:(d + 1) * 16], [pw.b], [a12.b])
        self.dma("sp", self.A12[l].ap, a12.ap.rearrange("p d k g c -> p (d k g c)"), [a12.b], [])
        self.cp("dve", S("pr"), S("iare"), [scb], [scb])
        self.cp("dve", S("pi"), S("iaim"), [scb], [scb])
        for _ in range(3):
            dv(S("t1"), S("pr"), S("pr"), ALU.mult)
            dv(S("t2"), S("pi"), S("pi"), ALU.mult)
            dv(S("t3"), S("pr"), S("pi"), ALU.mult)
            dv(S("pr"), S("t1"), S("t2"), ALU.subtract)
            self.ts("dve", S("pi"), S("t3"), 2.0, None, ALU.mult, None, [scb], [scb])
        Bbr = A.f32(16, 32); Bbi = A.f32(16, 32); T1 = A.f32(16, 32); T2 = A.f32(16, 32)

        def bc16(ap):
            return mkap(ap.tensor, ap.offset, [list(ap.ap[0]), [0, 16], [1, 32]])

        for (o, x1, x2, op) in ((Bbr, Bre, Bim, ALU.subtract), (Bbi, Bim, Bre, ALU.add)):
            self.tt("dve", T1.ap, x1.ap, bc16(S("cre")), ALU.mult, [x1.b, scb], [T1.b])
            self.tt("dve", T2.ap, x2.ap, bc16(S("cim")), ALU.mult, [x2.b, scb], [T2.b])
            self.tt("dve", o.ap, T1.ap, T2.ap, op, [T1.b, T2.b], [o.b])
        XS = A.f32(2, 16, 2, 8, 16)
        Q = A.f32(2, 16, 2, 8, 16)
        tmp = [[A.f32(16, 16), A.f32(16, 16)] for _ in range(2)]

        def pwv(j, c, d):
            a = pw.ap[:, j, c, d * 16:(d + 1) * 16]
            return mkap(a.tensor, a.offset, [list(a.ap[0]), [1, 16], [0, 16]])

        def mat(tl, d):
            return tl.ap[:, :, d * 16:(d + 1) * 16].rearrange("p x g -> p g x")

        k = 0
        for d in range(2):
            for s in range(8):
                for (dst, Mr, Mi, j, neg_im) in ((XS, Bbr, Bbi, (7 - s) if d == 0 else s, False),
                                                 (Q, Cre, Cim, (s + 1) if d == 0 else (8 - s), True)):
                    eng = "dve" if k % 2 == 0 else "pool"
                    t1, t2 = tmp[k % 2]
                    k += 1
                    rd = [Mr.b, Mi.b, pw.b]
                    self.tt(eng, t1.ap, mat(Mr, d), pwv(j, 0, d), ALU.mult, rd, [t1.b])
                    self.tt(eng, t2.ap, mat(Mi, d), pwv(j, 1, d), ALU.mult, rd, [t2.b])
                    self.tt(eng, dst.ap[:, d, :, 0, s, :], t1.ap, t2.ap, ALU.subtract, [t1.b, t2.b], [dst.b])
                    self.tt(eng, t1.ap, mat(Mr, d), pwv(j, 1, d), ALU.mult, rd, [t1.b])
                    self.tt(eng, t2.ap, mat(Mi, d), pwv(j, 0, d), ALU.mult, rd, [t2.b])
                    self.tt(eng, dst.ap[:, d, :, 1, s, :], t1.ap, t2.ap, ALU.add, [t1.b, t2.b], [dst.b])
        qim = Q.ap[:, :, :, 1].rearrange("p d g s c -> p (d g) (s c)")
        self.ts("pool", qim, qim, -1.0, None, ALU.mult, None, [Q.b], [Q.b])
        XM = A.f32(2, 16, 2, 128)
        X4 = XS.ap.rearrange("p d g c s i -> p d g c (s i)")
        big = [A.f32(16, 128), A.f32(16, 128)]

        def pv(nm, d):
            a = S(nm)[:, d * 16:(d + 1) * 16]
            return mkap(a.tensor, a.offset, [list(a.ap[0]), [1, 16], [0, 128]])

        for d in range(2):
            eng = "dve" if d == 0 else "pool"
            t1, t2 = big
            rd = [XS.b, scb]
            self.tt(eng, t1.ap, X4[:, d, :, 0, :], pv("pr", d), ALU.mult, rd, [t1.b])
            self.tt(eng, t2.ap, X4[:, d, :, 1, :], pv("pi", d), ALU.mult, rd, [t2.b])
            self.tt(eng, XM.ap[:, d, :, 0, :], t1.ap, t2.ap, ALU.subtract, [t1.b, t2.b], [XM.b])
            self.tt(eng, t1.ap, X4[:, d, :, 1, :], pv("pr", d), ALU.mult, rd, [t1.b])
            self.tt(eng, t2.ap, X4[:, d, :, 0, :], pv("pi", d), ALU.mult, rd, [t2.b])
            self.tt(eng, XM.ap[:, d, :, 1, :], t1.ap, t2.ap, ALU.add, [t1.b, t2.b], [XM.b])
        self.P.barrier()
        PFs = Tl(view(RR.ap[:, 0:4096].bitcast(BF16), (64, 128)))
        Q4 = Q.ap.rearrange("p d g c s i -> p (d g c) (s i)")
        XS3 = XS.ap.rearrange("p d g c s i -> p (d g c) (s i)")
        for q4 in range(16):
            bk = self.bank(3 + q4 % 2); bb = pb[3 + q4 % 2]
            for jj in range(4):
                self.tr(bk[:, jj * 128:(jj + 1) * 128], XS3[:, q4 * 4 + jj, :], idf.ap, [XS.b, idf.b], [bb])
            self.cp("act", PFs.ap[:, q4 * 4:(q4 + 1) * 4, :], view(bk, (4, 128)), [bb], [PFs.b])
        self.dma("sp", self.PFd[l].ap, PFs.ap.rearrange("p a b -> p (a b)"), [PFs.b], [])
        Qb = Tl(view(RR.ap[:, 4096:8192].bitcast(BF16), (64, 128)))
        self.cp("pool", Qb.ap, Q4, [Q.b], [Qb.b])
        self.dma("sp", self.Qd[l].ap, Qb.ap.rearrange("p a b -> p (a b)"), [Qb.b], [])
        MLs = A.bf16(32, 128)
        mt = [A.f32(128), A.f32(128)]
        mu = [A.f32(128), A.f32(128)]
        XM4 = XM.ap
        Q5 = Q.ap.rearrange("p d g c s i -> p d g c (s i)")
        for g in range(32):
            gp_, g2 = g // 2, g % 2
            bk = self.bank(5 + g % 2); bb = pb[5 + g % 2]
            sl = slice(g2 * 64, (g2 + 1) * 64)
            for d in range(2):
                for c in range(2):
                    self.mm(bk[:, d * 128:(d + 1) * 128], XM4[sl, d, gp_, c, :], Q5[sl, d, gp_, c, :], c == 0, c == 1,
                            [XM.b, Q.b], [bb])
            t = mt[g % 2]
            u = mu[g % 2]
            self.tt("dve", t.ap, bk[:, 0:128], cst.ap[:, 3, :], ALU.mult, [bb, cst.b], [t.b])
            self.tt("dve", u.ap, bk[:, 128:256], cst.ap[:, 4, :], ALU.mult, [bb, cst.b], [u.b])
            self.tt("pool", t.ap, t.ap, u.ap, ALU.add, [t.b, u.b], [t.b])
            self.stt(MLs.ap[:, g, :], idf.ap, dcol.ap[:, g:g + 1], t.ap, ALU.mult, ALU.add, [idf.b, dcol.b, t.b], [MLs.b])
        self.dma("sp", self.MLd[l].ap, MLs.ap.rearrange("p a b -> p (a b)"), [MLs.b], [])

    def alloc_norm(self):
        A = self.A
        self.n_rstd = A.f32(TT)
        self.n_tmp = [A.f32(TT), A.f32(TT)]
        self.n_k = 0

    def rstd_from(self, bankidx):
        r = self.n_rstd
        pbk = self.pbufs[bankidx]
        self.ts("dve", r.ap, self.bank(bankidx), 1.0 / D, EPS, ALU.mult, ALU.add, [pbk], [r.b])
        self.act(r.ap, r.ap, AF.Sqrt, [r.b], [r.b])
        self.recip(r.ap, r.ap, [r.b], [r.b])
        return r

    def load_x(self, xt, ti, first=False, alias=None):
        tok0 = ti * TT
        if not first:
            self.dma("sp", xt.ap, self.XT.ap[:, :, tok0:tok0 + TT].rearrange("c p t -> p c t"), [], xt.bs)
            return
        xtm, extra = alias
        self.dma("sp", xtm.ap, self.I["xin"][tok0:tok0 + TT].rearrange("(b p) d -> p b d", p=128), [], [xtm.b] + extra)
        for c in range(8):
            bi = c % 2
            for blk in range(4):
                self.tr(self.bank(bi)[:, blk * 128:(blk + 1) * 128], xtm.ap[:, blk, c * 128:(c + 1) * 128], self.identf.ap,
                        [xtm.b, self.identf.b] + extra, [self.pbufs[bi]])
            self.cp("act" if c % 2 else "dve", xt.ap[:, c, :], self.bank(bi), [self.pbufs[bi]], [xt.bs[c]])

    def store_x(self, xt, ti, last=False, alias=None):
        tok0 = ti * TT
        if not last:
            self.dma("sp", self.XT.ap[:, :, tok0:tok0 + TT].rearrange("c p t -> p c t"), xt.ap, xt.bs, [])
            return
        yst, extra = alias
        for blk in range(4):
            for half in range(2):
                bi = (blk * 2 + half) % 2
                for cc in range(4):
                    c = half * 4 + cc
                    self.tr(self.bank(bi)[:, cc * 128:(cc + 1) * 128], xt.ap[:, c, blk * 128:(blk + 1) * 128], self.identf.ap,
                            [xt.bs[c], self.identf.b], [self.pbufs[bi]])
                self.cp("act" if half else "dve", yst.ap[:, blk, half * 512:(half + 1) * 512], self.bank(bi),
                        [self.pbufs[bi]], [yst.b] + extra)
        self.dma("sp", self.O["y"][tok0:tok0 + TT].rearrange("(b p) d -> p b d", p=128), yst.ap, [yst.b] + extra, [])

    def prenorm(self, l, sub, which, xt, hT, sq, bankidx=7):
        pbk = self.pbufs[bankidx]
        for c in range(8):
            self.tt("pool", sq.ap[:, c, :], xt.ap[:, c, :], xt.ap[:, c, :], ALU.mult, [xt.bs[c]], [sq.b])
        for c in range(8):
            self.mm(self.bank(bankidx), self.onesb.ap, sq.ap[:, c, :], c == 0, c == 7, [self.onesb.b, sq.b], [pbk])
        r = self.rstd_from(bankidx)
        for c in range(8):
            t = self.n_tmp[self.n_k % 2]
            self.n_k += 1
            self.tt("dve", t.ap, xt.ap[:, c, :], r.ap, ALU.mult, [xt.bs[c], r.b], [t.b])
            self.ts("pool", hT.ap[:, c, :], t.ap, self.scal(l, 0, sub, c, which), self.scal(l, 1, sub, c, which),
                    ALU.mult, ALU.add, [t.b, self.SC.b], [hT.b])

    def post_chunk(self, m, pbank, fT, sqr, ssbank):
        pbk = self.pbufs[pbank]
        s = sqr[m % 2]
        self.cp("dve", fT.ap[:, m, :], self.bank(pbank), [pbk], [fT.b])
        self.tt("pool", s.ap, fT.ap[:, m, :], fT.ap[:, m, :], ALU.mult, [fT.b], [s.b])
        if m > 0:
            p_ = sqr[(m - 1) % 2]
            self.mm(self.bank(ssbank), self.onesb.ap, p_.ap, m == 1, False, [self.onesb.b, p_.b], [self.pbufs[ssbank]])
        if m == 7:
            self._last_sq = s

    def post_update(self, l, sub, which, xt, fT, ssbank):
        p_ = self._last_sq
        self.mm(self.bank(ssbank), self.onesb.ap, p_.ap, False, True, [self.onesb.b, p_.b], [self.pbufs[ssbank]])
        r = self.rstd_from(ssbank)
        for m in range(8):
            t = self.n_tmp[self.n_k % 2]
            self.n_k += 1
            self.stt(t.ap, fT.ap[:, m, :], self.scal(l, 2, sub, m, which), r.ap, ALU.mult, ALU.mult,
                     [fT.b, self.SC.b, r.b], [t.b])
            self.tt("pool", xt.ap[:, m, :], xt.ap[:, m, :], t.ap, ALU.add, [xt.bs[m], t.b], [xt.bs[m]])

    def ffn_pass(self, l, i, first=False, last=False):
        self.phase()
        A, I = self.A, self.I
        sub = 0 if i == 0 else 2
        wg = A.bf16(8, DFF, nb=2); wu = A.bf16(8, DFF, nb=2); wd = A.bf16(NFC, D, nb=2)
        for h in range(2):
            cs = slice(h * 1408, (h + 1) * 1408)
            self.dma("pool", wg.ap[:, :, cs], I["w_ffn_gate"][l, i][:, cs].rearrange("(kc p) f -> p kc f", p=128), [], [wg.bs[h]])
            self.dma("pool", wu.ap[:, :, cs], I["w_ffn_up"][l, i][:, cs].rearrange("(kc p) f -> p kc f", p=128), [], [wu.bs[h]])
        wdv = I["w_ffn_down"][l, i].rearrange("(fc p) d -> p fc d", p=128)
        for h in range(2):
            fs = slice(h * 11, (h + 1) * 11)
            self.dma("pool", wd.ap[:, fs, :], wdv[:, fs, :], [], [wd.bs[h]])
        xt = A.f32(8, TT, nb=8)
        r1 = A.f32(8, TT)
        hT = Tl(view(r1.ap.rearrange("p a b -> p (a b)")[:, 0:2048].bitcast(BF16), (8, TT)))
        sq = Tl(view(r1.ap.rearrange("p a b -> p (a b)")[:, 2048:4096].bitcast(BF16), (8, TT)))
        fT = r1
        hT.b = sq.b = fT.b
        actT = A.bf16(NFC, TT, nb=NFC)
        al = Tl(view(actT.ap.rearrange("p a b -> p (a b)")[:, 0:8192].bitcast(F32), (4, D)))
        sqr = [A.bf16(TT), A.bf16(TT)]
        sgr = [A.f32(TT), A.f32(TT)]
        self.alloc_norm()
        pb = self.pbufs
        import os
        nt_ = int(os.environ.get("DBG_NT", NT))
        lvl = int(os.environ.get("DBG_LVL", 9))
        for ti in range(nt_):
            which = 0 if ti < 8 else 1
            self.load_x(xt, ti, first, (al, actT.bs))
            if lvl < 1:
                self.store_x(xt, ti, last, (al, actT.bs))
                continue
            self.prenorm(l, sub, which, xt, hT, sq)
            if lvl < 2:
                self.store_x(xt, ti, last, (al, actT.bs))
                continue
            for j in range(NFC):
                bg, bu = j % 2, 2 + j % 2
                cs = slice(j * 128, (j + 1) * 128)
                for kc in range(8):
                    self.mm(self.bank(bg), wg.ap[:, kc, cs], hT.ap[:, kc, :], kc == 0, kc == 7, [wg.bs[j // 11], hT.b], [pb[bg]])
                for kc in range(8):
                    self.mm(self.bank(bu), wu.ap[:, kc, cs], hT.ap[:, kc, :], kc == 0, kc == 7, [wu.bs[j // 11], hT.b], [pb[bu]])
                s = sgr[j % 2]
                self.act(s.ap, self.bank(bg), AF.Silu, [pb[bg]], [s.b])
                self.tt("dve", actT.ap[:, j, :], s.ap, self.bank(bu), ALU.mult, [s.b, pb[bu]], [actT.bs[j]])
            if lvl < 3:
                self.store_x(xt, ti, last, (al, actT.bs))
                continue
            for m in range(8):
                bf = 4 + m % 2
                for j in range(NFC):
                    self.mm(self.bank(bf), wd.ap[:, j, m * 128:(m + 1) * 128], actT.ap[:, j, :], j == 0, j == NFC - 1,
                            [wd.bs[j // 11], actT.bs[j]], [pb[bf]])
                if lvl >= 4:
                    self.post_chunk(m, bf, fT, sqr, 6)
            if lvl >= 5:
                self.post_update(l, sub, which, xt, fT, 6)
            self.store_x(xt, ti, last, (al, actT.bs))


    def ffn_pass2(self, l, i, first=False, last=False):
        self.phase()
        A, I = self.A, self.I
        TF = 256
        NTF = NTOK // TF
        sub = 0 if i == 0 else 2
        wg = A.bf16(8, DFF, nb=2); wu = A.bf16(8, DFF, nb=2); wd = A.bf16(NFC, D, nb=2)
        for h in range(2):
            cs = slice(h * 1408, (h + 1) * 1408)
            self.dma("pool", wg.ap[:, :, cs], I["w_ffn_gate"][l, i][:, cs].rearrange("(kc p) f -> p kc f", p=128), [], [wg.bs[h]])
            self.dma("pool", wu.ap[:, :, cs], I["w_ffn_up"][l, i][:, cs].rearrange("(kc p) f -> p kc f", p=128), [], [wu.bs[h]])
        wdv = I["w_ffn_down"][l, i].rearrange("(fc p) d -> p fc d", p=128)
        for h in range(2):
            fs = slice(h * 11, (h + 1) * 11)
            self.dma("pool", wd.ap[:, fs, :], wdv[:, fs, :], [], [wd.bs[h]])
        xts = [A.f32(8, TF, nb=8), A.f32(8, TF, nb=8)]
        hTs = [A.bf16(8, TF), A.bf16(8, TF)]
        sq = A.bf16(8, TF)
        fT = A.f32(8, TF)
        actT = A.bf16(NFC, TF, nb=NFC)
        sqr = [A.bf16(TF), A.bf16(TF)]
        sgr = [A.f32(TF), A.f32(TF)]
        stg = A.f32(2, D) if (first or last) else None
        rs_pre = A.f32(TF); rs_post = A.f32(TF)
        tmp_pre = [A.f32(TF), A.f32(TF)]; tmp_post = [A.f32(TF), A.f32(TF)]
        pb = self.pbufs
        bk = lambda b_: self.bank(b_)[:, 0:TF]

        def rstd(r, bankidx):
            self.ts("dve", r.ap, bk(bankidx), 1.0 / D, EPS, ALU.mult, ALU.add, [pb[bankidx]], [r.b])
            self.act(r.ap, r.ap, AF.Sqrt, [r.b], [r.b])
            self.recip(r.ap, r.ap, [r.b], [r.b])

        def load(ti):
            xt = xts[ti % 2]
            tok0 = ti * TF
            if not first:
                self.dma("sp", xt.ap, self.XT.ap[:, :, tok0:tok0 + TF].rearrange("c p t -> p c t"), [], xt.bs)
                return
            self.dma("sp", stg.ap, I["xin"][tok0:tok0 + TF].rearrange("(b p) d -> p b d", p=128), [], [stg.b])
            for c in range(8):
                bi = c % 2
                for blk in range(2):
                    self.tr(self.bank(bi)[:, blk * 128:(blk + 1) * 128], stg.ap[:, blk, c * 128:(c + 1) * 128], self.identf.ap,
                            [stg.b, self.identf.b], [pb[bi]])
                self.cp("act", xt.ap[:, c, :], bk(bi), [pb[bi]], [xt.bs[c]])

        def store(ti):
            xt = xts[ti % 2]
            tok0 = ti * TF
            if not last:
                self.dma("sp", self.XT.ap[:, :, tok0:tok0 + TF].rearrange("c p t -> p c t"), xt.ap, xt.bs, [])
                return
            for blk in range(2):
                for half in range(2):
                    bi = half
                    for cc in range(4):
                        c = half * 4 + cc
                        self.tr(self.bank(bi)[:, cc * 128:(cc + 1) * 128], xt.ap[:, c, blk * 128:(blk + 1) * 128], self.identf.ap,
                                [xt.bs[c], self.identf.b], [pb[bi]])
                    self.cp("act", stg.ap[:, blk, half * 512:(half + 1) * 512], self.bank(bi), [pb[bi]], [stg.b])
            self.dma("sp", self.O["y"][tok0:tok0 + TF].rearrange("(b p) d -> p b d", p=128), stg.ap, [stg.b], [])

        def prenorm(ti):
            xt = xts[ti % 2]; hT = hTs[ti % 2]
            which = 0 if ti * TF < TS else 1
            for c in range(8):
                self.tt("pool", sq.ap[:, c, :], xt.ap[:, c, :], xt.ap[:, c, :], ALU.mult, [xt.bs[c]], [sq.b])
            for c in range(8):
                self.mm(bk(7), self.onesb.ap, sq.ap[:, c, :], c == 0, c == 7, [self.onesb.b, sq.b], [pb[7]])
            rstd(rs_pre, 7)
            for c in range(8):
                t = tmp_pre[c % 2]
                self.tt("dve", t.ap, xt.ap[:, c, :], rs_pre.ap, ALU.mult, [xt.bs[c], rs_pre.b], [t.b])
                self.ts("pool", hT.ap[:, c, :], t.ap, self.scal(l, 0, sub, c, which), self.scal(l, 1, sub, c, which),
                        ALU.mult, ALU.add, [t.b, self.SC.b], [hT.b])

        load(0)
        prenorm(0)

        def upd(tj, m):
            xt_ = xts[tj % 2]
            wh_ = 0 if tj * TF < TS else 1
            t = tmp_post[m % 2]
            self.stt(t.ap, fT.ap[:, m, :], self.scal(l, 2, sub, m, wh_), rs_post.ap, ALU.mult, ALU.mult,
                     [fT.b, self.SC.b, rs_post.b], [t.b])
            self.tt("pool", xt_.ap[:, m, :], xt_.ap[:, m, :], t.ap, ALU.add, [xt_.bs[m], t.b], [xt_.bs[m]])

        for ti in range(NTF):
            xt = xts[ti % 2]; hT = hTs[ti % 2]
            for j in range(NFC):
                bg, bu = j % 2, 2 + j % 2
                cs = slice(j * 128, (j + 1) * 128)
                for kc in range(8):
                    self.mm(bk(bg), wg.ap[:, kc, cs], hT.ap[:, kc, :], kc == 0, kc == 7, [wg.bs[j // 11], hT.b], [pb[bg]])
                for kc in range(8):
                    self.mm(bk(bu), wu.ap[:, kc, cs], hT.ap[:, kc, :], kc == 0, kc == 7, [wu.bs[j // 11], hT.b], [pb[bu]])
                s = sgr[j % 2]
                self.act(s.ap, bk(bg), AF.Silu, [pb[bg]], [s.b])
                self.tt("dve", actT.ap[:, j, :], s.ap, bk(bu), ALU.mult, [s.b, pb[bu]], [actT.bs[j]])
                if ti >= 1 and 2 <= j < 10:
                    upd(ti - 1, j - 2)
            if ti >= 1:
                store(ti - 1)
            if ti + 1 < NTF:
                load(ti + 1)
                prenorm(ti + 1)
            for m in range(8):
                bf = 4 + m % 2
                for j in range(NFC):
                    self.mm(bk(bf), wd.ap[:, j, m * 128:(m + 1) * 128], actT.ap[:, j, :], j == 0, j == NFC - 1,
                            [wd.bs[j // 11], actT.bs[j]], [pb[bf]])
                s = sqr[m % 2]
                self.cp("act", fT.ap[:, m, :], bk(bf), [pb[bf]], [fT.b])
                self.tt("pool", s.ap, fT.ap[:, m, :], fT.ap[:, m, :], ALU.mult, [fT.b], [s.b])
                if m > 0:
                    p_ = sqr[(m - 1) % 2]
                    self.mm(bk(6), self.onesb.ap, p_.ap, m == 1, False, [self.onesb.b, p_.b], [pb[6]])
            p_ = sqr[1]
            self.mm(bk(6), self.onesb.ap, p_.ap, False, True, [self.onesb.b, p_.b], [pb[6]])
            rstd(rs_post, 6)
        for m in range(8):
            upd(NTF - 1, m)
        store(NTF - 1)

    def p2_pass(self, l):
        self.phase()
        A, I, O = self.A, self.I, self.O
        W = A.bf16(8, WINP, nb=4)
        wv = I["w_in_p"][l].rearrange("(kc p) c -> p kc c", p=128)
        bounds = [0, C_V, C_U, C_G + 1536, WINP]
        for h in range(4):
            self.dma("pool", W.ap[:, :, bounds[h]:bounds[h + 1]], wv[:, :, bounds[h]:bounds[h + 1]], [], [W.bs[h]])

        def wb(col):
            for h in range(4):
                if col < bounds[h + 1]:
                    return W.bs[h]

        xt = A.f32(8, TT, nb=8)
        hTs = [A.bf16(8, TT), A.bf16(8, TT)]; sq = A.bf16(8, TT)
        rt = A.f32(2, TT)
        qst = [A.bf16(TT) for _ in range(3)]
        rtmp = [A.f32(TT) for _ in range(4)]
        xlst = A.f32(4, TT); ylst = A.bf16(4, TT)
        vst = A.bf16(4, 2, 65); vf = A.f32(4, 128); kf = A.f32(4, 128)
        utok = A.bf16(32, 8, 16); ufst = A.bf16(32, 64)
        gst = [A.bf16(8, TT), A.bf16(8, TT)]
        self.alloc_norm()
        pb = self.pbufs
        self.ms("pool", vst.ap[:, :, :, 64:65], 1.0, [vst.b])
        nfm = 0
        import os
        sec = os.environ.get("DBG_P2", "ABCDE")
        for ti in range(int(os.environ.get("DBG_NT", NT))):
            which = 0 if ti < 8 else 1
            sample = ti < 8
            tok0 = ti * TT
            hT = hTs[ti % 2]
            if ti == 0:
                self.load_x(xt, 0)
                self.prenorm(l, 1, 0, xt, hTs[0], sq)
            if sample:
                self.dma("act", rt.ap, I["rope"][:, :, tok0:tok0 + TT].rearrange("a p t -> p a t"), [], [rt.b])
            if ti + 1 < NT:
                t1_ = (ti + 1) * TT
                self.dma("act", xt.ap, self.XT.ap[:, :, t1_:t1_ + TT].rearrange("c p t -> p c t"), [], xt.bs)

            def fm(col, bi):
                for kc in range(8):
                    self.mm(self.bank(bi), W.ap[:, kc, col:col + 128], hT.ap[:, kc, :], kc == 0, kc == 7, [wb(col), hT.b], [pb[bi]])

            self.P.mute = "A" not in sec
            for ci in range(5):
                col = C_Q + ci * 128 if ci < 4 else C_K
                cols = C_QS + ci * 128 if ci < 4 else C_KS
                b0 = ci % 2
                fm(col, b0)
                q_ = qst[ci % 3]
                if sample:
                    fm(cols, 2 + b0)
                    t1 = rtmp[(ci % 2) * 2]; t2 = rtmp[(ci % 2) * 2 + 1]
                    self.tt("dve", t1.ap, self.bank(b0), rt.ap[:, 0, :], ALU.mult, [pb[b0], rt.b], [t1.b])
                    self.tt("dve", t2.ap, self.bank(2 + b0), rt.ap[:, 1, :], ALU.mult, [pb[2 + b0], rt.b], [t2.b])
                    self.tt("pool", q_.ap, t1.ap, t2.ap, ALU.add, [t1.b, t2.b], [q_.b])
                else:
                    self.cp("act", q_.ap, self.bank(b0), [pb[b0]], [q_.b])
                dst = self.Qs.ap[ci][:, tok0:tok0 + TT] if ci < 4 else self.Ks.ap[:, tok0:tok0 + TT]
                self.dma("sp", dst, q_.ap, [q_.b], [])
            self.P.mute = False
            if ti + 1 < NT:
                self.prenorm(l, 1, 0 if ti + 1 < 8 else 1, xt, hTs[(ti + 1) % 2], sq)
            self.P.mute = "B" not in sec
            for blk in range(4):
                for kc in range(8):
                    self.mm(self.bank(4)[:, blk * 128:(blk + 1) * 128], hT.ap[:, kc, blk * 128:(blk + 1) * 128],
                            W.ap[:, kc, C_V:C_V + 128], kc == 0, kc == 7, [hT.b, wb(C_V)], [pb[4]])
            self.cp("act", vst.ap[:, :, :, 0:64], view(self.bank(4), (4, 2, 64)), [pb[4]], [vst.b])
            if not sample:
                self.cp("act", vf.ap, view(self.bank(4), (4, 128)), [pb[4]], [vf.b])
            self.dma("sp", self.Vs.ap[tok0:tok0 + TT].rearrange("(b p) c -> p b c", p=128),
                     vst.ap.rearrange("p b h c -> p b (h c)"), [vst.b], [])
            if not sample:
                for pp in range(2):
                    pj = 2 * (ti - 8) + pp
                    self.dma("sp", O["nv"][pj, l].rearrange("(b p) c -> p b c", p=128), vf.ap[:, 2 * pp:2 * pp + 2, :],
                             [vf.b], [])
                for blk in range(4):
                    for kc in range(8):
                        self.mm(self.bank(4)[:, blk * 128:(blk + 1) * 128], hT.ap[:, kc, blk * 128:(blk + 1) * 128],
                                W.ap[:, kc, C_K:C_K + 128], kc == 0, kc == 7, [hT.b, wb(C_K)], [pb[4]])
                self.cp("dve", kf.ap, view(self.bank(4), (4, 128)), [pb[4]], [kf.b])
                for pp in range(2):
                    pj = 2 * (ti - 8) + pp
                    self.dma("sp", O["nk"][pj, l].rearrange("(b p) c -> p b c", p=128), kf.ap[:, 2 * pp:2 * pp + 2, :],
                             [kf.b], [])
            self.P.mute = "C" not in sec
            for c in range(4):
                b0 = c % 2
                fm(C_XL + c * 128, b0)
                self.cp("act", xlst.ap[:, c, :], self.bank(b0), [pb[b0]], [xlst.b])
            self.dma("sp", self.XLs.ap[:, :, tok0:tok0 + TT].rearrange("c p t -> p c t"), xlst.ap, [xlst.b], [])
            for c in range(4):
                b0 = c % 2
                fm(C_YL + c * 128, b0)
                self.cp("dve", ylst.ap[:, c, :], self.bank(b0), [pb[b0]], [ylst.b])
            self.dma("sp", self.YLs.ap[:, :, tok0:tok0 + TT].rearrange("c p t -> p c t"), ylst.ap, [ylst.b], [])
            self.P.mute = "D" not in sec
            hs = hT.ap.rearrange("p c (k s) -> p c s k", s=8)
            for s in range(8):
                bi = 5 + s % 2
                for kc in range(8):
                    self.mm(self.bank(bi)[0:64, :], hs[:, kc, s, :], W.ap[:, kc, C_U:C_U + 512], kc == 0, kc == 7,
                            [hT.b, wb(C_U)], [pb[bi]])
                self.cp("act" if s % 2 else "dve", utok.ap[0:64, :, s, :], view(self.bank(bi)[0:64, :], (32, 16)), [pb[bi]], [utok.b])
            for half in range(2):
                bi = 2 + half
                pbf = self.bank(bi).bitcast(BF16)
                for gg in range(16):
                    g = half * 16 + gg
                    self.tr(pbf[:, gg * 64:(gg + 1) * 64], utok.ap[0:64, g].rearrange("p s c -> p (s c)"),
                            self.identb.ap[0:64, 0:64], [utok.b, self.identb.b], [pb[bi]])
                self.cp("act" if half else "dve", ufst.ap[:, half * 16:(half + 1) * 16, :], view(pbf[:, 0:1024], (16, 64)),
                        [pb[bi]], [ufst.b])
            self.dma("sp", self.UFs.ap[ti], ufst.ap.rearrange("p g k -> p (g k)"), [ufst.b], [])
            self.P.mute = "E" not in sec
            for cc in range(24):
                b0 = cc % 2
                fm(C_G + cc * 128, b0)
                g_ = gst[(cc // 8) % 2]
                self.act(g_.ap[:, cc % 8, :], self.bank(b0), AF.Sigmoid, [pb[b0]], [g_.b])
                if cc % 8 == 7:
                    grp = cc // 8
                    self.dma("sp", self.Gs.ap[grp * 8:(grp + 1) * 8][:, :, tok0:tok0 + TT].rearrange("c p t -> p c t"), g_.ap,
                             [g_.b], [])
            self.P.mute = False

    def p4_pass(self, l):
        self.phase()
        A, I = self.A, self.I
        wol = A.bf16(4, D); woa = A.bf16(4, D); wgl = A.bf16(4, 2048); wo = A.bf16(8, D)
        self.dma("pool", wgl.ap, I["w_glu"][l].rearrange("(kc p) c -> p kc c", p=128), [], [wgl.b])
        self.dma("pool", wol.ap, I["w_o_lru"][l].rearrange("(kc p) c -> p kc c", p=128), [], [wol.b])
        self.dma("pool", woa.ap, I["w_o_attn_p"][l].rearrange("(kc p) c -> p kc c", p=128), [], [woa.b])
        self.dma("pool", wo.ap, I["w_out"][l].rearrange("(kc p) c -> p kc c", p=128), [], [wo.b])
        xt = A.f32(8, TT, nb=8)
        ins = [(A.bf16(4, TT), A.bf16(4, TT), A.bf16(4, TT), A.bf16(24, TT)) for _ in range(2)]
        mg = A.bf16(8, TT)
        fT = A.f32(8, TT)
        sqr = [A.bf16(TT), A.bf16(TT)]
        tmp = [A.f32(TT) for _ in range(6)]
        self.alloc_norm()
        pb = self.pbufs

        def loads(ti):
            tok0 = ti * TT
            lr, at, sy, G = ins[ti % 2]
            for (dst, src) in ((sy, self.SYs), (lr, self.LRs), (at, self.ATs), (G, self.Gs)):
                self.dma("sp", dst.ap, src.ap[:, :, tok0:tok0 + TT].rearrange("c p t -> p c t"), [], [dst.b])

        loads(0)
        for ti in range(NT):
            which = 0 if ti < 8 else 1
            if ti + 1 < NT:
                loads(ti + 1)
            self.load_x(xt, ti)
            lr, at, sy, G = ins[ti % 2]
            for m in range(8):
                ba, bz = m % 2, 2 + m % 2
                for kc in range(4):
                    self.mm(self.bank(ba), wgl.ap[:, kc, m * 128:(m + 1) * 128], sy.ap[:, kc, :], kc == 0, kc == 3, [wgl.b, sy.b], [pb[ba]])
                for kc in range(4):
                    self.mm(self.bank(bz), wgl.ap[:, kc, 1024 + m * 128:1024 + (m + 1) * 128], sy.ap[:, kc, :], kc == 0, kc == 3,
                            [wgl.b, sy.b], [pb[bz]])
                s = tmp[m % 2]; t = tmp[2 + m % 2]
                self.act(s.ap, self.bank(bz), AF.Sigmoid, [pb[bz]], [s.b])
                self.tt("dve", t.ap, self.bank(ba), s.ap, ALU.mult, [pb[ba], s.b], [t.b])
                self.tt("pool", fT.ap[:, m, :], t.ap, G.ap[:, 8 + m, :], ALU.mult, [t.b, G.b], [fT.b])
            for m in range(8):
                ba, bc = 4 + m % 2, 6 + m % 2
                for kc in range(4):
                    self.mm(self.bank(ba), wol.ap[:, kc, m * 128:(m + 1) * 128], lr.ap[:, kc, :], kc == 0, kc == 3, [wol.b, lr.b], [pb[ba]])
                for kc in range(4):
                    self.mm(self.bank(bc), woa.ap[:, kc, m * 128:(m + 1) * 128], at.ap[:, kc, :], kc == 0, kc == 3, [woa.b, at.b], [pb[bc]])
                u1 = tmp[m % 2]; u3 = tmp[2 + m % 2]; u4 = tmp[4 + m % 2]
                self.tt("dve", u1.ap, self.bank(ba), G.ap[:, m, :], ALU.mult, [pb[ba], G.b], [u1.b])
                self.tt("dve", u3.ap, self.bank(bc), G.ap[:, 16 + m, :], ALU.mult, [pb[bc], G.b], [u3.b])
                self.tt("pool", u4.ap, fT.ap[:, m, :], u1.ap, ALU.add, [fT.b, u1.b], [u4.b])
                self.tt("pool", mg.ap[:, m, :], u4.ap, u3.ap, ALU.add, [u4.b, u3.b], [mg.b])
            for m in range(8):
                bo = m % 2
                for kc in range(8):
                    self.mm(self.bank(bo), wo.ap[:, kc, m * 128:(m + 1) * 128], mg.ap[:, kc, :], kc == 0, kc == 7, [wo.b, mg.b], [pb[bo]])
                self.post_chunk(m, bo, fT, sqr, 2)
            self.post_update(l, 1, which, xt, fT, 2)
            self.store_x(xt, ti)

    def p3a_attention(self, l):
        self.phase()
        A, I = self.A, self.I
        kT = A.bf16(2, NTOK); Qt = A.bf16(4, NTOK); Vt = A.bf16(NTOK // 128, 130)
        self.ms("pool", kT.ap[64:128, 0, :], 0.0, [kT.b])
        self.ms("pool", kT.ap[0:64, 1, :], 0.0, [kT.b])
        for h in range(2):
            sl = slice(h * 2560, (h + 1) * 2560)
            self.dma("sp", Qt.ap[:, :, sl], self.Qs.ap[:, :, sl].rearrange("c p t -> p c t"), [], [Qt.b])
        self.dma("act", kT.ap[0:64, 0, :], self.Ks.ap[0:64, :], [], [kT.b])
        self.dma("act", kT.ap[64:128, 1, :], self.Ks.ap[64:128, :], [], [kT.b])
        self.dma("act", Vt.ap, self.Vs.ap.rearrange("(b p) c -> p b c", p=128), [], [Vt.b])
        ckr = A.f32(4, 128); cvr = A.f32(4, 128)
        self.dma("sp", ckr.ap, I["cache_k"][l].rearrange("(b p) c -> p b c", p=128), [], [ckr.b])
        self.dma("sp", cvr.ap, I["cache_v"][l].rearrange("(b p) c -> p b c", p=128), [], [cvr.b])
        ckT = A.bf16(2, 512); cv = A.bf16(4, 2, 65)
        pb = self.pbufs
        for blk in range(4):
            self.tr(self.bank(0)[:, blk * 128:(blk + 1) * 128], ckr.ap[:, blk, :], self.identf.ap, [ckr.b, self.identf.b], [pb[0]])
        self.ms("pool", ckT.ap, 0.0, [ckT.b])
        self.cp("dve", ckT.ap[0:64, 0, :], self.bank(0)[0:64, :], [pb[0]], [ckT.b])
        self.cp("dve", ckT.ap[64:128, 1, :], self.bank(0)[64:128, :], [pb[0]], [ckT.b])
        self.ms("pool", cv.ap[:, :, :, 64:65], 1.0, [cv.b])
        self.cp("dve", cv.ap[:, :, :, 0:64], cvr.ap.rearrange("p b (h c) -> p b h c", h=2), [cvr.b], [cv.b])
        cvf = cv.ap.rearrange("p b h c -> p b (h c)")
        sk = A.f32(8)
        self.dma("sp", sk.ap[64:65, :], I["attn_sink"][l:l + 1, :], [], [sk.b])
        self.act(sk.ap[64:65, :], sk.ap[64:65, :], AF.Exp, [sk.b], [sk.b])
        pT = [A.bf16(TT) for _ in range(4)]
        osb = [A.f32(TT) for _ in range(3)]
        rrow = [A.f32(TT) for _ in range(3)]
        ast = [A.bf16(4, TT), A.bf16(4, TT)]
        segs = [(0, TS, True)] + [(TS + j * TP, TP, False) for j in range(NPB)]
        its = []
        nst = 0
        for (tok0, T, samp) in segs:
            nqb = T // 128
            for qb in range(nqb):
                for kvh in range(2):
                    its.append((tok0, T, samp, nqb, qb, kvh, nst))
                if qb % 4 == 3 or qb == nqb - 1:
                    nst += 1
        npt = [0]

        def front(n):
            tok0, T, samp, nqb, qb, kvh, st_ = its[n]
            q0 = tok0 + qb * 128
            blocks = []
            if samp:
                for nb in (qb - 1, qb, qb + 1):
                    if 0 <= nb < nqb:
                        msk = self.mprev if nb == qb - 1 else (self.mnext if nb == qb + 1 else None)
                        blocks.append((kT.ap[:, kvh, tok0 + nb * 128:tok0 + (nb + 1) * 128], kT.b,
                                       Vt.ap[:, (tok0 // 128) + nb, kvh * 65:(kvh + 1) * 65], Vt.b, msk))
                for cb_ in range(4):
                    blocks.append((ckT.ap[:, kvh, cb_ * 128:(cb_ + 1) * 128], ckT.b, cvf[:, cb_, kvh * 65:(kvh + 1) * 65], cv.b, None))
            else:
                for nb in range(nqb):
                    blocks.append((kT.ap[:, kvh, tok0 + nb * 128:tok0 + (nb + 1) * 128], kT.b,
                                   Vt.ap[:, (tok0 // 128) + nb, kvh * 65:(kvh + 1) * 65], Vt.b, None))
            po = 3 + n % 3
            rhs_q = Qt.ap[:, :, q0:q0 + 128]
            pts = []
            for bi_, (kap, kb, vap, vb, msk) in enumerate(blocks):
                sb_ = npt[0] % 3
                p_ = pT[npt[0] % 4]
                npt[0] += 1
                self.mm(view(self.bank(sb_), (4, 128)), kap, rhs_q, True, True, [kb, Qt.b], [pb[sb_]])
                self.act(p_.ap, self.bank(sb_), AF.Exp, [pb[sb_]], [p_.b], scale=0.125)
                if msk is not None:
                    m_ = msk.ap
                    mb = mkap(m_.tensor, m_.offset, [list(m_.ap[0]), [0, 4], [1, 128]])
                    self.tt("pool", view(p_.ap, (4, 128)), view(p_.ap, (4, 128)), mb, ALU.mult, [p_.b, msk.b], [p_.b])
                pts.append((p_, vap, vb))
                if bi_ >= 1:
                    pp_, vap_, vb_ = pts[bi_ - 1]
                    self.mm(self.bank(po)[0:65, :], vap_, pp_.ap, bi_ == 1, False, [vb_, pp_.b], [pb[po]])
            pp_, vap_, vb_ = pts[-1]
            self.mm(self.bank(po)[0:65, :], vap_, pp_.ap, len(pts) == 1, True, [vb_, pp_.b], [pb[po]])

        def back1(n):
            tok0, T, samp, nqb, qb, kvh, st_ = its[n]
            po = 3 + n % 3
            o_ = osb[n % 3]; r_ = rrow[n % 3]
            self.cp("act", o_.ap[0:65, :], self.bank(po)[0:65, :], [pb[po]], [o_.b])
            s_ = sk.ap[64:65, kvh * 4:(kvh + 1) * 4]
            sbc = mkap(s_.tensor, s_.offset, [list(s_.ap[0]), [1, 4], [0, 128]])
            self.tt("dve", view(r_.ap[64:65, :], (4, 128)), view(o_.ap[64:65, :], (4, 128)), sbc, ALU.add, [o_.b, sk.b], [r_.b])
            self.recip(r_.ap[64:65, :], r_.ap[64:65, :], [r_.b], [r_.b])

        def back2(n):
            tok0, T, samp, nqb, qb, kvh, st_ = its[n]
            q0 = tok0 + qb * 128
            hs = slice(kvh * 64, (kvh + 1) * 64)
            a_t = ast[st_ % 2]
            qcol = (qb % 4) * 128
            o_ = osb[n % 3]; r_ = rrow[n % 3]
            bb = 6 + n % 2
            self.mm(self.bank(bb)[0:64, :], self.onesf.ap[64:65, 0:64], r_.ap[64:65, :], True, True, [self.onesf.b, r_.b], [pb[bb]])
            self.tt("dve", a_t.ap[hs, :, qcol:qcol + 128], view(o_.ap[0:64, :], (4, 128)), view(self.bank(bb)[0:64, :], (4, 128)),
                    ALU.mult, [o_.b, pb[bb]], [a_t.b])
            if kvh == 1 and (qb % 4 == 3 or qb == nqb - 1):
                nn = (qb % 4 + 1) * 128
                t0 = q0 + 128 - nn
                self.dma("sp", self.ATs.ap[:, :, t0:t0 + nn].rearrange("c p t -> p c t"), a_t.ap[:, :, 0:nn], [a_t.b], [])

        N = len(its)
        for n in range(N + 2):
            if n < N:
                front(n)
            if 0 <= n - 1 < N:
                back1(n - 1)
            if 0 <= n - 2 < N:
                back2(n - 2)

    def p3b_lru(self, l):
        self.phase()
        A, I, O = self.A, self.I, self.O
        pb = self.pbufs
        cw = A.f32(4, 4); cb = A.f32(4); onec = A.f32(1)
        src, sb_ = self.vecT(None, I["w_conv"][l].rearrange("j (c p) -> (j c) p", p=128), 16, None)
        self.cp("dve", cw.ap.rearrange("p c j -> p j c"), view(src, (4, 4)), [sb_], [cw.b])
        src, sb_ = self.vecT(None, I["b_conv"][l].rearrange("(c p) -> c p", p=128), 4, None)
        self.cp("dve", cb.ap, src, [sb_], [cb.b])
        self.ms("dve", onec.ap, 1.0, [onec.b])
        ba = A.f32(2, 4); bx = A.f32(2, 4); cl = A.f32(2, 4); st0 = A.f32(2, 4)
        for (t_, nm) in ((ba, "b_lru_a"), (bx, "b_lru_x"), (cl, "lru_lambda"), (st0, "state_lru")):
            src, sb_ = self.vecT(None, I[nm][l].rearrange("d (c p) -> (d c) p", p=128), 8, None)
            self.cp("dve", t_.ap.rearrange("p d c -> p (d c)"), src, [sb_], [t_.b])
        self.act(cl.ap, cl.ap, AF.Exp, [cl.b], [cl.b], scale=-1.0)
        self.act(cl.ap, cl.ap, AF.Ln, [cl.b, onec.b], [cl.b], bias=onec.ap[:, 0:1])
        self.ts("dve", cl.ap, cl.ap, -8.0, None, ALU.mult, None, [cl.b], [cl.b])
        cl2 = A.f32(2, 4)
        self.ts("dve", cl2.ap, cl.ap, 2.0, None, ALU.mult, None, [cl.b], [cl2.b])
        BD = {}
        for d in range(2):
            for nm in ("w_lru_a", "w_lru_x"):
                t_ = A.bf16(4, 128)
                self.ms("pool", t_.ap, 0.0, [t_.b])
                for c in range(4):
                    self.dma("pool", t_.ap[0:64, c, 0:64], I[nm][l, d, 2 * c], [], [t_.b])
                    self.dma("pool", t_.ap[64:128, c, 64:128], I[nm][l, d, 2 * c + 1], [], [t_.b])
                BD[(d, nm)] = t_
        hf = A.f32(4, TS); xc = A.f32(4, TS)
        xlt = A.f32(4, TT + 3); xcb = A.bf16(4, TT)
        R = A.f32(4, TT)
        IIs = [A.f32(4, TT), A.f32(4, TT)]
        AAs = [A.f32(4, TT), A.f32(4, TT)]
        hbt = A.f32(4, TT); carry = A.f32(4)
        ylts = [A.bf16(4, TT), A.bf16(4, TT)]
        fin = A.f32(NPB, 2, 4)
        segs = [(0, TS, True)] + [(TS + j * TP, TP, False) for j in range(NPB)]
        gk = [0]
        xcbufs = [Buf() for _ in range(8)]

        def rev(ap, n):
            return mkap(ap.tensor, ap.offset + n - 1, [list(ap.ap[0]), [-1, n]])

        def gates(d, t0l, n):
            AA = AAs[gk[0] % 2]; II = IIs[gk[0] % 2]
            gk[0] += 1
            SQ = R
            for c in range(4):
                self.cp("pool", xcb.ap[:, c, 0:n], xc.ap[:, c, t0l:t0l + n], [xcbufs[t0l // n]], [xcb.b])
            for c in range(4):
                self.mm(self.bank(c)[:, 0:n], BD[(d, "w_lru_a")].ap[:, c, :], xcb.ap[:, c, 0:n], True, True, [BD[(d, "w_lru_a")].b, xcb.b], [pb[c]])
                self.mm(self.bank(4 + c)[:, 0:n], BD[(d, "w_lru_x")].ap[:, c, :], xcb.ap[:, c, 0:n], True, True, [BD[(d, "w_lru_x")].b, xcb.b], [pb[4 + c]])
            for c in range(4):
                self.act(R.ap[:, c, 0:n], self.bank(c)[:, 0:n], AF.Sigmoid, [pb[c], ba.b], [R.b], bias=ba.ap[:, d, c:c + 1])
            for c in range(4):
                self.act(II.ap[:, c, 0:n], self.bank(4 + c)[:, 0:n], AF.Sigmoid, [pb[4 + c], bx.b], [II.b], bias=bx.ap[:, d, c:c + 1])
            for c in range(4):
                self.tt("pool", II.ap[:, c, 0:n], II.ap[:, c, 0:n], xc.ap[:, c, t0l:t0l + n], ALU.mult, [II.b, xcbufs[t0l // n]], [II.b])
            for c in range(4):
                self.act(AA.ap[:, c, 0:n], R.ap[:, c, 0:n], AF.Exp, [R.b, cl.b], [AA.b], scale=cl.ap[:, d, c:c + 1])
            for c in range(4):
                self.act(SQ.ap[:, c, 0:n], R.ap[:, c, 0:n], AF.Exp, [R.b, cl2.b], [SQ.b], scale=cl2.ap[:, d, c:c + 1])
            for c in range(4):
                self.act(SQ.ap[:, c, 0:n], SQ.ap[:, c, 0:n], AF.Sqrt, [SQ.b, onec.b], [SQ.b], scale=-1.0, bias=onec.ap[:, 0:1])
            for c in range(4):
                self.tt("dve", II.ap[:, c, 0:n], II.ap[:, c, 0:n], SQ.ap[:, c, 0:n], ALU.mult, [II.b, SQ.b], [II.b])
            return AA, II

        for si, (tok0, T, samp) in enumerate(segs):
            n = min(TT, T)
            ntl = T // n

            def load_conv(tl):
                t0l = tl * n
                lo = max(t0l - 2, 0); hi = min(t0l + n + 1, T)
                if lo > t0l - 2:
                    self.ms("dve", xlt.ap[:, :, 0:2], 0.0, [xlt.b])
                if hi < t0l + n + 1:
                    self.ms("dve", xlt.ap[:, :, n + 2:n + 3], 0.0, [xlt.b])
                self.dma("sp", xlt.ap[:, :, lo - (t0l - 2):hi - (t0l - 2)], self.XLs.ap[:, :, tok0 + lo:tok0 + hi].rearrange("c p t -> p c t"), [], [xlt.b])
                for c in range(4):
                    o_ = xc.ap[:, c, t0l:t0l + n]
                    self.act(o_, xlt.ap[:, c, 0:n], AF.Identity, [xlt.b, cw.b, cb.b], [xcbufs[tl]], scale=cw.ap[:, c, 0:1], bias=cb.ap[:, c:c + 1])
                    for j in range(1, 4):
                        self.stt(o_, xlt.ap[:, c, j:j + n], cw.ap[:, c, j:j + 1], o_, ALU.mult, ALU.add, [xlt.b, cw.b, xcbufs[tl]], [xcbufs[tl]])

            load_conv(0)
            for tl in range(ntl):
                t0l = tl * n
                if tl + 1 < ntl:
                    load_conv(tl + 1)
                AA, II = gates(0, t0l, n)
                for c in range(4):
                    if tl == 0:
                        init = st0.ap[:, 0, c:c + 1] if samp else 0.0
                    else:
                        init = hf.ap[:, c, t0l - 1:t0l]
                    self.P.op("dve", lambda e, o=hf.ap[:, c, t0l:t0l + n], a=AA.ap[:, c, 0:n], b=II.ap[:, c, 0:n], i0=init:
                              e.tensor_tensor_scan(out=o, data0=a, data1=b, initial=i0, op0=ALU.mult, op1=ALU.add),
                              [AA.b, II.b, hf.b, st0.b], [hf.b])
            if not samp:
                self.cp("dve", fin.ap[:, si - 1, 0, :], hf.ap[:, :, T - 1], [hf.b], [fin.b])
            def load_yl(tl):
                y_ = ylts[tl % 2]
                t0l_ = tl * n
                self.dma("sp", y_.ap[:, :, 0:n], self.YLs.ap[:, :, tok0 + t0l_:tok0 + t0l_ + n].rearrange("c p t -> p c t"), [], [y_.b])
                for c in range(4):
                    self.act(y_.ap[:, c, 0:n], y_.ap[:, c, 0:n], AF.Gelu_apprx_tanh, [y_.b], [y_.b])

            load_yl(ntl - 1)
            for tl in reversed(range(ntl)):
                t0l = tl * n
                cur = hbt
                ylt = ylts[tl % 2]
                AA, II = gates(1, t0l, n)
                if tl - 1 >= 0:
                    load_yl(tl - 1)
                for c in range(4):
                    if tl == ntl - 1:
                        init = st0.ap[:, 1, c:c + 1] if samp else 0.0
                    else:
                        init = carry.ap[:, c:c + 1]
                    self.P.op("dve", lambda e, o=rev(cur.ap[:, c, 0:n], n), a=rev(AA.ap[:, c, 0:n], n), b=rev(II.ap[:, c, 0:n], n), i0=init:
                              e.tensor_tensor_scan(out=o, data0=a, data1=b, initial=i0, op0=ALU.mult, op1=ALU.add),
                              [AA.b, II.b, carry.b, st0.b], [cur.b])
                self.cp("dve", carry.ap, cur.ap[:, :, 0], [cur.b], [carry.b])
                if not samp and tl == 0:
                    self.cp("dve", fin.ap[:, si - 1, 1, :], cur.ap[:, :, 0], [cur.b], [fin.b])
                for c in range(4):
                    self.tt("dve", cur.ap[:, c, 0:n], cur.ap[:, c, 0:n], hf.ap[:, c, t0l:t0l + n], ALU.add, [cur.b, hf.b], [cur.b])
                    self.tt("dve", ylt.ap[:, c, 0:n], cur.ap[:, c, 0:n], ylt.ap[:, c, 0:n], ALU.mult, [cur.b, ylt.b], [ylt.b])
                self.dma("sp", self.LRs.ap[:, :, tok0 + t0l:tok0 + t0l + n].rearrange("c p t -> p c t"), ylt.ap[:, :, 0:n], [ylt.b], [])
        self.tr(self.bank(0)[0:32, 0:128], fin.ap.rearrange("p s d c -> p (s d c)"), self.identf.ap, [fin.b, self.identf.b], [pb[0]])
        fo = A.f32(128)
        self.cp("dve", fo.ap[0:32, :], self.bank(0)[0:32, 0:128], [pb[0]], [fo.b])
        for s_ in range(NPB):
            self.dma("sp", O["nlru"][s_, l].rearrange("d (c p) -> (d c) p", p=128), fo.ap[s_ * 8:(s_ + 1) * 8, :], [fo.b], [])

    def p3c_s5(self, l):
        self.phase()
        A, I, O = self.A, self.I, self.O
        pb = self.pbufs
        Qw = A.bf16(2, 16, 2, 128); ML = A.bf16(32, 128)
        self.dma("sp", Qw.ap.rearrange("p d g c x -> p (d g c x)"), self.Qd[l].ap, [], [Qw.b])
        self.dma("sp", ML.ap.rearrange("p g x -> p (g x)"), self.MLd[l].ap, [], [ML.b])
        a12 = A.f32(2, 2, 16, 2)
        self.dma("sp", a12.ap.rearrange("p d k g c -> p (d k g c)"), self.A12[l].ap, [], [a12.b])
        h0r = A.f32(2, 128); h0s = A.f32(2, 16, 2)
        for d in range(2):
            self.dma("sp", h0r.ap[0:32, d, :], I["state_ssm"][l, d].rearrange("c (gp g2) n -> (c gp) (g2 n)", g2=2), [], [h0r.b])
            self.tr(self.bank(0)[:, d * 32:(d + 1) * 32], h0r.ap[0:32, d, :], self.identf.ap[0:32, 0:32], [h0r.b, self.identf.b], [pb[0]])
            self.cp("dve", h0s.ap[:, d].rearrange("p g c -> p c g"), view(self.bank(0)[:, d * 32:(d + 1) * 32], (2, 16)), [pb[0]], [h0s.b])
        zero = A.f32(16, 2, 4)
        self.ms("dve", zero.ap, 0.0, [zero.b])
        fin = A.f32(NPB, 2, 2, 16)
        mark0 = A.top
        for (tile0, ntile, nseq, Kseq, KB, tokbase) in ((0, 8, 1, 512, 256, 0), (8, 2, 4, 32, 128, TS)):
            A.top = mark0
            self.P.barrier()
            K = nseq * Kseq
            nblk = K // KB
            Uf = A.bf16(ntile, 32, 64)
            self.dma("sp", Uf.ap.rearrange("p t g k -> p t (g k)"), self.UFs.ap[tile0:tile0 + ntile].rearrange("t p x -> p t x"), [], [Uf.b])
            Hbf = [A.bf16(16, 2, nseq, Kseq + 1), A.bf16(16, 2, nseq, Kseq + 1)]
            mark1 = A.top
            PF = A.bf16(2, 16, 2, 128)
            self.dma("sp", PF.ap.rearrange("p d g c x -> p (d g c x)"), self.PFd[l].ap, [], [PF.b])
            Sd = [A.f32(16, 2, KB), A.f32(16, 2, KB)]
            tA = [A.f32(16, 2, nseq), A.f32(16, 2, nseq)]
            tB = [A.f32(16, 2, nseq), A.f32(16, 2, nseq)]
            hinit = [A.f32(16, 2, nseq), A.f32(16, 2, nseq)]
            tpb = KB // 64
            spb = KB // Kseq if nseq > 1 else 1
            for d in range(2):
                eng = "dve" if d == 0 else "pool"
                S = Sd[d]
                if nseq == 1:
                    self.cp(eng, hinit[d].ap[:, :, :, 0], h0s.ap[:, d], [h0s.b], [hinit[d].b])
                    self.cp("act", Hbf[d].ap[:, :, :, 0, 0 if d == 0 else Kseq], h0s.ap[:, d], [h0s.b], [Hbf[d].b])
                else:
                    self.ms(eng, hinit[d].ap, 0.0, [hinit[d].b])
                    self.ms(eng, Hbf[d].ap[:, :, :, :, 0 if d == 0 else Kseq], 0.0, [Hbf[d].b])
            for bidx in range(nblk):
                for d in range(2):
                    eng = "dve" if d == 0 else "pool"
                    S = Sd[d]
                    blk = bidx if d == 0 else nblk - 1 - bidx
                    for gp_ in range(16):
                        for c in range(2):
                            bi = (0 if d == 0 else 4) + (gp_ * 2 + c) % 4
                            for g2 in range(2):
                                g = 2 * gp_ + g2
                                self.mm(view(self.bank(bi)[g2 * 64:(g2 + 1) * 64, 0:KB], (tpb, 64)),
                                        PF.ap[:, d, gp_, c, g2 * 64:(g2 + 1) * 64],
                                        Uf.ap[:, blk * tpb:(blk + 1) * tpb, g, :], True, True, [PF.b, Uf.b], [pb[bi]])
                            self.cp("act", S.ap[:, gp_, c, :], self.bank(bi)[:, 0:KB], [pb[bi]], [S.b])
                for d in range(2):
                    eng = "dve" if d == 0 else "pool"
                    S = Sd[d]
                    blk = bidx if d == 0 else nblk - 1 - bidx
                    first_blk = bidx == 0
                    s_ = S.ap
                    base = s_.offset
                    pp = list(s_.ap[0])
                    nk = Kseq if nseq > 1 else KB

                    def col(kk, swap=False):
                        if swap:
                            return mkap(s_.tensor, base + KB + kk, [pp, [2 * KB, 16], [-KB, 2], [Kseq, spb]])
                        return mkap(s_.tensor, base + kk, [pp, [2 * KB, 16], [KB, 2], [Kseq, spb]])

                    def hv(t, swap=False):
                        a = t.ap
                        if swap:
                            return mkap(a.tensor, a.offset + nseq, [list(a.ap[0]), [2 * nseq, 16], [-nseq, 2], [1, spb]])
                        return mkap(a.tensor, a.offset, [list(a.ap[0]), [2 * nseq, 16], [nseq, 2], [1, spb]])

                    a1 = a12.ap[:, d, 0]
                    a2 = a12.ap[:, d, 1]
                    A1 = mkap(a1.tensor, a1.offset, [list(a1.ap[0]), [2, 16], [1, 2], [0, spb]])
                    A2 = mkap(a2.tensor, a2.offset, [list(a2.ap[0]), [2, 16], [1, 2], [0, spb]])
                    order = range(nk) if d == 0 else reversed(range(nk))
                    prev = None
                    for kk in order:
                        if prev is None:
                            if first_blk or nseq > 1:
                                pv_, psw = hv(hinit[d]), hv(hinit[d], True)
                                rdx = [hinit[d].b]
                            else:
                                pv_, psw = hv(hinit[d]), hv(hinit[d], True)
                                rdx = [hinit[d].b]
                        else:
                            pv_, psw = col(prev), col(prev, True)
                            rdx = []
                        ta, tb = tA[d], tB[d]
                        self.tt(eng, hv(ta), A1, pv_, ALU.mult, [a12.b, S.b] + rdx, [ta.b])
                        self.tt(eng, hv(tb), A2, psw, ALU.mult, [a12.b, S.b] + rdx, [tb.b])
                        self.tt(eng, hv(ta), hv(ta), hv(tb), ALU.add, [ta.b, tb.b], [ta.b])
                        self.tt(eng, col(kk), col(kk), hv(ta), ALU.add, [S.b, ta.b], [S.b])
                        prev = kk
                    if nseq == 1:
                        self.cp(eng, hv(hinit[d]), col(prev), [S.b], [hinit[d].b])
                    else:
                        for sq_ in range(spb):
                            seq = blk * spb + sq_
                            kcol = sq_ * Kseq + (Kseq - 1 if d == 0 else 0)
                            self.cp(eng, fin.ap[:, seq, d, :, :], S.ap[:, :, :, kcol].rearrange("p g c -> p c g"), [S.b], [fin.b])
                    for c in range(2):
                        if nseq == 1:
                            o0 = blk * KB + (1 if d == 0 else 0)
                            self.cp("act", Hbf[d].ap[:, :, c, 0, o0:o0 + KB], S.ap[:, :, c, :], [S.b], [Hbf[d].b])
                        else:
                            o0 = 1 if d == 0 else 0
                            self.cp("act", Hbf[d].ap[:, :, c, blk * spb:(blk + 1) * spb, o0:o0 + Kseq],
                                    S.ap[:, :, c, :].rearrange("p g (s k) -> p g s k", k=Kseq), [S.b], [Hbf[d].b])
            self.P.barrier()
            A.top = mark1
            Yf = A.bf16(32, K)
            Ytok = A.bf16(8, 512)
            YT = A.bf16(4, 1024)
            for g in range(32):
                gp_, g2 = g // 2, g % 2
                bi = g % 4
                hs = slice(g2 * 64, (g2 + 1) * 64)
                out = view(self.bank(bi)[:, 0:K], (ntile, 64))
                self.mm(out, ML.ap[:, g, :], Uf.ap[:, :, g, :], True, False, [ML.b, Uf.b], [pb[bi]])
                for d in range(2):
                    for c in range(2):
                        o0 = 0 if d == 0 else 1
                        self.mm(view(self.bank(bi)[:, 0:K], (nseq, Kseq)), Qw.ap[hs, d, gp_, c, :], Hbf[d].ap[hs, gp_, c, :, o0:o0 + Kseq],
                                False, d == 1 and c == 1, [Qw.b, Hbf[d].b], [pb[bi]])
                self.act(Yf.ap[:, g, :], self.bank(bi)[:, 0:K], AF.Gelu_apprx_tanh, [pb[bi]], [Yf.b])
            for kb in range(K // 128):
                for q4 in range(4):
                    bi = 4 + q4 % 2
                    pbf = self.bank(bi).bitcast(BF16)
                    for gg in range(8):
                        g = q4 * 8 + gg
                        self.tr(pbf[:, gg * 128:(gg + 1) * 128], Yf.ap[:, g, kb * 128:(kb + 1) * 128], self.identb.ap, [Yf.b, self.identb.b], [pb[bi]])
                    self.cp("dve" if q4 % 2 else "act", Ytok.ap[:, :, q4 * 128:(q4 + 1) * 128].rearrange("p t (g c) -> p g t c", c=16),
                            pbf[:, 0:1024].rearrange("p (g t c) -> p g t c", g=8, t=8), [pb[bi]], [Ytok.b])
                for t in range(8):
                    bi = 6 + t % 2
                    pbf = self.bank(bi).bitcast(BF16)
                    for cc in range(4):
                        self.tr(pbf[:, cc * 128:(cc + 1) * 128], Ytok.ap[:, t, cc * 128:(cc + 1) * 128], self.identb.ap, [Ytok.b, self.identb.b], [pb[bi]])
                    self.cp("dve" if t % 2 else "act", YT.ap.rearrange("p c (k s) -> p c s k", s=8)[:, :, t, :],
                            view(pbf[:, 0:512], (4, 128)), [pb[bi]], [YT.b])
                t0 = tokbase + kb * 1024
                self.dma("sp", self.SYs.ap[:, :, t0:t0 + 1024].rearrange("c p t -> p c t"), YT.ap, [YT.b], [])
        fo = A.f32(2, 128)
        ff = fin.ap.rearrange("p s d c g -> p (s d c g)")
        for h in range(2):
            self.tr(self.bank(h)[:, 0:128], ff[:, h * 128:(h + 1) * 128], self.identf.ap, [fin.b, self.identf.b], [pb[h]])
            self.cp("dve", fo.ap[:, h, :], self.bank(h)[:, 0:128], [pb[h]], [fo.b])
        for seq in range(NPB):
            h, r0 = seq // 2, (seq % 2) * 64
            self.dma("sp", O["nssm"][seq, l].rearrange("d c (gp g2) n -> (d c gp) (g2 n)", g2=2), fo.ap[r0:r0 + 64, h, :], [fo.b], [])


def build(dbg=False, stages=None):
    K = Kern(dbg)
    on = lambda nm: stages is None or nm in stages
    with K.st:
        if on("pro"):
            K.prologue()
        for l in range(L):
            if on("ffa%d" % l):
                K.ffn_pass2(l, 0, first=(l == 0))
            if on("p2%d" % l):
                K.p2_pass(l)
            if on("p3a%d" % l):
                K.p3a_attention(l)
            if on("p3b%d" % l):
                K.p3b_lru(l)
            if on("p3c%d" % l):
                K.p3c_s5(l)
            if on("p4%d" % l):
                K.p4_pass(l)
            if on("ffb%d" % l):
                K.ffn_pass2(l, 1, last=(l == L - 1))
        K.P.barrier()
        K.P.op("sp", lambda e: e.nop(), [], [])
        K.P.emit()
    return K


def _perm_q():
    idx = []
    for c in range(4):
        for h in (c, 4 + c):
            idx.extend(range(h * 64, (h + 1) * 64))
    return np.array(idx)


def _partner():
    p = np.zeros(64, np.int64)
    for d in range(64):
        p[d] = d + 16 if (d % 32) < 16 else d - 16
    return p


def _consts():
    cst = np.zeros((128, 5, 128), np.float32)
    j = np.arange(128)[:, None]
    i = np.arange(128)[None, :]
    cst[:, 0] = (j == i)
    cst[:, 1] = (j >= i)
    cst[:, 2] = (j <= i)
    cst[:, 3] = ((j // 16) <= (i // 16))
    cst[:, 4] = ((j // 16) >= (i // 16))
    t = np.arange(TS)
    row = (t // 64).astype(np.float64)
    colp = (t % 64).astype(np.float64)
    inv = 1.0 / (10000.0 ** (np.arange(16, dtype=np.float64) / 16))
    cos = np.zeros((64, TS)); sin = np.zeros((64, TS))
    for d in range(64):
        pos = row if d < 32 else colp
        ang = (pos.astype(np.float32) * np.float32(inv[d % 16]).astype(np.float32)).astype(np.float32)
        cos[d] = np.cos(ang)
        sgn = -1.0 if (d % 32) < 16 else 1.0
        sin[d] = sgn * np.sin(ang)
    rope = np.zeros((2, 128, TS), np.float32)
    rope[0, :64] = cos; rope[0, 64:] = cos
    rope[1, :64] = sin; rope[1, 64:] = sin
    return cst.reshape(128, 640), rope


_CACHE = {}


def kernel(**inp):
    f = lambda a: np.ascontiguousarray(np.asarray(a, dtype=np.float32))
    if "K" not in _CACHE:
        _CACHE["K"] = build()
    K = _CACHE["K"]
    pq = _perm_q()
    part = _partner()
    w_in = f(inp["w_in"])
    q_cols = pq
    qs_cols = np.array([(c // 64) * 64 + part[c % 64] for c in pq])
    k_cols = 512 + np.arange(128)
    ks_cols = 512 + np.array([(c // 64) * 64 + part[c % 64] for c in range(128)])
    rest = np.arange(640, 5376)
    cols = np.concatenate([q_cols, k_cols, qs_cols, ks_cols, rest])
    w_in_p = np.ascontiguousarray(w_in[:, :, cols])
    w_o_attn_p = np.ascontiguousarray(f(inp["w_o_attn"])[:, pq, :])
    cst, rope = _consts()
    shared = {k: f(inp[k]) for k in ("w_mod", "b_mod", "g_pre", "g_post", "w_ffn_gate", "w_ffn_up", "w_ffn_down", "w_conv",
                                     "b_conv", "w_lru_a", "b_lru_a", "w_lru_x", "b_lru_x", "lru_lambda", "s5_lambda_re",
                                     "s5_lambda_im", "s5_log_step", "s5_b_re", "s5_b_im", "s5_c_re", "s5_c_im", "s5_d",
                                     "w_glu", "attn_sink", "w_o_lru", "w_out")}
    shared["w_in_p"] = w_in_p
    shared["w_o_attn_p"] = w_o_attn_p
    shared["cst"] = cst
    shared["rope"] = rope
    xs = f(inp["x_sample"]); xp = f(inp["x_prompt"]); c = f(inp["c"]); cctx = f(inp["c_ctx"])
    ck = f(inp["cache_k"]); cv = f(inp["cache_v"]); sl = f(inp["state_lru"]); ss = f(inp["state_ssm"])
    in_maps = []
    for b in range(8):
        m = dict(shared)
        m["xin"] = np.ascontiguousarray(np.concatenate([xs[b], xp[4 * b:4 * b + 4].reshape(NPB * TP, D)], axis=0))
        m["cc"] = np.ascontiguousarray(np.stack([c[b], cctx], axis=0))
        m["cache_k"] = np.ascontiguousarray(ck[b].reshape(L, 512, 128))
        m["cache_v"] = np.ascontiguousarray(cv[b].reshape(L, 512, 128))
        m["state_lru"] = np.ascontiguousarray(sl[b])
        m["state_ssm"] = np.ascontiguousarray(ss[b])
        in_maps.append(m)
    res = run_bass_kernel_spmd(K.nc, in_maps, core_ids=list(range(8)))
    _CACHE["res"] = res
    R = res.results
    y_s = np.stack([R[b]["y"][:TS] for b in range(8)], axis=0)
    y_p = np.concatenate([R[b]["y"][TS:].reshape(NPB, TP, D) for b in range(8)], axis=0)
    nk = np.concatenate([R[b]["nk"].reshape(NPB, L, TP, 2, 64) for b in range(8)], axis=0)
    nv = np.concatenate([R[b]["nv"].reshape(NPB, L, TP, 2, 64) for b in range(8)], axis=0)
    nl = np.concatenate([R[b]["nlru"] for b in range(8)], axis=0)
    ns = np.concatenate([R[b]["nssm"] for b in range(8)], axis=0)
    return (y_p.astype(np.float32), y_s.astype(np.float32), nk.astype(np.float32), nv.astype(np.float32),
            nl.astype(np.float32), ns.astype(np.float32))
```

```python
import math
import contextlib
import numpy as np
import concourse.bass as bass
import concourse.mybir as mybir
from concourse.bass_utils import run_bass_kernel_spmd

F32 = mybir.dt.float32
BF16 = mybir.dt.bfloat16
AF = mybir.ActivationFunctionType
ALU = mybir.AluOpType
AX = mybir.AxisListType

NDMA_SEM = 8
L = 2
D = 1024
TS = 4096
TP = 256
NPB = 4
NTOK = TS + NPB * TP
TT = 512
NT = NTOK // TT
DFF = 2816
NFC = DFF // 128
WINP = 6016
C_Q, C_K, C_QS, C_KS, C_V, C_XL, C_YL, C_U, C_G = 0, 512, 640, 1152, 1280, 1408, 1920, 2432, 2944
ARENA_F = 52352
EPS = 1e-6


class Buf:
    __slots__ = ("w", "r")

    def __init__(self):
        self.w = {}
        self.r = {}


class Op:
    __slots__ = ("eng", "fn", "waits", "idx", "needed", "semval", "is_dma", "slot")

    def __init__(self, eng, fn):
        self.eng = eng
        self.fn = fn
        self.waits = {}
        self.needed = False
        self.semval = 0
        self.is_dma = False
        self.slot = 0


class Prog:
    ENGS = ("pe", "act", "dve", "pool", "sp")

    def __init__(self, nc):
        self.nc = nc
        self.ops = {e: [] for e in self.ENGS}
        self.ndma = {e: 0 for e in self.ENGS}
        self.pending = {e: {} for e in self.ENGS}
        self.mute = False

    def _add(self, eng, fn, reads, writes, is_dma):
        if self.mute:
            return None
        op = Op(eng, fn)
        op.is_dma = is_dma
        lst = self.ops[eng]
        op.idx = len(lst)
        lst.append(op)
        deps = op.waits
        if self.pending[eng]:
            deps.update(self.pending[eng])
            self.pending[eng] = {}
        if is_dma:
            didx = self.ndma[eng]
            self.ndma[eng] += 1
            op.slot = didx
            pkey = ("d", eng, didx % NDMA_SEM)
            pidx = didx
            if didx >= NDMA_SEM and deps.get(pkey, -1) < didx - NDMA_SEM:
                deps[pkey] = didx - NDMA_SEM
        else:
            pkey = ("c", eng)
            pidx = op.idx
        for b in reads:
            for k, v in b.w.items():
                if deps.get(k, -1) < v:
                    deps[k] = v
        for b in writes:
            for k, v in b.w.items():
                if deps.get(k, -1) < v:
                    deps[k] = v
            for k, v in b.r.items():
                if deps.get(k, -1) < v:
                    deps[k] = v
        for b in reads:
            if b.r.get(pkey, -1) < pidx:
                b.r[pkey] = pidx
        for b in writes:
            b.w = {pkey: pidx}
            b.r = {}
        if eng == "pe" and not is_dma:
            deps.pop(("c", "pe"), None)
        return op

    def op(self, eng, fn, reads=(), writes=()):
        return self._add(eng, fn, reads, writes, False)

    def dma(self, eng, fn, reads=(), writes=()):
        return self._add(eng, fn, reads, writes, True)

    def dma_group(self, eng, fns, reads=(), writes=()):
        if self.mute:
            return
        writes = list(writes)
        snap = [(dict(b.w), dict(b.r)) for b in writes]
        acc = [dict() for _ in writes]
        for fn in fns:
            for b, (w, r) in zip(writes, snap):
                b.w = dict(w)
                b.r = dict(r)
            self._add(eng, fn, reads, writes, True)
            for b, nw in zip(writes, acc):
                nw.update(b.w)
        for b, nw in zip(writes, acc):
            b.w = nw
            b.r = {}

    def barrier(self):
        deps = {}
        for e in self.ENGS:
            last = None
            for op in reversed(self.ops[e]):
                if not op.is_dma:
                    last = op.idx
                    break
            if last is not None:
                deps[("c", e)] = last
            n = self.ndma[e]
            for s in range(NDMA_SEM):
                if n > s:
                    li = ((n - 1 - s) // NDMA_SEM) * NDMA_SEM + s
                    deps[("d", e, s)] = li
        for e in self.ENGS:
            p = self.pending[e]
            for k, v in deps.items():
                if p.get(k, -1) < v:
                    p[k] = v

    def emit(self):
        nc = self.nc
        for e in self.ENGS:
            seen = {}
            for op in self.ops[e]:
                new = {}
                for k, v in op.waits.items():
                    if seen.get(k, -1) >= v:
                        continue
                    seen[k] = v
                    new[k] = v
                op.waits = new
        for e in self.ENGS:
            for op in self.ops[e]:
                for k, v in op.waits.items():
                    if k[0] == "c":
                        self.ops[k[1]][v].needed = True
        for e in self.ENGS:
            c = 0
            for op in self.ops[e]:
                if op.is_dma:
                    continue
                if op.needed:
                    c += 1
                op.semval = c
        handles = {"pe": "tensor", "act": "scalar", "dve": "vector", "pool": "gpsimd", "sp": "sync"}
        with contextlib.ExitStack() as st:
            csem = {e: st.enter_context(nc.semaphore("c_" + e)) for e in self.ENGS}
            dsem = {e: [st.enter_context(nc.semaphore("d_%s_%d" % (e, i))) for i in range(NDMA_SEM)]
                    for e in self.ENGS if self.ndma[e]}
            block = st.enter_context(nc.Block())
            prog = self

            def run(e, eng):
                for op in prog.ops[e]:
                    for k, v in op.waits.items():
                        if k[0] == "c":
                            eng.wait_ge(csem[k[1]], prog.ops[k[1]][v].semval)
                        else:
                            eng.wait_ge(dsem[k[1]][k[2]], 16 * (v // NDMA_SEM + 1))
                    ins = op.fn(eng)
                    if op.is_dma:
                        ins.then_inc(dsem[e][op.slot % NDMA_SEM], 16)
                    elif op.needed:
                        ins.then_inc(csem[e], 1)

            for e in self.ENGS:
                if not self.ops[e]:
                    continue
                getattr(block, handles[e])(lambda eng, e=e: run(e, eng))


def mkap(t, offset, pairs):
    return bass.AP(t, offset, [list(p) for p in pairs])


def view(ap2, shape):
    if len(shape) == 1:
        return ap2
    names = " ".join("a%d" % i for i in range(len(shape)))
    kw = {"a%d" % i: s for i, s in enumerate(shape)}
    return ap2.rearrange("p (%s) -> p %s" % (names, names), **kw)


class Tl:
    __slots__ = ("ap", "b", "bs")

    def __init__(self, ap, nb=0):
        self.ap = ap
        self.b = Buf()
        self.bs = [Buf() for _ in range(nb)]


class Arena:
    def __init__(self, t, size):
        self.t = t
        self.size = size
        self.top = 0

    def reset(self):
        self.top = 0

    def f32(self, *shape, nb=0):
        n = int(np.prod(shape))
        assert self.top + n <= self.size, ("arena overflow", self.top, n)
        ap = self.t[:, self.top:self.top + n]
        self.top += n
        return Tl(view(ap, shape), nb)

    def bf16(self, *shape, nb=0):
        n = int(np.prod(shape))
        nf = (n + 1) // 2
        assert self.top + nf <= self.size, ("arena overflow", self.top, nf)
        ap = self.t[:, self.top:self.top + nf].bitcast(BF16)[:, 0:n]
        self.top += nf
        return Tl(view(ap, shape), nb)


def bcast_free(ap, n):
    return mkap(ap.tensor, ap.offset, [list(ap.ap[0]), [0, n]])


class Kern:
    def __init__(self, dbg=False):
        self.dbg = dbg
        nc = self.nc = bass.Bass("TRN2", target_bir_lowering=False)
        self.P = Prog(nc)
        self.st = contextlib.ExitStack()
        I = self.I = {}
        O = self.O = {}

        def inp(name, shape, dt=F32):
            I[name] = nc.dram_tensor(name, list(shape), dt, kind="ExternalInput").ap()

        def outp(name, shape, dt=F32):
            O[name] = nc.dram_tensor(name, list(shape), dt, kind="ExternalOutput").ap()

        inp("xin", [NTOK, D]); inp("cc", [2, D]); inp("cache_k", [L, 512, 128]); inp("cache_v", [L, 512, 128])
        inp("state_lru", [L, 2, 512]); inp("state_ssm", [L, 2, 2, 32, 64])
        inp("w_mod", [L, D, 9 * D]); inp("b_mod", [L, 9 * D]); inp("g_pre", [L, 3, D]); inp("g_post", [L, 3, D])
        inp("w_ffn_gate", [L, 2, D, DFF]); inp("w_ffn_up", [L, 2, D, DFF]); inp("w_ffn_down", [L, 2, DFF, D])
        inp("w_in_p", [L, D, WINP]); inp("w_conv", [L, 4, 512]); inp("b_conv", [L, 512])
        inp("w_lru_a", [L, 2, 8, 64, 64]); inp("b_lru_a", [L, 2, 512]); inp("w_lru_x", [L, 2, 8, 64, 64])
        inp("b_lru_x", [L, 2, 512]); inp("lru_lambda", [L, 2, 512])
        inp("s5_lambda_re", [L, 2, 32, 64]); inp("s5_lambda_im", [L, 2, 32, 64]); inp("s5_log_step", [L, 2, 32])
        inp("s5_b_re", [L, 2, 32, 64, 16]); inp("s5_b_im", [L, 2, 32, 64, 16])
        inp("s5_c_re", [L, 2, 32, 16, 64]); inp("s5_c_im", [L, 2, 32, 16, 64]); inp("s5_d", [L, 512])
        inp("w_glu", [L, 512, 2048]); inp("attn_sink", [L, 8]); inp("w_o_lru", [L, 512, D])
        inp("w_o_attn_p", [L, 512, D]); inp("w_out", [L, D, D])
        inp("cst", [128, 5 * 128]); inp("rope", [2, 128, TS])
        outp("y", [NTOK, D]); outp("nk", [NPB, L, TP, 128]); outp("nv", [NPB, L, TP, 128])
        outp("nlru", [NPB, L, 2, 512]); outp("nssm", [NPB, L, 2, 2, 32, 64])
        self.obufs = {k: Buf() for k in O}
        if dbg:
            outp("d_sc", [128, L * 144])

        def scr(name, shape, dt):
            kind = "ExternalOutput" if dbg else "Internal"
            t = nc.dram_tensor(name, list(shape), dt, kind=kind).ap()
            return Tl(t)

        self.XT = scr("s_xt", [8, 128, NTOK], F32)
        self.Qs = scr("s_q", [4, 128, NTOK], BF16)
        self.Ks = scr("s_k", [128, NTOK], BF16)
        self.Vs = scr("s_v", [NTOK, 130], BF16)
        self.XLs = scr("s_xl", [4, 128, NTOK], F32)
        self.YLs = scr("s_yl", [4, 128, NTOK], BF16)
        self.UFs = scr("s_uf", [NT, 128, 32 * 64], BF16)
        self.Gs = scr("s_g", [24, 128, NTOK], BF16)
        self.ATs = scr("s_att", [4, 128, NTOK], BF16)
        self.LRs = scr("s_lru", [4, 128, NTOK], BF16)
        self.SYs = scr("s_s5y", [4, 128, NTOK], BF16)
        self.PFd = [scr("s_pf%d" % l, [128, 2 * 16 * 2 * 128], BF16) for l in range(L)]
        self.Qd = [scr("s_qd%d" % l, [128, 2 * 16 * 2 * 128], BF16) for l in range(L)]
        self.MLd = [scr("s_ml%d" % l, [128, 32 * 128], BF16) for l in range(L)]
        self.A12 = [scr("s_a12%d" % l, [128, 128], F32) for l in range(L)]

        self.sb_t = self.st.enter_context(nc.sbuf_tensor("arena", [128, ARENA_F], F32))
        self.pc_t = self.st.enter_context(nc.sbuf_tensor("persist", [128, 832], F32))
        self.ps_t = self.st.enter_context(nc.psum_tensor("psum", [128, 4096], F32))
        self.A = Arena(self.sb_t, ARENA_F)
        self.PA = Arena(self.pc_t, 832)
        self.pbufs = [Buf() for _ in range(8)]

    def bank(self, i):
        return self.ps_t[:, i * 512:(i + 1) * 512]

    def mm(self, out, lhsT, rhs, start, stop, reads, writes):
        self.P.op("pe", lambda e: e.matmul(out, lhsT=lhsT, rhs=rhs, start=start, stop=stop), reads, writes)

    def tr(self, out, in_, ident, reads, writes):
        self.P.op("pe", lambda e: e.transpose(out, in_, ident), reads, writes)

    def act(self, out, in_, func, reads, writes, scale=1.0, bias=0.0):
        self.P.op("act", lambda e: e.activation(out=out, in_=in_, func=func, scale=scale, bias=bias), reads, writes)

    def tt(self, eng, out, in0, in1, op, reads, writes):
        self.P.op(eng, lambda e: e.tensor_tensor(out=out, in0=in0, in1=in1, op=op), reads, writes)

    def ts(self, eng, out, in0, s1, s2, op0, op1, reads, writes):
        if s2 is None:
            self.P.op(eng, lambda e: e.tensor_scalar(out=out, in0=in0, scalar1=s1, scalar2=None, op0=op0), reads, writes)
        else:
            self.P.op(eng, lambda e: e.tensor_scalar(out=out, in0=in0, scalar1=s1, scalar2=s2, op0=op0, op1=op1), reads, writes)

    def stt(self, out, in0, scalar, in1, op0, op1, reads, writes):
        self.P.op("dve", lambda e: e.scalar_tensor_tensor(out=out, in0=in0, scalar=scalar, in1=in1, op0=op0, op1=op1), reads, writes)

    def cp(self, eng, out, in_, reads, writes):
        if eng == "act":
            self.P.op("act", lambda e: e.copy(out=out, in_=in_), reads, writes)
        else:
            self.P.op(eng, lambda e: e.tensor_copy(out=out, in_=in_), reads, writes)

    def ms(self, eng, out, val, writes):
        self.P.op(eng, lambda e: e.memset(out, val), (), writes)

    def dma(self, q, out, in_, reads, writes, slow=False):
        assert not slow
        shp = tuple(out.shape)
        if len(shp) >= 3 and shp[0] * shp[1] > 256 and shp[1] > 1 and tuple(in_.shape)[:2] == shp[:2]:
            step = max(1, 256 // shp[0])
            fns = []
            for a in range(0, shp[1], step):
                e_ = min(a + step, shp[1])
                o_ = out[:, a:e_]
                i_ = in_[:, a:e_]
                fns.append(lambda e, o_=o_, i_=i_: e.dma_start(out=o_, in_=i_))
            self.P.dma_group(q, fns, reads, writes)
            return
        self.P.dma(q, lambda e: e.dma_start(out=out, in_=in_), reads, writes)

    def vecT(self, dst, src_rows, n, writes):
        stg = self.A.f32(128)
        self.P.dma("sp", lambda e: e.dma_start(out=stg.ap[0:n, :], in_=src_rows), [], [stg.b])
        self.tr(self.bank(7)[:, 0:n], stg.ap[0:n, :], self.identf.ap[0:n, 0:n], [stg.b, self.identf.b], [self.pbufs[7]])
        return self.bank(7)[:, 0:n], self.pbufs[7]

    def recip(self, out, in_, reads, writes):
        self.P.op("dve", lambda e: e.reciprocal(out=out, in_=in_), reads, writes)

    def phase(self):
        self.P.barrier()
        self.A.reset()
        self.pbufs = [Buf() for _ in range(8)]

    def prologue(self):
        A, PA, I = self.A, self.PA, self.I
        cst = A.f32(5, 128)
        self.dma("sp", cst.ap, I["cst"].rearrange("p (a b) -> p a b", a=5), [], [cst.b])
        self.identf = PA.f32(128)
        self.identb = PA.bf16(128)
        self.onesb = PA.bf16(128)
        self.onesf = PA.f32(128)
        self.mprev = PA.bf16(128)
        self.mnext = PA.bf16(128)
        self.cp("dve", self.identf.ap, cst.ap[:, 0, :], [cst.b], [self.identf.b])
        self.cp("dve", self.identb.ap, cst.ap[:, 0, :], [cst.b], [self.identb.b])
        self.cp("dve", self.mprev.ap, cst.ap[:, 1, :], [cst.b], [self.mprev.b])
        self.cp("dve", self.mnext.ap, cst.ap[:, 2, :], [cst.b], [self.mnext.b])
        self.ms("pool", self.onesb.ap, 1.0, [self.onesb.b])
        self.ms("pool", self.onesf.ap, 1.0, [self.onesf.b])
        self.SC = PA.f32(L, 3, 3, 8, 2)
        ccT = A.f32(8, 2)
        src, sb_ = self.vecT(None, I["cc"].rearrange("w (kc p) -> (w kc) p", p=128), 16, None)
        self.cp("dve", ccT.ap.rearrange("p kc w -> p w kc"), view(src, (2, 8)), [sb_], [ccT.b])
        sg = A.f32(8, 2)
        self.act(sg.ap, ccT.ap, AF.Sigmoid, [ccT.b], [sg.b])
        self.tt("dve", ccT.ap, ccT.ap, sg.ap, ALU.mult, [ccT.b, sg.b], [ccT.b])
        wm = [A.f32(8, 1152), A.f32(8, 1152)]
        modt = A.f32(L, 72, 2)
        bm = A.f32(L, 72)
        gp = A.f32(L, 3, 8)
        gq = A.f32(L, 3, 8)
        for l in range(L):
            src, sb_ = self.vecT(None, I["b_mod"][l].rearrange("(c p) -> c p", p=128), 72, None)
            self.cp("dve", bm.ap[:, l, :], src, [sb_], [bm.b])
            src, sb_ = self.vecT(None, I["g_pre"][l].rearrange("i (c p) -> (i c) p", p=128), 24, None)
            self.cp("dve", gp.ap[:, l].rearrange("p i c -> p (i c)"), src, [sb_], [gp.b])
            src, sb_ = self.vecT(None, I["g_post"][l].rearrange("i (c p) -> (i c) p", p=128), 24, None)
            self.cp("dve", gq.ap[:, l].rearrange("p i c -> p (i c)"), src, [sb_], [gq.b])
        pm = self.bank(0)
        pmb = self.pbufs[0]
        k = 0
        for l in range(L):
            for piece in range(8):
                w_ = wm[k % 2]
                q = "sp" if k % 2 == 0 else "act"
                k += 1
                src = I["w_mod"][l][:, piece * 1152:(piece + 1) * 1152].rearrange("(kc p) c -> p kc c", p=128)
                self.dma(q, w_.ap, src, [], [w_.b])
                for cch in range(9):
                    col = (l * 72 + piece * 9 + cch) * 2
                    for kc in range(8):
                        self.mm(pm[:, col:col + 2], w_.ap[:, kc, cch * 128:(cch + 1) * 128], ccT.ap[:, kc, :],
                                kc == 0, kc == 7, [w_.b, ccT.b], [pmb])
        self.cp("dve", modt.ap, view(pm[:, 0:L * 144], (L, 72, 2)), [pmb], [modt.b])
        for w in range(2):
            self.tt("dve", modt.ap[:, :, :, w], modt.ap[:, :, :, w], bm.ap, ALU.add, [modt.b, bm.b], [modt.b])
        for l in range(L):
            m5 = modt.ap[:, l].rearrange("p (i k c) w -> p i k c w", i=3, k=3)
            for w in range(2):
                self.stt(self.SC.ap[:, l, 0, :, :, w], m5[:, :, 1, :, w], 1.0, gp.ap[:, l], ALU.add, ALU.mult,
                         [modt.b, gp.b], [self.SC.b])
                self.cp("dve", self.SC.ap[:, l, 1, :, :, w], m5[:, :, 0, :, w], [modt.b], [self.SC.b])
                self.tt("dve", self.SC.ap[:, l, 2, :, :, w], m5[:, :, 2, :, w], gq.ap[:, l], ALU.mult,
                        [modt.b, gq.b], [self.SC.b])
            for i in (0, 2):
                self.ts("dve", self.SC.ap[:, l, 2, i], self.SC.ap[:, l, 2, i], 0.5, None, ALU.mult, None,
                        [self.SC.b], [self.SC.b])
        if self.dbg:
            self.dma("sp", self.O["d_sc"], self.SC.ap.rearrange("p l k i c w -> p (l k i c w)"), [self.SC.b], [])
        for l in range(L):
            self.s5_prep(l, cst)

    def scal(self, l, kind, i, c, w):
        return self.SC.ap[:, l, kind, i, c, w:w + 1]

    def s5_prep(self, l, cst_unused=None):
        self.phase()
        A, I = self.A, self.I
        cst = A.f32(5, 128)
        self.dma("sp", cst.ap, I["cst"].rearrange("p (a b) -> p a b", a=5), [], [cst.b])
        idf = self.identf
        pb = self.pbufs
        lre_r = A.f32(128); lim_r = A.f32(128); lst_r = A.f32(2); lst_x = A.f32(128)
        self.dma("sp", lre_r.ap[0:32, :], I["s5_lambda_re"][l].rearrange("d (gp g2) n -> (d gp) (g2 n)", g2=2), [], [lre_r.b])
        self.dma("sp", lim_r.ap[0:32, :], I["s5_lambda_im"][l].rearrange("d (gp g2) n -> (d gp) (g2 n)", g2=2), [], [lim_r.b])
        self.dma("sp", lst_r.ap[0:32, :], I["s5_log_step"][l].rearrange("d (gp g2) -> (d gp) g2", g2=2), [], [lst_r.b])
        a_ = lst_r.ap[0:32, :]
        self.cp("dve", view(lst_x.ap[0:32, :], (2, 64)), mkap(a_.tensor, a_.offset, [list(a_.ap[0]), [1, 2], [0, 64]]),
                [lst_r.b], [lst_x.b])
        RR = A.f32(8192)
        braw = [Tl(view(RR.ap[:, c * 2048:(c + 1) * 2048], (128, 16))) for c in range(2)]
        craw = [Tl(view(RR.ap[:, (2 + c) * 2048:(3 + c) * 2048], (16, 2, 64))) for c in range(2)]
        for c, nm in enumerate(("s5_b_re", "s5_b_im")):
            self.dma("act", braw[c].ap[0:32], I[nm][l].rearrange("d (gp g2) n ci -> (d gp) (g2 n) ci", g2=2), [], [braw[c].b])
        for c, nm in enumerate(("s5_c_re", "s5_c_im")):
            srcv = I[nm][l].rearrange("d (gp g2) co n -> (d gp) g2 co n", g2=2)
            for g2 in range(2):
                self.dma("act", craw[c].ap[0:32, :, g2, :], srcv[:, g2], [], [craw[c].b])
        draw = A.f32(16); dx = A.f32(8, 16)
        self.dma("sp", draw.ap[0:32, :], I["s5_d"][l].rearrange("(g ci) -> g ci", ci=16), [], [draw.b])
        a_ = draw.ap[0:32, :]
        self.cp("dve", dx.ap[0:32], mkap(a_.tensor, a_.offset, [list(a_.ap[0]), [0, 8], [1, 16]]), [draw.b], [dx.b])
        sc = A.f32(40, 32)
        names = {}

        def S(nm):
            if nm not in names:
                names[nm] = len(names)
                assert len(names) <= 40
            return sc.ap[:, names[nm], :]

        scb = sc.b
        pt = self.bank(0)
        self.tr(pt[:, 0:32], lre_r.ap[0:32, :], idf.ap[0:32, 0:32], [lre_r.b, idf.b], [pb[0]])
        self.tr(pt[:, 32:64], lim_r.ap[0:32, :], idf.ap[0:32, 0:32], [lim_r.b, idf.b], [pb[0]])
        self.tr(pt[:, 64:96], lst_x.ap[0:32, :], idf.ap[0:32, 0:32], [lst_x.b, idf.b], [pb[0]])
        self.tr(pt[:, 96:128], dx.ap[0:32].rearrange("p a b -> p (a b)"), idf.ap[0:32, 0:32], [dx.b, idf.b], [pb[0]])
        self.cp("dve", S("lr"), pt[:, 0:32], [pb[0]], [scb])
        self.cp("dve", S("li"), pt[:, 32:64], [pb[0]], [scb])
        self.cp("dve", S("ls"), pt[:, 64:96], [pb[0]], [scb])
        dcol = A.f32(32)
        self.cp("dve", dcol.ap, pt[:, 96:128], [pb[0]], [dcol.b])
        BC = []
        for idx, raw in enumerate(braw + craw):
            bk = self.bank(1 + idx % 2)
            bb = pb[1 + idx % 2]
            for j in range(16):
                if idx < 2:
                    src = raw.ap[0:32, :, j]
                else:
                    src = raw.ap[0:32, j].rearrange("p a b -> p (a b)")
                self.tr(bk[:, j * 32:(j + 1) * 32], src, idf.ap[0:32, 0:32], [raw.b, idf.b], [bb])
            t = A.f32(16, 32)
            self.cp("act", t.ap, view(bk, (16, 32)), [bb], [t.b])
            BC.append(t)
        Bre, Bim, Cre, Cim = BC

        def dv(out, a, b, op):
            self.tt("dve", out, a, b, op, [scb], [scb])

        self.ts("dve", S("lr"), S("lr"), -1e-4, None, ALU.min, None, [scb], [scb])
        self.act(S("step"), S("ls"), AF.Exp, [scb], [scb])
        dv(S("xre"), S("lr"), S("step"), ALU.mult)
        dv(S("ang"), S("li"), S("step"), ALU.mult)
        self.act(S("mag"), S("xre"), AF.Exp, [scb], [scb])
        hp = A.f32(1)
        self.ms("dve", hp.ap, math.pi / 2, [hp.b])
        self.act(S("s"), S("ang"), AF.Sin, [scb], [scb], scale=1.0 / 16)
        self.act(S("c"), S("ang"), AF.Sin, [scb, hp.b], [scb], scale=-1.0 / 16, bias=hp.ap[:, 0:1])
        for _ in range(4):
            dv(S("t1"), S("c"), S("c"), ALU.mult)
            dv(S("t2"), S("s"), S("s"), ALU.mult)
            dv(S("t3"), S("c"), S("s"), ALU.mult)
            dv(S("c"), S("t1"), S("t2"), ALU.subtract)
            self.ts("dve", S("s"), S("t3"), 2.0, None, ALU.mult, None, [scb], [scb])
        dv(S("are"), S("mag"), S("c"), ALU.mult)
        dv(S("aim"), S("mag"), S("s"), ALU.mult)
        dv(S("t1"), S("lr"), S("lr"), ALU.mult)
        dv(S("t2"), S("li"), S("li"), ALU.mult)
        dv(S("den"), S("t1"), S("t2"), ALU.add)
        self.recip(S("rden"), S("den"), [scb], [scb])
        self.ts("dve", S("nre"), S("are"), -1.0, None, ALU.add, None, [scb], [scb])
        dv(S("t1"), S("nre"), S("lr"), ALU.mult)
        dv(S("t2"), S("aim"), S("li"), ALU.mult)
        dv(S("t1"), S("t1"), S("t2"), ALU.add)
        dv(S("cre"), S("t1"), S("rden"), ALU.mult)
        dv(S("t1"), S("aim"), S("lr"), ALU.mult)
        dv(S("t2"), S("nre"), S("li"), ALU.mult)
        dv(S("t1"), S("t1"), S("t2"), ALU.subtract)
        dv(S("cim"), S("t1"), S("rden"), ALU.mult)
        dv(S("t1"), S("mag"), S("mag"), ALU.mult)
        self.recip(S("t2"), S("t1"), [scb], [scb])
        dv(S("iare"), S("are"), S("t2"), ALU.mult)
        dv(S("t3"), S("aim"), S("t2"), ALU.mult)
        self.ts("dve", S("iaim"), S("t3"), -1.0, None, ALU.mult, None, [scb], [scb])
        pw = A.f32(9, 2, 32)
        self.ms("dve", pw.ap[:, 0, 0, :], 1.0, [pw.b])
        self.ms("dve", pw.ap[:, 0, 1, :], 0.0, [pw.b])
        for j in range(1, 9):
            for (o, x1, y1, x2, y2, op) in ((pw.ap[:, j, 0, :], pw.ap[:, j - 1, 0, :], S("are"), pw.ap[:, j - 1, 1, :], S("aim"), ALU.subtract),
                                            (pw.ap[:, j, 1, :], pw.ap[:, j - 1, 0, :], S("aim"), pw.ap[:, j - 1, 1, :], S("are"), ALU.add)):
                self.tt("dve", S("t1"), x1, y1, ALU.mult, [pw.b, scb], [scb])
                self.tt("dve", S("t2"), x2, y2, ALU.mult, [pw.b, scb], [scb])
                self.tt("dve", o, S("t1"), S("t2"), op, [scb], [pw.b])
        a12 = A.f32(2, 2, 16, 2)
        for d in range(2):
            for c in range(2):
                self.cp("dve", a12.ap[:, d, 0, :, c], pw.ap[:, 8, 0, d * 16:(d + 1) * 16], [pw.b], [a12.b])
            self.ts("dve", a12.ap[:, d, 1, :, 0], pw.ap[:, 8, 1, d * 16:(d + 1) * 16], -1.0, None, ALU.mult, None, [pw.b], [a12.b])
            self.cp("dve", a12.ap[:, d, 1, :, 1], pw.ap[:, 8, 1, d * 16:(d + 1) * 16], [pw.b], [a12.b])
        self.dma("sp", self.A12[l].ap, a12.ap.rearrange("p d k g c -> p (d k g c)"), [a12.b], [])
        self.cp("dve", S("pr"), S("iare"), [scb], [scb])
        self.cp("dve", S("pi"), S("iaim"), [scb], [scb])
        for _ in range(3):
            dv(S("t1"), S("pr"), S("pr"), ALU.mult)
            dv(S("t2"), S("pi"), S("pi"), ALU.mult)
            dv(S("t3"), S("pr"), S("pi"), ALU.mult)
            dv(S("pr"), S("t1"), S("t2"), ALU.subtract)
            self.ts("dve", S("pi"), S("t3"), 2.0, None, ALU.mult, None, [scb], [scb])
        Bbr = A.f32(16, 32); Bbi = A.f32(16, 32); T1 = A.f32(16, 32); T2 = A.f32(16, 32)

        def bc16(ap):
            return mkap(ap.tensor, ap.offset, [list(ap.ap[0]), [0, 16], [1, 32]])

        for (o, x1, x2, op) in ((Bbr, Bre, Bim, ALU.subtract), (Bbi, Bim, Bre, ALU.add)):
            self.tt("dve", T1.ap, x1.ap, bc16(S("cre")), ALU.mult, [x1.b, scb], [T1.b])
            self.tt("dve", T2.ap, x2.ap, bc16(S("cim")), ALU.mult, [x2.b, scb], [T2.b])
            self.tt("dve", o.ap, T1.ap, T2.ap, op, [T1.b, T2.b], [o.b])
        XS = A.f32(2, 16, 2, 8, 16)
        Q = A.f32(2, 16, 2, 8, 16)
        tmp = [[A.f32(16, 16), A.f32(16, 16)] for _ in range(2)]

        def pwv(j, c, d):
            a = pw.ap[:, j, c, d * 16:(d + 1) * 16]
            return mkap(a.tensor, a.offset, [list(a.ap[0]), [1, 16], [0, 16]])

        def mat(tl, d):
            return tl.ap[:, :, d * 16:(d + 1) * 16].rearrange("p x g -> p g x")

        k = 0
        for d in range(2):
            for s in range(8):
                for (dst, Mr, Mi, j, neg_im) in ((XS, Bbr, Bbi, (7 - s) if d == 0 else s, False),
                                                 (Q, Cre, Cim, (s + 1) if d == 0 else (8 - s), True)):
                    eng = "dve" if k % 2 == 0 else "pool"
                    t1, t2 = tmp[k % 2]
                    k += 1
                    rd = [Mr.b, Mi.b, pw.b]
                    self.tt(eng, t1.ap, mat(Mr, d), pwv(j, 0, d), ALU.mult, rd, [t1.b])
                    self.tt(eng, t2.ap, mat(Mi, d), pwv(j, 1, d), ALU.mult, rd, [t2.b])
                    self.tt(eng, dst.ap[:, d, :, 0, s, :], t1.ap, t2.ap, ALU.subtract, [t1.b, t2.b], [dst.b])
                    self.tt(eng, t1.ap, mat(Mr, d), pwv(j, 1, d), ALU.mult, rd, [t1.b])
                    self.tt(eng, t2.ap, mat(Mi, d), pwv(j, 0, d), ALU.mult, rd, [t2.b])
                    self.tt(eng, dst.ap[:, d, :, 1, s, :], t1.ap, t2.ap, ALU.add, [t1.b, t2.b], [dst.b])
        qim = Q.ap[:, :, :, 1].rearrange("p d g s c -> p (d g) (s c)")
        self.ts("pool", qim, qim, -1.0, None, ALU.mult, None, [Q.b], [Q.b])
        XM = A.f32(2, 16, 2, 128)
        X4 = XS.ap.rearrange("p d g c s i -> p d g c (s i)")
        big = [A.f32(16, 128), A.f32(16, 128)]

        def pv(nm, d):
            a = S(nm)[:, d * 16:(d + 1) * 16]
            return mkap(a.tensor, a.offset, [list(a.ap[0]), [1, 16], [0, 128]])

        for d in range(2):
            eng = "dve" if d == 0 else "pool"
            t1, t2 = big
            rd = [XS.b, scb]
            self.tt(eng, t1.ap, X4[:, d, :, 0, :], pv("pr", d), ALU.mult, rd, [t1.b])
            self.tt(eng, t2.ap, X4[:, d, :, 1, :], pv("pi", d), ALU.mult, rd, [t2.b])
            self.tt(eng, XM.ap[:, d, :, 0, :], t1.ap, t2.ap, ALU.subtract, [t1.b, t2.b], [XM.b])
            self.tt(eng, t1.ap, X4[:, d, :, 1, :], pv("pr", d), ALU.mult, rd, [t1.b])
            self.tt(eng, t2.ap, X4[:, d, :, 0, :], pv("pi", d), ALU.mult, rd, [t2.b])
            self.tt(eng, XM.ap[:, d, :, 1, :], t1.ap, t2.ap, ALU.add, [t1.b, t2.b], [XM.b])
        self.P.barrier()
        PFs = Tl(view(RR.ap[:, 0:4096].bitcast(BF16), (64, 128)))
        Q4 = Q.ap.rearrange("p d g c s i -> p (d g c) (s i)")
        XS3 = XS.ap.rearrange("p d g c s i -> p (d g c) (s i)")
        for q4 in range(16):
            bk = self.bank(3 + q4 % 2); bb = pb[3 + q4 % 2]
            for jj in range(4):
                self.tr(bk[:, jj * 128:(jj + 1) * 128], XS3[:, q4 * 4 + jj, :], idf.ap, [XS.b, idf.b], [bb])
            self.cp("act", PFs.ap[:, q4 * 4:(q4 + 1) * 4, :], view(bk, (4, 128)), [bb], [PFs.b])
        self.dma("sp", self.PFd[l].ap, PFs.ap.rearrange("p a b -> p (a b)"), [PFs.b], [])
        Qb = Tl(view(RR.ap[:, 4096:8192].bitcast(BF16), (64, 128)))
        self.cp("pool", Qb.ap, Q4, [Q.b], [Qb.b])
        self.dma("sp", self.Qd[l].ap, Qb.ap.rearrange("p a b -> p (a b)"), [Qb.b], [])
        MLs = A.bf16(32, 128)
        mt = [A.f32(128), A.f32(128)]
        mu = [A.f32(128), A.f32(128)]
        XM4 = XM.ap
        Q5 = Q.ap.rearrange("p d g c s i -> p d g c (s i)")
        for g in range(32):
            gp_, g2 = g // 2, g % 2
            bk = self.bank(5 + g % 2); bb = pb[5 + g % 2]
            sl = slice(g2 * 64, (g2 + 1) * 64)
            for d in range(2):
                for c in range(2):
                    self.mm(bk[:, d * 128:(d + 1) * 128], XM4[sl, d, gp_, c, :], Q5[sl, d, gp_, c, :], c == 0, c == 1,
                            [XM.b, Q.b], [bb])
            t = mt[g % 2]
            u = mu[g % 2]
            self.tt("dve", t.ap, bk[:, 0:128], cst.ap[:, 3, :], ALU.mult, [bb, cst.b], [t.b])
            self.tt("dve", u.ap, bk[:, 128:256], cst.ap[:, 4, :], ALU.mult, [bb, cst.b], [u.b])
            self.tt("pool", t.ap, t.ap, u.ap, ALU.add, [t.b, u.b], [t.b])
            self.stt(MLs.ap[:, g, :], idf.ap, dcol.ap[:, g:g + 1], t.ap, ALU.mult, ALU.add, [idf.b, dcol.b, t.b], [MLs.b])
        self.dma("sp", self.MLd[l].ap, MLs.ap.rearrange("p a b -> p (a b)"), [MLs.b], [])

    def alloc_norm(self):
        A = self.A
        self.n_rstd = A.f32(TT)
        self.n_tmp = [A.f32(TT), A.f32(TT)]
        self.n_k = 0

    def rstd_from(self, bankidx):
        r = self.n_rstd
        pbk = self.pbufs[bankidx]
        self.ts("dve", r.ap, self.bank(bankidx), 1.0 / D, EPS, ALU.mult, ALU.add, [pbk], [r.b])
        self.act(r.ap, r.ap, AF.Sqrt, [r.b], [r.b])
        self.recip(r.ap, r.ap, [r.b], [r.b])
        return r

    def load_x(self, xt, ti, first=False, alias=None):
        tok0 = ti * TT
        if not first:
            self.dma("sp", xt.ap, self.XT.ap[:, :, tok0:tok0 + TT].rearrange("c p t -> p c t"), [], xt.bs)
            return
        xtm, extra = alias
        self.dma("sp", xtm.ap, self.I["xin"][tok0:tok0 + TT].rearrange("(b p) d -> p b d", p=128), [], [xtm.b] + extra)
        for c in range(8):
            bi = c % 2
            for blk in range(4):
                self.tr(self.bank(bi)[:, blk * 128:(blk + 1) * 128], xtm.ap[:, blk, c * 128:(c + 1) * 128], self.identf.ap,
                        [xtm.b, self.identf.b] + extra, [self.pbufs[bi]])
            self.cp("act" if c % 2 else "dve", xt.ap[:, c, :], self.bank(bi), [self.pbufs[bi]], [xt.bs[c]])

    def store_x(self, xt, ti, last=False, alias=None):
        tok0 = ti * TT
        if not last:
            self.dma("sp", self.XT.ap[:, :, tok0:tok0 + TT].rearrange("c p t -> p c t"), xt.ap, xt.bs, [])
            return
        yst, extra = alias
        for blk in range(4):
            for half in range(2):
                bi = (blk * 2 + half) % 2
                for cc in range(4):
                    c = half * 4 + cc
                    self.tr(self.bank(bi)[:, cc * 128:(cc + 1) * 128], xt.ap[:, c, blk * 128:(blk + 1) * 128], self.identf.ap,
                            [xt.bs[c], self.identf.b], [self.pbufs[bi]])
                self.cp("act" if half else "dve", yst.ap[:, blk, half * 512:(half + 1) * 512], self.bank(bi),
                        [self.pbufs[bi]], [yst.b] + extra)
        self.dma("sp", self.O["y"][tok0:tok0 + TT].rearrange("(b p) d -> p b d", p=128), yst.ap, [yst.b] + extra, [])

    def prenorm(self, l, sub, which, xt, hT, sq, bankidx=7):
        pbk = self.pbufs[bankidx]
        for c in range(8):
            self.tt("pool", sq.ap[:, c, :], xt.ap[:, c, :], xt.ap[:, c, :], ALU.mult, [xt.bs[c]], [sq.b])
        for c in range(8):
            self.mm(self.bank(bankidx), self.onesb.ap, sq.ap[:, c, :], c == 0, c == 7, [self.onesb.b, sq.b], [pbk])
        r = self.rstd_from(bankidx)
        for c in range(8):
            t = self.n_tmp[self.n_k % 2]
            self.n_k += 1
            self.tt("dve", t.ap, xt.ap[:, c, :], r.ap, ALU.mult, [xt.bs[c], r.b], [t.b])
            self.ts("pool", hT.ap[:, c, :], t.ap, self.scal(l, 0, sub, c, which), self.scal(l, 1, sub, c, which),
                    ALU.mult, ALU.add, [t.b, self.SC.b], [hT.b])

    def post_chunk(self, m, pbank, fT, sqr, ssbank):
        pbk = self.pbufs[pbank]
        s = sqr[m % 2]
        self.cp("dve", fT.ap[:, m, :], self.bank(pbank), [pbk], [fT.b])
        self.tt("pool", s.ap, fT.ap[:, m, :], fT.ap[:, m, :], ALU.mult, [fT.b], [s.b])
        if m > 0:
            p_ = sqr[(m - 1) % 2]
            self.mm(self.bank(ssbank), self.onesb.ap, p_.ap, m == 1, False, [self.onesb.b, p_.b], [self.pbufs[ssbank]])
        if m == 7:
            self._last_sq = s

    def post_update(self, l, sub, which, xt, fT, ssbank):
        p_ = self._last_sq
        self.mm(self.bank(ssbank), self.onesb.ap, p_.ap, False, True, [self.onesb.b, p_.b], [self.pbufs[ssbank]])
        r = self.rstd_from(ssbank)
        for m in range(8):
            t = self.n_tmp[self.n_k % 2]
            self.n_k += 1
            self.stt(t.ap, fT.ap[:, m, :], self.scal(l, 2, sub, m, which), r.ap, ALU.mult, ALU.mult,
                     [fT.b, self.SC.b, r.b], [t.b])
            self.tt("pool", xt.ap[:, m, :], xt.ap[:, m, :], t.ap, ALU.add, [xt.bs[m], t.b], [xt.bs[m]])

    def ffn_pass(self, l, i, first=False, last=False):
        self.phase()
        A, I = self.A, self.I
        sub = 0 if i == 0 else 2
        wg = A.bf16(8, DFF, nb=2); wu = A.bf16(8, DFF, nb=2); wd = A.bf16(NFC, D, nb=2)
        for h in range(2):
            cs = slice(h * 1408, (h + 1) * 1408)
            self.dma("pool", wg.ap[:, :, cs], I["w_ffn_gate"][l, i][:, cs].rearrange("(kc p) f -> p kc f", p=128), [], [wg.bs[h]])
            self.dma("pool", wu.ap[:, :, cs], I["w_ffn_up"][l, i][:, cs].rearrange("(kc p) f -> p kc f", p=128), [], [wu.bs[h]])
        wdv = I["w_ffn_down"][l, i].rearrange("(fc p) d -> p fc d", p=128)
        for h in range(2):
            fs = slice(h * 11, (h + 1) * 11)
            self.dma("pool", wd.ap[:, fs, :], wdv[:, fs, :], [], [wd.bs[h]])
        xt = A.f32(8, TT, nb=8)
        r1 = A.f32(8, TT)
        hT = Tl(view(r1.ap.rearrange("p a b -> p (a b)")[:, 0:2048].bitcast(BF16), (8, TT)))
        sq = Tl(view(r1.ap.rearrange("p a b -> p (a b)")[:, 2048:4096].bitcast(BF16), (8, TT)))
        fT = r1
        hT.b = sq.b = fT.b
        actT = A.bf16(NFC, TT, nb=NFC)
        al = Tl(view(actT.ap.rearrange("p a b -> p (a b)")[:, 0:8192].bitcast(F32), (4, D)))
        sqr = [A.bf16(TT), A.bf16(TT)]
        sgr = [A.f32(TT), A.f32(TT)]
        self.alloc_norm()
        pb = self.pbufs
        import os
        nt_ = int(os.environ.get("DBG_NT", NT))
        lvl = int(os.environ.get("DBG_LVL", 9))
        for ti in range(nt_):
            which = 0 if ti < 8 else 1
            self.load_x(xt, ti, first, (al, actT.bs))
            if lvl < 1:
                self.store_x(xt, ti, last, (al, actT.bs))
                continue
            self.prenorm(l, sub, which, xt, hT, sq)
            if lvl < 2:
                self.store_x(xt, ti, last, (al, actT.bs))
                continue
            for j in range(NFC):
                bg, bu = j % 2, 2 + j % 2
                cs = slice(j * 128, (j + 1) * 128)
                for kc in range(8):
                    self.mm(self.bank(bg), wg.ap[:, kc, cs], hT.ap[:, kc, :], kc == 0, kc == 7, [wg.bs[j // 11], hT.b], [pb[bg]])
                for kc in range(8):
                    self.mm(self.bank(bu), wu.ap[:, kc, cs], hT.ap[:, kc, :], kc == 0, kc == 7, [wu.bs[j // 11], hT.b], [pb[bu]])
                s = sgr[j % 2]
                self.act(s.ap, self.bank(bg), AF.Silu, [pb[bg]], [s.b])
                self.tt("dve", actT.ap[:, j, :], s.ap, self.bank(bu), ALU.mult, [s.b, pb[bu]], [actT.bs[j]])
            if lvl < 3:
                self.store_x(xt, ti, last, (al, actT.bs))
                continue
            for m in range(8):
                bf = 4 + m % 2
                for j in range(NFC):
                    self.mm(self.bank(bf), wd.ap[:, j, m * 128:(m + 1) * 128], actT.ap[:, j, :], j == 0, j == NFC - 1,
                            [wd.bs[j // 11], actT.bs[j]], [pb[bf]])
                if lvl >= 4:
                    self.post_chunk(m, bf, fT, sqr, 6)
            if lvl >= 5:
                self.post_update(l, sub, which, xt, fT, 6)
            self.store_x(xt, ti, last, (al, actT.bs))


    def ffn_pass2(self, l, i, first=False, last=False):
        self.phase()
        A, I = self.A, self.I
        TF = 256
        NTF = NTOK // TF
        sub = 0 if i == 0 else 2
        wg = A.bf16(8, DFF, nb=2); wu = A.bf16(8, DFF, nb=2); wd = A.bf16(NFC, D, nb=2)
        for h in range(2):
            cs = slice(h * 1408, (h + 1) * 1408)
            self.dma("pool", wg.ap[:, :, cs], I["w_ffn_gate"][l, i][:, cs].rearrange("(kc p) f -> p kc f", p=128), [], [wg.bs[h]])
            self.dma("pool", wu.ap[:, :, cs], I["w_ffn_up"][l, i][:, cs].rearrange("(kc p) f -> p kc f", p=128), [], [wu.bs[h]])
        wdv = I["w_ffn_down"][l, i].rearrange("(fc p) d -> p fc d", p=128)
        for h in range(2):
            fs = slice(h * 11, (h + 1) * 11)
            self.dma("pool", wd.ap[:, fs, :], wdv[:, fs, :], [], [wd.bs[h]])
        xts = [A.f32(8, TF, nb=8), A.f32(8, TF, nb=8)]
        hTs = [A.bf16(8, TF), A.bf16(8, TF)]
        sq = A.bf16(8, TF)
        fT = A.f32(8, TF)
        actT = A.bf16(NFC, TF, nb=NFC)
        sqr = [A.bf16(TF), A.bf16(TF)]
        sgr = [A.f32(TF), A.f32(TF)]
        stg = A.f32(2, D) if (first or last) else None
        rs_pre = A.f32(TF); rs_post = A.f32(TF)
        tmp_pre = [A.f32(TF), A.f32(TF)]; tmp_post = [A.f32(TF), A.f32(TF)]
        pb = self.pbufs
        bk = lambda b_: self.bank(b_)[:, 0:TF]

        def rstd(r, bankidx):
            self.ts("dve", r.ap, bk(bankidx), 1.0 / D, EPS, ALU.mult, ALU.add, [pb[bankidx]], [r.b])
            self.act(r.ap, r.ap, AF.Sqrt, [r.b], [r.b])
            self.recip(r.ap, r.ap, [r.b], [r.b])

        def load(ti):
            xt = xts[ti % 2]
            tok0 = ti * TF
            if not first:
                self.dma("sp", xt.ap, self.XT.ap[:, :, tok0:tok0 + TF].rearrange("c p t -> p c t"), [], xt.bs)
                return
            self.dma("sp", stg.ap, I["xin"][tok0:tok0 + TF].rearrange("(b p) d -> p b d", p=128), [], [stg.b])
            for c in range(8):
                bi = c % 2
                for blk in range(2):
                    self.tr(self.bank(bi)[:, blk * 128:(blk + 1) * 128], stg.ap[:, blk, c * 128:(c + 1) * 128], self.identf.ap,
                            [stg.b, self.identf.b], [pb[bi]])
                self.cp("act", xt.ap[:, c, :], bk(bi), [pb[bi]], [xt.bs[c]])

        def store(ti):
            xt = xts[ti % 2]
            tok0 = ti * TF
            if not last:
                self.dma("sp", self.XT.ap[:, :, tok0:tok0 + TF].rearrange("c p t -> p c t"), xt.ap, xt.bs, [])
                return
            for blk in range(2):
                for half in range(2):
                    bi = half
                    for cc in range(4):
                        c = half * 4 + cc
                        self.tr(self.bank(bi)[:, cc * 128:(cc + 1) * 128], xt.ap[:, c, blk * 128:(blk + 1) * 128], self.identf.ap,
                                [xt.bs[c], self.identf.b], [pb[bi]])
                    self.cp("act", stg.ap[:, blk, half * 512:(half + 1) * 512], self.bank(bi), [pb[bi]], [stg.b])
            self.dma("sp", self.O["y"][tok0:tok0 + TF].rearrange("(b p) d -> p b d", p=128), stg.ap, [stg.b], [])

        def prenorm(ti):
            xt = xts[ti % 2]; hT = hTs[ti % 2]
            which = 0 if ti * TF < TS else 1
            for c in range(8):
                self.tt("pool", sq.ap[:, c, :], xt.ap[:, c, :], xt.ap[:, c, :], ALU.mult, [xt.bs[c]], [sq.b])
            for c in range(8):
                self.mm(bk(7), self.onesb.ap, sq.ap[:, c, :], c == 0, c == 7, [self.onesb.b, sq.b], [pb[7]])
            rstd(rs_pre, 7)
            for c in range(8):
                t = tmp_pre[c % 2]
                self.tt("dve", t.ap, xt.ap[:, c, :], rs_pre.ap, ALU.mult, [xt.bs[c], rs_pre.b], [t.b])
                self.ts("pool", hT.ap[:, c, :], t.ap, self.scal(l, 0, sub, c, which), self.scal(l, 1, sub, c, which),
                        ALU.mult, ALU.add, [t.b, self.SC.b], [hT.b])

        load(0)
        prenorm(0)

        def upd(tj, m):
            xt_ = xts[tj % 2]
            wh_ = 0 if tj * TF < TS else 1
            t = tmp_post[m % 2]
            self.stt(t.ap, fT.ap[:, m, :], self.scal(l, 2, sub, m, wh_), rs_post.ap, ALU.mult, ALU.mult,
                     [fT.b, self.SC.b, rs_post.b], [t.b])
            self.tt("pool", xt_.ap[:, m, :], xt_.ap[:, m, :], t.ap, ALU.add, [xt_.bs[m], t.b], [xt_.bs[m]])

        for ti in range(NTF):
            xt = xts[ti % 2]; hT = hTs[ti % 2]
            for j in range(NFC):
                bg, bu = j % 2, 2 + j % 2
                cs = slice(j * 128, (j + 1) * 128)
                for kc in range(8):
                    self.mm(bk(bg), wg.ap[:, kc, cs], hT.ap[:, kc, :], kc == 0, kc == 7, [wg.bs[j // 11], hT.b], [pb[bg]])
                for kc in range(8):
                    self.mm(bk(bu), wu.ap[:, kc, cs], hT.ap[:, kc, :], kc == 0, kc == 7, [wu.bs[j // 11], hT.b], [pb[bu]])
                s = sgr[j % 2]
                self.act(s.ap, bk(bg), AF.Silu, [pb[bg]], [s.b])
                self.tt("dve", actT.ap[:, j, :], s.ap, bk(bu), ALU.mult, [s.b, pb[bu]], [actT.bs[j]])
                if ti >= 1 and 2 <= j < 10:
                    upd(ti - 1, j - 2)
            if ti >= 1:
                store(ti - 1)
            if ti + 1 < NTF:
                load(ti + 1)
                prenorm(ti + 1)
            for m in range(8):
                bf = 4 + m % 2
                for j in range(NFC):
                    self.mm(bk(bf), wd.ap[:, j, m * 128:(m + 1) * 128], actT.ap[:, j, :], j == 0, j == NFC - 1,
                            [wd.bs[j // 11], actT.bs[j]], [pb[bf]])
                s = sqr[m % 2]
                self.cp("act", fT.ap[:, m, :], bk(bf), [pb[bf]], [fT.b])
                self.tt("pool", s.ap, fT.ap[:, m, :], fT.ap[:, m, :], ALU.mult, [fT.b], [s.b])
                if m > 0:
                    p_ = sqr[(m - 1) % 2]
                    self.mm(bk(6), self.onesb.ap, p_.ap, m == 1, False, [self.onesb.b, p_.b], [pb[6]])
            p_ = sqr[1]
            self.mm(bk(6), self.onesb.ap, p_.ap, False, True, [self.onesb.b, p_.b], [pb[6]])
            rstd(rs_post, 6)
        for m in range(8):
            upd(NTF - 1, m)
        store(NTF - 1)

    def p2_pass(self, l):
        self.phase()
        A, I, O = self.A, self.I, self.O
        W = A.bf16(8, WINP, nb=4)
        wv = I["w_in_p"][l].rearrange("(kc p) c -> p kc c", p=128)
        bounds = [0, C_V, C_U, C_G + 1536, WINP]
        for h in range(4):
            self.dma("pool", W.ap[:, :, bounds[h]:bounds[h + 1]], wv[:, :, bounds[h]:bounds[h + 1]], [], [W.bs[h]])

        def wb(col):
            for h in range(4):
                if col < bounds[h + 1]:
                    return W.bs[h]

        xt = A.f32(8, TT, nb=8)
        hTs = [A.bf16(8, TT), A.bf16(8, TT)]; sq = A.bf16(8, TT)
        rt = A.f32(2, TT)
        qst = [A.bf16(TT) for _ in range(3)]
        rtmp = [A.f32(TT) for _ in range(4)]
        xlst = A.f32(4, TT); ylst = A.bf16(4, TT)
        vst = A.bf16(4, 2, 65); vf = A.f32(4, 128); kf = A.f32(4, 128)
        utok = A.bf16(32, 8, 16); ufst = A.bf16(32, 64)
        gst = [A.bf16(8, TT), A.bf16(8, TT)]
        self.alloc_norm()
        pb = self.pbufs
        self.ms("pool", vst.ap[:, :, :, 64:65], 1.0, [vst.b])
        nfm = 0
        import os
        sec = os.environ.get("DBG_P2", "ABCDE")
        for ti in range(int(os.environ.get("DBG_NT", NT))):
            which = 0 if ti < 8 else 1
            sample = ti < 8
            tok0 = ti * TT
            hT = hTs[ti % 2]
            if ti == 0:
                self.load_x(xt, 0)
                self.prenorm(l, 1, 0, xt, hTs[0], sq)
            if sample:
                self.dma("act", rt.ap, I["rope"][:, :, tok0:tok0 + TT].rearrange("a p t -> p a t"), [], [rt.b])
            if ti + 1 < NT:
                t1_ = (ti + 1) * TT
                self.dma("act", xt.ap, self.XT.ap[:, :, t1_:t1_ + TT].rearrange("c p t -> p c t"), [], xt.bs)

            def fm(col, bi):
                for kc in range(8):
                    self.mm(self.bank(bi), W.ap[:, kc, col:col + 128], hT.ap[:, kc, :], kc == 0, kc == 7, [wb(col), hT.b], [pb[bi]])

            self.P.mute = "A" not in sec
            for ci in range(5):
                col = C_Q + ci * 128 if ci < 4 else C_K
                cols = C_QS + ci * 128 if ci < 4 else C_KS
                b0 = ci % 2
                fm(col, b0)
                q_ = qst[ci % 3]
                if sample:
                    fm(cols, 2 + b0)
                    t1 = rtmp[(ci % 2) * 2]; t2 = rtmp[(ci % 2) * 2 + 1]
                    self.tt("dve", t1.ap, self.bank(b0), rt.ap[:, 0, :], ALU.mult, [pb[b0], rt.b], [t1.b])
                    self.tt("dve", t2.ap, self.bank(2 + b0), rt.ap[:, 1, :], ALU.mult, [pb[2 + b0], rt.b], [t2.b])
                    self.tt("pool", q_.ap, t1.ap, t2.ap, ALU.add, [t1.b, t2.b], [q_.b])
                else:
                    self.cp("act", q_.ap, self.bank(b0), [pb[b0]], [q_.b])
                dst = self.Qs.ap[ci][:, tok0:tok0 + TT] if ci < 4 else self.Ks.ap[:, tok0:tok0 + TT]
                self.dma("sp", dst, q_.ap, [q_.b], [])
            self.P.mute = False
            if ti + 1 < NT:
                self.prenorm(l, 1, 0 if ti + 1 < 8 else 1, xt, hTs[(ti + 1) % 2], sq)
            self.P.mute = "B" not in sec
            for blk in range(4):
                for kc in range(8):
                    self.mm(self.bank(4)[:, blk * 128:(blk + 1) * 128], hT.ap[:, kc, blk * 128:(blk + 1) * 128],
                            W.ap[:, kc, C_V:C_V + 128], kc == 0, kc == 7, [hT.b, wb(C_V)], [pb[4]])
            self.cp("act", vst.ap[:, :, :, 0:64], view(self.bank(4), (4, 2, 64)), [pb[4]], [vst.b])
            if not sample:
                self.cp("act", vf.ap, view(self.bank(4), (4, 128)), [pb[4]], [vf.b])
            self.dma("sp", self.Vs.ap[tok0:tok0 + TT].rearrange("(b p) c -> p b c", p=128),
                     vst.ap.rearrange("p b h c -> p b (h c)"), [vst.b], [])
            if not sample:
                for pp in range(2):
                    pj = 2 * (ti - 8) + pp
                    self.dma("sp", O["nv"][pj, l].rearrange("(b p) c -> p b c", p=128), vf.ap[:, 2 * pp:2 * pp + 2, :],
                             [vf.b], [])
                for blk in range(4):
                    for kc in range(8):
                        self.mm(self.bank(4)[:, blk * 128:(blk + 1) * 128], hT.ap[:, kc, blk * 128:(blk + 1) * 128],
                                W.ap[:, kc, C_K:C_K + 128], kc == 0, kc == 7, [hT.b, wb(C_K)], [pb[4]])
                self.cp("dve", kf.ap, view(self.bank(4), (4, 128)), [pb[4]], [kf.b])
                for pp in range(2):
                    pj = 2 * (ti - 8) + pp
                    self.dma("sp", O["nk"][pj, l].rearrange("(b p) c -> p b c", p=128), kf.ap[:, 2 * pp:2 * pp + 2, :],
                             [kf.b], [])
            self.P.mute = "C" not in sec
            for c in range(4):
                b0 = c % 2
                fm(C_XL + c * 128, b0)
                self.cp("act", xlst.ap[:, c, :], self.bank(b0), [pb[b0]], [xlst.b])
            self.dma("sp", self.XLs.ap[:, :, tok0:tok0 + TT].rearrange("c p t -> p c t"), xlst.ap, [xlst.b], [])
            for c in range(4):
                b0 = c % 2
                fm(C_YL + c * 128, b0)
                self.cp("dve", ylst.ap[:, c, :], self.bank(b0), [pb[b0]], [ylst.b])
            self.dma("sp", self.YLs.ap[:, :, tok0:tok0 + TT].rearrange("c p t -> p c t"), ylst.ap, [ylst.b], [])
            self.P.mute = "D" not in sec
            hs = hT.ap.rearrange("p c (k s) -> p c s k", s=8)
            for s in range(8):
                bi = 5 + s % 2
                for kc in range(8):
                    self.mm(self.bank(bi)[0:64, :], hs[:, kc, s, :], W.ap[:, kc, C_U:C_U + 512], kc == 0, kc == 7,
                            [hT.b, wb(C_U)], [pb[bi]])
                self.cp("act" if s % 2 else "dve", utok.ap[0:64, :, s, :], view(self.bank(bi)[0:64, :], (32, 16)), [pb[bi]], [utok.b])
            for half in range(2):
                bi = 2 + half
                pbf = self.bank(bi).bitcast(BF16)
                for gg in range(16):
                    g = half * 16 + gg
                    self.tr(pbf[:, gg * 64:(gg + 1) * 64], utok.ap[0:64, g].rearrange("p s c -> p (s c)"),
                            self.identb.ap[0:64, 0:64], [utok.b, self.identb.b], [pb[bi]])
                self.cp("act" if half else "dve", ufst.ap[:, half * 16:(half + 1) * 16, :], view(pbf[:, 0:1024], (16, 64)),
                        [pb[bi]], [ufst.b])
            self.dma("sp", self.UFs.ap[ti], ufst.ap.rearrange("p g k -> p (g k)"), [ufst.b], [])
            self.P.mute = "E" not in sec
            for cc in range(24):
                b0 = cc % 2
                fm(C_G + cc * 128, b0)
                g_ = gst[(cc // 8) % 2]
                self.act(g_.ap[:, cc % 8, :], self.bank(b0), AF.Sigmoid, [pb[b0]], [g_.b])
                if cc % 8 == 7:
                    grp = cc // 8
                    self.dma("sp", self.Gs.ap[grp * 8:(grp + 1) * 8][:, :, tok0:tok0 + TT].rearrange("c p t -> p c t"), g_.ap,
                             [g_.b], [])
            self.P.mute = False

    def p4_pass(self, l):
        self.phase()
        A, I = self.A, self.I
        wol = A.bf16(4, D); woa = A.bf16(4, D); wgl = A.bf16(4, 2048); wo = A.bf16(8, D)
        self.dma("pool", wgl.ap, I["w_glu"][l].rearrange("(kc p) c -> p kc c", p=128), [], [wgl.b])
        self.dma("pool", wol.ap, I["w_o_lru"][l].rearrange("(kc p) c -> p kc c", p=128), [], [wol.b])
        self.dma("pool", woa.ap, I["w_o_attn_p"][l].rearrange("(kc p) c -> p kc c", p=128), [], [woa.b])
        self.dma("pool", wo.ap, I["w_out"][l].rearrange("(kc p) c -> p kc c", p=128), [], [wo.b])
        xt = A.f32(8, TT, nb=8)
        ins = [(A.bf16(4, TT), A.bf16(4, TT), A.bf16(4, TT), A.bf16(24, TT)) for _ in range(2)]
        mg = A.bf16(8, TT)
        fT = A.f32(8, TT)
        sqr = [A.bf16(TT), A.bf16(TT)]
        tmp = [A.f32(TT) for _ in range(6)]
        self.alloc_norm()
        pb = self.pbufs

        def loads(ti):
            tok0 = ti * TT
            lr, at, sy, G = ins[ti % 2]
            for (dst, src) in ((sy, self.SYs), (lr, self.LRs), (at, self.ATs), (G, self.Gs)):
                self.dma("sp", dst.ap, src.ap[:, :, tok0:tok0 + TT].rearrange("c p t -> p c t"), [], [dst.b])

        loads(0)
        for ti in range(NT):
            which = 0 if ti < 8 else 1
            if ti + 1 < NT:
                loads(ti + 1)
            self.load_x(xt, ti)
            lr, at, sy, G = ins[ti % 2]
            for m in range(8):
                ba, bz = m % 2, 2 + m % 2
                for kc in range(4):
                    self.mm(self.bank(ba), wgl.ap[:, kc, m * 128:(m + 1) * 128], sy.ap[:, kc, :], kc == 0, kc == 3, [wgl.b, sy.b], [pb[ba]])
                for kc in range(4):
                    self.mm(self.bank(bz), wgl.ap[:, kc, 1024 + m * 128:1024 + (m + 1) * 128], sy.ap[:, kc, :], kc == 0, kc == 3,
                            [wgl.b, sy.b], [pb[bz]])
                s = tmp[m % 2]; t = tmp[2 + m % 2]
                self.act(s.ap, self.bank(bz), AF.Sigmoid, [pb[bz]], [s.b])
                self.tt("dve", t.ap, self.bank(ba), s.ap, ALU.mult, [pb[ba], s.b], [t.b])
                self.tt("pool", fT.ap[:, m, :], t.ap, G.ap[:, 8 + m, :], ALU.mult, [t.b, G.b], [fT.b])
            for m in range(8):
                ba, bc = 4 + m % 2, 6 + m % 2
                for kc in range(4):
                    self.mm(self.bank(ba), wol.ap[:, kc, m * 128:(m + 1) * 128], lr.ap[:, kc, :], kc == 0, kc == 3, [wol.b, lr.b], [pb[ba]])
                for kc in range(4):
                    self.mm(self.bank(bc), woa.ap[:, kc, m * 128:(m + 1) * 128], at.ap[:, kc, :], kc == 0, kc == 3, [woa.b, at.b], [pb[bc]])
                u1 = tmp[m % 2]; u3 = tmp[2 + m % 2]; u4 = tmp[4 + m % 2]
                self.tt("dve", u1.ap, self.bank(ba), G.ap[:, m, :], ALU.mult, [pb[ba], G.b], [u1.b])
                self.tt("dve", u3.ap, self.bank(bc), G.ap[:, 16 + m, :], ALU.mult, [pb[bc], G.b], [u3.b])
                self.tt("pool", u4.ap, fT.ap[:, m, :], u1.ap, ALU.add, [fT.b, u1.b], [u4.b])
                self.tt("pool", mg.ap[:, m, :], u4.ap, u3.ap, ALU.add, [u4.b, u3.b], [mg.b])
            for m in range(8):
                bo = m % 2
                for kc in range(8):
                    self.mm(self.bank(bo), wo.ap[:, kc, m * 128:(m + 1) * 128], mg.ap[:, kc, :], kc == 0, kc == 7, [wo.b, mg.b], [pb[bo]])
                self.post_chunk(m, bo, fT, sqr, 2)
            self.post_update(l, 1, which, xt, fT, 2)
            self.store_x(xt, ti)

    def p3a_attention(self, l):
        self.phase()
        A, I = self.A, self.I
        kT = A.bf16(2, NTOK); Qt = A.bf16(4, NTOK); Vt = A.bf16(NTOK // 128, 130)
        self.ms("pool", kT.ap[64:128, 0, :], 0.0, [kT.b])
        self.ms("pool", kT.ap[0:64, 1, :], 0.0, [kT.b])
        for h in range(2):
            sl = slice(h * 2560, (h + 1) * 2560)
            self.dma("sp", Qt.ap[:, :, sl], self.Qs.ap[:, :, sl].rearrange("c p t -> p c t"), [], [Qt.b])
        self.dma("act", kT.ap[0:64, 0, :], self.Ks.ap[0:64, :], [], [kT.b])
        self.dma("act", kT.ap[64:128, 1, :], self.Ks.ap[64:128, :], [], [kT.b])
        self.dma("act", Vt.ap, self.Vs.ap.rearrange("(b p) c -> p b c", p=128), [], [Vt.b])
        ckr = A.f32(4, 128); cvr = A.f32(4, 128)
        self.dma("sp", ckr.ap, I["cache_k"][l].rearrange("(b p) c -> p b c", p=128), [], [ckr.b])
        self.dma("sp", cvr.ap, I["cache_v"][l].rearrange("(b p) c -> p b c", p=128), [], [cvr.b])
        ckT = A.bf16(2, 512); cv = A.bf16(4, 2, 65)
        pb = self.pbufs
        for blk in range(4):
            self.tr(self.bank(0)[:, blk * 128:(blk + 1) * 128], ckr.ap[:, blk, :], self.identf.ap, [ckr.b, self.identf.b], [pb[0]])
        self.ms("pool", ckT.ap, 0.0, [ckT.b])
        self.cp("dve", ckT.ap[0:64, 0, :], self.bank(0)[0:64, :], [pb[0]], [ckT.b])
        self.cp("dve", ckT.ap[64:128, 1, :], self.bank(0)[64:128, :], [pb[0]], [ckT.b])
        self.ms("pool", cv.ap[:, :, :, 64:65], 1.0, [cv.b])
        self.cp("dve", cv.ap[:, :, :, 0:64], cvr.ap.rearrange("p b (h c) -> p b h c", h=2), [cvr.b], [cv.b])
        cvf = cv.ap.rearrange("p b h c -> p b (h c)")
        sk = A.f32(8)
        self.dma("sp", sk.ap[64:65, :], I["attn_sink"][l:l + 1, :], [], [sk.b])
        self.act(sk.ap[64:65, :], sk.ap[64:65, :], AF.Exp, [sk.b], [sk.b])
        pT = [A.bf16(TT) for _ in range(4)]
        osb = [A.f32(TT) for _ in range(3)]
        rrow = [A.f32(TT) for _ in range(3)]
        ast = [A.bf16(4, TT), A.bf16(4, TT)]
        segs = [(0, TS, True)] + [(TS + j * TP, TP, False) for j in range(NPB)]
        its = []
        nst = 0
        for (tok0, T, samp) in segs:
            nqb = T // 128
            for qb in range(nqb):
                for kvh in range(2):
                    its.append((tok0, T, samp, nqb, qb, kvh, nst))
                if qb % 4 == 3 or qb == nqb - 1:
                    nst += 1
        npt = [0]

        def front(n):
            tok0, T, samp, nqb, qb, kvh, st_ = its[n]
            q0 = tok0 + qb * 128
            blocks = []
            if samp:
                for nb in (qb - 1, qb, qb + 1):
                    if 0 <= nb < nqb:
                        msk = self.mprev if nb == qb - 1 else (self.mnext if nb == qb + 1 else None)
                        blocks.append((kT.ap[:, kvh, tok0 + nb * 128:tok0 + (nb + 1) * 128], kT.b,
                                       Vt.ap[:, (tok0 // 128) + nb, kvh * 65:(kvh + 1) * 65], Vt.b, msk))
                for cb_ in range(4):
                    blocks.append((ckT.ap[:, kvh, cb_ * 128:(cb_ + 1) * 128], ckT.b, cvf[:, cb_, kvh * 65:(kvh + 1) * 65], cv.b, None))
            else:
                for nb in range(nqb):
                    blocks.append((kT.ap[:, kvh, tok0 + nb * 128:tok0 + (nb + 1) * 128], kT.b,
                                   Vt.ap[:, (tok0 // 128) + nb, kvh * 65:(kvh + 1) * 65], Vt.b, None))
            po = 3 + n % 3
            rhs_q = Qt.ap[:, :, q0:q0 + 128]
            pts = []
            for bi_, (kap, kb, vap, vb, msk) in enumerate(blocks):
                sb_ = npt[0] % 3
                p_ = pT[npt[0] % 4]
                npt[0] += 1
                self.mm(view(self.bank(sb_), (4, 128)), kap, rhs_q, True, True, [kb, Qt.b], [pb[sb_]])
                self.act(p_.ap, self.bank(sb_), AF.Exp, [pb[sb_]], [p_.b], scale=0.125)
                if msk is not None:
                    m_ = msk.ap
                    mb = mkap(m_.tensor, m_.offset, [list(m_.ap[0]), [0, 4], [1, 128]])
                    self.tt("pool", view(p_.ap, (4, 128)), view(p_.ap, (4, 128)), mb, ALU.mult, [p_.b, msk.b], [p_.b])
                pts.append((p_, vap, vb))
                if bi_ >= 1:
                    pp_, vap_, vb_ = pts[bi_ - 1]
                    self.mm(self.bank(po)[0:65, :], vap_, pp_.ap, bi_ == 1, False, [vb_, pp_.b], [pb[po]])
            pp_, vap_, vb_ = pts[-1]
            self.mm(self.bank(po)[0:65, :], vap_, pp_.ap, len(pts) == 1, True, [vb_, pp_.b], [pb[po]])

        def back1(n):
            tok0, T, samp, nqb, qb, kvh, st_ = its[n]
            po = 3 + n % 3
            o_ = osb[n % 3]; r_ = rrow[n % 3]
            self.cp("act", o_.ap[0:65, :], self.bank(po)[0:65, :], [pb[po]], [o_.b])
            s_ = sk.ap[64:65, kvh * 4:(kvh + 1) * 4]
            sbc = mkap(s_.tensor, s_.offset, [list(s_.ap[0]), [1, 4], [0, 128]])
            self.tt("dve", view(r_.ap[64:65, :], (4, 128)), view(o_.ap[64:65, :], (4, 128)), sbc, ALU.add, [o_.b, sk.b], [r_.b])
            self.recip(r_.ap[64:65, :], r_.ap[64:65, :], [r_.b], [r_.b])

        def back2(n):
            tok0, T, samp, nqb, qb, kvh, st_ = its[n]
            q0 = tok0 + qb * 128
            hs = slice(kvh * 64, (kvh + 1) * 64)
            a_t = ast[st_ % 2]
            qcol = (qb % 4) * 128
            o_ = osb[n % 3]; r_ = rrow[n % 3]
            bb = 6 + n % 2
            self.mm(self.bank(bb)[0:64, :], self.onesf.ap[64:65, 0:64], r_.ap[64:65, :], True, True, [self.onesf.b, r_.b], [pb[bb]])
            self.tt("dve", a_t.ap[hs, :, qcol:qcol + 128], view(o_.ap[0:64, :], (4, 128)), view(self.bank(bb)[0:64, :], (4, 128)),
                    ALU.mult, [o_.b, pb[bb]], [a_t.b])
            if kvh == 1 and (qb % 4 == 3 or qb == nqb - 1):
                nn = (qb % 4 + 1) * 128
                t0 = q0 + 128 - nn
                self.dma("sp", self.ATs.ap[:, :, t0:t0 + nn].rearrange("c p t -> p c t"), a_t.ap[:, :, 0:nn], [a_t.b], [])

        N = len(its)
        for n in range(N + 2):
            if n < N:
                front(n)
            if 0 <= n - 1 < N:
                back1(n - 1)
            if 0 <= n - 2 < N:
                back2(n - 2)

    def p3b_lru(self, l):
        self.phase()
        A, I, O = self.A, self.I, self.O
        pb = self.pbufs
        cw = A.f32(4, 4); cb = A.f32(4); onec = A.f32(1)
        src, sb_ = self.vecT(None, I["w_conv"][l].rearrange("j (c p) -> (j c) p", p=128), 16, None)
        self.cp("dve", cw.ap.rearrange("p c j -> p j c"), view(src, (4, 4)), [sb_], [cw.b])
        src, sb_ = self.vecT(None, I["b_conv"][l].rearrange("(c p) -> c p", p=128), 4, None)
        self.cp("dve", cb.ap, src, [sb_], [cb.b])
        self.ms("dve", onec.ap, 1.0, [onec.b])
        ba = A.f32(2, 4); bx = A.f32(2, 4); cl = A.f32(2, 4); st0 = A.f32(2, 4)
        for (t_, nm) in ((ba, "b_lru_a"), (bx, "b_lru_x"), (cl, "lru_lambda"), (st0, "state_lru")):
            src, sb_ = self.vecT(None, I[nm][l].rearrange("d (c p) -> (d c) p", p=128), 8, None)
            self.cp("dve", t_.ap.rearrange("p d c -> p (d c)"), src, [sb_], [t_.b])
        self.act(cl.ap, cl.ap, AF.Exp, [cl.b], [cl.b], scale=-1.0)
        self.act(cl.ap, cl.ap, AF.Ln, [cl.b, onec.b], [cl.b], bias=onec.ap[:, 0:1])
        self.ts("dve", cl.ap, cl.ap, -8.0, None, ALU.mult, None, [cl.b], [cl.b])
        cl2 = A.f32(2, 4)
        self.ts("dve", cl2.ap, cl.ap, 2.0, None, ALU.mult, None, [cl.b], [cl2.b])
        BD = {}
        for d in range(2):
            for nm in ("w_lru_a", "w_lru_x"):
                t_ = A.bf16(4, 128)
                self.ms("pool", t_.ap, 0.0, [t_.b])
                fns = []
                for c in range(4):
                    fns.append(lambda e, o_=t_.ap[0:64, c, 0:64], i_=I[nm][l, d, 2 * c]: e.dma_start(out=o_, in_=i_))
                    fns.append(lambda e, o_=t_.ap[64:128, c, 64:128], i_=I[nm][l, d, 2 * c + 1]: e.dma_start(out=o_, in_=i_))
                self.P.dma_group("pool", fns, [], [t_.b])
                BD[(d, nm)] = t_
        hf = A.f32(4, TS); xc = A.f32(4, TS)
        xlt = A.f32(4, TT + 3); xcb = A.bf16(4, TT)
        R = A.f32(4, TT)
        IIs = [A.f32(4, TT), A.f32(4, TT)]
        AAs = [A.f32(4, TT), A.f32(4, TT)]
        hbt = A.f32(4, TT); carry = A.f32(4)
        ylts = [A.bf16(4, TT), A.bf16(4, TT)]
        fin = A.f32(NPB, 2, 4)
        segs = [(0, TS, True)] + [(TS + j * TP, TP, False) for j in range(NPB)]
        gk = [0]
        xcbufs = [Buf() for _ in range(8)]

        def rev(ap, n):
            return mkap(ap.tensor, ap.offset + n - 1, [list(ap.ap[0]), [-1, n]])

        def gates(d, t0l, n):
            AA = AAs[gk[0] % 2]; II = IIs[gk[0] % 2]
            gk[0] += 1
            SQ = R
            for c in range(4):
                self.cp("pool", xcb.ap[:, c, 0:n], xc.ap[:, c, t0l:t0l + n], [xcbufs[t0l // n]], [xcb.b])
            for c in range(4):
                self.mm(self.bank(c)[:, 0:n], BD[(d, "w_lru_a")].ap[:, c, :], xcb.ap[:, c, 0:n], True, True, [BD[(d, "w_lru_a")].b, xcb.b], [pb[c]])
                self.mm(self.bank(4 + c)[:, 0:n], BD[(d, "w_lru_x")].ap[:, c, :], xcb.ap[:, c, 0:n], True, True, [BD[(d, "w_lru_x")].b, xcb.b], [pb[4 + c]])
            for c in range(4):
                self.act(R.ap[:, c, 0:n], self.bank(c)[:, 0:n], AF.Sigmoid, [pb[c], ba.b], [R.b], bias=ba.ap[:, d, c:c + 1])
            for c in range(4):
                self.act(II.ap[:, c, 0:n], self.bank(4 + c)[:, 0:n], AF.Sigmoid, [pb[4 + c], bx.b], [II.b], bias=bx.ap[:, d, c:c + 1])
            for c in range(4):
                self.tt("pool", II.ap[:, c, 0:n], II.ap[:, c, 0:n], xc.ap[:, c, t0l:t0l + n], ALU.mult, [II.b, xcbufs[t0l // n]], [II.b])
            for c in range(4):
                self.act(AA.ap[:, c, 0:n], R.ap[:, c, 0:n], AF.Exp, [R.b, cl.b], [AA.b], scale=cl.ap[:, d, c:c + 1])
            for c in range(4):
                self.act(SQ.ap[:, c, 0:n], R.ap[:, c, 0:n], AF.Exp, [R.b, cl2.b], [SQ.b], scale=cl2.ap[:, d, c:c + 1])
            for c in range(4):
                self.act(SQ.ap[:, c, 0:n], SQ.ap[:, c, 0:n], AF.Sqrt, [SQ.b, onec.b], [SQ.b], scale=-1.0, bias=onec.ap[:, 0:1])
            for c in range(4):
                self.tt("dve", II.ap[:, c, 0:n], II.ap[:, c, 0:n], SQ.ap[:, c, 0:n], ALU.mult, [II.b, SQ.b], [II.b])
            return AA, II

        for si, (tok0, T, samp) in enumerate(segs):
            n = min(TT, T)
            ntl = T // n

            def load_conv(tl):
                t0l = tl * n
                lo = max(t0l - 2, 0); hi = min(t0l + n + 1, T)
                if lo > t0l - 2:
                    self.ms("dve", xlt.ap[:, :, 0:2], 0.0, [xlt.b])
                if hi < t0l + n + 1:
                    self.ms("dve", xlt.ap[:, :, n + 2:n + 3], 0.0, [xlt.b])
                self.dma("sp", xlt.ap[:, :, lo - (t0l - 2):hi - (t0l - 2)], self.XLs.ap[:, :, tok0 + lo:tok0 + hi].rearrange("c p t -> p c t"), [], [xlt.b])
                for c in range(4):
                    o_ = xc.ap[:, c, t0l:t0l + n]
                    self.act(o_, xlt.ap[:, c, 0:n], AF.Identity, [xlt.b, cw.b, cb.b], [xcbufs[tl]], scale=cw.ap[:, c, 0:1], bias=cb.ap[:, c:c + 1])
                    for j in range(1, 4):
                        self.stt(o_, xlt.ap[:, c, j:j + n], cw.ap[:, c, j:j + 1], o_, ALU.mult, ALU.add, [xlt.b, cw.b, xcbufs[tl]], [xcbufs[tl]])

            load_conv(0)
            for tl in range(ntl):
                t0l = tl * n
                if tl + 1 < ntl:
                    load_conv(tl + 1)
                AA, II = gates(0, t0l, n)
                for c in range(4):
                    if tl == 0:
                        init = st0.ap[:, 0, c:c + 1] if samp else 0.0
                    else:
                        init = hf.ap[:, c, t0l - 1:t0l]
                    self.P.op("dve", lambda e, o=hf.ap[:, c, t0l:t0l + n], a=AA.ap[:, c, 0:n], b=II.ap[:, c, 0:n], i0=init:
                              e.tensor_tensor_scan(out=o, data0=a, data1=b, initial=i0, op0=ALU.mult, op1=ALU.add),
                              [AA.b, II.b, hf.b, st0.b], [hf.b])
            if not samp:
                self.cp("dve", fin.ap[:, si - 1, 0, :], hf.ap[:, :, T - 1], [hf.b], [fin.b])
            def load_yl(tl):
                y_ = ylts[tl % 2]
                t0l_ = tl * n
                self.dma("sp", y_.ap[:, :, 0:n], self.YLs.ap[:, :, tok0 + t0l_:tok0 + t0l_ + n].rearrange("c p t -> p c t"), [], [y_.b])
                for c in range(4):
                    self.act(y_.ap[:, c, 0:n], y_.ap[:, c, 0:n], AF.Gelu_apprx_tanh, [y_.b], [y_.b])

            load_yl(ntl - 1)
            for tl in reversed(range(ntl)):
                t0l = tl * n
                cur = hbt
                ylt = ylts[tl % 2]
                AA, II = gates(1, t0l, n)
                if tl - 1 >= 0:
                    load_yl(tl - 1)
                for c in range(4):
                    if tl == ntl - 1:
                        init = st0.ap[:, 1, c:c + 1] if samp else 0.0
                    else:
                        init = carry.ap[:, c:c + 1]
                    self.P.op("dve", lambda e, o=rev(cur.ap[:, c, 0:n], n), a=rev(AA.ap[:, c, 0:n], n), b=rev(II.ap[:, c, 0:n], n), i0=init:
                              e.tensor_tensor_scan(out=o, data0=a, data1=b, initial=i0, op0=ALU.mult, op1=ALU.add),
                              [AA.b, II.b, carry.b, st0.b], [cur.b])
                self.cp("dve", carry.ap, cur.ap[:, :, 0], [cur.b], [carry.b])
                if not samp and tl == 0:
                    self.cp("dve", fin.ap[:, si - 1, 1, :], cur.ap[:, :, 0], [cur.b], [fin.b])
                for c in range(4):
                    self.tt("dve", cur.ap[:, c, 0:n], cur.ap[:, c, 0:n], hf.ap[:, c, t0l:t0l + n], ALU.add, [cur.b, hf.b], [cur.b])
                    self.tt("dve", ylt.ap[:, c, 0:n], cur.ap[:, c, 0:n], ylt.ap[:, c, 0:n], ALU.mult, [cur.b, ylt.b], [ylt.b])
                self.dma("sp", self.LRs.ap[:, :, tok0 + t0l:tok0 + t0l + n].rearrange("c p t -> p c t"), ylt.ap[:, :, 0:n], [ylt.b], [])
        self.tr(self.bank(0)[0:32, 0:128], fin.ap.rearrange("p s d c -> p (s d c)"), self.identf.ap, [fin.b, self.identf.b], [pb[0]])
        fo = A.f32(128)
        self.cp("dve", fo.ap[0:32, :], self.bank(0)[0:32, 0:128], [pb[0]], [fo.b])
        for s_ in range(NPB):
            self.dma("sp", O["nlru"][s_, l].rearrange("d (c p) -> (d c) p", p=128), fo.ap[s_ * 8:(s_ + 1) * 8, :], [fo.b], [])

    def p3c_s5(self, l):
        self.phase()
        A, I, O = self.A, self.I, self.O
        pb = self.pbufs
        Qw = A.bf16(2, 16, 2, 128); ML = A.bf16(32, 128)
        self.dma("sp", Qw.ap.rearrange("p d g c x -> p (d g c x)"), self.Qd[l].ap, [], [Qw.b])
        self.dma("sp", ML.ap.rearrange("p g x -> p (g x)"), self.MLd[l].ap, [], [ML.b])
        a12 = A.f32(2, 2, 16, 2)
        self.dma("sp", a12.ap.rearrange("p d k g c -> p (d k g c)"), self.A12[l].ap, [], [a12.b])
        h0r = A.f32(2, 128); h0s = A.f32(2, 16, 2)
        for d in range(2):
            self.dma("sp", h0r.ap[0:32, d, :], I["state_ssm"][l, d].rearrange("c (gp g2) n -> (c gp) (g2 n)", g2=2), [], [h0r.b])
            self.tr(self.bank(0)[:, d * 32:(d + 1) * 32], h0r.ap[0:32, d, :], self.identf.ap[0:32, 0:32], [h0r.b, self.identf.b], [pb[0]])
            self.cp("dve", h0s.ap[:, d].rearrange("p g c -> p c g"), view(self.bank(0)[:, d * 32:(d + 1) * 32], (2, 16)), [pb[0]], [h0s.b])
        zero = A.f32(16, 2, 4)
        self.ms("dve", zero.ap, 0.0, [zero.b])
        fin = A.f32(NPB, 2, 2, 16)
        mark0 = A.top
        for (tile0, ntile, nseq, Kseq, KB, tokbase) in ((0, 8, 1, 512, 256, 0), (8, 2, 4, 32, 128, TS)):
            A.top = mark0
            self.P.barrier()
            K = nseq * Kseq
            nblk = K // KB
            Uf = A.bf16(ntile, 32, 64)
            self.dma("sp", Uf.ap.rearrange("p t g k -> p t (g k)"), self.UFs.ap[tile0:tile0 + ntile].rearrange("t p x -> p t x"), [], [Uf.b])
            Hbf = [A.bf16(16, 2, nseq, Kseq + 1), A.bf16(16, 2, nseq, Kseq + 1)]
            mark1 = A.top
            PF = A.bf16(2, 16, 2, 128)
            self.dma("sp", PF.ap.rearrange("p d g c x -> p (d g c x)"), self.PFd[l].ap, [], [PF.b])
            Sd = [A.f32(16, 2, KB), A.f32(16, 2, KB)]
            tA = [A.f32(16, 2, nseq), A.f32(16, 2, nseq)]
            tB = [A.f32(16, 2, nseq), A.f32(16, 2, nseq)]
            hinit = [A.f32(16, 2, nseq), A.f32(16, 2, nseq)]
            tpb = KB // 64
            spb = KB // Kseq if nseq > 1 else 1
            for d in range(2):
                eng = "dve" if d == 0 else "pool"
                S = Sd[d]
                if nseq == 1:
                    self.cp(eng, hinit[d].ap[:, :, :, 0], h0s.ap[:, d], [h0s.b], [hinit[d].b])
                    self.cp("act", Hbf[d].ap[:, :, :, 0, 0 if d == 0 else Kseq], h0s.ap[:, d], [h0s.b], [Hbf[d].b])
                else:
                    self.ms(eng, hinit[d].ap, 0.0, [hinit[d].b])
                    self.ms(eng, Hbf[d].ap[:, :, :, :, 0 if d == 0 else Kseq], 0.0, [Hbf[d].b])
            for bidx in range(nblk):
                for d in range(2):
                    eng = "dve" if d == 0 else "pool"
                    S = Sd[d]
                    blk = bidx if d == 0 else nblk - 1 - bidx
                    for gp_ in range(16):
                        for c in range(2):
                            bi = (0 if d == 0 else 4) + (gp_ * 2 + c) % 4
                            for g2 in range(2):
                                g = 2 * gp_ + g2
                                self.mm(view(self.bank(bi)[g2 * 64:(g2 + 1) * 64, 0:KB], (tpb, 64)),
                                        PF.ap[:, d, gp_, c, g2 * 64:(g2 + 1) * 64],
                                        Uf.ap[:, blk * tpb:(blk + 1) * tpb, g, :], True, True, [PF.b, Uf.b], [pb[bi]])
                            self.cp("act", S.ap[:, gp_, c, :], self.bank(bi)[:, 0:KB], [pb[bi]], [S.b])
                for d in range(2):
                    eng = "dve" if d == 0 else "pool"
                    S = Sd[d]
                    blk = bidx if d == 0 else nblk - 1 - bidx
                    first_blk = bidx == 0
                    s_ = S.ap
                    base = s_.offset
                    pp = list(s_.ap[0])
                    nk = Kseq if nseq > 1 else KB

                    def col(kk, swap=False):
                        if swap:
                            return mkap(s_.tensor, base + KB + kk, [pp, [2 * KB, 16], [-KB, 2], [Kseq, spb]])
                        return mkap(s_.tensor, base + kk, [pp, [2 * KB, 16], [KB, 2], [Kseq, spb]])

                    def hv(t, swap=False):
                        a = t.ap
                        if swap:
                            return mkap(a.tensor, a.offset + nseq, [list(a.ap[0]), [2 * nseq, 16], [-nseq, 2], [1, spb]])
                        return mkap(a.tensor, a.offset, [list(a.ap[0]), [2 * nseq, 16], [nseq, 2], [1, spb]])

                    a1 = a12.ap[:, d, 0]
                    a2 = a12.ap[:, d, 1]
                    A1 = mkap(a1.tensor, a1.offset, [list(a1.ap[0]), [2, 16], [1, 2], [0, spb]])
                    A2 = mkap(a2.tensor, a2.offset, [list(a2.ap[0]), [2, 16], [1, 2], [0, spb]])
                    order = range(nk) if d == 0 else reversed(range(nk))
                    prev = None
                    for kk in order:
                        if prev is None:
                            if first_blk or nseq > 1:
                                pv_, psw = hv(hinit[d]), hv(hinit[d], True)
                                rdx = [hinit[d].b]
                            else:
                                pv_, psw = hv(hinit[d]), hv(hinit[d], True)
                                rdx = [hinit[d].b]
                        else:
                            pv_, psw = col(prev), col(prev, True)
                            rdx = []
                        ta, tb = tA[d], tB[d]
                        self.tt(eng, hv(ta), A1, pv_, ALU.mult, [a12.b, S.b] + rdx, [ta.b])
                        self.tt(eng, hv(tb), A2, psw, ALU.mult, [a12.b, S.b] + rdx, [tb.b])
                        self.tt(eng, hv(ta), hv(ta), hv(tb), ALU.add, [ta.b, tb.b], [ta.b])
                        self.tt(eng, col(kk), col(kk), hv(ta), ALU.add, [S.b, ta.b], [S.b])
                        prev = kk
                    if nseq == 1:
                        self.cp(eng, hv(hinit[d]), col(prev), [S.b], [hinit[d].b])
                    else:
                        for sq_ in range(spb):
                            seq = blk * spb + sq_
                            kcol = sq_ * Kseq + (Kseq - 1 if d == 0 else 0)
                            self.cp(eng, fin.ap[:, seq, d, :, :], S.ap[:, :, :, kcol].rearrange("p g c -> p c g"), [S.b], [fin.b])
                    for c in range(2):
                        if nseq == 1:
                            o0 = blk * KB + (1 if d == 0 else 0)
                            self.cp("act", Hbf[d].ap[:, :, c, 0, o0:o0 + KB], S.ap[:, :, c, :], [S.b], [Hbf[d].b])
                        else:
                            o0 = 1 if d == 0 else 0
                            self.cp("act", Hbf[d].ap[:, :, c, blk * spb:(blk + 1) * spb, o0:o0 + Kseq],
                                    S.ap[:, :, c, :].rearrange("p g (s k) -> p g s k", k=Kseq), [S.b], [Hbf[d].b])
            self.P.barrier()
            A.top = mark1
            Yf = A.bf16(32, K)
            Ytok = A.bf16(8, 512)
            YT = A.bf16(4, 1024)
            for g in range(32):
                gp_, g2 = g // 2, g % 2
                bi = g % 4
                hs = slice(g2 * 64, (g2 + 1) * 64)
                out = view(self.bank(bi)[:, 0:K], (ntile, 64))
                self.mm(out, ML.ap[:, g, :], Uf.ap[:, :, g, :], True, False, [ML.b, Uf.b], [pb[bi]])
                for d in range(2):
                    for c in range(2):
                        o0 = 0 if d == 0 else 1
                        self.mm(view(self.bank(bi)[:, 0:K], (nseq, Kseq)), Qw.ap[hs, d, gp_, c, :], Hbf[d].ap[hs, gp_, c, :, o0:o0 + Kseq],
                                False, d == 1 and c == 1, [Qw.b, Hbf[d].b], [pb[bi]])
                self.act(Yf.ap[:, g, :], self.bank(bi)[:, 0:K], AF.Gelu_apprx_tanh, [pb[bi]], [Yf.b])
            for kb in range(K // 128):
                for q4 in range(4):
                    bi = 4 + q4 % 2
                    pbf = self.bank(bi).bitcast(BF16)
                    for gg in range(8):
                        g = q4 * 8 + gg
                        self.tr(pbf[:, gg * 128:(gg + 1) * 128], Yf.ap[:, g, kb * 128:(kb + 1) * 128], self.identb.ap, [Yf.b, self.identb.b], [pb[bi]])
                    self.cp("dve" if q4 % 2 else "act", Ytok.ap[:, :, q4 * 128:(q4 + 1) * 128].rearrange("p t (g c) -> p g t c", c=16),
                            pbf[:, 0:1024].rearrange("p (g t c) -> p g t c", g=8, t=8), [pb[bi]], [Ytok.b])
                for t in range(8):
                    bi = 6 + t % 2
                    pbf = self.bank(bi).bitcast(BF16)
                    for cc in range(4):
                        self.tr(pbf[:, cc * 128:(cc + 1) * 128], Ytok.ap[:, t, cc * 128:(cc + 1) * 128], self.identb.ap, [Ytok.b, self.identb.b], [pb[bi]])
                    self.cp("dve" if t % 2 else "act", YT.ap.rearrange("p c (k s) -> p c s k", s=8)[:, :, t, :],
                            view(pbf[:, 0:512], (4, 128)), [pb[bi]], [YT.b])
                t0 = tokbase + kb * 1024
                self.dma("sp", self.SYs.ap[:, :, t0:t0 + 1024].rearrange("c p t -> p c t"), YT.ap, [YT.b], [])
        fo = A.f32(2, 128)
        ff = fin.ap.rearrange("p s d c g -> p (s d c g)")
        for h in range(2):
            self.tr(self.bank(h)[:, 0:128], ff[:, h * 128:(h + 1) * 128], self.identf.ap, [fin.b, self.identf.b], [pb[h]])
            self.cp("dve", fo.ap[:, h, :], self.bank(h)[:, 0:128], [pb[h]], [fo.b])
        for seq in range(NPB):
            h, r0 = seq // 2, (seq % 2) * 64
            self.dma("sp", O["nssm"][seq, l].rearrange("d c (gp g2) n -> (d c gp) (g2 n)", g2=2), fo.ap[r0:r0 + 64, h, :], [fo.b], [])


def build(dbg=False, stages=None):
    K = Kern(dbg)
    on = lambda nm: stages is None or nm in stages
    with K.st:
        if on("pro"):
            K.prologue()
        for l in range(L):
            if on("ffa%d" % l):
                K.ffn_pass2(l, 0, first=(l == 0))
            if on("p2%d" % l):
                K.p2_pass(l)
            if on("p3a%d" % l):
                K.p3a_attention(l)
            if on("p3b%d" % l):
                K.p3b_lru(l)
            if on("p3c%d" % l):
                K.p3c_s5(l)
            if on("p4%d" % l):
                K.p4_pass(l)
            if on("ffb%d" % l):
                K.ffn_pass2(l, 1, last=(l == L - 1))
        K.P.barrier()
        K.P.op("sp", lambda e: e.nop(), [], [])
        K.P.emit()
    return K


def _perm_q():
    idx = []
    for c in range(4):
        for h in (c, 4 + c):
            idx.extend(range(h * 64, (h + 1) * 64))
    return np.array(idx)


def _partner():
    p = np.zeros(64, np.int64)
    for d in range(64):
        p[d] = d + 16 if (d % 32) < 16 else d - 16
    return p


def _consts():
    cst = np.zeros((128, 5, 128), np.float32)
    j = np.arange(128)[:, None]
    i = np.arange(128)[None, :]
    cst[:, 0] = (j == i)
    cst[:, 1] = (j >= i)
    cst[:, 2] = (j <= i)
    cst[:, 3] = ((j // 16) <= (i // 16))
    cst[:, 4] = ((j // 16) >= (i // 16))
    t = np.arange(TS)
    row = (t // 64).astype(np.float64)
    colp = (t % 64).astype(np.float64)
    inv = 1.0 / (10000.0 ** (np.arange(16, dtype=np.float64) / 16))
    cos = np.zeros((64, TS)); sin = np.zeros((64, TS))
    for d in range(64):
        pos = row if d < 32 else colp
        ang = (pos.astype(np.float32) * np.float32(inv[d % 16]).astype(np.float32)).astype(np.float32)
        cos[d] = np.cos(ang)
        sgn = -1.0 if (d % 32) < 16 else 1.0
        sin[d] = sgn * np.sin(ang)
    rope = np.zeros((2, 128, TS), np.float32)
    rope[0, :64] = cos; rope[0, 64:] = cos
    rope[1, :64] = sin; rope[1, 64:] = sin
    return cst.reshape(128, 640), rope


_CACHE = {}


def kernel(**inp):
    f = lambda a: np.ascontiguousarray(np.asarray(a, dtype=np.float32))
    if "K" not in _CACHE:
        _CACHE["K"] = build()
    K = _CACHE["K"]
    pq = _perm_q()
    part = _partner()
    w_in = f(inp["w_in"])
    q_cols = pq
    qs_cols = np.array([(c // 64) * 64 + part[c % 64] for c in pq])
    k_cols = 512 + np.arange(128)
    ks_cols = 512 + np.array([(c // 64) * 64 + part[c % 64] for c in range(128)])
    rest = np.arange(640, 5376)
    cols = np.concatenate([q_cols, k_cols, qs_cols, ks_cols, rest])
    w_in_p = np.ascontiguousarray(w_in[:, :, cols])
    w_o_attn_p = np.ascontiguousarray(f(inp["w_o_attn"])[:, pq, :])
    cst, rope = _consts()
    shared = {k: f(inp[k]) for k in ("w_mod", "b_mod", "g_pre", "g_post", "w_ffn_gate", "w_ffn_up", "w_ffn_down", "w_conv",
                                     "b_conv", "w_lru_a", "b_lru_a", "w_lru_x", "b_lru_x", "lru_lambda", "s5_lambda_re",
                                     "s5_lambda_im", "s5_log_step", "s5_b_re", "s5_b_im", "s5_c_re", "s5_c_im", "s5_d",
                                     "w_glu", "attn_sink", "w_o_lru", "w_out")}
    shared["w_in_p"] = w_in_p
    shared["w_o_attn_p"] = w_o_attn_p
    shared["cst"] = cst
    shared["rope"] = rope
    xs = f(inp["x_sample"]); xp = f(inp["x_prompt"]); c = f(inp["c"]); cctx = f(inp["c_ctx"])
    ck = f(inp["cache_k"]); cv = f(inp["cache_v"]); sl = f(inp["state_lru"]); ss = f(inp["state_ssm"])
    in_maps = []
    for b in range(8):
        m = dict(shared)
        m["xin"] = np.ascontiguousarray(np.concatenate([xs[b], xp[4 * b:4 * b + 4].reshape(NPB * TP, D)], axis=0))
        m["cc"] = np.ascontiguousarray(np.stack([c[b], cctx], axis=0))
        m["cache_k"] = np.ascontiguousarray(ck[b].reshape(L, 512, 128))
        m["cache_v"] = np.ascontiguousarray(cv[b].reshape(L, 512, 128))
        m["state_lru"] = np.ascontiguousarray(sl[b])
        m["state_ssm"] = np.ascontiguousarray(ss[b])
        in_maps.append(m)
    res = run_bass_kernel_spmd(K.nc, in_maps, core_ids=list(range(8)))
    _CACHE["res"] = res
    R = res.results
    y_s = np.stack([R[b]["y"][:TS] for b in range(8)], axis=0)
    y_p = np.concatenate([R[b]["y"][TS:].reshape(NPB, TP, D) for b in range(8)], axis=0)
    nk = np.concatenate([R[b]["nk"].reshape(NPB, L, TP, 2, 64) for b in range(8)], axis=0)
    nv = np.concatenate([R[b]["nv"].reshape(NPB, L, TP, 2, 64) for b in range(8)], axis=0)
    nl = np.concatenate([R[b]["nlru"] for b in range(8)], axis=0)
    ns = np.concatenate([R[b]["nssm"] for b in range(8)], axis=0)
    return (y_p.astype(np.float32), y_s.astype(np.float32), nk.astype(np.float32), nv.astype(np.float32),
            nl.astype(np.float32), ns.astype(np.float32))
```

```python
import math
import contextlib
import numpy as np
import concourse.bass as bass
import concourse.mybir as mybir
from concourse.bass_utils import run_bass_kernel_spmd

F32 = mybir.dt.float32
BF16 = mybir.dt.bfloat16
AF = mybir.ActivationFunctionType
ALU = mybir.AluOpType
AX = mybir.AxisListType

NDMA_SEM = 8
L = 2
D = 1024
TS = 4096
TP = 256
NPB = 4
NTOK = TS + NPB * TP
TT = 512
NT = NTOK // TT
DFF = 2816
NFC = DFF // 128
WINP = 6016
C_Q, C_K, C_QS, C_KS, C_V, C_XL, C_YL, C_U, C_G = 0, 512, 640, 1152, 1280, 1408, 1920, 2432, 2944
ARENA_F = 52352
EPS = 1e-6


class Buf:
    __slots__ = ("w", "r")

    def __init__(self):
        self.w = {}
        self.r = {}


class Op:
    __slots__ = ("eng", "fn", "waits", "idx", "needed", "semval", "is_dma", "slot")

    def __init__(self, eng, fn):
        self.eng = eng
        self.fn = fn
        self.waits = {}
        self.needed = False
        self.semval = 0
        self.is_dma = False
        self.slot = 0


class Prog:
    ENGS = ("pe", "act", "dve", "pool", "sp")

    def __init__(self, nc):
        self.nc = nc
        self.ops = {e: [] for e in self.ENGS}
        self.ndma = {e: 0 for e in self.ENGS}
        self.pending = {e: {} for e in self.ENGS}
        self.mute = False

    def _add(self, eng, fn, reads, writes, is_dma):
        if self.mute:
            return None
        op = Op(eng, fn)
        op.is_dma = is_dma
        lst = self.ops[eng]
        op.idx = len(lst)
        lst.append(op)
        deps = op.waits
        if self.pending[eng]:
            deps.update(self.pending[eng])
            self.pending[eng] = {}
        if is_dma:
            didx = self.ndma[eng]
            self.ndma[eng] += 1
            op.slot = didx
            pkey = ("d", eng, didx % NDMA_SEM)
            pidx = didx
            if didx >= NDMA_SEM and deps.get(pkey, -1) < didx - NDMA_SEM:
                deps[pkey] = didx - NDMA_SEM
        else:
            pkey = ("c", eng)
            pidx = op.idx
        for b in reads:
            for k, v in b.w.items():
                if deps.get(k, -1) < v:
                    deps[k] = v
        for b in writes:
            for k, v in b.w.items():
                if deps.get(k, -1) < v:
                    deps[k] = v
            for k, v in b.r.items():
                if deps.get(k, -1) < v:
                    deps[k] = v
        for b in reads:
            if b.r.get(pkey, -1) < pidx:
                b.r[pkey] = pidx
        for b in writes:
            b.w = {pkey: pidx}
            b.r = {}
        if eng == "pe" and not is_dma:
            deps.pop(("c", "pe"), None)
        return op

    def op(self, eng, fn, reads=(), writes=()):
        return self._add(eng, fn, reads, writes, False)

    def dma(self, eng, fn, reads=(), writes=()):
        return self._add(eng, fn, reads, writes, True)

    def dma_group(self, eng, fns, reads=(), writes=()):
        if self.mute:
            return
        writes = list(writes)
        snap = [(dict(b.w), dict(b.r)) for b in writes]
        acc = [dict() for _ in writes]
        for fn in fns:
            for b, (w, r) in zip(writes, snap):
                b.w = dict(w)
                b.r = dict(r)
            self._add(eng, fn, reads, writes, True)
            for b, nw in zip(writes, acc):
                nw.update(b.w)
        for b, nw in zip(writes, acc):
            b.w = nw
            b.r = {}

    def barrier(self):
        deps = {}
        for e in self.ENGS:
            last = None
            for op in reversed(self.ops[e]):
                if not op.is_dma:
                    last = op.idx
                    break
            if last is not None:
                deps[("c", e)] = last
            n = self.ndma[e]
            for s in range(NDMA_SEM):
                if n > s:
                    li = ((n - 1 - s) // NDMA_SEM) * NDMA_SEM + s
                    deps[("d", e, s)] = li
        for e in self.ENGS:
            p = self.pending[e]
            for k, v in deps.items():
                if p.get(k, -1) < v:
                    p[k] = v

    def emit(self):
        nc = self.nc
        for e in self.ENGS:
            seen = {}
            for op in self.ops[e]:
                new = {}
                for k, v in op.waits.items():
                    if seen.get(k, -1) >= v:
                        continue
                    seen[k] = v
                    new[k] = v
                op.waits = new
        for e in self.ENGS:
            for op in self.ops[e]:
                for k, v in op.waits.items():
                    if k[0] == "c":
                        self.ops[k[1]][v].needed = True
        for e in self.ENGS:
            c = 0
            for op in self.ops[e]:
                if op.is_dma:
                    continue
                if op.needed:
                    c += 1
                op.semval = c
        handles = {"pe": "tensor", "act": "scalar", "dve": "vector", "pool": "gpsimd", "sp": "sync"}
        with contextlib.ExitStack() as st:
            csem = {e: st.enter_context(nc.semaphore("c_" + e)) for e in self.ENGS}
            dsem = {e: [st.enter_context(nc.semaphore("d_%s_%d" % (e, i))) for i in range(NDMA_SEM)]
                    for e in self.ENGS if self.ndma[e]}
            block = st.enter_context(nc.Block())
            prog = self

            def run(e, eng):
                for op in prog.ops[e]:
                    for k, v in op.waits.items():
                        if k[0] == "c":
                            eng.wait_ge(csem[k[1]], prog.ops[k[1]][v].semval)
                        else:
                            eng.wait_ge(dsem[k[1]][k[2]], 16 * (v // NDMA_SEM + 1))
                    ins = op.fn(eng)
                    if op.is_dma:
                        ins.then_inc(dsem[e][op.slot % NDMA_SEM], 16)
                    elif op.needed:
                        ins.then_inc(csem[e], 1)

            for e in self.ENGS:
                if not self.ops[e]:
                    continue
                getattr(block, handles[e])(lambda eng, e=e: run(e, eng))


def mkap(t, offset, pairs):
    return bass.AP(t, offset, [list(p) for p in pairs])


def view(ap2, shape):
    if len(shape) == 1:
        return ap2
    names = " ".join("a%d" % i for i in range(len(shape)))
    kw = {"a%d" % i: s for i, s in enumerate(shape)}
    return ap2.rearrange("p (%s) -> p %s" % (names, names), **kw)


class Tl:
    __slots__ = ("ap", "b", "bs")

    def __init__(self, ap, nb=0):
        self.ap = ap
        self.b = Buf()
        self.bs = [Buf() for _ in range(nb)]


class Arena:
    def __init__(self, t, size):
        self.t = t
        self.size = size
        self.top = 0

    def reset(self):
        self.top = 0

    def f32(self, *shape, nb=0):
        n = int(np.prod(shape))
        assert self.top + n <= self.size, ("arena overflow", self.top, n)
        ap = self.t[:, self.top:self.top + n]
        self.top += n
        return Tl(view(ap, shape), nb)

    def bf16(self, *shape, nb=0):
        n = int(np.prod(shape))
        nf = (n + 1) // 2
        assert self.top + nf <= self.size, ("arena overflow", self.top, nf)
        ap = self.t[:, self.top:self.top + nf].bitcast(BF16)[:, 0:n]
        self.top += nf
        return Tl(view(ap, shape), nb)


def bcast_free(ap, n):
    return mkap(ap.tensor, ap.offset, [list(ap.ap[0]), [0, n]])


class Kern:
    def __init__(self, dbg=False):
        self.dbg = dbg
        nc = self.nc = bass.Bass("TRN2", target_bir_lowering=False)
        self.P = Prog(nc)
        self.st = contextlib.ExitStack()
        I = self.I = {}
        O = self.O = {}

        def inp(name, shape, dt=F32):
            I[name] = nc.dram_tensor(name, list(shape), dt, kind="ExternalInput").ap()

        def outp(name, shape, dt=F32):
            O[name] = nc.dram_tensor(name, list(shape), dt, kind="ExternalOutput").ap()

        inp("xin", [NTOK, D]); inp("cc", [2, D]); inp("cache_k", [L, 512, 128]); inp("cache_v", [L, 512, 128])
        inp("state_lru", [L, 2, 512]); inp("state_ssm", [L, 2, 2, 32, 64])
        inp("w_mod", [L, D, 9 * D]); inp("b_mod", [L, 9 * D]); inp("g_pre", [L, 3, D]); inp("g_post", [L, 3, D])
        inp("w_ffn_gate", [L, 2, D, DFF]); inp("w_ffn_up", [L, 2, D, DFF]); inp("w_ffn_down", [L, 2, DFF, D])
        inp("w_in_p", [L, D, WINP]); inp("w_conv", [L, 4, 512]); inp("b_conv", [L, 512])
        inp("w_lru_a", [L, 2, 8, 64, 64]); inp("b_lru_a", [L, 2, 512]); inp("w_lru_x", [L, 2, 8, 64, 64])
        inp("b_lru_x", [L, 2, 512]); inp("lru_lambda", [L, 2, 512])
        inp("s5_lambda_re", [L, 2, 32, 64]); inp("s5_lambda_im", [L, 2, 32, 64]); inp("s5_log_step", [L, 2, 32])
        inp("s5_b_re", [L, 2, 32, 64, 16]); inp("s5_b_im", [L, 2, 32, 64, 16])
        inp("s5_c_re", [L, 2, 32, 16, 64]); inp("s5_c_im", [L, 2, 32, 16, 64]); inp("s5_d", [L, 512])
        inp("w_glu", [L, 512, 2048]); inp("attn_sink", [L, 8]); inp("w_o_lru", [L, 512, D])
        inp("w_o_attn_p", [L, 512, D]); inp("w_out", [L, D, D])
        inp("cst", [128, 5 * 128]); inp("rope", [2, 128, TS])
        outp("y", [NTOK, D]); outp("nk", [NPB, L, TP, 128]); outp("nv", [NPB, L, TP, 128])
        outp("nlru", [NPB, L, 2, 512]); outp("nssm", [NPB, L, 2, 2, 32, 64])
        self.obufs = {k: Buf() for k in O}
        if dbg:
            outp("d_sc", [128, L * 144])

        def scr(name, shape, dt):
            kind = "ExternalOutput" if dbg else "Internal"
            t = nc.dram_tensor(name, list(shape), dt, kind=kind).ap()
            return Tl(t)

        self.XT = scr("s_xt", [8, 128, NTOK], F32)
        self.Qs = scr("s_q", [4, 128, NTOK], BF16)
        self.Ks = scr("s_k", [128, NTOK], BF16)
        self.Vs = scr("s_v", [NTOK, 130], BF16)
        self.XLs = scr("s_xl", [4, 128, NTOK], F32)
        self.YLs = scr("s_yl", [4, 128, NTOK], BF16)
        self.UFs = scr("s_uf", [NT, 128, 32 * 64], BF16)
        self.Gs = scr("s_g", [24, 128, NTOK], BF16)
        self.ATs = scr("s_att", [4, 128, NTOK], BF16)
        self.LRs = scr("s_lru", [4, 128, NTOK], BF16)
        self.SYs = scr("s_s5y", [4, 128, NTOK], BF16)
        self.PFd = [scr("s_pf%d" % l, [128, 2 * 16 * 2 * 128], BF16) for l in range(L)]
        self.Qd = [scr("s_qd%d" % l, [128, 2 * 16 * 2 * 128], BF16) for l in range(L)]
        self.MLd = [scr("s_ml%d" % l, [128, 32 * 128], BF16) for l in range(L)]
        self.A12 = [scr("s_a12%d" % l, [128, 128], F32) for l in range(L)]

        self.sb_t = self.st.enter_context(nc.sbuf_tensor("arena", [128, ARENA_F], F32))
        self.pc_t = self.st.enter_context(nc.sbuf_tensor("persist", [128, 832], F32))
        self.ps_t = self.st.enter_context(nc.psum_tensor("psum", [128, 4096], F32))
        self.A = Arena(self.sb_t, ARENA_F)
        self.PA = Arena(self.pc_t, 832)
        self.pbufs = [Buf() for _ in range(8)]

    def bank(self, i):
        return self.ps_t[:, i * 512:(i + 1) * 512]

    def mm(self, out, lhsT, rhs, start, stop, reads, writes):
        self.P.op("pe", lambda e: e.matmul(out, lhsT=lhsT, rhs=rhs, start=start, stop=stop), reads, writes)

    def tr(self, out, in_, ident, reads, writes):
        self.P.op("pe", lambda e: e.transpose(out, in_, ident), reads, writes)

    def act(self, out, in_, func, reads, writes, scale=1.0, bias=0.0):
        self.P.op("act", lambda e: e.activation(out=out, in_=in_, func=func, scale=scale, bias=bias), reads, writes)

    def tt(self, eng, out, in0, in1, op, reads, writes):
        self.P.op(eng, lambda e: e.tensor_tensor(out=out, in0=in0, in1=in1, op=op), reads, writes)

    def ts(self, eng, out, in0, s1, s2, op0, op1, reads, writes):
        if s2 is None:
            self.P.op(eng, lambda e: e.tensor_scalar(out=out, in0=in0, scalar1=s1, scalar2=None, op0=op0), reads, writes)
        else:
            self.P.op(eng, lambda e: e.tensor_scalar(out=out, in0=in0, scalar1=s1, scalar2=s2, op0=op0, op1=op1), reads, writes)

    def stt(self, out, in0, scalar, in1, op0, op1, reads, writes):
        self.P.op("dve", lambda e: e.scalar_tensor_tensor(out=out, in0=in0, scalar=scalar, in1=in1, op0=op0, op1=op1), reads, writes)

    def cp(self, eng, out, in_, reads, writes):
        if eng == "act":
            self.P.op("act", lambda e: e.copy(out=out, in_=in_), reads, writes)
        else:
            self.P.op(eng, lambda e: e.tensor_copy(out=out, in_=in_), reads, writes)

    def ms(self, eng, out, val, writes):
        self.P.op(eng, lambda e: e.memset(out, val), (), writes)

    def dma(self, q, out, in_, reads, writes, slow=False):
        assert not slow
        shp = tuple(out.shape)
        if len(shp) >= 3 and shp[0] * shp[1] > 256 and shp[1] > 1 and tuple(in_.shape)[:2] == shp[:2]:
            step = max(1, 256 // shp[0])
            fns = []
            for a in range(0, shp[1], step):
                e_ = min(a + step, shp[1])
                o_ = out[:, a:e_]
                i_ = in_[:, a:e_]
                fns.append(lambda e, o_=o_, i_=i_: e.dma_start(out=o_, in_=i_))
            self.P.dma_group(q, fns, reads, writes)
            return
        self.P.dma(q, lambda e: e.dma_start(out=out, in_=in_), reads, writes)

    def vecT(self, dst, src_rows, n, writes):
        stg = self.A.f32(128)
        self.P.dma("sp", lambda e: e.dma_start(out=stg.ap[0:n, :], in_=src_rows), [], [stg.b])
        self.tr(self.bank(7)[:, 0:n], stg.ap[0:n, :], self.identf.ap[0:n, 0:n], [stg.b, self.identf.b], [self.pbufs[7]])
        return self.bank(7)[:, 0:n], self.pbufs[7]

    def recip(self, out, in_, reads, writes):
        self.P.op("dve", lambda e: e.reciprocal(out=out, in_=in_), reads, writes)

    def phase(self):
        self.P.barrier()
        self.A.reset()
        self.pbufs = [Buf() for _ in range(8)]

    def prologue(self):
        A, PA, I = self.A, self.PA, self.I
        cst = A.f32(5, 128)
        self.dma("sp", cst.ap, I["cst"].rearrange("p (a b) -> p a b", a=5), [], [cst.b])
        self.identf = PA.f32(128)
        self.identb = PA.bf16(128)
        self.onesb = PA.bf16(128)
        self.onesf = PA.f32(128)
        self.mprev = PA.bf16(128)
        self.mnext = PA.bf16(128)
        self.cp("dve", self.identf.ap, cst.ap[:, 0, :], [cst.b], [self.identf.b])
        self.cp("dve", self.identb.ap, cst.ap[:, 0, :], [cst.b], [self.identb.b])
        self.cp("dve", self.mprev.ap, cst.ap[:, 1, :], [cst.b], [self.mprev.b])
        self.cp("dve", self.mnext.ap, cst.ap[:, 2, :], [cst.b], [self.mnext.b])
        self.ms("pool", self.onesb.ap, 1.0, [self.onesb.b])
        self.ms("pool", self.onesf.ap, 1.0, [self.onesf.b])
        self.SC = PA.f32(L, 3, 3, 8, 2)
        ccT = A.f32(8, 2)
        src, sb_ = self.vecT(None, I["cc"].rearrange("w (kc p) -> (w kc) p", p=128), 16, None)
        self.cp("dve", ccT.ap.rearrange("p kc w -> p w kc"), view(src, (2, 8)), [sb_], [ccT.b])
        sg = A.f32(8, 2)
        self.act(sg.ap, ccT.ap, AF.Sigmoid, [ccT.b], [sg.b])
        self.tt("dve", ccT.ap, ccT.ap, sg.ap, ALU.mult, [ccT.b, sg.b], [ccT.b])
        wm = [A.f32(8, 1152), A.f32(8, 1152)]
        modt = A.f32(L, 72, 2)
        bm = A.f32(L, 72)
        gp = A.f32(L, 3, 8)
        gq = A.f32(L, 3, 8)
        for l in range(L):
            src, sb_ = self.vecT(None, I["b_mod"][l].rearrange("(c p) -> c p", p=128), 72, None)
            self.cp("dve", bm.ap[:, l, :], src, [sb_], [bm.b])
            src, sb_ = self.vecT(None, I["g_pre"][l].rearrange("i (c p) -> (i c) p", p=128), 24, None)
            self.cp("dve", gp.ap[:, l].rearrange("p i c -> p (i c)"), src, [sb_], [gp.b])
            src, sb_ = self.vecT(None, I["g_post"][l].rearrange("i (c p) -> (i c) p", p=128), 24, None)
            self.cp("dve", gq.ap[:, l].rearrange("p i c -> p (i c)"), src, [sb_], [gq.b])
        pm = self.bank(0)
        pmb = self.pbufs[0]
        k = 0
        for l in range(L):
            for piece in range(8):
                w_ = wm[k % 2]
                q = "sp" if k % 2 == 0 else "act"
                k += 1
                src = I["w_mod"][l][:, piece * 1152:(piece + 1) * 1152].rearrange("(kc p) c -> p kc c", p=128)
                self.dma(q, w_.ap, src, [], [w_.b])
                for cch in range(9):
                    col = (l * 72 + piece * 9 + cch) * 2
                    for kc in range(8):
                        self.mm(pm[:, col:col + 2], w_.ap[:, kc, cch * 128:(cch + 1) * 128], ccT.ap[:, kc, :],
                                kc == 0, kc == 7, [w_.b, ccT.b], [pmb])
        self.cp("dve", modt.ap, view(pm[:, 0:L * 144], (L, 72, 2)), [pmb], [modt.b])
        for w in range(2):
            self.tt("dve", modt.ap[:, :, :, w], modt.ap[:, :, :, w], bm.ap, ALU.add, [modt.b, bm.b], [modt.b])
        for l in range(L):
            m5 = modt.ap[:, l].rearrange("p (i k c) w -> p i k c w", i=3, k=3)
            for w in range(2):
                self.stt(self.SC.ap[:, l, 0, :, :, w], m5[:, :, 1, :, w], 1.0, gp.ap[:, l], ALU.add, ALU.mult,
                         [modt.b, gp.b], [self.SC.b])
                self.cp("dve", self.SC.ap[:, l, 1, :, :, w], m5[:, :, 0, :, w], [modt.b], [self.SC.b])
                self.tt("dve", self.SC.ap[:, l, 2, :, :, w], m5[:, :, 2, :, w], gq.ap[:, l], ALU.mult,
                        [modt.b, gq.b], [self.SC.b])
            for i in (0, 2):
                self.ts("dve", self.SC.ap[:, l, 2, i], self.SC.ap[:, l, 2, i], 0.5, None, ALU.mult, None,
                        [self.SC.b], [self.SC.b])
        if self.dbg:
            self.dma("sp", self.O["d_sc"], self.SC.ap.rearrange("p l k i c w -> p (l k i c w)"), [self.SC.b], [])
        for l in range(L):
            self.s5_prep(l, cst)

    def scal(self, l, kind, i, c, w):
        return self.SC.ap[:, l, kind, i, c, w:w + 1]

    def s5_prep(self, l, cst_unused=None):
        self.phase()
        A, I = self.A, self.I
        cst = A.f32(5, 128)
        self.dma("sp", cst.ap, I["cst"].rearrange("p (a b) -> p a b", a=5), [], [cst.b])
        idf = self.identf
        pb = self.pbufs
        lre_r = A.f32(128); lim_r = A.f32(128); lst_r = A.f32(2); lst_x = A.f32(128)
        self.dma("sp", lre_r.ap[0:32, :], I["s5_lambda_re"][l].rearrange("d (gp g2) n -> (d gp) (g2 n)", g2=2), [], [lre_r.b])
        self.dma("sp", lim_r.ap[0:32, :], I["s5_lambda_im"][l].rearrange("d (gp g2) n -> (d gp) (g2 n)", g2=2), [], [lim_r.b])
        self.dma("sp", lst_r.ap[0:32, :], I["s5_log_step"][l].rearrange("d (gp g2) -> (d gp) g2", g2=2), [], [lst_r.b])
        a_ = lst_r.ap[0:32, :]
        self.cp("dve", view(lst_x.ap[0:32, :], (2, 64)), mkap(a_.tensor, a_.offset, [list(a_.ap[0]), [1, 2], [0, 64]]),
                [lst_r.b], [lst_x.b])
        RR = A.f32(8192)
        braw = [Tl(view(RR.ap[:, c * 2048:(c + 1) * 2048], (128, 16))) for c in range(2)]
        craw = [Tl(view(RR.ap[:, (2 + c) * 2048:(3 + c) * 2048], (16, 2, 64))) for c in range(2)]
        for c, nm in enumerate(("s5_b_re", "s5_b_im")):
            self.dma("act", braw[c].ap[0:32], I[nm][l].rearrange("d (gp g2) n ci -> (d gp) (g2 n) ci", g2=2), [], [braw[c].b])
        for c, nm in enumerate(("s5_c_re", "s5_c_im")):
            srcv = I[nm][l].rearrange("d (gp g2) co n -> (d gp) g2 co n", g2=2)
            for g2 in range(2):
                self.dma("act", craw[c].ap[0:32, :, g2, :], srcv[:, g2], [], [craw[c].b])
        draw = A.f32(16); dx = A.f32(8, 16)
        self.dma("sp", draw.ap[0:32, :], I["s5_d"][l].rearrange("(g ci) -> g ci", ci=16), [], [draw.b])
        a_ = draw.ap[0:32, :]
        self.cp("dve", dx.ap[0:32], mkap(a_.tensor, a_.offset, [list(a_.ap[0]), [0, 8], [1, 16]]), [draw.b], [dx.b])
        sc = A.f32(40, 32)
        names = {}

        def S(nm):
            if nm not in names:
                names[nm] = len(names)
                assert len(names) <= 40
            return sc.ap[:, names[nm], :]

        scb = sc.b
        pt = self.bank(0)
        self.tr(pt[:, 0:32], lre_r.ap[0:32, :], idf.ap[0:32, 0:32], [lre_r.b, idf.b], [pb[0]])
        self.tr(pt[:, 32:64], lim_r.ap[0:32, :], idf.ap[0:32, 0:32], [lim_r.b, idf.b], [pb[0]])
        self.tr(pt[:, 64:96], lst_x.ap[0:32, :], idf.ap[0:32, 0:32], [lst_x.b, idf.b], [pb[0]])
        self.tr(pt[:, 96:128], dx.ap[0:32].rearrange("p a b -> p (a b)"), idf.ap[0:32, 0:32], [dx.b, idf.b], [pb[0]])
        self.cp("dve", S("lr"), pt[:, 0:32], [pb[0]], [scb])
        self.cp("dve", S("li"), pt[:, 32:64], [pb[0]], [scb])
        self.cp("dve", S("ls"), pt[:, 64:96], [pb[0]], [scb])
        dcol = A.f32(32)
        self.cp("dve", dcol.ap, pt[:, 96:128], [pb[0]], [dcol.b])
        BC = []
        for idx, raw in enumerate(braw + craw):
            bk = self.bank(1 + idx % 2)
            bb = pb[1 + idx % 2]
            for j in range(16):
                if idx < 2:
                    src = raw.ap[0:32, :, j]
                else:
                    src = raw.ap[0:32, j].rearrange("p a b -> p (a b)")
                self.tr(bk[:, j * 32:(j + 1) * 32], src, idf.ap[0:32, 0:32], [raw.b, idf.b], [bb])
            t = A.f32(16, 32)
            self.cp("act", t.ap, view(bk, (16, 32)), [bb], [t.b])
            BC.append(t)
        Bre, Bim, Cre, Cim = BC

        def dv(out, a, b, op):
            self.tt("dve", out, a, b, op, [scb], [scb])

        self.ts("dve", S("lr"), S("lr"), -1e-4, None, ALU.min, None, [scb], [scb])
        self.act(S("step"), S("ls"), AF.Exp, [scb], [scb])
        dv(S("xre"), S("lr"), S("step"), ALU.mult)
        dv(S("ang"), S("li"), S("step"), ALU.mult)
        self.act(S("mag"), S("xre"), AF.Exp, [scb], [scb])
        hp = A.f32(1)
        self.ms("dve", hp.ap, math.pi / 2, [hp.b])
        self.act(S("s"), S("ang"), AF.Sin, [scb], [scb], scale=1.0 / 16)
        self.act(S("c"), S("ang"), AF.Sin, [scb, hp.b], [scb], scale=-1.0 / 16, bias=hp.ap[:, 0:1])
        for _ in range(4):
            dv(S("t1"), S("c"), S("c"), ALU.mult)
            dv(S("t2"), S("s"), S("s"), ALU.mult)
            dv(S("t3"), S("c"), S("s"), ALU.mult)
            dv(S("c"), S("t1"), S("t2"), ALU.subtract)
            self.ts("dve", S("s"), S("t3"), 2.0, None, ALU.mult, None, [scb], [scb])
        dv(S("are"), S("mag"), S("c"), ALU.mult)
        dv(S("aim"), S("mag"), S("s"), ALU.mult)
        dv(S("t1"), S("lr"), S("lr"), ALU.mult)
        dv(S("t2"), S("li"), S("li"), ALU.mult)
        dv(S("den"), S("t1"), S("t2"), ALU.add)
        self.recip(S("rden"), S("den"), [scb], [scb])
        self.ts("dve", S("nre"), S("are"), -1.0, None, ALU.add, None, [scb], [scb])
        dv(S("t1"), S("nre"), S("lr"), ALU.mult)
        dv(S("t2"), S("aim"), S("li"), ALU.mult)
        dv(S("t1"), S("t1"), S("t2"), ALU.add)
        dv(S("cre"), S("t1"), S("rden"), ALU.mult)
        dv(S("t1"), S("aim"), S("lr"), ALU.mult)
        dv(S("t2"), S("nre"), S("li"), ALU.mult)
        dv(S("t1"), S("t1"), S("t2"), ALU.subtract)
        dv(S("cim"), S("t1"), S("rden"), ALU.mult)
        dv(S("t1"), S("mag"), S("mag"), ALU.mult)
        self.recip(S("t2"), S("t1"), [scb], [scb])
        dv(S("iare"), S("are"), S("t2"), ALU.mult)
        dv(S("t3"), S("aim"), S("t2"), ALU.mult)
        self.ts("dve", S("iaim"), S("t3"), -1.0, None, ALU.mult, None, [scb], [scb])
        pw = A.f32(9, 2, 32)
        self.ms("dve", pw.ap[:, 0, 0, :], 1.0, [pw.b])
        self.ms("dve", pw.ap[:, 0, 1, :], 0.0, [pw.b])
        for j in range(1, 9):
            for (o, x1, y1, x2, y2, op) in ((pw.ap[:, j, 0, :], pw.ap[:, j - 1, 0, :], S("are"), pw.ap[:, j - 1, 1, :], S("aim"), ALU.subtract),
                                            (pw.ap[:, j, 1, :], pw.ap[:, j - 1, 0, :], S("aim"), pw.ap[:, j - 1, 1, :], S("are"), ALU.add)):
                self.tt("dve", S("t1"), x1, y1, ALU.mult, [pw.b, scb], [scb])
                self.tt("dve", S("t2"), x2, y2, ALU.mult, [pw.b, scb], [scb])
                self.tt("dve", o, S("t1"), S("t2"), op, [scb], [pw.b])
        a12 = A.f32(2, 2, 16, 2)
        for d in range(2):
            for c in range(2):
                self.cp("dve", a12.ap[:, d, 0, :, c], pw.ap[:, 8, 0, d * 16:(d + 1) * 16], [pw.b], [a12.b])
            self.ts("dve", a12.ap[:, d, 1, :, 0], pw.ap[:, 8, 1, d * 16:(d + 1) * 16], -1.0, None, ALU.mult, None, [pw.b], [a12.b])
            self.cp("dve", a12.ap[:, d, 1, :, 1], pw.ap[:, 8, 1, d * 16:(d + 1) * 16], [pw.b], [a12.b])
        self.dma("sp", self.A12[l].ap, a12.ap.rearrange("p d k g c -> p (d k g c)"), [a12.b], [])
        self.cp("dve", S("pr"), S("iare"), [scb], [scb])
        self.cp("dve", S("pi"), S("iaim"), [scb], [scb])
        for _ in range(3):
            dv(S("t1"), S("pr"), S("pr"), ALU.mult)
            dv(S("t2"), S("pi"), S("pi"), ALU.mult)
            dv(S("t3"), S("pr"), S("pi"), ALU.mult)
            dv(S("pr"), S("t1"), S("t2"), ALU.subtract)
            self.ts("dve", S("pi"), S("t3"), 2.0, None, ALU.mult, None, [scb], [scb])
        Bbr = A.f32(16, 32); Bbi = A.f32(16, 32); T1 = A.f32(16, 32); T2 = A.f32(16, 32)

        def bc16(ap):
            return mkap(ap.tensor, ap.offset, [list(ap.ap[0]), [0, 16], [1, 32]])

        for (o, x1, x2, op) in ((Bbr, Bre, Bim, ALU.subtract), (Bbi, Bim, Bre, ALU.add)):
            self.tt("dve", T1.ap, x1.ap, bc16(S("cre")), ALU.mult, [x1.b, scb], [T1.b])
            self.tt("dve", T2.ap, x2.ap, bc16(S("cim")), ALU.mult, [x2.b, scb], [T2.b])
            self.tt("dve", o.ap, T1.ap, T2.ap, op, [T1.b, T2.b], [o.b])
        XS = A.f32(2, 16, 2, 8, 16)
        Q = A.f32(2, 16, 2, 8, 16)
        tmp = [[A.f32(16, 16), A.f32(16, 16)] for _ in range(2)]

        def pwv(j, c, d):
            a = pw.ap[:, j, c, d * 16:(d + 1) * 16]
            return mkap(a.tensor, a.offset, [list(a.ap[0]), [1, 16], [0, 16]])

        def mat(tl, d):
            return tl.ap[:, :, d * 16:(d + 1) * 16].rearrange("p x g -> p g x")

        k = 0
        for d in range(2):
            for s in range(8):
                for (dst, Mr, Mi, j, neg_im) in ((XS, Bbr, Bbi, (7 - s) if d == 0 else s, False),
                                                 (Q, Cre, Cim, (s + 1) if d == 0 else (8 - s), True)):
                    eng = "dve" if k % 2 == 0 else "pool"
                    t1, t2 = tmp[k % 2]
                    k += 1
                    rd = [Mr.b, Mi.b, pw.b]
                    self.tt(eng, t1.ap, mat(Mr, d), pwv(j, 0, d), ALU.mult, rd, [t1.b])
                    self.tt(eng, t2.ap, mat(Mi, d), pwv(j, 1, d), ALU.mult, rd, [t2.b])
                    self.tt(eng, dst.ap[:, d, :, 0, s, :], t1.ap, t2.ap, ALU.subtract, [t1.b, t2.b], [dst.b])
                    self.tt(eng, t1.ap, mat(Mr, d), pwv(j, 1, d), ALU.mult, rd, [t1.b])
                    self.tt(eng, t2.ap, mat(Mi, d), pwv(j, 0, d), ALU.mult, rd, [t2.b])
                    self.tt(eng, dst.ap[:, d, :, 1, s, :], t1.ap, t2.ap, ALU.add, [t1.b, t2.b], [dst.b])
        qim = Q.ap[:, :, :, 1].rearrange("p d g s c -> p (d g) (s c)")
        self.ts("pool", qim, qim, -1.0, None, ALU.mult, None, [Q.b], [Q.b])
        XM = A.f32(2, 16, 2, 128)
        X4 = XS.ap.rearrange("p d g c s i -> p d g c (s i)")
        big = [A.f32(16, 128), A.f32(16, 128)]

        def pv(nm, d):
            a = S(nm)[:, d * 16:(d + 1) * 16]
            return mkap(a.tensor, a.offset, [list(a.ap[0]), [1, 16], [0, 128]])

        for d in range(2):
            eng = "dve" if d == 0 else "pool"
            t1, t2 = big
            rd = [XS.b, scb]
            self.tt(eng, t1.ap, X4[:, d, :, 0, :], pv("pr", d), ALU.mult, rd, [t1.b])
            self.tt(eng, t2.ap, X4[:, d, :, 1, :], pv("pi", d), ALU.mult, rd, [t2.b])
            self.tt(eng, XM.ap[:, d, :, 0, :], t1.ap, t2.ap, ALU.subtract, [t1.b, t2.b], [XM.b])
            self.tt(eng, t1.ap, X4[:, d, :, 1, :], pv("pr", d), ALU.mult, rd, [t1.b])
            self.tt(eng, t2.ap, X4[:, d, :, 0, :], pv("pi", d), ALU.mult, rd, [t2.b])
            self.tt(eng, XM.ap[:, d, :, 1, :], t1.ap, t2.ap, ALU.add, [t1.b, t2.b], [XM.b])
        self.P.barrier()
        PFs = Tl(view(RR.ap[:, 0:4096].bitcast(BF16), (64, 128)))
        Q4 = Q.ap.rearrange("p d g c s i -> p (d g c) (s i)")
        XS3 = XS.ap.rearrange("p d g c s i -> p (d g c) (s i)")
        for q4 in range(16):
            bk = self.bank(3 + q4 % 2); bb = pb[3 + q4 % 2]
            for jj in range(4):
                self.tr(bk[:, jj * 128:(jj + 1) * 128], XS3[:, q4 * 4 + jj, :], idf.ap, [XS.b, idf.b], [bb])
            self.cp("act", PFs.ap[:, q4 * 4:(q4 + 1) * 4, :], view(bk, (4, 128)), [bb], [PFs.b])
        self.dma("sp", self.PFd[l].ap, PFs.ap.rearrange("p a b -> p (a b)"), [PFs.b], [])
        Qb = Tl(view(RR.ap[:, 4096:8192].bitcast(BF16), (64, 128)))
        self.cp("pool", Qb.ap, Q4, [Q.b], [Qb.b])
        self.dma("sp", self.Qd[l].ap, Qb.ap.rearrange("p a b -> p (a b)"), [Qb.b], [])
        MLs = A.bf16(32, 128)
        mt = [A.f32(128), A.f32(128)]
        mu = [A.f32(128), A.f32(128)]
        XM4 = XM.ap
        Q5 = Q.ap.rearrange("p d g c s i -> p d g c (s i)")
        for g in range(32):
            gp_, g2 = g // 2, g % 2
            bk = self.bank(5 + g % 2); bb = pb[5 + g % 2]
            sl = slice(g2 * 64, (g2 + 1) * 64)
            for d in range(2):
                for c in range(2):
                    self.mm(bk[:, d * 128:(d + 1) * 128], XM4[sl, d, gp_, c, :], Q5[sl, d, gp_, c, :], c == 0, c == 1,
                            [XM.b, Q.b], [bb])
            t = mt[g % 2]
            u = mu[g % 2]
            self.tt("dve", t.ap, bk[:, 0:128], cst.ap[:, 3, :], ALU.mult, [bb, cst.b], [t.b])
            self.tt("dve", u.ap, bk[:, 128:256], cst.ap[:, 4, :], ALU.mult, [bb, cst.b], [u.b])
            self.tt("pool", t.ap, t.ap, u.ap, ALU.add, [t.b, u.b], [t.b])
            self.stt(MLs.ap[:, g, :], idf.ap, dcol.ap[:, g:g + 1], t.ap, ALU.mult, ALU.add, [idf.b, dcol.b, t.b], [MLs.b])
        self.dma("sp", self.MLd[l].ap, MLs.ap.rearrange("p a b -> p (a b)"), [MLs.b], [])

    def alloc_norm(self):
        A = self.A
        self.n_rstd = A.f32(TT)
        self.n_tmp = [A.f32(TT), A.f32(TT)]
        self.n_k = 0

    def rstd_from(self, bankidx):
        r = self.n_rstd
        pbk = self.pbufs[bankidx]
        self.ts("dve", r.ap, self.bank(bankidx), 1.0 / D, EPS, ALU.mult, ALU.add, [pbk], [r.b])
        self.act(r.ap, r.ap, AF.Sqrt, [r.b], [r.b])
        self.recip(r.ap, r.ap, [r.b], [r.b])
        return r

    def load_x(self, xt, ti, first=False, alias=None):
        tok0 = ti * TT
        if not first:
            self.dma("sp", xt.ap, self.XT.ap[:, :, tok0:tok0 + TT].rearrange("c p t -> p c t"), [], xt.bs)
            return
        xtm, extra = alias
        self.dma("sp", xtm.ap, self.I["xin"][tok0:tok0 + TT].rearrange("(b p) d -> p b d", p=128), [], [xtm.b] + extra)
        for c in range(8):
            bi = c % 2
            for blk in range(4):
                self.tr(self.bank(bi)[:, blk * 128:(blk + 1) * 128], xtm.ap[:, blk, c * 128:(c + 1) * 128], self.identf.ap,
                        [xtm.b, self.identf.b] + extra, [self.pbufs[bi]])
            self.cp("act" if c % 2 else "dve", xt.ap[:, c, :], self.bank(bi), [self.pbufs[bi]], [xt.bs[c]])

    def store_x(self, xt, ti, last=False, alias=None):
        tok0 = ti * TT
        if not last:
            self.dma("sp", self.XT.ap[:, :, tok0:tok0 + TT].rearrange("c p t -> p c t"), xt.ap, xt.bs, [])
            return
        yst, extra = alias
        for blk in range(4):
            for half in range(2):
                bi = (blk * 2 + half) % 2
                for cc in range(4):
                    c = half * 4 + cc
                    self.tr(self.bank(bi)[:, cc * 128:(cc + 1) * 128], xt.ap[:, c, blk * 128:(blk + 1) * 128], self.identf.ap,
                            [xt.bs[c], self.identf.b], [self.pbufs[bi]])
                self.cp("act" if half else "dve", yst.ap[:, blk, half * 512:(half + 1) * 512], self.bank(bi),
                        [self.pbufs[bi]], [yst.b] + extra)
        self.dma("sp", self.O["y"][tok0:tok0 + TT].rearrange("(b p) d -> p b d", p=128), yst.ap, [yst.b] + extra, [])

    def prenorm(self, l, sub, which, xt, hT, sq, bankidx=7):
        pbk = self.pbufs[bankidx]
        for c in range(8):
            self.tt("pool", sq.ap[:, c, :], xt.ap[:, c, :], xt.ap[:, c, :], ALU.mult, [xt.bs[c]], [sq.b])
        for c in range(8):
            self.mm(self.bank(bankidx), self.onesb.ap, sq.ap[:, c, :], c == 0, c == 7, [self.onesb.b, sq.b], [pbk])
        r = self.rstd_from(bankidx)
        for c in range(8):
            t = self.n_tmp[self.n_k % 2]
            self.n_k += 1
            self.tt("dve", t.ap, xt.ap[:, c, :], r.ap, ALU.mult, [xt.bs[c], r.b], [t.b])
            self.ts("pool", hT.ap[:, c, :], t.ap, self.scal(l, 0, sub, c, which), self.scal(l, 1, sub, c, which),
                    ALU.mult, ALU.add, [t.b, self.SC.b], [hT.b])

    def post_chunk(self, m, pbank, fT, sqr, ssbank):
        pbk = self.pbufs[pbank]
        s = sqr[m % 2]
        self.cp("dve", fT.ap[:, m, :], self.bank(pbank), [pbk], [fT.b])
        self.tt("pool", s.ap, fT.ap[:, m, :], fT.ap[:, m, :], ALU.mult, [fT.b], [s.b])
        if m > 0:
            p_ = sqr[(m - 1) % 2]
            self.mm(self.bank(ssbank), self.onesb.ap, p_.ap, m == 1, False, [self.onesb.b, p_.b], [self.pbufs[ssbank]])
        if m == 7:
            self._last_sq = s

    def post_update(self, l, sub, which, xt, fT, ssbank):
        p_ = self._last_sq
        self.mm(self.bank(ssbank), self.onesb.ap, p_.ap, False, True, [self.onesb.b, p_.b], [self.pbufs[ssbank]])
        r = self.rstd_from(ssbank)
        for m in range(8):
            t = self.n_tmp[self.n_k % 2]
            self.n_k += 1
            self.stt(t.ap, fT.ap[:, m, :], self.scal(l, 2, sub, m, which), r.ap, ALU.mult, ALU.mult,
                     [fT.b, self.SC.b, r.b], [t.b])
            self.tt("pool", xt.ap[:, m, :], xt.ap[:, m, :], t.ap, ALU.add, [xt.bs[m], t.b], [xt.bs[m]])

    def ffn_pass(self, l, i, first=False, last=False):
        self.phase()
        A, I = self.A, self.I
        sub = 0 if i == 0 else 2
        wg = A.bf16(8, DFF, nb=2); wu = A.bf16(8, DFF, nb=2); wd = A.bf16(NFC, D, nb=2)
        for h in range(2):
            cs = slice(h * 1408, (h + 1) * 1408)
            self.dma("pool", wg.ap[:, :, cs], I["w_ffn_gate"][l, i][:, cs].rearrange("(kc p) f -> p kc f", p=128), [], [wg.bs[h]])
            self.dma("pool", wu.ap[:, :, cs], I["w_ffn_up"][l, i][:, cs].rearrange("(kc p) f -> p kc f", p=128), [], [wu.bs[h]])
        wdv = I["w_ffn_down"][l, i].rearrange("(fc p) d -> p fc d", p=128)
        for h in range(2):
            fs = slice(h * 11, (h + 1) * 11)
            self.dma("pool", wd.ap[:, fs, :], wdv[:, fs, :], [], [wd.bs[h]])
        xt = A.f32(8, TT, nb=8)
        r1 = A.f32(8, TT)
        hT = Tl(view(r1.ap.rearrange("p a b -> p (a b)")[:, 0:2048].bitcast(BF16), (8, TT)))
        sq = Tl(view(r1.ap.rearrange("p a b -> p (a b)")[:, 2048:4096].bitcast(BF16), (8, TT)))
        fT = r1
        hT.b = sq.b = fT.b
        actT = A.bf16(NFC, TT, nb=NFC)
        al = Tl(view(actT.ap.rearrange("p a b -> p (a b)")[:, 0:8192].bitcast(F32), (4, D)))
        sqr = [A.bf16(TT), A.bf16(TT)]
        sgr = [A.f32(TT), A.f32(TT)]
        self.alloc_norm()
        pb = self.pbufs
        import os
        nt_ = int(os.environ.get("DBG_NT", NT))
        lvl = int(os.environ.get("DBG_LVL", 9))
        for ti in range(nt_):
            which = 0 if ti < 8 else 1
            self.load_x(xt, ti, first, (al, actT.bs))
            if lvl < 1:
                self.store_x(xt, ti, last, (al, actT.bs))
                continue
            self.prenorm(l, sub, which, xt, hT, sq)
            if lvl < 2:
                self.store_x(xt, ti, last, (al, actT.bs))
                continue
            for j in range(NFC):
                bg, bu = j % 2, 2 + j % 2
                cs = slice(j * 128, (j + 1) * 128)
                for kc in range(8):
                    self.mm(self.bank(bg), wg.ap[:, kc, cs], hT.ap[:, kc, :], kc == 0, kc == 7, [wg.bs[j // 11], hT.b], [pb[bg]])
                for kc in range(8):
                    self.mm(self.bank(bu), wu.ap[:, kc, cs], hT.ap[:, kc, :], kc == 0, kc == 7, [wu.bs[j // 11], hT.b], [pb[bu]])
                s = sgr[j % 2]
                self.act(s.ap, self.bank(bg), AF.Silu, [pb[bg]], [s.b])
                self.tt("dve", actT.ap[:, j, :], s.ap, self.bank(bu), ALU.mult, [s.b, pb[bu]], [actT.bs[j]])
            if lvl < 3:
                self.store_x(xt, ti, last, (al, actT.bs))
                continue
            for m in range(8):
                bf = 4 + m % 2
                for j in range(NFC):
                    self.mm(self.bank(bf), wd.ap[:, j, m * 128:(m + 1) * 128], actT.ap[:, j, :], j == 0, j == NFC - 1,
                            [wd.bs[j // 11], actT.bs[j]], [pb[bf]])
                if lvl >= 4:
                    self.post_chunk(m, bf, fT, sqr, 6)
            if lvl >= 5:
                self.post_update(l, sub, which, xt, fT, 6)
            self.store_x(xt, ti, last, (al, actT.bs))


    def ffn_pass2(self, l, i, first=False, last=False):
        self.phase()
        A, I = self.A, self.I
        TF = 256
        NTF = NTOK // TF
        sub = 0 if i == 0 else 2
        wg = A.bf16(8, DFF, nb=2); wu = A.bf16(8, DFF, nb=2); wd = A.bf16(NFC, D, nb=2)
        for h in range(2):
            cs = slice(h * 1408, (h + 1) * 1408)
            self.dma("pool", wg.ap[:, :, cs], I["w_ffn_gate"][l, i][:, cs].rearrange("(kc p) f -> p kc f", p=128), [], [wg.bs[h]])
            self.dma("pool", wu.ap[:, :, cs], I["w_ffn_up"][l, i][:, cs].rearrange("(kc p) f -> p kc f", p=128), [], [wu.bs[h]])
        wdv = I["w_ffn_down"][l, i].rearrange("(fc p) d -> p fc d", p=128)
        for h in range(2):
            fs = slice(h * 11, (h + 1) * 11)
            self.dma("pool", wd.ap[:, fs, :], wdv[:, fs, :], [], [wd.bs[h]])
        xts = [A.f32(8, TF, nb=8), A.f32(8, TF, nb=8)]
        hTs = [A.bf16(8, TF), A.bf16(8, TF)]
        sq = A.bf16(8, TF)
        fT = A.f32(8, TF)
        actT = A.bf16(NFC, TF, nb=NFC)
        sqr = [A.bf16(TF), A.bf16(TF)]
        sgr = [A.f32(TF), A.f32(TF)]
        stg = A.f32(2, D) if (first or last) else None
        rs_pre = A.f32(TF); rs_post = A.f32(TF)
        tmp_pre = [A.f32(TF), A.f32(TF)]; tmp_post = [A.f32(TF), A.f32(TF)]
        pb = self.pbufs
        bk = lambda b_: self.bank(b_)[:, 0:TF]

        def rstd(r, bankidx):
            self.ts("dve", r.ap, bk(bankidx), 1.0 / D, EPS, ALU.mult, ALU.add, [pb[bankidx]], [r.b])
            self.act(r.ap, r.ap, AF.Sqrt, [r.b], [r.b])
            self.recip(r.ap, r.ap, [r.b], [r.b])

        def load(ti):
            xt = xts[ti % 2]
            tok0 = ti * TF
            if not first:
                self.dma("sp", xt.ap, self.XT.ap[:, :, tok0:tok0 + TF].rearrange("c p t -> p c t"), [], xt.bs)
                return
            self.dma("sp", stg.ap, I["xin"][tok0:tok0 + TF].rearrange("(b p) d -> p b d", p=128), [], [stg.b])
            for c in range(8):
                bi = c % 2
                for blk in range(2):
                    self.tr(self.bank(bi)[:, blk * 128:(blk + 1) * 128], stg.ap[:, blk, c * 128:(c + 1) * 128], self.identf.ap,
                            [stg.b, self.identf.b], [pb[bi]])
                self.cp("act", xt.ap[:, c, :], bk(bi), [pb[bi]], [xt.bs[c]])

        def store(ti):
            xt = xts[ti % 2]
            tok0 = ti * TF
            if not last:
                self.dma("sp", self.XT.ap[:, :, tok0:tok0 + TF].rearrange("c p t -> p c t"), xt.ap, xt.bs, [])
                return
            for blk in range(2):
                for half in range(2):
                    bi = half
                    for cc in range(4):
                        c = half * 4 + cc
                        self.tr(self.bank(bi)[:, cc * 128:(cc + 1) * 128], xt.ap[:, c, blk * 128:(blk + 1) * 128], self.identf.ap,
                                [xt.bs[c], self.identf.b], [pb[bi]])
                    self.cp("act", stg.ap[:, blk, half * 512:(half + 1) * 512], self.bank(bi), [pb[bi]], [stg.b])
            self.dma("sp", self.O["y"][tok0:tok0 + TF].rearrange("(b p) d -> p b d", p=128), stg.ap, [stg.b], [])

        def prenorm(ti):
            xt = xts[ti % 2]; hT = hTs[ti % 2]
            which = 0 if ti * TF < TS else 1
            for c in range(8):
                self.tt("pool", sq.ap[:, c, :], xt.ap[:, c, :], xt.ap[:, c, :], ALU.mult, [xt.bs[c]], [sq.b])
            for c in range(8):
                self.mm(bk(7), self.onesb.ap, sq.ap[:, c, :], c == 0, c == 7, [self.onesb.b, sq.b], [pb[7]])
            rstd(rs_pre, 7)
            for c in range(8):
                t = tmp_pre[c % 2]
                self.tt("dve", t.ap, xt.ap[:, c, :], rs_pre.ap, ALU.mult, [xt.bs[c], rs_pre.b], [t.b])
                self.ts("pool", hT.ap[:, c, :], t.ap, self.scal(l, 0, sub, c, which), self.scal(l, 1, sub, c, which),
                        ALU.mult, ALU.add, [t.b, self.SC.b], [hT.b])

        load(0)
        prenorm(0)

        def upd(tj, m):
            xt_ = xts[tj % 2]
            wh_ = 0 if tj * TF < TS else 1
            t = tmp_post[m % 2]
            self.stt(t.ap, fT.ap[:, m, :], self.scal(l, 2, sub, m, wh_), rs_post.ap, ALU.mult, ALU.mult,
                     [fT.b, self.SC.b, rs_post.b], [t.b])
            self.tt("pool", xt_.ap[:, m, :], xt_.ap[:, m, :], t.ap, ALU.add, [xt_.bs[m], t.b], [xt_.bs[m]])

        for ti in range(NTF):
            xt = xts[ti % 2]; hT = hTs[ti % 2]
            for j in range(NFC):
                bg, bu = j % 2, 2 + j % 2
                cs = slice(j * 128, (j + 1) * 128)
                for kc in range(8):
                    self.mm(bk(bg), wg.ap[:, kc, cs], hT.ap[:, kc, :], kc == 0, kc == 7, [wg.bs[j // 11], hT.b], [pb[bg]])
                for kc in range(8):
                    self.mm(bk(bu), wu.ap[:, kc, cs], hT.ap[:, kc, :], kc == 0, kc == 7, [wu.bs[j // 11], hT.b], [pb[bu]])
                s = sgr[j % 2]
                self.act(s.ap, bk(bg), AF.Silu, [pb[bg]], [s.b])
                self.tt("dve", actT.ap[:, j, :], s.ap, bk(bu), ALU.mult, [s.b, pb[bu]], [actT.bs[j]])
                if ti >= 1 and 2 <= j < 10:
                    upd(ti - 1, j - 2)
            if ti >= 1:
                store(ti - 1)
            if ti + 1 < NTF:
                load(ti + 1)
                prenorm(ti + 1)
            for m in range(8):
                bf = 4 + m % 2
                for j in range(NFC):
                    self.mm(bk(bf), wd.ap[:, j, m * 128:(m + 1) * 128], actT.ap[:, j, :], j == 0, j == NFC - 1,
                            [wd.bs[j // 11], actT.bs[j]], [pb[bf]])
                s = sqr[m % 2]
                self.cp("act", fT.ap[:, m, :], bk(bf), [pb[bf]], [fT.b])
                self.act(s.ap, bk(bf), AF.Square, [pb[bf]], [s.b])
                if m > 0:
                    p_ = sqr[(m - 1) % 2]
                    self.mm(bk(6), self.onesb.ap, p_.ap, m == 1, False, [self.onesb.b, p_.b], [pb[6]])
            p_ = sqr[1]
            self.mm(bk(6), self.onesb.ap, p_.ap, False, True, [self.onesb.b, p_.b], [pb[6]])
            rstd(rs_post, 6)
        for m in range(8):
            upd(NTF - 1, m)
        store(NTF - 1)

    def p2_pass(self, l):
        self.phase()
        A, I, O = self.A, self.I, self.O
        W = A.bf16(8, WINP, nb=4)
        wv = I["w_in_p"][l].rearrange("(kc p) c -> p kc c", p=128)
        bounds = [0, C_V, C_U, C_G + 1536, WINP]
        for h in range(4):
            self.dma("pool", W.ap[:, :, bounds[h]:bounds[h + 1]], wv[:, :, bounds[h]:bounds[h + 1]], [], [W.bs[h]])

        def wb(col):
            for h in range(4):
                if col < bounds[h + 1]:
                    return W.bs[h]

        xt = A.f32(8, TT, nb=8)
        hTs = [A.bf16(8, TT), A.bf16(8, TT)]; sq = A.bf16(8, TT)
        rt = A.f32(2, TT)
        qst = [A.bf16(TT) for _ in range(3)]
        rtmp = [A.f32(TT) for _ in range(4)]
        xlst = A.f32(4, TT); ylst = A.bf16(4, TT)
        vst = A.bf16(4, 2, 65); vf = A.f32(4, 128); kf = A.f32(4, 128)
        utok = A.bf16(32, 8, 16); ufst = A.bf16(32, 64)
        gst = [A.bf16(8, TT), A.bf16(8, TT)]
        self.alloc_norm()
        pb = self.pbufs
        self.ms("pool", vst.ap[:, :, :, 64:65], 1.0, [vst.b])
        nfm = 0
        import os
        sec = os.environ.get("DBG_P2", "ABCDE")
        for ti in range(int(os.environ.get("DBG_NT", NT))):
            which = 0 if ti < 8 else 1
            sample = ti < 8
            tok0 = ti * TT
            hT = hTs[ti % 2]
            if ti == 0:
                self.load_x(xt, 0)
                self.prenorm(l, 1, 0, xt, hTs[0], sq)
            if sample:
                self.dma("act", rt.ap, I["rope"][:, :, tok0:tok0 + TT].rearrange("a p t -> p a t"), [], [rt.b])
            if ti + 1 < NT:
                t1_ = (ti + 1) * TT
                self.dma("act", xt.ap, self.XT.ap[:, :, t1_:t1_ + TT].rearrange("c p t -> p c t"), [], xt.bs)

            def fm(col, bi):
                for kc in range(8):
                    self.mm(self.bank(bi), W.ap[:, kc, col:col + 128], hT.ap[:, kc, :], kc == 0, kc == 7, [wb(col), hT.b], [pb[bi]])

            self.P.mute = "A" not in sec
            for ci in range(5):
                col = C_Q + ci * 128 if ci < 4 else C_K
                cols = C_QS + ci * 128 if ci < 4 else C_KS
                b0 = ci % 2
                fm(col, b0)
                q_ = qst[ci % 3]
                if sample:
                    fm(cols, 2 + b0)
                    t1 = rtmp[(ci % 2) * 2]; t2 = rtmp[(ci % 2) * 2 + 1]
                    self.tt("dve", t1.ap, self.bank(b0), rt.ap[:, 0, :], ALU.mult, [pb[b0], rt.b], [t1.b])
                    self.tt("dve", t2.ap, self.bank(2 + b0), rt.ap[:, 1, :], ALU.mult, [pb[2 + b0], rt.b], [t2.b])
                    self.tt("pool", q_.ap, t1.ap, t2.ap, ALU.add, [t1.b, t2.b], [q_.b])
                else:
                    self.cp("act", q_.ap, self.bank(b0), [pb[b0]], [q_.b])
                dst = self.Qs.ap[ci][:, tok0:tok0 + TT] if ci < 4 else self.Ks.ap[:, tok0:tok0 + TT]
                self.dma("sp", dst, q_.ap, [q_.b], [])
            self.P.mute = False
            if ti + 1 < NT:
                self.prenorm(l, 1, 0 if ti + 1 < 8 else 1, xt, hTs[(ti + 1) % 2], sq)
            self.P.mute = "B" not in sec
            for blk in range(4):
                for kc in range(8):
                    self.mm(self.bank(4)[:, blk * 128:(blk + 1) * 128], hT.ap[:, kc, blk * 128:(blk + 1) * 128],
                            W.ap[:, kc, C_V:C_V + 128], kc == 0, kc == 7, [hT.b, wb(C_V)], [pb[4]])
            self.cp("act", vst.ap[:, :, :, 0:64], view(self.bank(4), (4, 2, 64)), [pb[4]], [vst.b])
            if not sample:
                self.cp("act", vf.ap, view(self.bank(4), (4, 128)), [pb[4]], [vf.b])
            self.dma("sp", self.Vs.ap[tok0:tok0 + TT].rearrange("(b p) c -> p b c", p=128),
                     vst.ap.rearrange("p b h c -> p b (h c)"), [vst.b], [])
            if not sample:
                for pp in range(2):
                    pj = 2 * (ti - 8) + pp
                    self.dma("sp", O["nv"][pj, l].rearrange("(b p) c -> p b c", p=128), vf.ap[:, 2 * pp:2 * pp + 2, :],
                             [vf.b], [])
                for blk in range(4):
                    for kc in range(8):
                        self.mm(self.bank(4)[:, blk * 128:(blk + 1) * 128], hT.ap[:, kc, blk * 128:(blk + 1) * 128],
                                W.ap[:, kc, C_K:C_K + 128], kc == 0, kc == 7, [hT.b, wb(C_K)], [pb[4]])
                self.cp("dve", kf.ap, view(self.bank(4), (4, 128)), [pb[4]], [kf.b])
                for pp in range(2):
                    pj = 2 * (ti - 8) + pp
                    self.dma("sp", O["nk"][pj, l].rearrange("(b p) c -> p b c", p=128), kf.ap[:, 2 * pp:2 * pp + 2, :],
                             [kf.b], [])
            self.P.mute = "C" not in sec
            for c in range(4):
                b0 = c % 2
                fm(C_XL + c * 128, b0)
                self.cp("act", xlst.ap[:, c, :], self.bank(b0), [pb[b0]], [xlst.b])
            self.dma("sp", self.XLs.ap[:, :, tok0:tok0 + TT].rearrange("c p t -> p c t"), xlst.ap, [xlst.b], [])
            for c in range(4):
                b0 = c % 2
                fm(C_YL + c * 128, b0)
                self.cp("dve", ylst.ap[:, c, :], self.bank(b0), [pb[b0]], [ylst.b])
            self.dma("sp", self.YLs.ap[:, :, tok0:tok0 + TT].rearrange("c p t -> p c t"), ylst.ap, [ylst.b], [])
            self.P.mute = "D" not in sec
            hs = hT.ap.rearrange("p c (k s) -> p c s k", s=8)
            for s in range(8):
                bi = 5 + s % 2
                for kc in range(8):
                    self.mm(self.bank(bi)[0:64, :], hs[:, kc, s, :], W.ap[:, kc, C_U:C_U + 512], kc == 0, kc == 7,
                            [hT.b, wb(C_U)], [pb[bi]])
                self.cp("act" if s % 2 else "dve", utok.ap[0:64, :, s, :], view(self.bank(bi)[0:64, :], (32, 16)), [pb[bi]], [utok.b])
            for half in range(2):
                bi = 2 + half
                pbf = self.bank(bi).bitcast(BF16)
                for gg in range(16):
                    g = half * 16 + gg
                    self.tr(pbf[:, gg * 64:(gg + 1) * 64], utok.ap[0:64, g].rearrange("p s c -> p (s c)"),
                            self.identb.ap[0:64, 0:64], [utok.b, self.identb.b], [pb[bi]])
                self.cp("act" if half else "dve", ufst.ap[:, half * 16:(half + 1) * 16, :], view(pbf[:, 0:1024], (16, 64)),
                        [pb[bi]], [ufst.b])
            self.dma("sp", self.UFs.ap[ti], ufst.ap.rearrange("p g k -> p (g k)"), [ufst.b], [])
            self.P.mute = "E" not in sec
            for cc in range(24):
                b0 = cc % 2
                fm(C_G + cc * 128, b0)
                g_ = gst[(cc // 8) % 2]
                self.act(g_.ap[:, cc % 8, :], self.bank(b0), AF.Sigmoid, [pb[b0]], [g_.b])
                if cc % 8 == 7:
                    grp = cc // 8
                    self.dma("sp", self.Gs.ap[grp * 8:(grp + 1) * 8][:, :, tok0:tok0 + TT].rearrange("c p t -> p c t"), g_.ap,
                             [g_.b], [])
            self.P.mute = False

    def p4_pass(self, l):
        self.phase()
        A, I = self.A, self.I
        wol = A.bf16(4, D); woa = A.bf16(4, D); wgl = A.bf16(4, 2048); wo = A.bf16(8, D)
        self.dma("pool", wgl.ap, I["w_glu"][l].rearrange("(kc p) c -> p kc c", p=128), [], [wgl.b])
        self.dma("pool", wol.ap, I["w_o_lru"][l].rearrange("(kc p) c -> p kc c", p=128), [], [wol.b])
        self.dma("pool", woa.ap, I["w_o_attn_p"][l].rearrange("(kc p) c -> p kc c", p=128), [], [woa.b])
        self.dma("pool", wo.ap, I["w_out"][l].rearrange("(kc p) c -> p kc c", p=128), [], [wo.b])
        xt = A.f32(8, TT, nb=8)
        ins = [(A.bf16(4, TT), A.bf16(4, TT), A.bf16(4, TT), A.bf16(24, TT)) for _ in range(2)]
        mg = A.bf16(8, TT)
        fT = A.f32(8, TT)
        sqr = [A.bf16(TT), A.bf16(TT)]
        tmp = [A.f32(TT) for _ in range(6)]
        self.alloc_norm()
        pb = self.pbufs

        def loads(ti):
            tok0 = ti * TT
            lr, at, sy, G = ins[ti % 2]
            for (dst, src) in ((sy, self.SYs), (lr, self.LRs), (at, self.ATs), (G, self.Gs)):
                self.dma("sp", dst.ap, src.ap[:, :, tok0:tok0 + TT].rearrange("c p t -> p c t"), [], [dst.b])

        loads(0)
        for ti in range(NT):
            which = 0 if ti < 8 else 1
            if ti + 1 < NT:
                loads(ti + 1)
            self.load_x(xt, ti)
            lr, at, sy, G = ins[ti % 2]
            for m in range(8):
                ba, bz = m % 2, 2 + m % 2
                for kc in range(4):
                    self.mm(self.bank(ba), wgl.ap[:, kc, m * 128:(m + 1) * 128], sy.ap[:, kc, :], kc == 0, kc == 3, [wgl.b, sy.b], [pb[ba]])
                for kc in range(4):
                    self.mm(self.bank(bz), wgl.ap[:, kc, 1024 + m * 128:1024 + (m + 1) * 128], sy.ap[:, kc, :], kc == 0, kc == 3,
                            [wgl.b, sy.b], [pb[bz]])
                s = tmp[m % 2]; t = tmp[2 + m % 2]
                self.act(s.ap, self.bank(bz), AF.Sigmoid, [pb[bz]], [s.b])
                self.tt("dve", t.ap, self.bank(ba), s.ap, ALU.mult, [pb[ba], s.b], [t.b])
                self.tt("pool", fT.ap[:, m, :], t.ap, G.ap[:, 8 + m, :], ALU.mult, [t.b, G.b], [fT.b])
            for m in range(8):
                ba, bc = 4 + m % 2, 6 + m % 2
                for kc in range(4):
                    self.mm(self.bank(ba), wol.ap[:, kc, m * 128:(m + 1) * 128], lr.ap[:, kc, :], kc == 0, kc == 3, [wol.b, lr.b], [pb[ba]])
                for kc in range(4):
                    self.mm(self.bank(bc), woa.ap[:, kc, m * 128:(m + 1) * 128], at.ap[:, kc, :], kc == 0, kc == 3, [woa.b, at.b], [pb[bc]])
                u1 = tmp[m % 2]; u3 = tmp[2 + m % 2]; u4 = tmp[4 + m % 2]
                self.tt("dve", u1.ap, self.bank(ba), G.ap[:, m, :], ALU.mult, [pb[ba], G.b], [u1.b])
                self.tt("dve", u3.ap, self.bank(bc), G.ap[:, 16 + m, :], ALU.mult, [pb[bc], G.b], [u3.b])
                self.tt("pool", u4.ap, fT.ap[:, m, :], u1.ap, ALU.add, [fT.b, u1.b], [u4.b])
                self.tt("pool", mg.ap[:, m, :], u4.ap, u3.ap, ALU.add, [u4.b, u3.b], [mg.b])
            for m in range(8):
                bo = m % 2
                for kc in range(8):
                    self.mm(self.bank(bo), wo.ap[:, kc, m * 128:(m + 1) * 128], mg.ap[:, kc, :], kc == 0, kc == 7, [wo.b, mg.b], [pb[bo]])
                self.post_chunk(m, bo, fT, sqr, 2)
            self.post_update(l, 1, which, xt, fT, 2)
            self.store_x(xt, ti)

    def p3a_attention(self, l):
        self.phase()
        A, I = self.A, self.I
        kT = A.bf16(2, NTOK); Qt = A.bf16(4, NTOK); Vt = A.bf16(NTOK // 128, 130)
        self.ms("pool", kT.ap[64:128, 0, :], 0.0, [kT.b])
        self.ms("pool", kT.ap[0:64, 1, :], 0.0, [kT.b])
        for h in range(2):
            sl = slice(h * 2560, (h + 1) * 2560)
            self.dma("sp", Qt.ap[:, :, sl], self.Qs.ap[:, :, sl].rearrange("c p t -> p c t"), [], [Qt.b])
        self.dma("act", kT.ap[0:64, 0, :], self.Ks.ap[0:64, :], [], [kT.b])
        self.dma("act", kT.ap[64:128, 1, :], self.Ks.ap[64:128, :], [], [kT.b])
        self.dma("act", Vt.ap, self.Vs.ap.rearrange("(b p) c -> p b c", p=128), [], [Vt.b])
        ckr = A.f32(4, 128); cvr = A.f32(4, 128)
        self.dma("sp", ckr.ap, I["cache_k"][l].rearrange("(b p) c -> p b c", p=128), [], [ckr.b])
        self.dma("sp", cvr.ap, I["cache_v"][l].rearrange("(b p) c -> p b c", p=128), [], [cvr.b])
        ckT = A.bf16(2, 512); cv = A.bf16(4, 2, 65)
        pb = self.pbufs
        for blk in range(4):
            self.tr(self.bank(0)[:, blk * 128:(blk + 1) * 128], ckr.ap[:, blk, :], self.identf.ap, [ckr.b, self.identf.b], [pb[0]])
        self.ms("pool", ckT.ap, 0.0, [ckT.b])
        self.cp("dve", ckT.ap[0:64, 0, :], self.bank(0)[0:64, :], [pb[0]], [ckT.b])
        self.cp("dve", ckT.ap[64:128, 1, :], self.bank(0)[64:128, :], [pb[0]], [ckT.b])
        self.ms("pool", cv.ap[:, :, :, 64:65], 1.0, [cv.b])
        self.cp("dve", cv.ap[:, :, :, 0:64], cvr.ap.rearrange("p b (h c) -> p b h c", h=2), [cvr.b], [cv.b])
        cvf = cv.ap.rearrange("p b h c -> p b (h c)")
        sk = A.f32(8)
        self.dma("sp", sk.ap[64:65, :], I["attn_sink"][l:l + 1, :], [], [sk.b])
        self.act(sk.ap[64:65, :], sk.ap[64:65, :], AF.Exp, [sk.b], [sk.b])
        pT = [A.bf16(TT) for _ in range(4)]
        osb = [A.f32(TT) for _ in range(3)]
        rrow = [A.f32(TT) for _ in range(3)]
        ast = [A.bf16(4, TT), A.bf16(4, TT)]
        segs = [(0, TS, True)] + [(TS + j * TP, TP, False) for j in range(NPB)]
        its = []
        nst = 0
        for (tok0, T, samp) in segs:
            nqb = T // 128
            for qb in range(nqb):
                for kvh in range(2):
                    its.append((tok0, T, samp, nqb, qb, kvh, nst))
                if qb % 4 == 3 or qb == nqb - 1:
                    nst += 1
        npt = [0]

        def front(n):
            tok0, T, samp, nqb, qb, kvh, st_ = its[n]
            q0 = tok0 + qb * 128
            blocks = []
            if samp:
                for nb in (qb - 1, qb, qb + 1):
                    if 0 <= nb < nqb:
                        msk = self.mprev if nb == qb - 1 else (self.mnext if nb == qb + 1 else None)
                        blocks.append((kT.ap[:, kvh, tok0 + nb * 128:tok0 + (nb + 1) * 128], kT.b,
                                       Vt.ap[:, (tok0 // 128) + nb, kvh * 65:(kvh + 1) * 65], Vt.b, msk))
                for cb_ in range(4):
                    blocks.append((ckT.ap[:, kvh, cb_ * 128:(cb_ + 1) * 128], ckT.b, cvf[:, cb_, kvh * 65:(kvh + 1) * 65], cv.b, None))
            else:
                for nb in range(nqb):
                    blocks.append((kT.ap[:, kvh, tok0 + nb * 128:tok0 + (nb + 1) * 128], kT.b,
                                   Vt.ap[:, (tok0 // 128) + nb, kvh * 65:(kvh + 1) * 65], Vt.b, None))
            po = 3 + n % 3
            rhs_q = Qt.ap[:, :, q0:q0 + 128]
            pts = []
            for bi_, (kap, kb, vap, vb, msk) in enumerate(blocks):
                sb_ = npt[0] % 3
                p_ = pT[npt[0] % 4]
                npt[0] += 1
                self.mm(view(self.bank(sb_), (4, 128)), kap, rhs_q, True, True, [kb, Qt.b], [pb[sb_]])
                self.act(p_.ap, self.bank(sb_), AF.Exp, [pb[sb_]], [p_.b], scale=0.125)
                if msk is not None:
                    m_ = msk.ap
                    mb = mkap(m_.tensor, m_.offset, [list(m_.ap[0]), [0, 4], [1, 128]])
                    self.tt("pool", view(p_.ap, (4, 128)), view(p_.ap, (4, 128)), mb, ALU.mult, [p_.b, msk.b], [p_.b])
                pts.append((p_, vap, vb))
                if bi_ >= 1:
                    pp_, vap_, vb_ = pts[bi_ - 1]
                    self.mm(self.bank(po)[0:65, :], vap_, pp_.ap, bi_ == 1, False, [vb_, pp_.b], [pb[po]])
            pp_, vap_, vb_ = pts[-1]
            self.mm(self.bank(po)[0:65, :], vap_, pp_.ap, len(pts) == 1, True, [vb_, pp_.b], [pb[po]])

        def back1(n):
            tok0, T, samp, nqb, qb, kvh, st_ = its[n]
            po = 3 + n % 3
            o_ = osb[n % 3]; r_ = rrow[n % 3]
            self.cp("act", o_.ap[0:65, :], self.bank(po)[0:65, :], [pb[po]], [o_.b])
            s_ = sk.ap[64:65, kvh * 4:(kvh + 1) * 4]
            sbc = mkap(s_.tensor, s_.offset, [list(s_.ap[0]), [1, 4], [0, 128]])
            self.tt("dve", view(r_.ap[64:65, :], (4, 128)), view(o_.ap[64:65, :], (4, 128)), sbc, ALU.add, [o_.b, sk.b], [r_.b])
            self.recip(r_.ap[64:65, :], r_.ap[64:65, :], [r_.b], [r_.b])

        def back2(n):
            tok0, T, samp, nqb, qb, kvh, st_ = its[n]
            q0 = tok0 + qb * 128
            hs = slice(kvh * 64, (kvh + 1) * 64)
            a_t = ast[st_ % 2]
            qcol = (qb % 4) * 128
            o_ = osb[n % 3]; r_ = rrow[n % 3]
            bb = 6 + n % 2
            self.mm(self.bank(bb)[0:64, :], self.onesf.ap[64:65, 0:64], r_.ap[64:65, :], True, True, [self.onesf.b, r_.b], [pb[bb]])
            self.tt("dve", a_t.ap[hs, :, qcol:qcol + 128], view(o_.ap[0:64, :], (4, 128)), view(self.bank(bb)[0:64, :], (4, 128)),
                    ALU.mult, [o_.b, pb[bb]], [a_t.b])
            if kvh == 1 and (qb % 4 == 3 or qb == nqb - 1):
                nn = (qb % 4 + 1) * 128
                t0 = q0 + 128 - nn
                self.dma("sp", self.ATs.ap[:, :, t0:t0 + nn].rearrange("c p t -> p c t"), a_t.ap[:, :, 0:nn], [a_t.b], [])

        N = len(its)
        for n in range(N + 2):
            if n < N:
                front(n)
            if 0 <= n - 1 < N:
                back1(n - 1)
            if 0 <= n - 2 < N:
                back2(n - 2)

    def p3b_lru(self, l):
        self.phase()
        A, I, O = self.A, self.I, self.O
        pb = self.pbufs
        cw = A.f32(4, 4); cb = A.f32(4); onec = A.f32(1)
        src, sb_ = self.vecT(None, I["w_conv"][l].rearrange("j (c p) -> (j c) p", p=128), 16, None)
        self.cp("dve", cw.ap.rearrange("p c j -> p j c"), view(src, (4, 4)), [sb_], [cw.b])
        src, sb_ = self.vecT(None, I["b_conv"][l].rearrange("(c p) -> c p", p=128), 4, None)
        self.cp("dve", cb.ap, src, [sb_], [cb.b])
        self.ms("dve", onec.ap, 1.0, [onec.b])
        ba = A.f32(2, 4); bx = A.f32(2, 4); cl = A.f32(2, 4); st0 = A.f32(2, 4)
        for (t_, nm) in ((ba, "b_lru_a"), (bx, "b_lru_x"), (cl, "lru_lambda"), (st0, "state_lru")):
            src, sb_ = self.vecT(None, I[nm][l].rearrange("d (c p) -> (d c) p", p=128), 8, None)
            self.cp("dve", t_.ap.rearrange("p d c -> p (d c)"), src, [sb_], [t_.b])
        self.act(cl.ap, cl.ap, AF.Exp, [cl.b], [cl.b], scale=-1.0)
        self.act(cl.ap, cl.ap, AF.Ln, [cl.b, onec.b], [cl.b], bias=onec.ap[:, 0:1])
        self.ts("dve", cl.ap, cl.ap, -8.0, None, ALU.mult, None, [cl.b], [cl.b])
        cl2 = A.f32(2, 4)
        self.ts("dve", cl2.ap, cl.ap, 2.0, None, ALU.mult, None, [cl.b], [cl2.b])
        BD = {}
        for d in range(2):
            for nm in ("w_lru_a", "w_lru_x"):
                t_ = A.bf16(4, 128)
                self.ms("pool", t_.ap, 0.0, [t_.b])
                fns = []
                for c in range(4):
                    fns.append(lambda e, o_=t_.ap[0:64, c, 0:64], i_=I[nm][l, d, 2 * c]: e.dma_start(out=o_, in_=i_))
                    fns.append(lambda e, o_=t_.ap[64:128, c, 64:128], i_=I[nm][l, d, 2 * c + 1]: e.dma_start(out=o_, in_=i_))
                self.P.dma_group("pool", fns, [], [t_.b])
                BD[(d, nm)] = t_
        hf = A.f32(4, TS); xc = A.f32(4, TS)
        xlt = A.f32(4, TT + 3); xcb = A.bf16(4, TT)
        R = A.f32(4, TT)
        IIs = [A.f32(4, TT), A.f32(4, TT)]
        AAs = [A.f32(4, TT), A.f32(4, TT)]
        hbt = A.f32(4, TT); carry = A.f32(4)
        ylts = [A.bf16(4, TT), A.bf16(4, TT)]
        fin = A.f32(NPB, 2, 4)
        segs = [(0, TS, True)] + [(TS + j * TP, TP, False) for j in range(NPB)]
        gk = [0]
        xcbufs = [Buf() for _ in range(8)]

        def rev(ap, n):
            return mkap(ap.tensor, ap.offset + n - 1, [list(ap.ap[0]), [-1, n]])

        def gates(d, t0l, n):
            AA = AAs[gk[0] % 2]; II = IIs[gk[0] % 2]
            gk[0] += 1
            SQ = R
            for c in range(4):
                self.cp("pool", xcb.ap[:, c, 0:n], xc.ap[:, c, t0l:t0l + n], [xcbufs[t0l // n]], [xcb.b])
            for c in range(4):
                self.mm(self.bank(c)[:, 0:n], BD[(d, "w_lru_a")].ap[:, c, :], xcb.ap[:, c, 0:n], True, True, [BD[(d, "w_lru_a")].b, xcb.b], [pb[c]])
                self.mm(self.bank(4 + c)[:, 0:n], BD[(d, "w_lru_x")].ap[:, c, :], xcb.ap[:, c, 0:n], True, True, [BD[(d, "w_lru_x")].b, xcb.b], [pb[4 + c]])
            for c in range(4):
                self.act(R.ap[:, c, 0:n], self.bank(c)[:, 0:n], AF.Sigmoid, [pb[c], ba.b], [R.b], bias=ba.ap[:, d, c:c + 1])
            for c in range(4):
                self.act(II.ap[:, c, 0:n], self.bank(4 + c)[:, 0:n], AF.Sigmoid, [pb[4 + c], bx.b], [II.b], bias=bx.ap[:, d, c:c + 1])
            for c in range(4):
                self.tt("pool", II.ap[:, c, 0:n], II.ap[:, c, 0:n], xc.ap[:, c, t0l:t0l + n], ALU.mult, [II.b, xcbufs[t0l // n]], [II.b])
            for c in range(4):
                self.act(AA.ap[:, c, 0:n], R.ap[:, c, 0:n], AF.Exp, [R.b, cl.b], [AA.b], scale=cl.ap[:, d, c:c + 1])
            for c in range(4):
                self.act(SQ.ap[:, c, 0:n], R.ap[:, c, 0:n], AF.Exp, [R.b, cl2.b], [SQ.b], scale=cl2.ap[:, d, c:c + 1])
            for c in range(4):
                self.act(SQ.ap[:, c, 0:n], SQ.ap[:, c, 0:n], AF.Sqrt, [SQ.b, onec.b], [SQ.b], scale=-1.0, bias=onec.ap[:, 0:1])
            for c in range(4):
                self.tt("dve", II.ap[:, c, 0:n], II.ap[:, c, 0:n], SQ.ap[:, c, 0:n], ALU.mult, [II.b, SQ.b], [II.b])
            return AA, II

        for si, (tok0, T, samp) in enumerate(segs):
            n = min(TT, T)
            ntl = T // n

            def load_conv(tl):
                t0l = tl * n
                lo = max(t0l - 2, 0); hi = min(t0l + n + 1, T)
                if lo > t0l - 2:
                    self.ms("dve", xlt.ap[:, :, 0:2], 0.0, [xlt.b])
                if hi < t0l + n + 1:
                    self.ms("dve", xlt.ap[:, :, n + 2:n + 3], 0.0, [xlt.b])
                self.dma("sp", xlt.ap[:, :, lo - (t0l - 2):hi - (t0l - 2)], self.XLs.ap[:, :, tok0 + lo:tok0 + hi].rearrange("c p t -> p c t"), [], [xlt.b])
                for c in range(4):
                    o_ = xc.ap[:, c, t0l:t0l + n]
                    self.act(o_, xlt.ap[:, c, 0:n], AF.Identity, [xlt.b, cw.b, cb.b], [xcbufs[tl]], scale=cw.ap[:, c, 0:1], bias=cb.ap[:, c:c + 1])
                    for j in range(1, 4):
                        self.stt(o_, xlt.ap[:, c, j:j + n], cw.ap[:, c, j:j + 1], o_, ALU.mult, ALU.add, [xlt.b, cw.b, xcbufs[tl]], [xcbufs[tl]])

            load_conv(0)
            for tl in range(ntl):
                t0l = tl * n
                if tl + 1 < ntl:
                    load_conv(tl + 1)
                AA, II = gates(0, t0l, n)
                for c in range(4):
                    if tl == 0:
                        init = st0.ap[:, 0, c:c + 1] if samp else 0.0
                    else:
                        init = hf.ap[:, c, t0l - 1:t0l]
                    self.P.op("dve", lambda e, o=hf.ap[:, c, t0l:t0l + n], a=AA.ap[:, c, 0:n], b=II.ap[:, c, 0:n], i0=init:
                              e.tensor_tensor_scan(out=o, data0=a, data1=b, initial=i0, op0=ALU.mult, op1=ALU.add),
                              [AA.b, II.b, hf.b, st0.b], [hf.b])
            if not samp:
                self.cp("dve", fin.ap[:, si - 1, 0, :], hf.ap[:, :, T - 1], [hf.b], [fin.b])
            def load_yl(tl):
                y_ = ylts[tl % 2]
                t0l_ = tl * n
                self.dma("sp", y_.ap[:, :, 0:n], self.YLs.ap[:, :, tok0 + t0l_:tok0 + t0l_ + n].rearrange("c p t -> p c t"), [], [y_.b])
                for c in range(4):
                    self.act(y_.ap[:, c, 0:n], y_.ap[:, c, 0:n], AF.Gelu_apprx_tanh, [y_.b], [y_.b])

            load_yl(ntl - 1)
            for tl in reversed(range(ntl)):
                t0l = tl * n
                cur = hbt
                ylt = ylts[tl % 2]
                AA, II = gates(1, t0l, n)
                if tl - 1 >= 0:
                    load_yl(tl - 1)
                for c in range(4):
                    if tl == ntl - 1:
                        init = st0.ap[:, 1, c:c + 1] if samp else 0.0
                    else:
                        init = carry.ap[:, c:c + 1]
                    self.P.op("dve", lambda e, o=rev(cur.ap[:, c, 0:n], n), a=rev(AA.ap[:, c, 0:n], n), b=rev(II.ap[:, c, 0:n], n), i0=init:
                              e.tensor_tensor_scan(out=o, data0=a, data1=b, initial=i0, op0=ALU.mult, op1=ALU.add),
                              [AA.b, II.b, carry.b, st0.b], [cur.b])
                self.cp("dve", carry.ap, cur.ap[:, :, 0], [cur.b], [carry.b])
                if not samp and tl == 0:
                    self.cp("dve", fin.ap[:, si - 1, 1, :], cur.ap[:, :, 0], [cur.b], [fin.b])
                for c in range(4):
                    self.tt("dve", cur.ap[:, c, 0:n], cur.ap[:, c, 0:n], hf.ap[:, c, t0l:t0l + n], ALU.add, [cur.b, hf.b], [cur.b])
                    self.tt("dve", ylt.ap[:, c, 0:n], cur.ap[:, c, 0:n], ylt.ap[:, c, 0:n], ALU.mult, [cur.b, ylt.b], [ylt.b])
                self.dma("sp", self.LRs.ap[:, :, tok0 + t0l:tok0 + t0l + n].rearrange("c p t -> p c t"), ylt.ap[:, :, 0:n], [ylt.b], [])
        self.tr(self.bank(0)[0:32, 0:128], fin.ap.rearrange("p s d c -> p (s d c)"), self.identf.ap, [fin.b, self.identf.b], [pb[0]])
        fo = A.f32(128)
        self.cp("dve", fo.ap[0:32, :], self.bank(0)[0:32, 0:128], [pb[0]], [fo.b])
        for s_ in range(NPB):
            self.dma("sp", O["nlru"][s_, l].rearrange("d (c p) -> (d c) p", p=128), fo.ap[s_ * 8:(s_ + 1) * 8, :], [fo.b], [])

    def p3c_s5(self, l):
        self.phase()
        A, I, O = self.A, self.I, self.O
        pb = self.pbufs
        Qw = A.bf16(2, 16, 2, 128); ML = A.bf16(32, 128)
        self.dma("sp", Qw.ap.rearrange("p d g c x -> p (d g c x)"), self.Qd[l].ap, [], [Qw.b])
        self.dma("sp", ML.ap.rearrange("p g x -> p (g x)"), self.MLd[l].ap, [], [ML.b])
        a12 = A.f32(2, 2, 16, 2)
        self.dma("sp", a12.ap.rearrange("p d k g c -> p (d k g c)"), self.A12[l].ap, [], [a12.b])
        h0r = A.f32(2, 128); h0s = A.f32(2, 16, 2)
        for d in range(2):
            self.dma("sp", h0r.ap[0:32, d, :], I["state_ssm"][l, d].rearrange("c (gp g2) n -> (c gp) (g2 n)", g2=2), [], [h0r.b])
            self.tr(self.bank(0)[:, d * 32:(d + 1) * 32], h0r.ap[0:32, d, :], self.identf.ap[0:32, 0:32], [h0r.b, self.identf.b], [pb[0]])
            self.cp("dve", h0s.ap[:, d].rearrange("p g c -> p c g"), view(self.bank(0)[:, d * 32:(d + 1) * 32], (2, 16)), [pb[0]], [h0s.b])
        zero = A.f32(16, 2, 4)
        self.ms("dve", zero.ap, 0.0, [zero.b])
        fin = A.f32(NPB, 2, 2, 16)
        mark0 = A.top
        for (tile0, ntile, nseq, Kseq, KB, tokbase) in ((0, 8, 1, 512, 256, 0), (8, 2, 4, 32, 128, TS)):
            A.top = mark0
            self.P.barrier()
            K = nseq * Kseq
            nblk = K // KB
            Uf = A.bf16(ntile, 32, 64)
            self.dma("sp", Uf.ap.rearrange("p t g k -> p t (g k)"), self.UFs.ap[tile0:tile0 + ntile].rearrange("t p x -> p t x"), [], [Uf.b])
            Hbf = [A.bf16(16, 2, nseq, Kseq + 1), A.bf16(16, 2, nseq, Kseq + 1)]
            mark1 = A.top
            PF = A.bf16(2, 16, 2, 128)
            self.dma("sp", PF.ap.rearrange("p d g c x -> p (d g c x)"), self.PFd[l].ap, [], [PF.b])
            Sd = [A.f32(16, 2, KB), A.f32(16, 2, KB)]
            tA = [A.f32(16, 2, nseq), A.f32(16, 2, nseq)]
            tB = [A.f32(16, 2, nseq), A.f32(16, 2, nseq)]
            hinit = [A.f32(16, 2, nseq), A.f32(16, 2, nseq)]
            tpb = KB // 64
            spb = KB // Kseq if nseq > 1 else 1
            for d in range(2):
                eng = "dve" if d == 0 else "pool"
                S = Sd[d]
                if nseq == 1:
                    self.cp(eng, hinit[d].ap[:, :, :, 0], h0s.ap[:, d], [h0s.b], [hinit[d].b])
                    self.cp("act", Hbf[d].ap[:, :, :, 0, 0 if d == 0 else Kseq], h0s.ap[:, d], [h0s.b], [Hbf[d].b])
                else:
                    self.ms(eng, hinit[d].ap, 0.0, [hinit[d].b])
                    self.ms(eng, Hbf[d].ap[:, :, :, :, 0 if d == 0 else Kseq], 0.0, [Hbf[d].b])
            for bidx in range(nblk):
                for d in range(2):
                    eng = "dve" if d == 0 else "pool"
                    S = Sd[d]
                    blk = bidx if d == 0 else nblk - 1 - bidx
                    for gp_ in range(16):
                        for c in range(2):
                            bi = (0 if d == 0 else 4) + (gp_ * 2 + c) % 4
                            for g2 in range(2):
                                g = 2 * gp_ + g2
                                self.mm(view(self.bank(bi)[g2 * 64:(g2 + 1) * 64, 0:KB], (tpb, 64)),
                                        PF.ap[:, d, gp_, c, g2 * 64:(g2 + 1) * 64],
                                        Uf.ap[:, blk * tpb:(blk + 1) * tpb, g, :], True, True, [PF.b, Uf.b], [pb[bi]])
                            self.cp("act", S.ap[:, gp_, c, :], self.bank(bi)[:, 0:KB], [pb[bi]], [S.b])
                for d in range(2):
                    eng = "dve" if d == 0 else "pool"
                    S = Sd[d]
                    blk = bidx if d == 0 else nblk - 1 - bidx
                    first_blk = bidx == 0
                    s_ = S.ap
                    base = s_.offset
                    pp = list(s_.ap[0])
                    nk = Kseq if nseq > 1 else KB

                    def col(kk, swap=False):
                        if swap:
                            return mkap(s_.tensor, base + KB + kk, [pp, [2 * KB, 16], [-KB, 2], [Kseq, spb]])
                        return mkap(s_.tensor, base + kk, [pp, [2 * KB, 16], [KB, 2], [Kseq, spb]])

                    def hv(t, swap=False):
                        a = t.ap
                        if swap:
                            return mkap(a.tensor, a.offset + nseq, [list(a.ap[0]), [2 * nseq, 16], [-nseq, 2], [1, spb]])
                        return mkap(a.tensor, a.offset, [list(a.ap[0]), [2 * nseq, 16], [nseq, 2], [1, spb]])

                    a1 = a12.ap[:, d, 0]
                    a2 = a12.ap[:, d, 1]
                    A1 = mkap(a1.tensor, a1.offset, [list(a1.ap[0]), [2, 16], [1, 2], [0, spb]])
                    A2 = mkap(a2.tensor, a2.offset, [list(a2.ap[0]), [2, 16], [1, 2], [0, spb]])
                    order = range(nk) if d == 0 else reversed(range(nk))
                    prev = None
                    for kk in order:
                        if prev is None:
                            if first_blk or nseq > 1:
                                pv_, psw = hv(hinit[d]), hv(hinit[d], True)
                                rdx = [hinit[d].b]
                            else:
                                pv_, psw = hv(hinit[d]), hv(hinit[d], True)
                                rdx = [hinit[d].b]
                        else:
                            pv_, psw = col(prev), col(prev, True)
                            rdx = []
                        ta, tb = tA[d], tB[d]
                        self.tt(eng, hv(ta), A1, pv_, ALU.mult, [a12.b, S.b] + rdx, [ta.b])
                        self.tt(eng, hv(tb), A2, psw, ALU.mult, [a12.b, S.b] + rdx, [tb.b])
                        self.tt(eng, hv(ta), hv(ta), hv(tb), ALU.add, [ta.b, tb.b], [ta.b])
                        self.tt(eng, col(kk), col(kk), hv(ta), ALU.add, [S.b, ta.b], [S.b])
                        prev = kk
                    if nseq == 1:
                        self.cp(eng, hv(hinit[d]), col(prev), [S.b], [hinit[d].b])
                    else:
                        for sq_ in range(spb):
                            seq = blk * spb + sq_
                            kcol = sq_ * Kseq + (Kseq - 1 if d == 0 else 0)
                            self.cp(eng, fin.ap[:, seq, d, :, :], S.ap[:, :, :, kcol].rearrange("p g c -> p c g"), [S.b], [fin.b])
                    for c in range(2):
                        if nseq == 1:
                            o0 = blk * KB + (1 if d == 0 else 0)
                            self.cp("act", Hbf[d].ap[:, :, c, 0, o0:o0 + KB], S.ap[:, :, c, :], [S.b], [Hbf[d].b])
                        else:
                            o0 = 1 if d == 0 else 0
                            self.cp("act", Hbf[d].ap[:, :, c, blk * spb:(blk + 1) * spb, o0:o0 + Kseq],
                                    S.ap[:, :, c, :].rearrange("p g (s k) -> p g s k", k=Kseq), [S.b], [Hbf[d].b])
            self.P.barrier()
            A.top = mark1
            Yf = A.bf16(32, K)
            Ytok = A.bf16(8, 512)
            YT = A.bf16(4, 1024)
            for g in range(32):
                gp_, g2 = g // 2, g % 2
                bi = g % 4
                hs = slice(g2 * 64, (g2 + 1) * 64)
                out = view(self.bank(bi)[:, 0:K], (ntile, 64))
                self.mm(out, ML.ap[:, g, :], Uf.ap[:, :, g, :], True, False, [ML.b, Uf.b], [pb[bi]])
                for d in range(2):
                    for c in range(2):
                        o0 = 0 if d == 0 else 1
                        self.mm(view(self.bank(bi)[:, 0:K], (nseq, Kseq)), Qw.ap[hs, d, gp_, c, :], Hbf[d].ap[hs, gp_, c, :, o0:o0 + Kseq],
                                False, d == 1 and c == 1, [Qw.b, Hbf[d].b], [pb[bi]])
                self.act(Yf.ap[:, g, :], self.bank(bi)[:, 0:K], AF.Gelu_apprx_tanh, [pb[bi]], [Yf.b])
            for kb in range(K // 128):
                for q4 in range(4):
                    bi = 4 + q4 % 2
                    pbf = self.bank(bi).bitcast(BF16)
                    for gg in range(8):
                        g = q4 * 8 + gg
                        self.tr(pbf[:, gg * 128:(gg + 1) * 128], Yf.ap[:, g, kb * 128:(kb + 1) * 128], self.identb.ap, [Yf.b, self.identb.b], [pb[bi]])
                    self.cp("dve" if q4 % 2 else "act", Ytok.ap[:, :, q4 * 128:(q4 + 1) * 128].rearrange("p t (g c) -> p g t c", c=16),
                            pbf[:, 0:1024].rearrange("p (g t c) -> p g t c", g=8, t=8), [pb[bi]], [Ytok.b])
                for t in range(8):
                    bi = 6 + t % 2
                    pbf = self.bank(bi).bitcast(BF16)
                    for cc in range(4):
                        self.tr(pbf[:, cc * 128:(cc + 1) * 128], Ytok.ap[:, t, cc * 128:(cc + 1) * 128], self.identb.ap, [Ytok.b, self.identb.b], [pb[bi]])
                    self.cp("dve" if t % 2 else "act", YT.ap.rearrange("p c (k s) -> p c s k", s=8)[:, :, t, :],
                            view(pbf[:, 0:512], (4, 128)), [pb[bi]], [YT.b])
                t0 = tokbase + kb * 1024
                self.dma("sp", self.SYs.ap[:, :, t0:t0 + 1024].rearrange("c p t -> p c t"), YT.ap, [YT.b], [])
        fo = A.f32(2, 128)
        ff = fin.ap.rearrange("p s d c g -> p (s d c g)")
        for h in range(2):
            self.tr(self.bank(h)[:, 0:128], ff[:, h * 128:(h + 1) * 128], self.identf.ap, [fin.b, self.identf.b], [pb[h]])
            self.cp("dve", fo.ap[:, h, :], self.bank(h)[:, 0:128], [pb[h]], [fo.b])
        for seq in range(NPB):
            h, r0 = seq // 2, (seq % 2) * 64
            self.dma("sp", O["nssm"][seq, l].rearrange("d c (gp g2) n -> (d c gp) (g2 n)", g2=2), fo.ap[r0:r0 + 64, h, :], [fo.b], [])


def build(dbg=False, stages=None):
    K = Kern(dbg)
    on = lambda nm: stages is None or nm in stages
    with K.st:
        if on("pro"):
            K.prologue()
        for l in range(L):
            if on("ffa%d" % l):
                K.ffn_pass2(l, 0, first=(l == 0))
            if on("p2%d" % l):
                K.p2_pass(l)
            if on("p3a%d" % l):
                K.p3a_attention(l)
            if on("p3b%d" % l):
                K.p3b_lru(l)
            if on("p3c%d" % l):
                K.p3c_s5(l)
            if on("p4%d" % l):
                K.p4_pass(l)
            if on("ffb%d" % l):
                K.ffn_pass2(l, 1, last=(l == L - 1))
        K.P.barrier()
        K.P.op("sp", lambda e: e.nop(), [], [])
        K.P.emit()
    return K


def _perm_q():
    idx = []
    for c in range(4):
        for h in (c, 4 + c):
            idx.extend(range(h * 64, (h + 1) * 64))
    return np.array(idx)


def _partner():
    p = np.zeros(64, np.int64)
    for d in range(64):
        p[d] = d + 16 if (d % 32) < 16 else d - 16
    return p


def _consts():
    cst = np.zeros((128, 5, 128), np.float32)
    j = np.arange(128)[:, None]
    i = np.arange(128)[None, :]
    cst[:, 0] = (j == i)
    cst[:, 1] = (j >= i)
    cst[:, 2] = (j <= i)
    cst[:, 3] = ((j // 16) <= (i // 16))
    cst[:, 4] = ((j // 16) >= (i // 16))
    t = np.arange(TS)
    row = (t // 64).astype(np.float64)
    colp = (t % 64).astype(np.float64)
    inv = 1.0 / (10000.0 ** (np.arange(16, dtype=np.float64) / 16))
    cos = np.zeros((64, TS)); sin = np.zeros((64, TS))
    for d in range(64):
        pos = row if d < 32 else colp
        ang = (pos.astype(np.float32) * np.float32(inv[d % 16]).astype(np.float32)).astype(np.float32)
        cos[d] = np.cos(ang)
        sgn = -1.0 if (d % 32) < 16 else 1.0
        sin[d] = sgn * np.sin(ang)
    rope = np.zeros((2, 128, TS), np.float32)
    rope[0, :64] = cos; rope[0, 64:] = cos
    rope[1, :64] = sin; rope[1, 64:] = sin
    return cst.reshape(128, 640), rope


_CACHE = {}


def kernel(**inp):
    f = lambda a: np.ascontiguousarray(np.asarray(a, dtype=np.float32))
    if "K" not in _CACHE:
        _CACHE["K"] = build()
    K = _CACHE["K"]
    pq = _perm_q()
    part = _partner()
    w_in = f(inp["w_in"])
    q_cols = pq
    qs_cols = np.array([(c // 64) * 64 + part[c % 64] for c in pq])
    k_cols = 512 + np.arange(128)
    ks_cols = 512 + np.array([(c // 64) * 64 + part[c % 64] for c in range(128)])
    rest = np.arange(640, 5376)
    cols = np.concatenate([q_cols, k_cols, qs_cols, ks_cols, rest])
    w_in_p = np.ascontiguousarray(w_in[:, :, cols])
    w_o_attn_p = np.ascontiguousarray(f(inp["w_o_attn"])[:, pq, :])
    cst, rope = _consts()
    shared = {k: f(inp[k]) for k in ("w_mod", "b_mod", "g_pre", "g_post", "w_ffn_gate", "w_ffn_up", "w_ffn_down", "w_conv",
                                     "b_conv", "w_lru_a", "b_lru_a", "w_lru_x", "b_lru_x", "lru_lambda", "s5_lambda_re",
                                     "s5_lambda_im", "s5_log_step", "s5_b_re", "s5_b_im", "s5_c_re", "s5_c_im", "s5_d",
                                     "w_glu", "attn_sink", "w_o_lru", "w_out")}
    shared["w_in_p"] = w_in_p
    shared["w_o_attn_p"] = w_o_attn_p
    shared["cst"] = cst
    shared["rope"] = rope
    xs = f(inp["x_sample"]); xp = f(inp["x_prompt"]); c = f(inp["c"]); cctx = f(inp["c_ctx"])
    ck = f(inp["cache_k"]); cv = f(inp["cache_v"]); sl = f(inp["state_lru"]); ss = f(inp["state_ssm"])
    in_maps = []
    for b in range(8):
        m = dict(shared)
        m["xin"] = np.ascontiguousarray(np.concatenate([xs[b], xp[4 * b:4 * b + 4].reshape(NPB * TP, D)], axis=0))
        m["cc"] = np.ascontiguousarray(np.stack([c[b], cctx], axis=0))
        m["cache_k"] = np.ascontiguousarray(ck[b].reshape(L, 512, 128))
        m["cache_v"] = np.ascontiguousarray(cv[b].reshape(L, 512, 128))
        m["state_lru"] = np.ascontiguousarray(sl[b])
        m["state_ssm"] = np.ascontiguousarray(ss[b])
        in_maps.append(m)
    res = run_bass_kernel_spmd(K.nc, in_maps, core_ids=list(range(8)))
    _CACHE["res"] = res
    R = res.results
    y_s = np.stack([R[b]["y"][:TS] for b in range(8)], axis=0)
    y_p = np.concatenate([R[b]["y"][TS:].reshape(NPB, TP, D) for b in range(8)], axis=0)
    nk = np.concatenate([R[b]["nk"].reshape(NPB, L, TP, 2, 64) for b in range(8)], axis=0)
    nv = np.concatenate([R[b]["nv"].reshape(NPB, L, TP, 2, 64) for b in range(8)], axis=0)
    nl = np.concatenate([R[b]["nlru"] for b in range(8)], axis=0)
    ns = np.concatenate([R[b]["nssm"] for b in range(8)], axis=0)
    return (y_p.astype(np.float32), y_s.astype(np.float32), nk.astype(np.float32), nv.astype(np.float32),
            nl.astype(np.float32), ns.astype(np.float32))
```

```python
import math
import contextlib
import numpy as np
import concourse.bass as bass
import concourse.mybir as mybir
from concourse.bass_utils import run_bass_kernel_spmd

F32 = mybir.dt.float32
BF16 = mybir.dt.bfloat16
AF = mybir.ActivationFunctionType
ALU = mybir.AluOpType
AX = mybir.AxisListType

NDMA_SEM = 8
L = 2
D = 1024
TS = 4096
TP = 256
NPB = 4
NTOK = TS + NPB * TP
TT = 512
NT = NTOK // TT
DFF = 2816
NFC = DFF // 128
WINP = 6016
C_Q, C_K, C_QS, C_KS, C_V, C_XL, C_YL, C_U, C_G = 0, 512, 640, 1152, 1280, 1408, 1920, 2432, 2944
ARENA_F = 52352
EPS = 1e-6


class Buf:
    __slots__ = ("w", "r")

    def __init__(self):
        self.w = {}
        self.r = {}


class Op:
    __slots__ = ("eng", "fn", "waits", "idx", "needed", "semval", "is_dma", "slot")

    def __init__(self, eng, fn):
        self.eng = eng
        self.fn = fn
        self.waits = {}
        self.needed = False
        self.semval = 0
        self.is_dma = False
        self.slot = 0


class Prog:
    ENGS = ("pe", "act", "dve", "pool", "sp")

    def __init__(self, nc):
        self.nc = nc
        self.ops = {e: [] for e in self.ENGS}
        self.ndma = {e: 0 for e in self.ENGS}
        self.pending = {e: {} for e in self.ENGS}
        self.mute = False

    def _add(self, eng, fn, reads, writes, is_dma):
        if self.mute:
            return None
        op = Op(eng, fn)
        op.is_dma = is_dma
        lst = self.ops[eng]
        op.idx = len(lst)
        lst.append(op)
        deps = op.waits
        if self.pending[eng]:
            deps.update(self.pending[eng])
            self.pending[eng] = {}
        if is_dma:
            didx = self.ndma[eng]
            self.ndma[eng] += 1
            op.slot = didx
            pkey = ("d", eng, didx % NDMA_SEM)
            pidx = didx
            if didx >= NDMA_SEM and deps.get(pkey, -1) < didx - NDMA_SEM:
                deps[pkey] = didx - NDMA_SEM
        else:
            pkey = ("c", eng)
            pidx = op.idx
        for b in reads:
            for k, v in b.w.items():
                if deps.get(k, -1) < v:
                    deps[k] = v
        for b in writes:
            for k, v in b.w.items():
                if deps.get(k, -1) < v:
                    deps[k] = v
            for k, v in b.r.items():
                if deps.get(k, -1) < v:
                    deps[k] = v
        for b in reads:
            if b.r.get(pkey, -1) < pidx:
                b.r[pkey] = pidx
        for b in writes:
            b.w = {pkey: pidx}
            b.r = {}
        if eng == "pe" and not is_dma:
            deps.pop(("c", "pe"), None)
        return op

    def op(self, eng, fn, reads=(), writes=()):
        return self._add(eng, fn, reads, writes, False)

    def dma(self, eng, fn, reads=(), writes=()):
        return self._add(eng, fn, reads, writes, True)

    def dma_group(self, eng, fns, reads=(), writes=()):
        if self.mute:
            return
        writes = list(writes)
        snap = [(dict(b.w), dict(b.r)) for b in writes]
        acc = [dict() for _ in writes]
        for fn in fns:
            for b, (w, r) in zip(writes, snap):
                b.w = dict(w)
                b.r = dict(r)
            self._add(eng, fn, reads, writes, True)
            for b, nw in zip(writes, acc):
                nw.update(b.w)
        for b, nw in zip(writes, acc):
            b.w = nw
            b.r = {}

    def barrier(self):
        deps = {}
        for e in self.ENGS:
            last = None
            for op in reversed(self.ops[e]):
                if not op.is_dma:
                    last = op.idx
                    break
            if last is not None:
                deps[("c", e)] = last
            n = self.ndma[e]
            for s in range(NDMA_SEM):
                if n > s:
                    li = ((n - 1 - s) // NDMA_SEM) * NDMA_SEM + s
                    deps[("d", e, s)] = li
        for e in self.ENGS:
            p = self.pending[e]
            for k, v in deps.items():
                if p.get(k, -1) < v:
                    p[k] = v

    def emit(self):
        nc = self.nc
        for e in self.ENGS:
            seen = {}
            for op in self.ops[e]:
                new = {}
                for k, v in op.waits.items():
                    if seen.get(k, -1) >= v:
                        continue
                    seen[k] = v
                    new[k] = v
                op.waits = new
        for e in self.ENGS:
            for op in self.ops[e]:
                for k, v in op.waits.items():
                    if k[0] == "c":
                        self.ops[k[1]][v].needed = True
        for e in self.ENGS:
            c = 0
            for op in self.ops[e]:
                if op.is_dma:
                    continue
                if op.needed:
                    c += 1
                op.semval = c
        handles = {"pe": "tensor", "act": "scalar", "dve": "vector", "pool": "gpsimd", "sp": "sync"}
        with contextlib.ExitStack() as st:
            csem = {e: st.enter_context(nc.semaphore("c_" + e)) for e in self.ENGS}
            dsem = {e: [st.enter_context(nc.semaphore("d_%s_%d" % (e, i))) for i in range(NDMA_SEM)]
                    for e in self.ENGS if self.ndma[e]}
            block = st.enter_context(nc.Block())
            prog = self

            def run(e, eng):
                for op in prog.ops[e]:
                    for k, v in op.waits.items():
                        if k[0] == "c":
                            eng.wait_ge(csem[k[1]], prog.ops[k[1]][v].semval)
                        else:
                            eng.wait_ge(dsem[k[1]][k[2]], 16 * (v // NDMA_SEM + 1))
                    ins = op.fn(eng)
                    if op.is_dma:
                        ins.then_inc(dsem[e][op.slot % NDMA_SEM], 16)
                    elif op.needed:
                        ins.then_inc(csem[e], 1)

            for e in self.ENGS:
                if not self.ops[e]:
                    continue
                getattr(block, handles[e])(lambda eng, e=e: run(e, eng))


def mkap(t, offset, pairs):
    return bass.AP(t, offset, [list(p) for p in pairs])


def view(ap2, shape):
    if len(shape) == 1:
        return ap2
    names = " ".join("a%d" % i for i in range(len(shape)))
    kw = {"a%d" % i: s for i, s in enumerate(shape)}
    return ap2.rearrange("p (%s) -> p %s" % (names, names), **kw)


class Tl:
    __slots__ = ("ap", "b", "bs")

    def __init__(self, ap, nb=0):
        self.ap = ap
        self.b = Buf()
        self.bs = [Buf() for _ in range(nb)]


class Arena:
    def __init__(self, t, size):
        self.t = t
        self.size = size
        self.top = 0

    def reset(self):
        self.top = 0

    def f32(self, *shape, nb=0):
        n = int(np.prod(shape))
        assert self.top + n <= self.size, ("arena overflow", self.top, n)
        ap = self.t[:, self.top:self.top + n]
        self.top += n
        return Tl(view(ap, shape), nb)

    def bf16(self, *shape, nb=0):
        n = int(np.prod(shape))
        nf = (n + 1) // 2
        assert self.top + nf <= self.size, ("arena overflow", self.top, nf)
        ap = self.t[:, self.top:self.top + nf].bitcast(BF16)[:, 0:n]
        self.top += nf
        return Tl(view(ap, shape), nb)


def bcast_free(ap, n):
    return mkap(ap.tensor, ap.offset, [list(ap.ap[0]), [0, n]])


class Kern:
    def __init__(self, dbg=False):
        self.dbg = dbg
        nc = self.nc = bass.Bass("TRN2", target_bir_lowering=False)
        self.P = Prog(nc)
        self.st = contextlib.ExitStack()
        I = self.I = {}
        O = self.O = {}

        def inp(name, shape, dt=F32):
            I[name] = nc.dram_tensor(name, list(shape), dt, kind="ExternalInput").ap()

        def outp(name, shape, dt=F32):
            O[name] = nc.dram_tensor(name, list(shape), dt, kind="ExternalOutput").ap()

        inp("xin", [NTOK, D]); inp("cc", [2, D]); inp("cache_k", [L, 512, 128]); inp("cache_v", [L, 512, 128])
        inp("state_lru", [L, 2, 512]); inp("state_ssm", [L, 2, 2, 32, 64])
        inp("w_mod", [L, D, 9 * D]); inp("b_mod", [L, 9 * D]); inp("g_pre", [L, 3, D]); inp("g_post", [L, 3, D])
        inp("w_ffn_gate", [L, 2, D, DFF]); inp("w_ffn_up", [L, 2, D, DFF]); inp("w_ffn_down", [L, 2, DFF, D])
        inp("w_in_p", [L, D, WINP]); inp("w_conv", [L, 4, 512]); inp("b_conv", [L, 512])
        inp("w_lru_a", [L, 2, 8, 64, 64]); inp("b_lru_a", [L, 2, 512]); inp("w_lru_x", [L, 2, 8, 64, 64])
        inp("b_lru_x", [L, 2, 512]); inp("lru_lambda", [L, 2, 512])
        inp("s5_lambda_re", [L, 2, 32, 64]); inp("s5_lambda_im", [L, 2, 32, 64]); inp("s5_log_step", [L, 2, 32])
        inp("s5_b_re", [L, 2, 32, 64, 16]); inp("s5_b_im", [L, 2, 32, 64, 16])
        inp("s5_c_re", [L, 2, 32, 16, 64]); inp("s5_c_im", [L, 2, 32, 16, 64]); inp("s5_d", [L, 512])
        inp("w_glu", [L, 512, 2048]); inp("attn_sink", [L, 8]); inp("w_o_lru", [L, 512, D])
        inp("w_o_attn_p", [L, 512, D]); inp("w_out", [L, D, D])
        inp("cst", [128, 5 * 128]); inp("rope", [2, 128, TS])
        outp("y", [NTOK, D]); outp("nk", [NPB, L, TP, 128]); outp("nv", [NPB, L, TP, 128])
        outp("nlru", [NPB, L, 2, 512]); outp("nssm", [NPB, L, 2, 2, 32, 64])
        self.obufs = {k: Buf() for k in O}
        if dbg:
            outp("d_sc", [128, L * 144])

        def scr(name, shape, dt):
            kind = "ExternalOutput" if dbg else "Internal"
            t = nc.dram_tensor(name, list(shape), dt, kind=kind).ap()
            return Tl(t)

        self.XT = scr("s_xt", [8, 128, NTOK], F32)
        self.Qs = scr("s_q", [4, 128, NTOK], BF16)
        self.Ks = scr("s_k", [128, NTOK], BF16)
        self.Vs = scr("s_v", [NTOK, 130], BF16)
        self.XLs = scr("s_xl", [4, 128, NTOK], F32)
        self.YLs = scr("s_yl", [4, 128, NTOK], BF16)
        self.UFs = scr("s_uf", [NT, 128, 32 * 64], BF16)
        self.Gs = scr("s_g", [24, 128, NTOK], BF16)
        self.ATs = scr("s_att", [4, 128, NTOK], BF16)
        self.LRs = scr("s_lru", [4, 128, NTOK], BF16)
        self.SYs = scr("s_s5y", [4, 128, NTOK], BF16)
        self.PFd = [scr("s_pf%d" % l, [128, 2 * 16 * 2 * 128], BF16) for l in range(L)]
        self.Qd = [scr("s_qd%d" % l, [128, 2 * 16 * 2 * 128], BF16) for l in range(L)]
        self.MLd = [scr("s_ml%d" % l, [128, 32 * 128], BF16) for l in range(L)]
        self.A12 = [scr("s_a12%d" % l, [128, 128], F32) for l in range(L)]

        self.sb_t = self.st.enter_context(nc.sbuf_tensor("arena", [128, ARENA_F], F32))
        self.pc_t = self.st.enter_context(nc.sbuf_tensor("persist", [128, 832], F32))
        self.ps_t = self.st.enter_context(nc.psum_tensor("psum", [128, 4096], F32))
        self.A = Arena(self.sb_t, ARENA_F)
        self.PA = Arena(self.pc_t, 832)
        self.pbufs = [Buf() for _ in range(8)]

    def bank(self, i):
        return self.ps_t[:, i * 512:(i + 1) * 512]

    def mm(self, out, lhsT, rhs, start, stop, reads, writes):
        self.P.op("pe", lambda e: e.matmul(out, lhsT=lhsT, rhs=rhs, start=start, stop=stop), reads, writes)

    def tr(self, out, in_, ident, reads, writes):
        self.P.op("pe", lambda e: e.transpose(out, in_, ident), reads, writes)

    def act(self, out, in_, func, reads, writes, scale=1.0, bias=0.0):
        self.P.op("act", lambda e: e.activation(out=out, in_=in_, func=func, scale=scale, bias=bias), reads, writes)

    def tt(self, eng, out, in0, in1, op, reads, writes):
        self.P.op(eng, lambda e: e.tensor_tensor(out=out, in0=in0, in1=in1, op=op), reads, writes)

    def ts(self, eng, out, in0, s1, s2, op0, op1, reads, writes):
        if s2 is None:
            self.P.op(eng, lambda e: e.tensor_scalar(out=out, in0=in0, scalar1=s1, scalar2=None, op0=op0), reads, writes)
        else:
            self.P.op(eng, lambda e: e.tensor_scalar(out=out, in0=in0, scalar1=s1, scalar2=s2, op0=op0, op1=op1), reads, writes)

    def stt(self, out, in0, scalar, in1, op0, op1, reads, writes):
        self.P.op("dve", lambda e: e.scalar_tensor_tensor(out=out, in0=in0, scalar=scalar, in1=in1, op0=op0, op1=op1), reads, writes)

    def cp(self, eng, out, in_, reads, writes):
        if eng == "act":
            self.P.op("act", lambda e: e.copy(out=out, in_=in_), reads, writes)
        else:
            self.P.op(eng, lambda e: e.tensor_copy(out=out, in_=in_), reads, writes)

    def ms(self, eng, out, val, writes):
        self.P.op(eng, lambda e: e.memset(out, val), (), writes)

    def dma(self, q, out, in_, reads, writes, slow=False):
        assert not slow
        shp = tuple(out.shape)
        if len(shp) >= 3 and shp[0] * shp[1] > 256 and shp[1] > 1 and tuple(in_.shape)[:2] == shp[:2]:
            step = max(1, 256 // shp[0])
            fns = []
            for a in range(0, shp[1], step):
                e_ = min(a + step, shp[1])
                o_ = out[:, a:e_]
                i_ = in_[:, a:e_]
                fns.append(lambda e, o_=o_, i_=i_: e.dma_start(out=o_, in_=i_))
            self.P.dma_group(q, fns, reads, writes)
            return
        self.P.dma(q, lambda e: e.dma_start(out=out, in_=in_), reads, writes)

    def vecT(self, dst, src_rows, n, writes):
        stg = self.A.f32(128)
        self.P.dma("sp", lambda e: e.dma_start(out=stg.ap[0:n, :], in_=src_rows), [], [stg.b])
        self.tr(self.bank(7)[:, 0:n], stg.ap[0:n, :], self.identf.ap[0:n, 0:n], [stg.b, self.identf.b], [self.pbufs[7]])
        return self.bank(7)[:, 0:n], self.pbufs[7]

    def recip(self, out, in_, reads, writes):
        self.P.op("dve", lambda e: e.reciprocal(out=out, in_=in_), reads, writes)

    def phase(self):
        self.P.barrier()
        self.A.reset()
        self.pbufs = [Buf() for _ in range(8)]

    def prologue(self):
        A, PA, I = self.A, self.PA, self.I
        cst = A.f32(5, 128)
        self.dma("sp", cst.ap, I["cst"].rearrange("p (a b) -> p a b", a=5), [], [cst.b])
        self.identf = PA.f32(128)
        self.identb = PA.bf16(128)
        self.onesb = PA.bf16(128)
        self.onesf = PA.f32(128)
        self.mprev = PA.bf16(128)
        self.mnext = PA.bf16(128)
        self.cp("dve", self.identf.ap, cst.ap[:, 0, :], [cst.b], [self.identf.b])
        self.cp("dve", self.identb.ap, cst.ap[:, 0, :], [cst.b], [self.identb.b])
        self.cp("dve", self.mprev.ap, cst.ap[:, 1, :], [cst.b], [self.mprev.b])
        self.cp("dve", self.mnext.ap, cst.ap[:, 2, :], [cst.b], [self.mnext.b])
        self.ms("pool", self.onesb.ap, 1.0, [self.onesb.b])
        self.ms("pool", self.onesf.ap, 1.0, [self.onesf.b])
        self.SC = PA.f32(L, 3, 3, 8, 2)
        ccT = A.f32(8, 2)
        src, sb_ = self.vecT(None, I["cc"].rearrange("w (kc p) -> (w kc) p", p=128), 16, None)
        self.cp("dve", ccT.ap.rearrange("p kc w -> p w kc"), view(src, (2, 8)), [sb_], [ccT.b])
        sg = A.f32(8, 2)
        self.act(sg.ap, ccT.ap, AF.Sigmoid, [ccT.b], [sg.b])
        self.tt("dve", ccT.ap, ccT.ap, sg.ap, ALU.mult, [ccT.b, sg.b], [ccT.b])
        wm = [A.f32(8, 1152), A.f32(8, 1152)]
        modt = A.f32(L, 72, 2)
        bm = A.f32(L, 72)
        gp = A.f32(L, 3, 8)
        gq = A.f32(L, 3, 8)
        for l in range(L):
            src, sb_ = self.vecT(None, I["b_mod"][l].rearrange("(c p) -> c p", p=128), 72, None)
            self.cp("dve", bm.ap[:, l, :], src, [sb_], [bm.b])
            src, sb_ = self.vecT(None, I["g_pre"][l].rearrange("i (c p) -> (i c) p", p=128), 24, None)
            self.cp("dve", gp.ap[:, l].rearrange("p i c -> p (i c)"), src, [sb_], [gp.b])
            src, sb_ = self.vecT(None, I["g_post"][l].rearrange("i (c p) -> (i c) p", p=128), 24, None)
            self.cp("dve", gq.ap[:, l].rearrange("p i c -> p (i c)"), src, [sb_], [gq.b])
        pm = self.bank(0)
        pmb = self.pbufs[0]
        k = 0
        for l in range(L):
            for piece in range(8):
                w_ = wm[k % 2]
                q = "sp" if k % 2 == 0 else "act"
                k += 1
                src = I["w_mod"][l][:, piece * 1152:(piece + 1) * 1152].rearrange("(kc p) c -> p kc c", p=128)
                self.dma(q, w_.ap, src, [], [w_.b])
                for cch in range(9):
                    col = (l * 72 + piece * 9 + cch) * 2
                    for kc in range(8):
                        self.mm(pm[:, col:col + 2], w_.ap[:, kc, cch * 128:(cch + 1) * 128], ccT.ap[:, kc, :],
                                kc == 0, kc == 7, [w_.b, ccT.b], [pmb])
        self.cp("dve", modt.ap, view(pm[:, 0:L * 144], (L, 72, 2)), [pmb], [modt.b])
        for w in range(2):
            self.tt("dve", modt.ap[:, :, :, w], modt.ap[:, :, :, w], bm.ap, ALU.add, [modt.b, bm.b], [modt.b])
        for l in range(L):
            m5 = modt.ap[:, l].rearrange("p (i k c) w -> p i k c w", i=3, k=3)
            for w in range(2):
                self.stt(self.SC.ap[:, l, 0, :, :, w], m5[:, :, 1, :, w], 1.0, gp.ap[:, l], ALU.add, ALU.mult,
                         [modt.b, gp.b], [self.SC.b])
                self.cp("dve", self.SC.ap[:, l, 1, :, :, w], m5[:, :, 0, :, w], [modt.b], [self.SC.b])
                self.tt("dve", self.SC.ap[:, l, 2, :, :, w], m5[:, :, 2, :, w], gq.ap[:, l], ALU.mult,
                        [modt.b, gq.b], [self.SC.b])
            for i in (0, 2):
                self.ts("dve", self.SC.ap[:, l, 2, i], self.SC.ap[:, l, 2, i], 0.5, None, ALU.mult, None,
                        [self.SC.b], [self.SC.b])
        if self.dbg:
            self.dma("sp", self.O["d_sc"], self.SC.ap.rearrange("p l k i c w -> p (l k i c w)"), [self.SC.b], [])
        for l in range(L):
            self.s5_prep(l, cst)

    def scal(self, l, kind, i, c, w):
        return self.SC.ap[:, l, kind, i, c, w:w + 1]

    def s5_prep(self, l, cst_unused=None):
        self.phase()
        A, I = self.A, self.I
        cst = A.f32(5, 128)
        self.dma("sp", cst.ap, I["cst"].rearrange("p (a b) -> p a b", a=5), [], [cst.b])
        idf = self.identf
        pb = self.pbufs
        lre_r = A.f32(128); lim_r = A.f32(128); lst_r = A.f32(2); lst_x = A.f32(128)
        self.dma("sp", lre_r.ap[0:32, :], I["s5_lambda_re"][l].rearrange("d (gp g2) n -> (d gp) (g2 n)", g2=2), [], [lre_r.b])
        self.dma("sp", lim_r.ap[0:32, :], I["s5_lambda_im"][l].rearrange("d (gp g2) n -> (d gp) (g2 n)", g2=2), [], [lim_r.b])
        self.dma("sp", lst_r.ap[0:32, :], I["s5_log_step"][l].rearrange("d (gp g2) -> (d gp) g2", g2=2), [], [lst_r.b])
        a_ = lst_r.ap[0:32, :]
        self.cp("dve", view(lst_x.ap[0:32, :], (2, 64)), mkap(a_.tensor, a_.offset, [list(a_.ap[0]), [1, 2], [0, 64]]),
                [lst_r.b], [lst_x.b])
        RR = A.f32(8192)
        braw = [Tl(view(RR.ap[:, c * 2048:(c + 1) * 2048], (128, 16))) for c in range(2)]
        craw = [Tl(view(RR.ap[:, (2 + c) * 2048:(3 + c) * 2048], (16, 2, 64))) for c in range(2)]
        for c, nm in enumerate(("s5_b_re", "s5_b_im")):
            self.dma("act", braw[c].ap[0:32], I[nm][l].rearrange("d (gp g2) n ci -> (d gp) (g2 n) ci", g2=2), [], [braw[c].b])
        for c, nm in enumerate(("s5_c_re", "s5_c_im")):
            srcv = I[nm][l].rearrange("d (gp g2) co n -> (d gp) g2 co n", g2=2)
            for g2 in range(2):
                self.dma("act", craw[c].ap[0:32, :, g2, :], srcv[:, g2], [], [craw[c].b])
        draw = A.f32(16); dx = A.f32(8, 16)
        self.dma("sp", draw.ap[0:32, :], I["s5_d"][l].rearrange("(g ci) -> g ci", ci=16), [], [draw.b])
        a_ = draw.ap[0:32, :]
        self.cp("dve", dx.ap[0:32], mkap(a_.tensor, a_.offset, [list(a_.ap[0]), [0, 8], [1, 16]]), [draw.b], [dx.b])
        sc = A.f32(40, 32)
        names = {}

        def S(nm):
            if nm not in names:
                names[nm] = len(names)
                assert len(names) <= 40
            return sc.ap[:, names[nm], :]

        scb = sc.b
        pt = self.bank(0)
        self.tr(pt[:, 0:32], lre_r.ap[0:32, :], idf.ap[0:32, 0:32], [lre_r.b, idf.b], [pb[0]])
        self.tr(pt[:, 32:64], lim_r.ap[0:32, :], idf.ap[0:32, 0:32], [lim_r.b, idf.b], [pb[0]])
        self.tr(pt[:, 64:96], lst_x.ap[0:32, :], idf.ap[0:32, 0:32], [lst_x.b, idf.b], [pb[0]])
        self.tr(pt[:, 96:128], dx.ap[0:32].rearrange("p a b -> p (a b)"), idf.ap[0:32, 0:32], [dx.b, idf.b], [pb[0]])
        self.cp("dve", S("lr"), pt[:, 0:32], [pb[0]], [scb])
        self.cp("dve", S("li"), pt[:, 32:64], [pb[0]], [scb])
        self.cp("dve", S("ls"), pt[:, 64:96], [pb[0]], [scb])
        dcol = A.f32(32)
        self.cp("dve", dcol.ap, pt[:, 96:128], [pb[0]], [dcol.b])
        BC = []
        for idx, raw in enumerate(braw + craw):
            bk = self.bank(1 + idx % 2)
            bb = pb[1 + idx % 2]
            for j in range(16):
                if idx < 2:
                    src = raw.ap[0:32, :, j]
                else:
                    src = raw.ap[0:32, j].rearrange("p a b -> p (a b)")
                self.tr(bk[:, j * 32:(j + 1) * 32], src, idf.ap[0:32, 0:32], [raw.b, idf.b], [bb])
            t = A.f32(16, 32)
            self.cp("act", t.ap, view(bk, (16, 32)), [bb], [t.b])
            BC.append(t)
        Bre, Bim, Cre, Cim = BC

        def dv(out, a, b, op):
            self.tt("dve", out, a, b, op, [scb], [scb])

        self.ts("dve", S("lr"), S("lr"), -1e-4, None, ALU.min, None, [scb], [scb])
        self.act(S("step"), S("ls"), AF.Exp, [scb], [scb])
        dv(S("xre"), S("lr"), S("step"), ALU.mult)
        dv(S("ang"), S("li"), S("step"), ALU.mult)
        self.act(S("mag"), S("xre"), AF.Exp, [scb], [scb])
        hp = A.f32(1)
        self.ms("dve", hp.ap, math.pi / 2, [hp.b])
        self.act(S("s"), S("ang"), AF.Sin, [scb], [scb], scale=1.0 / 16)
        self.act(S("c"), S("ang"), AF.Sin, [scb, hp.b], [scb], scale=-1.0 / 16, bias=hp.ap[:, 0:1])
        for _ in range(4):
            dv(S("t1"), S("c"), S("c"), ALU.mult)
            dv(S("t2"), S("s"), S("s"), ALU.mult)
            dv(S("t3"), S("c"), S("s"), ALU.mult)
            dv(S("c"), S("t1"), S("t2"), ALU.subtract)
            self.ts("dve", S("s"), S("t3"), 2.0, None, ALU.mult, None, [scb], [scb])
        dv(S("are"), S("mag"), S("c"), ALU.mult)
        dv(S("aim"), S("mag"), S("s"), ALU.mult)
        dv(S("t1"), S("lr"), S("lr"), ALU.mult)
        dv(S("t2"), S("li"), S("li"), ALU.mult)
        dv(S("den"), S("t1"), S("t2"), ALU.add)
        self.recip(S("rden"), S("den"), [scb], [scb])
        self.ts("dve", S("nre"), S("are"), -1.0, None, ALU.add, None, [scb], [scb])
        dv(S("t1"), S("nre"), S("lr"), ALU.mult)
        dv(S("t2"), S("aim"), S("li"), ALU.mult)
        dv(S("t1"), S("t1"), S("t2"), ALU.add)
        dv(S("cre"), S("t1"), S("rden"), ALU.mult)
        dv(S("t1"), S("aim"), S("lr"), ALU.mult)
        dv(S("t2"), S("nre"), S("li"), ALU.mult)
        dv(S("t1"), S("t1"), S("t2"), ALU.subtract)
        dv(S("cim"), S("t1"), S("rden"), ALU.mult)
        dv(S("t1"), S("mag"), S("mag"), ALU.mult)
        self.recip(S("t2"), S("t1"), [scb], [scb])
        dv(S("iare"), S("are"), S("t2"), ALU.mult)
        dv(S("t3"), S("aim"), S("t2"), ALU.mult)
        self.ts("dve", S("iaim"), S("t3"), -1.0, None, ALU.mult, None, [scb], [scb])
        pw = A.f32(9, 2, 32)
        self.ms("dve", pw.ap[:, 0, 0, :], 1.0, [pw.b])
        self.ms("dve", pw.ap[:, 0, 1, :], 0.0, [pw.b])
        for j in range(1, 9):
            for (o, x1, y1, x2, y2, op) in ((pw.ap[:, j, 0, :], pw.ap[:, j - 1, 0, :], S("are"), pw.ap[:, j - 1, 1, :], S("aim"), ALU.subtract),
                                            (pw.ap[:, j, 1, :], pw.ap[:, j - 1, 0, :], S("aim"), pw.ap[:, j - 1, 1, :], S("are"), ALU.add)):
                self.tt("dve", S("t1"), x1, y1, ALU.mult, [pw.b, scb], [scb])
                self.tt("dve", S("t2"), x2, y2, ALU.mult, [pw.b, scb], [scb])
                self.tt("dve", o, S("t1"), S("t2"), op, [scb], [pw.b])
        a12 = A.f32(2, 2, 16, 2)
        for d in range(2):
            for c in range(2):
                self.cp("dve", a12.ap[:, d, 0, :, c], pw.ap[:, 8, 0, d * 16:(d + 1) * 16], [pw.b], [a12.b])
            self.ts("dve", a12.ap[:, d, 1, :, 0], pw.ap[:, 8, 1, d * 16:(d + 1) * 16], -1.0, None, ALU.mult, None, [pw.b], [a12.b])
            self.cp("dve", a12.ap[:, d, 1, :, 1], pw.ap[:, 8, 1, d * 16:(d + 1) * 16], [pw.b], [a12.b])
        self.dma("sp", self.A12[l].ap, a12.ap.rearrange("p d k g c -> p (d k g c)"), [a12.b], [])
        self.cp("dve", S("pr"), S("iare"), [scb], [scb])
        self.cp("dve", S("pi"), S("iaim"), [scb], [scb])
        for _ in range(3):
            dv(S("t1"), S("pr"), S("pr"), ALU.mult)
            dv(S("t2"), S("pi"), S("pi"), ALU.mult)
            dv(S("t3"), S("pr"), S("pi"), ALU.mult)
            dv(S("pr"), S("t1"), S("t2"), ALU.subtract)
            self.ts("dve", S("pi"), S("t3"), 2.0, None, ALU.mult, None, [scb], [scb])
        Bbr = A.f32(16, 32); Bbi = A.f32(16, 32); T1 = A.f32(16, 32); T2 = A.f32(16, 32)

        def bc16(ap):
            return mkap(ap.tensor, ap.offset, [list(ap.ap[0]), [0, 16], [1, 32]])

        for (o, x1, x2, op) in ((Bbr, Bre, Bim, ALU.subtract), (Bbi, Bim, Bre, ALU.add)):
            self.tt("dve", T1.ap, x1.ap, bc16(S("cre")), ALU.mult, [x1.b, scb], [T1.b])
            self.tt("dve", T2.ap, x2.ap, bc16(S("cim")), ALU.mult, [x2.b, scb], [T2.b])
            self.tt("dve", o.ap, T1.ap, T2.ap, op, [T1.b, T2.b], [o.b])
        XS = A.f32(2, 16, 2, 8, 16)
        Q = A.f32(2, 16, 2, 8, 16)
        tmp = [[A.f32(16, 16), A.f32(16, 16)] for _ in range(2)]

        def pwv(j, c, d):
            a = pw.ap[:, j, c, d * 16:(d + 1) * 16]
            return mkap(a.tensor, a.offset, [list(a.ap[0]), [1, 16], [0, 16]])

        def mat(tl, d):
            return tl.ap[:, :, d * 16:(d + 1) * 16].rearrange("p x g -> p g x")

        k = 0
        for d in range(2):
            for s in range(8):
                for (dst, Mr, Mi, j, neg_im) in ((XS, Bbr, Bbi, (7 - s) if d == 0 else s, False),
                                                 (Q, Cre, Cim, (s + 1) if d == 0 else (8 - s), True)):
                    eng = "dve" if k % 2 == 0 else "pool"
                    t1, t2 = tmp[k % 2]
                    k += 1
                    rd = [Mr.b, Mi.b, pw.b]
                    self.tt(eng, t1.ap, mat(Mr, d), pwv(j, 0, d), ALU.mult, rd, [t1.b])
                    self.tt(eng, t2.ap, mat(Mi, d), pwv(j, 1, d), ALU.mult, rd, [t2.b])
                    self.tt(eng, dst.ap[:, d, :, 0, s, :], t1.ap, t2.ap, ALU.subtract, [t1.b, t2.b], [dst.b])
                    self.tt(eng, t1.ap, mat(Mr, d), pwv(j, 1, d), ALU.mult, rd, [t1.b])
                    self.tt(eng, t2.ap, mat(Mi, d), pwv(j, 0, d), ALU.mult, rd, [t2.b])
                    self.tt(eng, dst.ap[:, d, :, 1, s, :], t1.ap, t2.ap, ALU.add, [t1.b, t2.b], [dst.b])
        qim = Q.ap[:, :, :, 1].rearrange("p d g s c -> p (d g) (s c)")
        self.ts("pool", qim, qim, -1.0, None, ALU.mult, None, [Q.b], [Q.b])
        XM = A.f32(2, 16, 2, 128)
        X4 = XS.ap.rearrange("p d g c s i -> p d g c (s i)")
        big = [A.f32(16, 128), A.f32(16, 128)]

        def pv(nm, d):
            a = S(nm)[:, d * 16:(d + 1) * 16]
            return mkap(a.tensor, a.offset, [list(a.ap[0]), [1, 16], [0, 128]])

        for d in range(2):
            eng = "dve" if d == 0 else "pool"
            t1, t2 = big
            rd = [XS.b, scb]
            self.tt(eng, t1.ap, X4[:, d, :, 0, :], pv("pr", d), ALU.mult, rd, [t1.b])
            self.tt(eng, t2.ap, X4[:, d, :, 1, :], pv("pi", d), ALU.mult, rd, [t2.b])
            self.tt(eng, XM.ap[:, d, :, 0, :], t1.ap, t2.ap, ALU.subtract, [t1.b, t2.b], [XM.b])
            self.tt(eng, t1.ap, X4[:, d, :, 1, :], pv("pr", d), ALU.mult, rd, [t1.b])
            self.tt(eng, t2.ap, X4[:, d, :, 0, :], pv("pi", d), ALU.mult, rd, [t2.b])
            self.tt(eng, XM.ap[:, d, :, 1, :], t1.ap, t2.ap, ALU.add, [t1.b, t2.b], [XM.b])
        self.P.barrier()
        PFs = Tl(view(RR.ap[:, 0:4096].bitcast(BF16), (64, 128)))
        Q4 = Q.ap.rearrange("p d g c s i -> p (d g c) (s i)")
        XS3 = XS.ap.rearrange("p d g c s i -> p (d g c) (s i)")
        for q4 in range(16):
            bk = self.bank(3 + q4 % 2); bb = pb[3 + q4 % 2]
            for jj in range(4):
                self.tr(bk[:, jj * 128:(jj + 1) * 128], XS3[:, q4 * 4 + jj, :], idf.ap, [XS.b, idf.b], [bb])
            self.cp("act", PFs.ap[:, q4 * 4:(q4 + 1) * 4, :], view(bk, (4, 128)), [bb], [PFs.b])
        self.dma("sp", self.PFd[l].ap, PFs.ap.rearrange("p a b -> p (a b)"), [PFs.b], [])
        Qb = Tl(view(RR.ap[:, 4096:8192].bitcast(BF16), (64, 128)))
        self.cp("pool", Qb.ap, Q4, [Q.b], [Qb.b])
        self.dma("sp", self.Qd[l].ap, Qb.ap.rearrange("p a b -> p (a b)"), [Qb.b], [])
        MLs = A.bf16(32, 128)
        mt = [A.f32(128), A.f32(128)]
        mu = [A.f32(128), A.f32(128)]
        XM4 = XM.ap
        Q5 = Q.ap.rearrange("p d g c s i -> p d g c (s i)")
        for g in range(32):
            gp_, g2 = g // 2, g % 2
            bk = self.bank(5 + g % 2); bb = pb[5 + g % 2]
            sl = slice(g2 * 64, (g2 + 1) * 64)
            for d in range(2):
                for c in range(2):
                    self.mm(bk[:, d * 128:(d + 1) * 128], XM4[sl, d, gp_, c, :], Q5[sl, d, gp_, c, :], c == 0, c == 1,
                            [XM.b, Q.b], [bb])
            t = mt[g % 2]
            u = mu[g % 2]
            self.tt("dve", t.ap, bk[:, 0:128], cst.ap[:, 3, :], ALU.mult, [bb, cst.b], [t.b])
            self.tt("dve", u.ap, bk[:, 128:256], cst.ap[:, 4, :], ALU.mult, [bb, cst.b], [u.b])
            self.tt("pool", t.ap, t.ap, u.ap, ALU.add, [t.b, u.b], [t.b])
            self.stt(MLs.ap[:, g, :], idf.ap, dcol.ap[:, g:g + 1], t.ap, ALU.mult, ALU.add, [idf.b, dcol.b, t.b], [MLs.b])
        self.dma("sp", self.MLd[l].ap, MLs.ap.rearrange("p a b -> p (a b)"), [MLs.b], [])

    def alloc_norm(self):
        A = self.A
        self.n_rstd = A.f32(TT)
        self.n_tmp = [A.f32(TT), A.f32(TT)]
        self.n_k = 0

    def rstd_from(self, bankidx):
        r = self.n_rstd
        pbk = self.pbufs[bankidx]
        self.ts("dve", r.ap, self.bank(bankidx), 1.0 / D, EPS, ALU.mult, ALU.add, [pbk], [r.b])
        self.act(r.ap, r.ap, AF.Sqrt, [r.b], [r.b])
        self.recip(r.ap, r.ap, [r.b], [r.b])
        return r

    def load_x(self, xt, ti, first=False, alias=None):
        tok0 = ti * TT
        if not first:
            self.dma("sp", xt.ap, self.XT.ap[:, :, tok0:tok0 + TT].rearrange("c p t -> p c t"), [], xt.bs)
            return
        xtm, extra = alias
        self.dma("sp", xtm.ap, self.I["xin"][tok0:tok0 + TT].rearrange("(b p) d -> p b d", p=128), [], [xtm.b] + extra)
        for c in range(8):
            bi = c % 2
            for blk in range(4):
                self.tr(self.bank(bi)[:, blk * 128:(blk + 1) * 128], xtm.ap[:, blk, c * 128:(c + 1) * 128], self.identf.ap,
                        [xtm.b, self.identf.b] + extra, [self.pbufs[bi]])
            self.cp("act" if c % 2 else "dve", xt.ap[:, c, :], self.bank(bi), [self.pbufs[bi]], [xt.bs[c]])

    def store_x(self, xt, ti, last=False, alias=None):
        tok0 = ti * TT
        if not last:
            self.dma("sp", self.XT.ap[:, :, tok0:tok0 + TT].rearrange("c p t -> p c t"), xt.ap, xt.bs, [])
            return
        yst, extra = alias
        for blk in range(4):
            for half in range(2):
                bi = (blk * 2 + half) % 2
                for cc in range(4):
                    c = half * 4 + cc
                    self.tr(self.bank(bi)[:, cc * 128:(cc + 1) * 128], xt.ap[:, c, blk * 128:(blk + 1) * 128], self.identf.ap,
                            [xt.bs[c], self.identf.b], [self.pbufs[bi]])
                self.cp("act" if half else "dve", yst.ap[:, blk, half * 512:(half + 1) * 512], self.bank(bi),
                        [self.pbufs[bi]], [yst.b] + extra)
        self.dma("sp", self.O["y"][tok0:tok0 + TT].rearrange("(b p) d -> p b d", p=128), yst.ap, [yst.b] + extra, [])

    def prenorm(self, l, sub, which, xt, hT, sq, bankidx=7):
        pbk = self.pbufs[bankidx]
        for c in range(8):
            self.tt("pool", sq.ap[:, c, :], xt.ap[:, c, :], xt.ap[:, c, :], ALU.mult, [xt.bs[c]], [sq.b])
        for c in range(8):
            self.mm(self.bank(bankidx), self.onesb.ap, sq.ap[:, c, :], c == 0, c == 7, [self.onesb.b, sq.b], [pbk])
        r = self.rstd_from(bankidx)
        for c in range(8):
            t = self.n_tmp[self.n_k % 2]
            self.n_k += 1
            self.tt("dve", t.ap, xt.ap[:, c, :], r.ap, ALU.mult, [xt.bs[c], r.b], [t.b])
            self.ts("pool", hT.ap[:, c, :], t.ap, self.scal(l, 0, sub, c, which), self.scal(l, 1, sub, c, which),
                    ALU.mult, ALU.add, [t.b, self.SC.b], [hT.b])

    def post_chunk(self, m, pbank, fT, sqr, ssbank):
        pbk = self.pbufs[pbank]
        s = sqr[m % 2]
        self.cp("act", fT.ap[:, m, :], self.bank(pbank), [pbk], [fT.b])
        self.act(s.ap, self.bank(pbank), AF.Square, [pbk], [s.b])
        if m > 0:
            p_ = sqr[(m - 1) % 2]
            self.mm(self.bank(ssbank), self.onesb.ap, p_.ap, m == 1, False, [self.onesb.b, p_.b], [self.pbufs[ssbank]])
        if m == 7:
            self._last_sq = s

    def post_update(self, l, sub, which, xt, fT, ssbank):
        p_ = self._last_sq
        self.mm(self.bank(ssbank), self.onesb.ap, p_.ap, False, True, [self.onesb.b, p_.b], [self.pbufs[ssbank]])
        r = self.rstd_from(ssbank)
        for m in range(8):
            t = self.n_tmp[self.n_k % 2]
            self.n_k += 1
            self.stt(t.ap, fT.ap[:, m, :], self.scal(l, 2, sub, m, which), r.ap, ALU.mult, ALU.mult,
                     [fT.b, self.SC.b, r.b], [t.b])
            self.tt("pool", xt.ap[:, m, :], xt.ap[:, m, :], t.ap, ALU.add, [xt.bs[m], t.b], [xt.bs[m]])

    def ffn_pass(self, l, i, first=False, last=False):
        self.phase()
        A, I = self.A, self.I
        sub = 0 if i == 0 else 2
        wg = A.bf16(8, DFF, nb=2); wu = A.bf16(8, DFF, nb=2); wd = A.bf16(NFC, D, nb=2)
        for h in range(2):
            cs = slice(h * 1408, (h + 1) * 1408)
            self.dma("pool", wg.ap[:, :, cs], I["w_ffn_gate"][l, i][:, cs].rearrange("(kc p) f -> p kc f", p=128), [], [wg.bs[h]])
            self.dma("pool", wu.ap[:, :, cs], I["w_ffn_up"][l, i][:, cs].rearrange("(kc p) f -> p kc f", p=128), [], [wu.bs[h]])
        wdv = I["w_ffn_down"][l, i].rearrange("(fc p) d -> p fc d", p=128)
        for h in range(2):
            fs = slice(h * 11, (h + 1) * 11)
            self.dma("pool", wd.ap[:, fs, :], wdv[:, fs, :], [], [wd.bs[h]])
        xt = A.f32(8, TT, nb=8)
        r1 = A.f32(8, TT)
        hT = Tl(view(r1.ap.rearrange("p a b -> p (a b)")[:, 0:2048].bitcast(BF16), (8, TT)))
        sq = Tl(view(r1.ap.rearrange("p a b -> p (a b)")[:, 2048:4096].bitcast(BF16), (8, TT)))
        fT = r1
        hT.b = sq.b = fT.b
        actT = A.bf16(NFC, TT, nb=NFC)
        al = Tl(view(actT.ap.rearrange("p a b -> p (a b)")[:, 0:8192].bitcast(F32), (4, D)))
        sqr = [A.bf16(TT), A.bf16(TT)]
        sgr = [A.f32(TT), A.f32(TT)]
        self.alloc_norm()
        pb = self.pbufs
        import os
        nt_ = int(os.environ.get("DBG_NT", NT))
        lvl = int(os.environ.get("DBG_LVL", 9))
        for ti in range(nt_):
            which = 0 if ti < 8 else 1
            self.load_x(xt, ti, first, (al, actT.bs))
            if lvl < 1:
                self.store_x(xt, ti, last, (al, actT.bs))
                continue
            self.prenorm(l, sub, which, xt, hT, sq)
            if lvl < 2:
                self.store_x(xt, ti, last, (al, actT.bs))
                continue
            for j in range(NFC):
                bg, bu = j % 2, 2 + j % 2
                cs = slice(j * 128, (j + 1) * 128)
                for kc in range(8):
                    self.mm(self.bank(bg), wg.ap[:, kc, cs], hT.ap[:, kc, :], kc == 0, kc == 7, [wg.bs[j // 11], hT.b], [pb[bg]])
                for kc in range(8):
                    self.mm(self.bank(bu), wu.ap[:, kc, cs], hT.ap[:, kc, :], kc == 0, kc == 7, [wu.bs[j // 11], hT.b], [pb[bu]])
                s = sgr[j % 2]
                self.act(s.ap, self.bank(bg), AF.Silu, [pb[bg]], [s.b])
                self.tt("dve", actT.ap[:, j, :], s.ap, self.bank(bu), ALU.mult, [s.b, pb[bu]], [actT.bs[j]])
            if lvl < 3:
                self.store_x(xt, ti, last, (al, actT.bs))
                continue
            for m in range(8):
                bf = 4 + m % 2
                for j in range(NFC):
                    self.mm(self.bank(bf), wd.ap[:, j, m * 128:(m + 1) * 128], actT.ap[:, j, :], j == 0, j == NFC - 1,
                            [wd.bs[j // 11], actT.bs[j]], [pb[bf]])
                if lvl >= 4:
                    self.post_chunk(m, bf, fT, sqr, 6)
            if lvl >= 5:
                self.post_update(l, sub, which, xt, fT, 6)
            self.store_x(xt, ti, last, (al, actT.bs))


    def ffn_pass2(self, l, i, first=False, last=False):
        self.phase()
        A, I = self.A, self.I
        TF = 256
        NTF = NTOK // TF
        sub = 0 if i == 0 else 2
        wg = A.bf16(8, DFF, nb=2); wu = A.bf16(8, DFF, nb=2); wd = A.bf16(NFC, D, nb=2)
        for h in range(2):
            cs = slice(h * 1408, (h + 1) * 1408)
            self.dma("pool", wg.ap[:, :, cs], I["w_ffn_gate"][l, i][:, cs].rearrange("(kc p) f -> p kc f", p=128), [], [wg.bs[h]])
            self.dma("pool", wu.ap[:, :, cs], I["w_ffn_up"][l, i][:, cs].rearrange("(kc p) f -> p kc f", p=128), [], [wu.bs[h]])
        wdv = I["w_ffn_down"][l, i].rearrange("(fc p) d -> p fc d", p=128)
        for h in range(2):
            fs = slice(h * 11, (h + 1) * 11)
            self.dma("pool", wd.ap[:, fs, :], wdv[:, fs, :], [], [wd.bs[h]])
        xts = [A.f32(8, TF, nb=8), A.f32(8, TF, nb=8)]
        hTs = [A.bf16(8, TF), A.bf16(8, TF)]
        sq = A.bf16(8, TF)
        fT = A.f32(8, TF)
        actT = A.bf16(NFC, TF, nb=NFC)
        sqr = [A.bf16(TF), A.bf16(TF)]
        sgr = [A.f32(TF), A.f32(TF)]
        stg = A.f32(2, D) if (first or last) else None
        rs_pre = A.f32(TF); rs_post = A.f32(TF)
        tmp_pre = [A.f32(TF), A.f32(TF)]; tmp_post = [A.f32(TF), A.f32(TF)]
        pb = self.pbufs
        bk = lambda b_: self.bank(b_)[:, 0:TF]

        def rstd(r, bankidx):
            self.ts("dve", r.ap, bk(bankidx), 1.0 / D, EPS, ALU.mult, ALU.add, [pb[bankidx]], [r.b])
            self.act(r.ap, r.ap, AF.Sqrt, [r.b], [r.b])
            self.recip(r.ap, r.ap, [r.b], [r.b])

        def load(ti):
            xt = xts[ti % 2]
            tok0 = ti * TF
            if not first:
                self.dma("sp", xt.ap, self.XT.ap[:, :, tok0:tok0 + TF].rearrange("c p t -> p c t"), [], xt.bs)
                return
            self.dma("sp", stg.ap, I["xin"][tok0:tok0 + TF].rearrange("(b p) d -> p b d", p=128), [], [stg.b])
            for c in range(8):
                bi = c % 2
                for blk in range(2):
                    self.tr(self.bank(bi)[:, blk * 128:(blk + 1) * 128], stg.ap[:, blk, c * 128:(c + 1) * 128], self.identf.ap,
                            [stg.b, self.identf.b], [pb[bi]])
                self.cp("act", xt.ap[:, c, :], bk(bi), [pb[bi]], [xt.bs[c]])

        def store(ti):
            xt = xts[ti % 2]
            tok0 = ti * TF
            if not last:
                self.dma("sp", self.XT.ap[:, :, tok0:tok0 + TF].rearrange("c p t -> p c t"), xt.ap, xt.bs, [])
                return
            for blk in range(2):
                for half in range(2):
                    bi = half
                    for cc in range(4):
                        c = half * 4 + cc
                        self.tr(self.bank(bi)[:, cc * 128:(cc + 1) * 128], xt.ap[:, c, blk * 128:(blk + 1) * 128], self.identf.ap,
                                [xt.bs[c], self.identf.b], [pb[bi]])
                    self.cp("act", stg.ap[:, blk, half * 512:(half + 1) * 512], self.bank(bi), [pb[bi]], [stg.b])
            self.dma("sp", self.O["y"][tok0:tok0 + TF].rearrange("(b p) d -> p b d", p=128), stg.ap, [stg.b], [])

        def prenorm(ti):
            xt = xts[ti % 2]; hT = hTs[ti % 2]
            which = 0 if ti * TF < TS else 1
            for c in range(8):
                self.tt("pool", sq.ap[:, c, :], xt.ap[:, c, :], xt.ap[:, c, :], ALU.mult, [xt.bs[c]], [sq.b])
            for c in range(8):
                self.mm(bk(7), self.onesb.ap, sq.ap[:, c, :], c == 0, c == 7, [self.onesb.b, sq.b], [pb[7]])
            rstd(rs_pre, 7)
            for c in range(8):
                t = tmp_pre[c % 2]
                self.tt("dve", t.ap, xt.ap[:, c, :], rs_pre.ap, ALU.mult, [xt.bs[c], rs_pre.b], [t.b])
                self.ts("pool", hT.ap[:, c, :], t.ap, self.scal(l, 0, sub, c, which), self.scal(l, 1, sub, c, which),
                        ALU.mult, ALU.add, [t.b, self.SC.b], [hT.b])

        load(0)
        prenorm(0)

        def upd(tj, m):
            xt_ = xts[tj % 2]
            wh_ = 0 if tj * TF < TS else 1
            t = tmp_post[m % 2]
            self.stt(t.ap, fT.ap[:, m, :], self.scal(l, 2, sub, m, wh_), rs_post.ap, ALU.mult, ALU.mult,
                     [fT.b, self.SC.b, rs_post.b], [t.b])
            self.tt("pool", xt_.ap[:, m, :], xt_.ap[:, m, :], t.ap, ALU.add, [xt_.bs[m], t.b], [xt_.bs[m]])

        for ti in range(NTF):
            xt = xts[ti % 2]; hT = hTs[ti % 2]
            for j in range(NFC):
                bg, bu = j % 2, 2 + j % 2
                cs = slice(j * 128, (j + 1) * 128)
                for kc in range(8):
                    self.mm(bk(bg), wg.ap[:, kc, cs], hT.ap[:, kc, :], kc == 0, kc == 7, [wg.bs[j // 11], hT.b], [pb[bg]])
                for kc in range(8):
                    self.mm(bk(bu), wu.ap[:, kc, cs], hT.ap[:, kc, :], kc == 0, kc == 7, [wu.bs[j // 11], hT.b], [pb[bu]])
                s = sgr[j % 2]
                self.act(s.ap, bk(bg), AF.Silu, [pb[bg]], [s.b])
                self.tt("dve", actT.ap[:, j, :], s.ap, bk(bu), ALU.mult, [s.b, pb[bu]], [actT.bs[j]])
                if ti >= 1 and 2 <= j < 10:
                    upd(ti - 1, j - 2)
            if ti >= 1:
                store(ti - 1)
            if ti + 1 < NTF:
                load(ti + 1)
                prenorm(ti + 1)
            for m in range(8):
                bf = 4 + m % 2
                for j in range(NFC):
                    self.mm(bk(bf), wd.ap[:, j, m * 128:(m + 1) * 128], actT.ap[:, j, :], j == 0, j == NFC - 1,
                            [wd.bs[j // 11], actT.bs[j]], [pb[bf]])
                s = sqr[m % 2]
                self.cp("act", fT.ap[:, m, :], bk(bf), [pb[bf]], [fT.b])
                self.act(s.ap, bk(bf), AF.Square, [pb[bf]], [s.b])
                if m > 0:
                    p_ = sqr[(m - 1) % 2]
                    self.mm(bk(6), self.onesb.ap, p_.ap, m == 1, False, [self.onesb.b, p_.b], [pb[6]])
            p_ = sqr[1]
            self.mm(bk(6), self.onesb.ap, p_.ap, False, True, [self.onesb.b, p_.b], [pb[6]])
            rstd(rs_post, 6)
        for m in range(8):
            upd(NTF - 1, m)
        store(NTF - 1)

    def p2_pass(self, l):
        self.phase()
        A, I, O = self.A, self.I, self.O
        W = A.bf16(8, WINP, nb=4)
        wv = I["w_in_p"][l].rearrange("(kc p) c -> p kc c", p=128)
        bounds = [0, C_V, C_U, C_G + 1536, WINP]
        for h in range(4):
            self.dma("pool", W.ap[:, :, bounds[h]:bounds[h + 1]], wv[:, :, bounds[h]:bounds[h + 1]], [], [W.bs[h]])

        def wb(col):
            for h in range(4):
                if col < bounds[h + 1]:
                    return W.bs[h]

        xt = A.f32(8, TT, nb=8)
        hTs = [A.bf16(8, TT), A.bf16(8, TT)]; sq = A.bf16(8, TT)
        rt = A.f32(2, TT)
        qst = [A.bf16(TT) for _ in range(3)]
        rtmp = [A.f32(TT) for _ in range(4)]
        xlst = A.f32(4, TT); ylst = A.bf16(4, TT)
        vst = A.bf16(4, 2, 65); vf = A.f32(4, 128); kf = A.f32(4, 128)
        utok = A.bf16(32, 8, 16); ufst = A.bf16(32, 64)
        gst = [A.bf16(8, TT), A.bf16(8, TT)]
        self.alloc_norm()
        pb = self.pbufs
        self.ms("pool", vst.ap[:, :, :, 64:65], 1.0, [vst.b])
        nfm = 0
        import os
        sec = os.environ.get("DBG_P2", "ABCDE")
        for ti in range(int(os.environ.get("DBG_NT", NT))):
            which = 0 if ti < 8 else 1
            sample = ti < 8
            tok0 = ti * TT
            hT = hTs[ti % 2]
            if ti == 0:
                self.load_x(xt, 0)
                self.prenorm(l, 1, 0, xt, hTs[0], sq)
            if sample:
                self.dma("act", rt.ap, I["rope"][:, :, tok0:tok0 + TT].rearrange("a p t -> p a t"), [], [rt.b])
            if ti + 1 < NT:
                t1_ = (ti + 1) * TT
                self.dma("act", xt.ap, self.XT.ap[:, :, t1_:t1_ + TT].rearrange("c p t -> p c t"), [], xt.bs)

            def fm(col, bi):
                for kc in range(8):
                    self.mm(self.bank(bi), W.ap[:, kc, col:col + 128], hT.ap[:, kc, :], kc == 0, kc == 7, [wb(col), hT.b], [pb[bi]])

            self.P.mute = "A" not in sec
            for ci in range(5):
                col = C_Q + ci * 128 if ci < 4 else C_K
                cols = C_QS + ci * 128 if ci < 4 else C_KS
                b0 = ci % 2
                fm(col, b0)
                q_ = qst[ci % 3]
                if sample:
                    fm(cols, 2 + b0)
                    t1 = rtmp[(ci % 2) * 2]; t2 = rtmp[(ci % 2) * 2 + 1]
                    self.tt("dve", t1.ap, self.bank(b0), rt.ap[:, 0, :], ALU.mult, [pb[b0], rt.b], [t1.b])
                    self.tt("dve", t2.ap, self.bank(2 + b0), rt.ap[:, 1, :], ALU.mult, [pb[2 + b0], rt.b], [t2.b])
                    self.tt("pool", q_.ap, t1.ap, t2.ap, ALU.add, [t1.b, t2.b], [q_.b])
                else:
                    self.cp("act", q_.ap, self.bank(b0), [pb[b0]], [q_.b])
                dst = self.Qs.ap[ci][:, tok0:tok0 + TT] if ci < 4 else self.Ks.ap[:, tok0:tok0 + TT]
                self.dma("sp", dst, q_.ap, [q_.b], [])
            self.P.mute = False
            if ti + 1 < NT:
                self.prenorm(l, 1, 0 if ti + 1 < 8 else 1, xt, hTs[(ti + 1) % 2], sq)
            self.P.mute = "B" not in sec
            for blk in range(4):
                for kc in range(8):
                    self.mm(self.bank(4)[:, blk * 128:(blk + 1) * 128], hT.ap[:, kc, blk * 128:(blk + 1) * 128],
                            W.ap[:, kc, C_V:C_V + 128], kc == 0, kc == 7, [hT.b, wb(C_V)], [pb[4]])
            self.cp("act", vst.ap[:, :, :, 0:64], view(self.bank(4), (4, 2, 64)), [pb[4]], [vst.b])
            if not sample:
                self.cp("act", vf.ap, view(self.bank(4), (4, 128)), [pb[4]], [vf.b])
            self.dma("sp", self.Vs.ap[tok0:tok0 + TT].rearrange("(b p) c -> p b c", p=128),
                     vst.ap.rearrange("p b h c -> p b (h c)"), [vst.b], [])
            if not sample:
                for pp in range(2):
                    pj = 2 * (ti - 8) + pp
                    self.dma("sp", O["nv"][pj, l].rearrange("(b p) c -> p b c", p=128), vf.ap[:, 2 * pp:2 * pp + 2, :],
                             [vf.b], [])
                for blk in range(4):
                    for kc in range(8):
                        self.mm(self.bank(4)[:, blk * 128:(blk + 1) * 128], hT.ap[:, kc, blk * 128:(blk + 1) * 128],
                                W.ap[:, kc, C_K:C_K + 128], kc == 0, kc == 7, [hT.b, wb(C_K)], [pb[4]])
                self.cp("dve", kf.ap, view(self.bank(4), (4, 128)), [pb[4]], [kf.b])
                for pp in range(2):
                    pj = 2 * (ti - 8) + pp
                    self.dma("sp", O["nk"][pj, l].rearrange("(b p) c -> p b c", p=128), kf.ap[:, 2 * pp:2 * pp + 2, :],
                             [kf.b], [])
            self.P.mute = "C" not in sec
            for c in range(4):
                b0 = c % 2
                fm(C_XL + c * 128, b0)
                self.cp("act", xlst.ap[:, c, :], self.bank(b0), [pb[b0]], [xlst.b])
            self.dma("sp", self.XLs.ap[:, :, tok0:tok0 + TT].rearrange("c p t -> p c t"), xlst.ap, [xlst.b], [])
            for c in range(4):
                b0 = c % 2
                fm(C_YL + c * 128, b0)
                self.cp("dve", ylst.ap[:, c, :], self.bank(b0), [pb[b0]], [ylst.b])
            self.dma("sp", self.YLs.ap[:, :, tok0:tok0 + TT].rearrange("c p t -> p c t"), ylst.ap, [ylst.b], [])
            self.P.mute = "D" not in sec
            hs = hT.ap.rearrange("p c (k s) -> p c s k", s=8)
            for s in range(8):
                bi = 5 + s % 2
                for kc in range(8):
                    self.mm(self.bank(bi)[0:64, :], hs[:, kc, s, :], W.ap[:, kc, C_U:C_U + 512], kc == 0, kc == 7,
                            [hT.b, wb(C_U)], [pb[bi]])
                self.cp("act" if s % 2 else "dve", utok.ap[0:64, :, s, :], view(self.bank(bi)[0:64, :], (32, 16)), [pb[bi]], [utok.b])
            for half in range(2):
                bi = 2 + half
                pbf = self.bank(bi).bitcast(BF16)
                for gg in range(16):
                    g = half * 16 + gg
                    self.tr(pbf[:, gg * 64:(gg + 1) * 64], utok.ap[0:64, g].rearrange("p s c -> p (s c)"),
                            self.identb.ap[0:64, 0:64], [utok.b, self.identb.b], [pb[bi]])
                self.cp("act" if half else "dve", ufst.ap[:, half * 16:(half + 1) * 16, :], view(pbf[:, 0:1024], (16, 64)),
                        [pb[bi]], [ufst.b])
            self.dma("sp", self.UFs.ap[ti], ufst.ap.rearrange("p g k -> p (g k)"), [ufst.b], [])
            self.P.mute = "E" not in sec
            for cc in range(24):
                b0 = cc % 2
                fm(C_G + cc * 128, b0)
                g_ = gst[(cc // 8) % 2]
                self.act(g_.ap[:, cc % 8, :], self.bank(b0), AF.Sigmoid, [pb[b0]], [g_.b])
                if cc % 8 == 7:
                    grp = cc // 8
                    self.dma("sp", self.Gs.ap[grp * 8:(grp + 1) * 8][:, :, tok0:tok0 + TT].rearrange("c p t -> p c t"), g_.ap,
                             [g_.b], [])
            self.P.mute = False

    def p4_pass(self, l):
        self.phase()
        A, I = self.A, self.I
        wol = A.bf16(4, D); woa = A.bf16(4, D); wgl = A.bf16(4, 2048); wo = A.bf16(8, D)
        self.dma("pool", wgl.ap, I["w_glu"][l].rearrange("(kc p) c -> p kc c", p=128), [], [wgl.b])
        self.dma("pool", wol.ap, I["w_o_lru"][l].rearrange("(kc p) c -> p kc c", p=128), [], [wol.b])
        self.dma("pool", woa.ap, I["w_o_attn_p"][l].rearrange("(kc p) c -> p kc c", p=128), [], [woa.b])
        self.dma("pool", wo.ap, I["w_out"][l].rearrange("(kc p) c -> p kc c", p=128), [], [wo.b])
        xt = A.f32(8, TT, nb=8)
        ins = [(A.bf16(4, TT), A.bf16(4, TT), A.bf16(4, TT), A.bf16(24, TT)) for _ in range(2)]
        mg = A.bf16(8, TT)
        fT = A.f32(8, TT)
        sqr = [A.bf16(TT), A.bf16(TT)]
        tmp = [A.f32(TT) for _ in range(6)]
        self.alloc_norm()
        pb = self.pbufs

        def loads(ti):
            tok0 = ti * TT
            lr, at, sy, G = ins[ti % 2]
            for (dst, src) in ((sy, self.SYs), (lr, self.LRs), (at, self.ATs), (G, self.Gs)):
                self.dma("sp", dst.ap, src.ap[:, :, tok0:tok0 + TT].rearrange("c p t -> p c t"), [], [dst.b])

        loads(0)
        for ti in range(NT):
            which = 0 if ti < 8 else 1
            if ti + 1 < NT:
                loads(ti + 1)
            self.load_x(xt, ti)
            lr, at, sy, G = ins[ti % 2]
            for m in range(8):
                ba, bz = m % 2, 2 + m % 2
                for kc in range(4):
                    self.mm(self.bank(ba), wgl.ap[:, kc, m * 128:(m + 1) * 128], sy.ap[:, kc, :], kc == 0, kc == 3, [wgl.b, sy.b], [pb[ba]])
                for kc in range(4):
                    self.mm(self.bank(bz), wgl.ap[:, kc, 1024 + m * 128:1024 + (m + 1) * 128], sy.ap[:, kc, :], kc == 0, kc == 3,
                            [wgl.b, sy.b], [pb[bz]])
                s = tmp[m % 2]; t = tmp[2 + m % 2]
                self.act(s.ap, self.bank(bz), AF.Sigmoid, [pb[bz]], [s.b])
                self.tt("dve", t.ap, self.bank(ba), s.ap, ALU.mult, [pb[ba], s.b], [t.b])
                self.tt("pool", fT.ap[:, m, :], t.ap, G.ap[:, 8 + m, :], ALU.mult, [t.b, G.b], [fT.b])
            for m in range(8):
                ba, bc = 4 + m % 2, 6 + m % 2
                for kc in range(4):
                    self.mm(self.bank(ba), wol.ap[:, kc, m * 128:(m + 1) * 128], lr.ap[:, kc, :], kc == 0, kc == 3, [wol.b, lr.b], [pb[ba]])
                for kc in range(4):
                    self.mm(self.bank(bc), woa.ap[:, kc, m * 128:(m + 1) * 128], at.ap[:, kc, :], kc == 0, kc == 3, [woa.b, at.b], [pb[bc]])
                u1 = tmp[m % 2]; u3 = tmp[2 + m % 2]; u4 = tmp[4 + m % 2]
                self.tt("dve", u1.ap, self.bank(ba), G.ap[:, m, :], ALU.mult, [pb[ba], G.b], [u1.b])
                self.tt("dve", u3.ap, self.bank(bc), G.ap[:, 16 + m, :], ALU.mult, [pb[bc], G.b], [u3.b])
                self.tt("pool", u4.ap, fT.ap[:, m, :], u1.ap, ALU.add, [fT.b, u1.b], [u4.b])
                self.tt("pool", mg.ap[:, m, :], u4.ap, u3.ap, ALU.add, [u4.b, u3.b], [mg.b])
            for m in range(8):
                bo = m % 2
                for kc in range(8):
                    self.mm(self.bank(bo), wo.ap[:, kc, m * 128:(m + 1) * 128], mg.ap[:, kc, :], kc == 0, kc == 7, [wo.b, mg.b], [pb[bo]])
                self.post_chunk(m, bo, fT, sqr, 2)
            self.post_update(l, 1, which, xt, fT, 2)
            self.store_x(xt, ti)

    def p3a_attention(self, l):
        self.phase()
        A, I = self.A, self.I
        kT = A.bf16(2, NTOK); Qt = A.bf16(4, NTOK); Vt = A.bf16(NTOK // 128, 130)
        self.ms("pool", kT.ap[64:128, 0, :], 0.0, [kT.b])
        self.ms("pool", kT.ap[0:64, 1, :], 0.0, [kT.b])
        for h in range(2):
            sl = slice(h * 2560, (h + 1) * 2560)
            self.dma("sp", Qt.ap[:, :, sl], self.Qs.ap[:, :, sl].rearrange("c p t -> p c t"), [], [Qt.b])
        self.dma("act", kT.ap[0:64, 0, :], self.Ks.ap[0:64, :], [], [kT.b])
        self.dma("act", kT.ap[64:128, 1, :], self.Ks.ap[64:128, :], [], [kT.b])
        self.dma("act", Vt.ap, self.Vs.ap.rearrange("(b p) c -> p b c", p=128), [], [Vt.b])
        ckr = A.f32(4, 128); cvr = A.f32(4, 128)
        self.dma("sp", ckr.ap, I["cache_k"][l].rearrange("(b p) c -> p b c", p=128), [], [ckr.b])
        self.dma("sp", cvr.ap, I["cache_v"][l].rearrange("(b p) c -> p b c", p=128), [], [cvr.b])
        ckT = A.bf16(2, 512); cv = A.bf16(4, 2, 65)
        pb = self.pbufs
        for blk in range(4):
            self.tr(self.bank(0)[:, blk * 128:(blk + 1) * 128], ckr.ap[:, blk, :], self.identf.ap, [ckr.b, self.identf.b], [pb[0]])
        self.ms("pool", ckT.ap, 0.0, [ckT.b])
        self.cp("dve", ckT.ap[0:64, 0, :], self.bank(0)[0:64, :], [pb[0]], [ckT.b])
        self.cp("dve", ckT.ap[64:128, 1, :], self.bank(0)[64:128, :], [pb[0]], [ckT.b])
        self.ms("pool", cv.ap[:, :, :, 64:65], 1.0, [cv.b])
        self.cp("dve", cv.ap[:, :, :, 0:64], cvr.ap.rearrange("p b (h c) -> p b h c", h=2), [cvr.b], [cv.b])
        cvf = cv.ap.rearrange("p b h c -> p b (h c)")
        sk = A.f32(8)
        self.dma("sp", sk.ap[64:65, :], I["attn_sink"][l:l + 1, :], [], [sk.b])
        self.act(sk.ap[64:65, :], sk.ap[64:65, :], AF.Exp, [sk.b], [sk.b])
        pT = [A.bf16(TT) for _ in range(4)]
        osb = [A.f32(TT) for _ in range(3)]
        rrow = [A.f32(TT) for _ in range(3)]
        ast = [A.bf16(4, TT), A.bf16(4, TT)]
        segs = [(0, TS, True)] + [(TS + j * TP, TP, False) for j in range(NPB)]
        its = []
        nst = 0
        for (tok0, T, samp) in segs:
            nqb = T // 128
            for qb in range(nqb):
                for kvh in range(2):
                    its.append((tok0, T, samp, nqb, qb, kvh, nst))
                if qb % 4 == 3 or qb == nqb - 1:
                    nst += 1
        npt = [0]

        def front(n):
            tok0, T, samp, nqb, qb, kvh, st_ = its[n]
            q0 = tok0 + qb * 128
            blocks = []
            if samp:
                for nb in (qb - 1, qb, qb + 1):
                    if 0 <= nb < nqb:
                        msk = self.mprev if nb == qb - 1 else (self.mnext if nb == qb + 1 else None)
                        blocks.append((kT.ap[:, kvh, tok0 + nb * 128:tok0 + (nb + 1) * 128], kT.b,
                                       Vt.ap[:, (tok0 // 128) + nb, kvh * 65:(kvh + 1) * 65], Vt.b, msk))
                for cb_ in range(4):
                    blocks.append((ckT.ap[:, kvh, cb_ * 128:(cb_ + 1) * 128], ckT.b, cvf[:, cb_, kvh * 65:(kvh + 1) * 65], cv.b, None))
            else:
                for nb in range(nqb):
                    blocks.append((kT.ap[:, kvh, tok0 + nb * 128:tok0 + (nb + 1) * 128], kT.b,
                                   Vt.ap[:, (tok0 // 128) + nb, kvh * 65:(kvh + 1) * 65], Vt.b, None))
            po = 3 + n % 3
            rhs_q = Qt.ap[:, :, q0:q0 + 128]
            pts = []
            for bi_, (kap, kb, vap, vb, msk) in enumerate(blocks):
                sb_ = npt[0] % 3
                p_ = pT[npt[0] % 4]
                npt[0] += 1
                self.mm(view(self.bank(sb_), (4, 128)), kap, rhs_q, True, True, [kb, Qt.b], [pb[sb_]])
                self.act(p_.ap, self.bank(sb_), AF.Exp, [pb[sb_]], [p_.b], scale=0.125)
                if msk is not None:
                    m_ = msk.ap
                    mb = mkap(m_.tensor, m_.offset, [list(m_.ap[0]), [0, 4], [1, 128]])
                    self.tt("pool", view(p_.ap, (4, 128)), view(p_.ap, (4, 128)), mb, ALU.mult, [p_.b, msk.b], [p_.b])
                pts.append((p_, vap, vb))
                if bi_ >= 1:
                    pp_, vap_, vb_ = pts[bi_ - 1]
                    self.mm(self.bank(po)[0:65, :], vap_, pp_.ap, bi_ == 1, False, [vb_, pp_.b], [pb[po]])
            pp_, vap_, vb_ = pts[-1]
            self.mm(self.bank(po)[0:65, :], vap_, pp_.ap, len(pts) == 1, True, [vb_, pp_.b], [pb[po]])

        def back1(n):
            tok0, T, samp, nqb, qb, kvh, st_ = its[n]
            po = 3 + n % 3
            o_ = osb[n % 3]; r_ = rrow[n % 3]
            self.cp("act", o_.ap[0:65, :], self.bank(po)[0:65, :], [pb[po]], [o_.b])
            s_ = sk.ap[64:65, kvh * 4:(kvh + 1) * 4]
            sbc = mkap(s_.tensor, s_.offset, [list(s_.ap[0]), [1, 4], [0, 128]])
            self.tt("dve", view(r_.ap[64:65, :], (4, 128)), view(o_.ap[64:65, :], (4, 128)), sbc, ALU.add, [o_.b, sk.b], [r_.b])
            self.recip(r_.ap[64:65, :], r_.ap[64:65, :], [r_.b], [r_.b])

        def back2(n):
            tok0, T, samp, nqb, qb, kvh, st_ = its[n]
            q0 = tok0 + qb * 128
            hs = slice(kvh * 64, (kvh + 1) * 64)
            a_t = ast[st_ % 2]
            qcol = (qb % 4) * 128
            o_ = osb[n % 3]; r_ = rrow[n % 3]
            bb = 6 + n % 2
            self.mm(self.bank(bb)[0:64, :], self.onesf.ap[64:65, 0:64], r_.ap[64:65, :], True, True, [self.onesf.b, r_.b], [pb[bb]])
            self.tt("dve", a_t.ap[hs, :, qcol:qcol + 128], view(o_.ap[0:64, :], (4, 128)), view(self.bank(bb)[0:64, :], (4, 128)),
                    ALU.mult, [o_.b, pb[bb]], [a_t.b])
            if kvh == 1 and (qb % 4 == 3 or qb == nqb - 1):
                nn = (qb % 4 + 1) * 128
                t0 = q0 + 128 - nn
                self.dma("sp", self.ATs.ap[:, :, t0:t0 + nn].rearrange("c p t -> p c t"), a_t.ap[:, :, 0:nn], [a_t.b], [])

        N = len(its)
        for n in range(N + 2):
            if n < N:
                front(n)
            if 0 <= n - 1 < N:
                back1(n - 1)
            if 0 <= n - 2 < N:
                back2(n - 2)

    def p3b_lru(self, l):
        self.phase()
        A, I, O = self.A, self.I, self.O
        pb = self.pbufs
        cw = A.f32(4, 4); cb = A.f32(4); onec = A.f32(1)
        src, sb_ = self.vecT(None, I["w_conv"][l].rearrange("j (c p) -> (j c) p", p=128), 16, None)
        self.cp("dve", cw.ap.rearrange("p c j -> p j c"), view(src, (4, 4)), [sb_], [cw.b])
        src, sb_ = self.vecT(None, I["b_conv"][l].rearrange("(c p) -> c p", p=128), 4, None)
        self.cp("dve", cb.ap, src, [sb_], [cb.b])
        self.ms("dve", onec.ap, 1.0, [onec.b])
        ba = A.f32(2, 4); bx = A.f32(2, 4); cl = A.f32(2, 4); st0 = A.f32(2, 4)
        for (t_, nm) in ((ba, "b_lru_a"), (bx, "b_lru_x"), (cl, "lru_lambda"), (st0, "state_lru")):
            src, sb_ = self.vecT(None, I[nm][l].rearrange("d (c p) -> (d c) p", p=128), 8, None)
            self.cp("dve", t_.ap.rearrange("p d c -> p (d c)"), src, [sb_], [t_.b])
        self.act(cl.ap, cl.ap, AF.Exp, [cl.b], [cl.b], scale=-1.0)
        self.act(cl.ap, cl.ap, AF.Ln, [cl.b, onec.b], [cl.b], bias=onec.ap[:, 0:1])
        self.ts("dve", cl.ap, cl.ap, -8.0, None, ALU.mult, None, [cl.b], [cl.b])
        cl2 = A.f32(2, 4)
        self.ts("dve", cl2.ap, cl.ap, 2.0, None, ALU.mult, None, [cl.b], [cl2.b])
        BD = {}
        for d in range(2):
            for nm in ("w_lru_a", "w_lru_x"):
                t_ = A.bf16(4, 128)
                self.ms("pool", t_.ap, 0.0, [t_.b])
                fns = []
                for c in range(4):
                    fns.append(lambda e, o_=t_.ap[0:64, c, 0:64], i_=I[nm][l, d, 2 * c]: e.dma_start(out=o_, in_=i_))
                    fns.append(lambda e, o_=t_.ap[64:128, c, 64:128], i_=I[nm][l, d, 2 * c + 1]: e.dma_start(out=o_, in_=i_))
                self.P.dma_group("pool", fns, [], [t_.b])
                BD[(d, nm)] = t_
        hf = A.f32(4, TS); xc = A.f32(4, TS)
        xlt = A.f32(4, TT + 3); xcb = A.bf16(4, TT)
        R = A.f32(4, TT)
        IIs = [A.f32(4, TT), A.f32(4, TT)]
        AAs = [A.f32(4, TT), A.f32(4, TT)]
        hbt = A.f32(4, TT); carry = A.f32(4)
        ylts = [A.bf16(4, TT), A.bf16(4, TT)]
        fin = A.f32(NPB, 2, 4)
        segs = [(0, TS, True)] + [(TS + j * TP, TP, False) for j in range(NPB)]
        gk = [0]
        xcbufs = [Buf() for _ in range(8)]

        def rev(ap, n):
            return mkap(ap.tensor, ap.offset + n - 1, [list(ap.ap[0]), [-1, n]])

        def gates(d, t0l, n):
            AA = AAs[gk[0] % 2]; II = IIs[gk[0] % 2]
            gk[0] += 1
            SQ = R
            for c in range(4):
                self.cp("pool", xcb.ap[:, c, 0:n], xc.ap[:, c, t0l:t0l + n], [xcbufs[t0l // n]], [xcb.b])
            for c in range(4):
                self.mm(self.bank(c)[:, 0:n], BD[(d, "w_lru_a")].ap[:, c, :], xcb.ap[:, c, 0:n], True, True, [BD[(d, "w_lru_a")].b, xcb.b], [pb[c]])
                self.mm(self.bank(4 + c)[:, 0:n], BD[(d, "w_lru_x")].ap[:, c, :], xcb.ap[:, c, 0:n], True, True, [BD[(d, "w_lru_x")].b, xcb.b], [pb[4 + c]])
            for c in range(4):
                self.act(R.ap[:, c, 0:n], self.bank(c)[:, 0:n], AF.Sigmoid, [pb[c], ba.b], [R.b], bias=ba.ap[:, d, c:c + 1])
            for c in range(4):
                self.act(II.ap[:, c, 0:n], self.bank(4 + c)[:, 0:n], AF.Sigmoid, [pb[4 + c], bx.b], [II.b], bias=bx.ap[:, d, c:c + 1])
            for c in range(4):
                self.tt("pool", II.ap[:, c, 0:n], II.ap[:, c, 0:n], xc.ap[:, c, t0l:t0l + n], ALU.mult, [II.b, xcbufs[t0l // n]], [II.b])
            for c in range(4):
                self.act(AA.ap[:, c, 0:n], R.ap[:, c, 0:n], AF.Exp, [R.b, cl.b], [AA.b], scale=cl.ap[:, d, c:c + 1])
            for c in range(4):
                self.act(SQ.ap[:, c, 0:n], R.ap[:, c, 0:n], AF.Exp, [R.b, cl2.b], [SQ.b], scale=cl2.ap[:, d, c:c + 1])
            for c in range(4):
                self.act(SQ.ap[:, c, 0:n], SQ.ap[:, c, 0:n], AF.Sqrt, [SQ.b, onec.b], [SQ.b], scale=-1.0, bias=onec.ap[:, 0:1])
            for c in range(4):
                self.tt("dve", II.ap[:, c, 0:n], II.ap[:, c, 0:n], SQ.ap[:, c, 0:n], ALU.mult, [II.b, SQ.b], [II.b])
            return AA, II

        for si, (tok0, T, samp) in enumerate(segs):
            n = min(TT, T)
            ntl = T // n

            def load_conv(tl):
                t0l = tl * n
                lo = max(t0l - 2, 0); hi = min(t0l + n + 1, T)
                if lo > t0l - 2:
                    self.ms("dve", xlt.ap[:, :, 0:2], 0.0, [xlt.b])
                if hi < t0l + n + 1:
                    self.ms("dve", xlt.ap[:, :, n + 2:n + 3], 0.0, [xlt.b])
                self.dma("sp", xlt.ap[:, :, lo - (t0l - 2):hi - (t0l - 2)], self.XLs.ap[:, :, tok0 + lo:tok0 + hi].rearrange("c p t -> p c t"), [], [xlt.b])
                for c in range(4):
                    o_ = xc.ap[:, c, t0l:t0l + n]
                    self.act(o_, xlt.ap[:, c, 0:n], AF.Identity, [xlt.b, cw.b, cb.b], [xcbufs[tl]], scale=cw.ap[:, c, 0:1], bias=cb.ap[:, c:c + 1])
                    for j in range(1, 4):
                        self.stt(o_, xlt.ap[:, c, j:j + n], cw.ap[:, c, j:j + 1], o_, ALU.mult, ALU.add, [xlt.b, cw.b, xcbufs[tl]], [xcbufs[tl]])

            load_conv(0)
            for tl in range(ntl):
                t0l = tl * n
                if tl + 1 < ntl:
                    load_conv(tl + 1)
                AA, II = gates(0, t0l, n)
                for c in range(4):
                    if tl == 0:
                        init = st0.ap[:, 0, c:c + 1] if samp else 0.0
                    else:
                        init = hf.ap[:, c, t0l - 1:t0l]
                    self.P.op("dve", lambda e, o=hf.ap[:, c, t0l:t0l + n], a=AA.ap[:, c, 0:n], b=II.ap[:, c, 0:n], i0=init:
                              e.tensor_tensor_scan(out=o, data0=a, data1=b, initial=i0, op0=ALU.mult, op1=ALU.add),
                              [AA.b, II.b, hf.b, st0.b], [hf.b])
            if not samp:
                self.cp("dve", fin.ap[:, si - 1, 0, :], hf.ap[:, :, T - 1], [hf.b], [fin.b])
            def load_yl(tl):
                y_ = ylts[tl % 2]
                t0l_ = tl * n
                self.dma("sp", y_.ap[:, :, 0:n], self.YLs.ap[:, :, tok0 + t0l_:tok0 + t0l_ + n].rearrange("c p t -> p c t"), [], [y_.b])
                for c in range(4):
                    self.act(y_.ap[:, c, 0:n], y_.ap[:, c, 0:n], AF.Gelu_apprx_tanh, [y_.b], [y_.b])

            load_yl(ntl - 1)
            for tl in reversed(range(ntl)):
                t0l = tl * n
                cur = hbt
                ylt = ylts[tl % 2]
                AA, II = gates(1, t0l, n)
                if tl - 1 >= 0:
                    load_yl(tl - 1)
                for c in range(4):
                    if tl == ntl - 1:
                        init = st0.ap[:, 1, c:c + 1] if samp else 0.0
                    else:
                        init = carry.ap[:, c:c + 1]
                    self.P.op("dve", lambda e, o=rev(cur.ap[:, c, 0:n], n), a=rev(AA.ap[:, c, 0:n], n), b=rev(II.ap[:, c, 0:n], n), i0=init:
                              e.tensor_tensor_scan(out=o, data0=a, data1=b, initial=i0, op0=ALU.mult, op1=ALU.add),
                              [AA.b, II.b, carry.b, st0.b], [cur.b])
                self.cp("dve", carry.ap, cur.ap[:, :, 0], [cur.b], [carry.b])
                if not samp and tl == 0:
                    self.cp("dve", fin.ap[:, si - 1, 1, :], cur.ap[:, :, 0], [cur.b], [fin.b])
                for c in range(4):
                    self.tt("dve", cur.ap[:, c, 0:n], cur.ap[:, c, 0:n], hf.ap[:, c, t0l:t0l + n], ALU.add, [cur.b, hf.b], [cur.b])
                    self.tt("dve", ylt.ap[:, c, 0:n], cur.ap[:, c, 0:n], ylt.ap[:, c, 0:n], ALU.mult, [cur.b, ylt.b], [ylt.b])
                self.dma("sp", self.LRs.ap[:, :, tok0 + t0l:tok0 + t0l + n].rearrange("c p t -> p c t"), ylt.ap[:, :, 0:n], [ylt.b], [])
        self.tr(self.bank(0)[0:32, 0:128], fin.ap.rearrange("p s d c -> p (s d c)"), self.identf.ap, [fin.b, self.identf.b], [pb[0]])
        fo = A.f32(128)
        self.cp("dve", fo.ap[0:32, :], self.bank(0)[0:32, 0:128], [pb[0]], [fo.b])
        for s_ in range(NPB):
            self.dma("sp", O["nlru"][s_, l].rearrange("d (c p) -> (d c) p", p=128), fo.ap[s_ * 8:(s_ + 1) * 8, :], [fo.b], [])

    def p3c_s5(self, l):
        self.phase()
        A, I, O = self.A, self.I, self.O
        pb = self.pbufs
        Qw = A.bf16(2, 16, 2, 128); ML = A.bf16(32, 128)
        self.dma("sp", Qw.ap.rearrange("p d g c x -> p (d g c x)"), self.Qd[l].ap, [], [Qw.b])
        self.dma("sp", ML.ap.rearrange("p g x -> p (g x)"), self.MLd[l].ap, [], [ML.b])
        a12 = A.f32(2, 2, 16, 2)
        self.dma("sp", a12.ap.rearrange("p d k g c -> p (d k g c)"), self.A12[l].ap, [], [a12.b])
        h0r = A.f32(2, 128); h0s = A.f32(2, 16, 2)
        for d in range(2):
            self.dma("sp", h0r.ap[0:32, d, :], I["state_ssm"][l, d].rearrange("c (gp g2) n -> (c gp) (g2 n)", g2=2), [], [h0r.b])
            self.tr(self.bank(0)[:, d * 32:(d + 1) * 32], h0r.ap[0:32, d, :], self.identf.ap[0:32, 0:32], [h0r.b, self.identf.b], [pb[0]])
            self.cp("dve", h0s.ap[:, d].rearrange("p g c -> p c g"), view(self.bank(0)[:, d * 32:(d + 1) * 32], (2, 16)), [pb[0]], [h0s.b])
        zero = A.f32(16, 2, 4)
        self.ms("dve", zero.ap, 0.0, [zero.b])
        fin = A.f32(NPB, 2, 2, 16)
        mark0 = A.top
        for (tile0, ntile, nseq, Kseq, KB, tokbase) in ((0, 8, 1, 512, 256, 0), (8, 2, 4, 32, 128, TS)):
            A.top = mark0
            self.P.barrier()
            K = nseq * Kseq
            nblk = K // KB
            Uf = A.bf16(ntile, 32, 64)
            self.dma("sp", Uf.ap.rearrange("p t g k -> p t (g k)"), self.UFs.ap[tile0:tile0 + ntile].rearrange("t p x -> p t x"), [], [Uf.b])
            Hbf = [A.bf16(16, 2, nseq, Kseq + 1), A.bf16(16, 2, nseq, Kseq + 1)]
            mark1 = A.top
            PF = A.bf16(2, 16, 2, 128)
            self.dma("sp", PF.ap.rearrange("p d g c x -> p (d g c x)"), self.PFd[l].ap, [], [PF.b])
            Sd = [A.f32(16, 2, KB), A.f32(16, 2, KB)]
            tA = [A.f32(16, 2, nseq), A.f32(16, 2, nseq)]
            tB = [A.f32(16, 2, nseq), A.f32(16, 2, nseq)]
            hinit = [A.f32(16, 2, nseq), A.f32(16, 2, nseq)]
            tpb = KB // 64
            spb = KB // Kseq if nseq > 1 else 1
            for d in range(2):
                eng = "dve" if d == 0 else "pool"
                S = Sd[d]
                if nseq == 1:
                    self.cp(eng, hinit[d].ap[:, :, :, 0], h0s.ap[:, d], [h0s.b], [hinit[d].b])
                    self.cp("act", Hbf[d].ap[:, :, :, 0, 0 if d == 0 else Kseq], h0s.ap[:, d], [h0s.b], [Hbf[d].b])
                else:
                    self.ms(eng, hinit[d].ap, 0.0, [hinit[d].b])
                    self.ms(eng, Hbf[d].ap[:, :, :, :, 0 if d == 0 else Kseq], 0.0, [Hbf[d].b])
            for bidx in range(nblk):
                for d in range(2):
                    eng = "dve" if d == 0 else "pool"
                    S = Sd[d]
                    blk = bidx if d == 0 else nblk - 1 - bidx
                    for gp_ in range(16):
                        for c in range(2):
                            bi = (0 if d == 0 else 4) + (gp_ * 2 + c) % 4
                            for g2 in range(2):
                                g = 2 * gp_ + g2
                                self.mm(view(self.bank(bi)[g2 * 64:(g2 + 1) * 64, 0:KB], (tpb, 64)),
                                        PF.ap[:, d, gp_, c, g2 * 64:(g2 + 1) * 64],
                                        Uf.ap[:, blk * tpb:(blk + 1) * tpb, g, :], True, True, [PF.b, Uf.b], [pb[bi]])
                            self.cp("act", S.ap[:, gp_, c, :], self.bank(bi)[:, 0:KB], [pb[bi]], [S.b])
                for d in range(2):
                    eng = "dve" if d == 0 else "pool"
                    S = Sd[d]
                    blk = bidx if d == 0 else nblk - 1 - bidx
                    first_blk = bidx == 0
                    s_ = S.ap
                    base = s_.offset
                    pp = list(s_.ap[0])
                    nk = Kseq if nseq > 1 else KB

                    def col(kk, swap=False):
                        if swap:
                            return mkap(s_.tensor, base + KB + kk, [pp, [2 * KB, 16], [-KB, 2], [Kseq, spb]])
                        return mkap(s_.tensor, base + kk, [pp, [2 * KB, 16], [KB, 2], [Kseq, spb]])

                    def hv(t, swap=False):
                        a = t.ap
                        if swap:
                            return mkap(a.tensor, a.offset + nseq, [list(a.ap[0]), [2 * nseq, 16], [-nseq, 2], [1, spb]])
                        return mkap(a.tensor, a.offset, [list(a.ap[0]), [2 * nseq, 16], [nseq, 2], [1, spb]])

                    a1 = a12.ap[:, d, 0]
                    a2 = a12.ap[:, d, 1]
                    A1 = mkap(a1.tensor, a1.offset, [list(a1.ap[0]), [2, 16], [1, 2], [0, spb]])
                    A2 = mkap(a2.tensor, a2.offset, [list(a2.ap[0]), [2, 16], [1, 2], [0, spb]])
                    order = range(nk) if d == 0 else reversed(range(nk))
                    prev = None
                    for kk in order:
                        if prev is None:
                            if first_blk or nseq > 1:
                                pv_, psw = hv(hinit[d]), hv(hinit[d], True)
                                rdx = [hinit[d].b]
                            else:
                                pv_, psw = hv(hinit[d]), hv(hinit[d], True)
                                rdx = [hinit[d].b]
                        else:
                            pv_, psw = col(prev), col(prev, True)
                            rdx = []
                        ta, tb = tA[d], tB[d]
                        self.tt(eng, hv(ta), A1, pv_, ALU.mult, [a12.b, S.b] + rdx, [ta.b])
                        self.tt(eng, hv(tb), A2, psw, ALU.mult, [a12.b, S.b] + rdx, [tb.b])
                        self.tt(eng, hv(ta), hv(ta), hv(tb), ALU.add, [ta.b, tb.b], [ta.b])
                        self.tt(eng, col(kk), col(kk), hv(ta), ALU.add, [S.b, ta.b], [S.b])
                        prev = kk
                    if nseq == 1:
                        self.cp(eng, hv(hinit[d]), col(prev), [S.b], [hinit[d].b])
                    else:
                        for sq_ in range(spb):
                            seq = blk * spb + sq_
                            kcol = sq_ * Kseq + (Kseq - 1 if d == 0 else 0)
                            self.cp(eng, fin.ap[:, seq, d, :, :], S.ap[:, :, :, kcol].rearrange("p g c -> p c g"), [S.b], [fin.b])
                    for c in range(2):
                        if nseq == 1:
                            o0 = blk * KB + (1 if d == 0 else 0)
                            self.cp("act", Hbf[d].ap[:, :, c, 0, o0:o0 + KB], S.ap[:, :, c, :], [S.b], [Hbf[d].b])
                        else:
                            o0 = 1 if d == 0 else 0
                            self.cp("act", Hbf[d].ap[:, :, c, blk * spb:(blk + 1) * spb, o0:o0 + Kseq],
                                    S.ap[:, :, c, :].rearrange("p g (s k) -> p g s k", k=Kseq), [S.b], [Hbf[d].b])
            self.P.barrier()
            A.top = mark1
            Yf = A.bf16(32, K)
            Ytok = A.bf16(8, 512)
            YT = A.bf16(4, 1024)
            for g in range(32):
                gp_, g2 = g // 2, g % 2
                bi = g % 4
                hs = slice(g2 * 64, (g2 + 1) * 64)
                out = view(self.bank(bi)[:, 0:K], (ntile, 64))
                self.mm(out, ML.ap[:, g, :], Uf.ap[:, :, g, :], True, False, [ML.b, Uf.b], [pb[bi]])
                for d in range(2):
                    for c in range(2):
                        o0 = 0 if d == 0 else 1
                        self.mm(view(self.bank(bi)[:, 0:K], (nseq, Kseq)), Qw.ap[hs, d, gp_, c, :], Hbf[d].ap[hs, gp_, c, :, o0:o0 + Kseq],
                                False, d == 1 and c == 1, [Qw.b, Hbf[d].b], [pb[bi]])
                self.act(Yf.ap[:, g, :], self.bank(bi)[:, 0:K], AF.Gelu_apprx_tanh, [pb[bi]], [Yf.b])
            for kb in range(K // 128):
                for q4 in range(4):
                    bi = 4 + q4 % 2
                    pbf = self.bank(bi).bitcast(BF16)
                    for gg in range(8):
                        g = q4 * 8 + gg
                        self.tr(pbf[:, gg * 128:(gg + 1) * 128], Yf.ap[:, g, kb * 128:(kb + 1) * 128], self.identb.ap, [Yf.b, self.identb.b], [pb[bi]])
                    self.cp("dve" if q4 % 2 else "act", Ytok.ap[:, :, q4 * 128:(q4 + 1) * 128].rearrange("p t (g c) -> p g t c", c=16),
                            pbf[:, 0:1024].rearrange("p (g t c) -> p g t c", g=8, t=8), [pb[bi]], [Ytok.b])
                for t in range(8):
                    bi = 6 + t % 2
                    pbf = self.bank(bi).bitcast(BF16)
                    for cc in range(4):
                        self.tr(pbf[:, cc * 128:(cc + 1) * 128], Ytok.ap[:, t, cc * 128:(cc + 1) * 128], self.identb.ap, [Ytok.b, self.identb.b], [pb[bi]])
                    self.cp("dve" if t % 2 else "act", YT.ap.rearrange("p c (k s) -> p c s k", s=8)[:, :, t, :],
                            view(pbf[:, 0:512], (4, 128)), [pb[bi]], [YT.b])
                t0 = tokbase + kb * 1024
                self.dma("sp", self.SYs.ap[:, :, t0:t0 + 1024].rearrange("c p t -> p c t"), YT.ap, [YT.b], [])
        fo = A.f32(2, 128)
        ff = fin.ap.rearrange("p s d c g -> p (s d c g)")
        for h in range(2):
            self.tr(self.bank(h)[:, 0:128], ff[:, h * 128:(h + 1) * 128], self.identf.ap, [fin.b, self.identf.b], [pb[h]])
            self.cp("dve", fo.ap[:, h, :], self.bank(h)[:, 0:128], [pb[h]], [fo.b])
        for seq in range(NPB):
            h, r0 = seq // 2, (seq % 2) * 64
            self.dma("sp", O["nssm"][seq, l].rearrange("d c (gp g2) n -> (d c gp) (g2 n)", g2=2), fo.ap[r0:r0 + 64, h, :], [fo.b], [])


def build(dbg=False, stages=None):
    K = Kern(dbg)
    on = lambda nm: stages is None or nm in stages
    with K.st:
        if on("pro"):
            K.prologue()
        for l in range(L):
            if on("ffa%d" % l):
                K.ffn_pass2(l, 0, first=(l == 0))
            if on("p2%d" % l):
                K.p2_pass(l)
            if on("p3a%d" % l):
                K.p3a_attention(l)
            if on("p3b%d" % l):
                K.p3b_lru(l)
            if on("p3c%d" % l):
                K.p3c_s5(l)
            if on("p4%d" % l):
                K.p4_pass(l)
            if on("ffb%d" % l):
                K.ffn_pass2(l, 1, last=(l == L - 1))
        K.P.barrier()
        K.P.op("sp", lambda e: e.nop(), [], [])
        K.P.emit()
    return K


def _perm_q():
    idx = []
    for c in range(4):
        for h in (c, 4 + c):
            idx.extend(range(h * 64, (h + 1) * 64))
    return np.array(idx)


def _partner():
    p = np.zeros(64, np.int64)
    for d in range(64):
        p[d] = d + 16 if (d % 32) < 16 else d - 16
    return p


def _consts():
    cst = np.zeros((128, 5, 128), np.float32)
    j = np.arange(128)[:, None]
    i = np.arange(128)[None, :]
    cst[:, 0] = (j == i)
    cst[:, 1] = (j >= i)
    cst[:, 2] = (j <= i)
    cst[:, 3] = ((j // 16) <= (i // 16))
    cst[:, 4] = ((j // 16) >= (i // 16))
    t = np.arange(TS)
    row = (t // 64).astype(np.float64)
    colp = (t % 64).astype(np.float64)
    inv = 1.0 / (10000.0 ** (np.arange(16, dtype=np.float64) / 16))
    cos = np.zeros((64, TS)); sin = np.zeros((64, TS))
    for d in range(64):
        pos = row if d < 32 else colp
        ang = (pos.astype(np.float32) * np.float32(inv[d % 16]).astype(np.float32)).astype(np.float32)
        cos[d] = np.cos(ang)
        sgn = -1.0 if (d % 32) < 16 else 1.0
        sin[d] = sgn * np.sin(ang)
    rope = np.zeros((2, 128, TS), np.float32)
    rope[0, :64] = cos; rope[0, 64:] = cos
    rope[1, :64] = sin; rope[1, 64:] = sin
    return cst.reshape(128, 640), rope


_CACHE = {}


def kernel(**inp):
    f = lambda a: np.ascontiguousarray(np.asarray(a, dtype=np.float32))
    if "K" not in _CACHE:
        _CACHE["K"] = build()
    K = _CACHE["K"]
    pq = _perm_q()
    part = _partner()
    w_in = f(inp["w_in"])
    q_cols = pq
    qs_cols = np.array([(c // 64) * 64 + part[c % 64] for c in pq])
    k_cols = 512 + np.arange(128)
    ks_cols = 512 + np.array([(c // 64) * 64 + part[c % 64] for c in range(128)])
    rest = np.arange(640, 5376)
    cols = np.concatenate([q_cols, k_cols, qs_cols, ks_cols, rest])
    w_in_p = np.ascontiguousarray(w_in[:, :, cols])
    w_o_attn_p = np.ascontiguousarray(f(inp["w_o_attn"])[:, pq, :])
    cst, rope = _consts()
    shared = {k: f(inp[k]) for k in ("w_mod", "b_mod", "g_pre", "g_post", "w_ffn_gate", "w_ffn_up", "w_ffn_down", "w_conv",
                                     "b_conv", "w_lru_a", "b_lru_a", "w_lru_x", "b_lru_x", "lru_lambda", "s5_lambda_re",
                                     "s5_lambda_im", "s5_log_step", "s5_b_re", "s5_b_im", "s5_c_re", "s5_c_im", "s5_d",
                                     "w_glu", "attn_sink", "w_o_lru", "w_out")}
    shared["w_in_p"] = w_in_p
    shared["w_o_attn_p"] = w_o_attn_p
    shared["cst"] = cst
    shared["rope"] = rope
    xs = f(inp["x_sample"]); xp = f(inp["x_prompt"]); c = f(inp["c"]); cctx = f(inp["c_ctx"])
    ck = f(inp["cache_k"]); cv = f(inp["cache_v"]); sl = f(inp["state_lru"]); ss = f(inp["state_ssm"])
    in_maps = []
    for b in range(8):
        m = dict(shared)
        m["xin"] = np.ascontiguousarray(np.concatenate([xs[b], xp[4 * b:4 * b + 4].reshape(NPB * TP, D)], axis=0))
        m["cc"] = np.ascontiguousarray(np.stack([c[b], cctx], axis=0))
        m["cache_k"] = np.ascontiguousarray(ck[b].reshape(L, 512, 128))
        m["cache_v"] = np.ascontiguousarray(cv[b].reshape(L, 512, 128))
        m["state_lru"] = np.ascontiguousarray(sl[b])
        m["state_ssm"] = np.ascontiguousarray(ss[b])
        in_maps.append(m)
    res = run_bass_kernel_spmd(K.nc, in_maps, core_ids=list(range(8)))
    _CACHE["res"] = res
    R = res.results
    y_s = np.stack([R[b]["y"][:TS] for b in range(8)], axis=0)
    y_p = np.concatenate([R[b]["y"][TS:].reshape(NPB, TP, D) for b in range(8)], axis=0)
    nk = np.concatenate([R[b]["nk"].reshape(NPB, L, TP, 2, 64) for b in range(8)], axis=0)
    nv = np.concatenate([R[b]["nv"].reshape(NPB, L, TP, 2, 64) for b in range(8)], axis=0)
    nl = np.concatenate([R[b]["nlru"] for b in range(8)], axis=0)
    ns = np.concatenate([R[b]["nssm"] for b in range(8)], axis=0)
    return (y_p.astype(np.float32), y_s.astype(np.float32), nk.astype(np.float32), nv.astype(np.float32),
            nl.astype(np.float32), ns.astype(np.float32))
```
